# Optimizing a Trainium2 kernel written in Bass

```python
import math
import jax, jax.numpy as jnp
from jax import lax
import numpy as np

D_MODEL = 2048
BATCH = 4
SEQ = 2048
DEPTH = 4

N_REC = (DEPTH + 1) // 2
N_ATT = DEPTH // 2

RG_WIDTH = 2048
RG_HEADS = 8
RG_BLOCK = RG_WIDTH // RG_HEADS
CONV_WIDTH = 4
RG_C = 8.0

HG_HEADS = 16
HG_DK = 128
HG_DV = 128
HG_QK = HG_HEADS * HG_DK
HG_WIDTH = HG_HEADS * HG_DV
HG_CHUNK = 64

REC_IN = 2 * RG_WIDTH + 2 * HG_QK + 2 * HG_WIDTH
REC_OUT = RG_WIDTH + HG_WIDTH
REC_SPLITS = (RG_WIDTH, 2 * RG_WIDTH, 2 * RG_WIDTH + HG_QK,
              2 * RG_WIDTH + 2 * HG_QK, 2 * RG_WIDTH + 2 * HG_QK + HG_WIDTH)

ATT_HEADS = 8
ATT_DH = 128
ATT_QK = ATT_HEADS * 2 * ATT_DH
ATT_WIDTH = ATT_HEADS * 2 * ATT_DH
ATT_IN = 2 * ATT_QK + 2 * ATT_WIDTH
ATT_SPLITS = (ATT_QK, 2 * ATT_QK, 2 * ATT_QK + ATT_WIDTH)
ATT_BLOCK = 128
ROPE_DIM = ATT_DH // 4
ROPE_THETA = 500000.0

NORM_EPS = 1e-6
SUBLN_EPS = 1e-5

kernel_name = 'hybrid_rglru_hgrn2_diffattn'


def rms_norm(x, g, eps=NORM_EPS):
    xf = x.astype(jnp.float32)
    y = xf * lax.rsqrt(jnp.mean(xf * xf, axis=-1, keepdims=True) + eps)
    return (y * g.astype(jnp.float32)).astype(x.dtype)


def causal_depthwise_conv(x, w, b):
    s = x.shape[1]
    xp = jnp.pad(x, ((0, 0), (CONV_WIDTH - 1, 0), (0, 0)))
    y = b
    for tap in range(CONV_WIDTH):
        y = y + xp[:, tap:tap + s] * w[tap]
    return y


def rg_lru(x, w_a, b_a, w_x, b_x, lam):
    bsz, s, _ = x.shape
    xb = x.reshape(bsz, s, RG_HEADS, RG_BLOCK)
    r = jax.nn.sigmoid(jnp.einsum('bshi,hij->bshj', xb, w_a).reshape(bsz, s, RG_WIDTH) + b_a)
    i = jax.nn.sigmoid(jnp.einsum('bshi,hij->bshj', xb, w_x).reshape(bsz, s, RG_WIDTH) + b_x)
    log_a = -RG_C * r.astype(jnp.float32) * jax.nn.softplus(-lam.astype(jnp.float32))
    a = jnp.exp(log_a)
    u = jnp.sqrt(-jnp.expm1(2.0 * log_a)) * (i * x).astype(jnp.float32)

    def combine(c1, c2):
        a1, u1 = c1
        a2, u2 = c2
        return a1 * a2, a2 * u1 + u2

    _, h = lax.associative_scan(combine, (a, u), axis=1)
    return h.astype(x.dtype)


def hgrn2_chunkwise(q, k, v, log_f):
    bsz, s, h, dk = q.shape
    dv = v.shape[-1]
    nc = s // HG_CHUNK

    def chunks(t):
        return t.reshape(bsz, nc, HG_CHUNK, h, t.shape[-1]).transpose(1, 0, 3, 2, 4)

    mask = jnp.tril(jnp.ones((HG_CHUNK, HG_CHUNK), dtype=bool))[:, :, None]

    def step(state, inp):
        qc, kc, vc, gc = inp
        b = jnp.cumsum(gc, axis=2)
        o_inter = jnp.einsum('bhtk,bhkv->bhtv', qc * jnp.exp(b), state)
        decay = jnp.exp(jnp.where(mask, b[:, :, :, None, :] - b[:, :, None, :, :], -jnp.inf))
        scores = jnp.einsum('bhtk,bhsk,bhtsk->bhts', qc, kc, decay)
        o = o_inter + jnp.einsum('bhts,bhsv->bhtv', scores, vc)
        b_last = b[:, :, -1:, :]
        state = (jnp.exp(b_last[:, :, 0, :, None]) * state
                 + jnp.einsum('bhsk,bhsv->bhkv', kc * jnp.exp(b_last - b), vc))
        return state, o

    s0 = jnp.zeros((bsz, h, dk, dv), jnp.float32)
    _, o = lax.scan(step, s0, (chunks(q), chunks(k), chunks(v), chunks(log_f)))
    return o.transpose(1, 0, 3, 2, 4).reshape(bsz, s, h, dv)


def partial_rope(x, pos):
    half = ROPE_DIM // 2
    inv_freq = ROPE_THETA ** (-jnp.arange(0, ROPE_DIM, 2, dtype=jnp.float32) / ROPE_DIM)
    ang = pos.astype(jnp.float32)[:, None] * inv_freq
    cos = jnp.cos(ang)[:, None, None, :]
    sin = jnp.sin(ang)[:, None, None, :]
    xr = x[..., :ROPE_DIM].astype(jnp.float32)
    x1, x2 = xr[..., :half], xr[..., half:]
    rot = jnp.concatenate([x1 * cos - x2 * sin, x2 * cos + x1 * sin], axis=-1)
    return jnp.concatenate([rot.astype(x.dtype), x[..., ROPE_DIM:]], axis=-1)


def diff_attention(q, k, v, lam):
    bsz, s, h, _, dh = q.shape
    nb = s // ATT_BLOCK
    scale = dh ** -0.5
    qb = q.reshape(bsz, nb, ATT_BLOCK, h, 2, dh).transpose(1, 0, 3, 4, 2, 5)
    kt = k.transpose(0, 2, 3, 1, 4)
    vt = v.transpose(0, 2, 1, 3)
    k_pos = jnp.arange(s)

    def one_block(args):
        q_blk, blk = args
        sc = jnp.einsum('bhiqd,bhikd->bhiqk', q_blk, kt).astype(jnp.float32) * scale
        q_pos = blk * ATT_BLOCK + jnp.arange(ATT_BLOCK)
        causal = k_pos[None, :] <= q_pos[:, None]
        p = jax.nn.softmax(jnp.where(causal, sc, -jnp.inf), axis=-1)
        attn = p[:, :, 0] - lam * p[:, :, 1]
        return jnp.einsum('bhqk,bhkv->bhqv', attn.astype(v.dtype), vt)

    o = lax.map(one_block, (qb, jnp.arange(nb)))
    return o.transpose(1, 0, 3, 2, 4).reshape(bsz, s, h, 2 * dh)


def setup_inputs(seed: int = 0) -> dict:
    key = jax.random.key(seed)
    ks = jax.random.split(key, 24)
    f32 = jnp.float32

    def nrm(k, shape, scale):
        return jax.random.normal(k, shape, f32) * scale

    u = jax.random.uniform(ks[10], (N_REC, RG_WIDTH), f32, 0.9, 0.999)
    a0 = u ** (1.0 / RG_C)
    rg_lambda = jnp.log(a0) - jnp.log1p(-a0)
    return {
        'x': nrm(ks[0], (BATCH, SEQ, D_MODEL), 1.0),
        'rec_norm': 1.0 + nrm(ks[1], (N_REC, D_MODEL), 0.01),
        'rec_w_in': nrm(ks[2], (N_REC, D_MODEL, REC_IN), D_MODEL ** -0.5),
        'rg_conv_w': nrm(ks[3], (N_REC, CONV_WIDTH, RG_WIDTH), CONV_WIDTH ** -0.5),
        'rg_conv_b': nrm(ks[4], (N_REC, RG_WIDTH), 0.01),
        'rg_w_gate_a': nrm(ks[5], (N_REC, RG_HEADS, RG_BLOCK, RG_BLOCK), RG_BLOCK ** -0.5),
        'rg_b_gate_a': nrm(ks[6], (N_REC, RG_WIDTH), 0.01),
        'rg_w_gate_x': nrm(ks[7], (N_REC, RG_HEADS, RG_BLOCK, RG_BLOCK), RG_BLOCK ** -0.5),
        'rg_b_gate_x': nrm(ks[8], (N_REC, RG_WIDTH), 0.01),
        'rg_lambda': rg_lambda,
        'hg_lb': nrm(ks[11], (N_REC, HG_QK), 0.5),
        'hg_out_norm': 1.0 + nrm(ks[12], (N_REC, HG_DV), 0.01),
        'rec_w_out': nrm(ks[13], (N_REC, REC_OUT, D_MODEL), REC_OUT ** -0.5),
        'att_norm': 1.0 + nrm(ks[14], (N_ATT, D_MODEL), 0.01),
        'att_w_in': nrm(ks[15], (N_ATT, D_MODEL, ATT_IN), D_MODEL ** -0.5),
        'att_q_norm': 1.0 + nrm(ks[16], (N_ATT, ATT_DH), 0.01),
        'att_k_norm': 1.0 + nrm(ks[17], (N_ATT, ATT_DH), 0.01),
        'att_lambda': nrm(ks[18], (N_ATT, 4, ATT_DH), 0.1),
        'att_sub_norm': 1.0 + nrm(ks[19], (N_ATT, 2 * ATT_DH), 0.01),
        'att_w_out': nrm(ks[20], (N_ATT, ATT_WIDTH, D_MODEL), ATT_WIDTH ** -0.5),
    }


def reference(x, rec_norm, rec_w_in, rg_conv_w, rg_conv_b, rg_w_gate_a, rg_b_gate_a,
              rg_w_gate_x, rg_b_gate_x, rg_lambda, hg_lb, hg_out_norm, rec_w_out,
              att_norm, att_w_in, att_q_norm, att_k_norm, att_lambda, att_sub_norm,
              att_w_out):
    bsz, s, _ = x.shape
    pos = jnp.arange(s)
    lb_all = jnp.cumsum(jax.nn.softmax(hg_lb.astype(jnp.float32), axis=0), axis=0)
    lb_all = lb_all - lb_all[0]

    for layer in range(DEPTH):
        j = layer // 2
        if layer % 2 == 0:
            h = rms_norm(x, rec_norm[j])
            proj = h @ rec_w_in[j]
            rg_x, rg_g, hg_q, hg_f, hg_i, hg_g = jnp.split(proj, REC_SPLITS, axis=-1)
            rg_x = causal_depthwise_conv(rg_x, rg_conv_w[j], rg_conv_b[j])
            rg_h = rg_lru(rg_x, rg_w_gate_a[j], rg_b_gate_a[j], rg_w_gate_x[j],
                          rg_b_gate_x[j], rg_lambda[j])
            y_a = rg_h * jax.nn.silu(rg_g)
            lb = lb_all[j]
            q = (jax.nn.silu(hg_q).astype(jnp.float32) * HG_DK ** -0.5).reshape(bsz, s, HG_HEADS, HG_DK)
            log_f = jnp.logaddexp(jnp.log(lb), jnp.log1p(-lb) + jax.nn.log_sigmoid(hg_f.astype(jnp.float32)))
            k = -jnp.expm1(log_f)
            o = hgrn2_chunkwise(q,
                                k.reshape(bsz, s, HG_HEADS, HG_DK),
                                hg_i.astype(jnp.float32).reshape(bsz, s, HG_HEADS, HG_DV),
                                log_f.reshape(bsz, s, HG_HEADS, HG_DK))
            o = rms_norm(o.astype(x.dtype), hg_out_norm[j])
            y_b = o.reshape(bsz, s, HG_WIDTH) * jax.nn.silu(hg_g)
            y = jnp.concatenate([y_a, y_b], axis=-1) @ rec_w_out[j]
        else:
            h = rms_norm(x, att_norm[j])
            proj = h @ att_w_in[j]
            q, k, v, g = jnp.split(proj, ATT_SPLITS, axis=-1)
            q = partial_rope(rms_norm(q.reshape(bsz, s, ATT_HEADS, 2, ATT_DH), att_q_norm[j]), pos)
            k = partial_rope(rms_norm(k.reshape(bsz, s, ATT_HEADS, 2, ATT_DH), att_k_norm[j]), pos)
            v = v.reshape(bsz, s, ATT_HEADS, 2 * ATT_DH)
            lam_init = 0.8 - 0.6 * math.exp(-0.3 * layer)
            lp = att_lambda[j].astype(jnp.float32)
            lam = jnp.exp(jnp.sum(lp[0] * lp[1])) - jnp.exp(jnp.sum(lp[2] * lp[3])) + lam_init
            o = diff_attention(q, k, v, lam)
            o = rms_norm(o, att_sub_norm[j], SUBLN_EPS) * (1.0 - lam_init)
            y = (o.reshape(bsz, s, ATT_WIDTH) * jax.nn.silu(g)) @ att_w_out[j]
        x = x + y.astype(x.dtype)
    return x
```

```python
import math
from contextlib import ExitStack

import numpy as np
import concourse.bass as bass
import concourse.mybir as mybir
from concourse.bass_utils import run_bass_kernel_spmd

F32 = mybir.dt.float32
BF16 = mybir.dt.bfloat16
AF = mybir.ActivationFunctionType
ALU = mybir.AluOpType

D = 2048
KC = 16
SEQ = 2048
BATCH = 4
TC = 512
NORM_EPS = 1e-6
SUBLN_EPS = 1e-5
SAME_ENGINE_SYNC = True


def _fs(ap):
    try:
        return int(ap.free_size())
    except Exception:
        return 512


RESCHEDULE = True


class Prog:
    ENGS = ("pe", "act", "dve", "pool", "sp")

    def __init__(self, nc):
        self.nc = nc
        self.ops = []

    def add(self, eng, fn, reads=(), writes=(), dma=None, cost=0.5):
        self.ops.append(dict(kind="op", eng=eng, fn=fn, reads=tuple(reads), writes=tuple(writes), dma=dma, cost=cost))

    def barrier(self):
        self.ops.append(dict(kind="barrier"))

    def mm(self, out, lhsT, rhs, start, stop, reads, writes):
        self.add("pe", lambda e: e.matmul(out, lhsT, rhs, start=start, stop=stop), reads, writes,
                 cost=0.035 + _fs(rhs) / 2400.0)

    def tr(self, out, in_, ident, reads, writes):
        self.add("pe", lambda e: e.transpose(out, in_, ident), reads, writes, cost=0.1)

    def act(self, out, in_, func, reads, writes, bias=0.0, scale=1.0, eng="act"):
        self.add(eng, lambda e: e.activation(out, in_, func, bias=bias, scale=scale), reads, writes,
                 cost=0.2 + _fs(out) * 0.0008)

    def ts(self, out, in0, s1, s2, op0, op1, reads, writes, eng="dve"):
        self.add(eng, lambda e: e.tensor_scalar(out, in0, s1, s2, op0, op1), reads, writes, cost=0.15 + _fs(out) * 0.001)

    def stt(self, out, in0, scalar, in1, op0, op1, reads, writes, eng="dve"):
        self.add(eng, lambda e: e.scalar_tensor_tensor(out, in0, scalar, in1, op0, op1), reads, writes,
                 cost=0.2 + _fs(out) * 0.0011)

    def tt(self, out, in0, in1, op, reads, writes, eng="dve"):
        c = 0.15 + _fs(out) * (0.001 if eng == "dve" else 0.0022)
        self.add(eng, lambda e: e.tensor_tensor(out, in0, in1, op), reads, writes, cost=c)

    def cp(self, out, in_, reads, writes, eng="dve"):
        c = 0.15 + _fs(out) * 0.001
        self.add(eng, lambda e: e.tensor_copy(out, in_), reads, writes, cost=c)

    def memset(self, ap, val, writes, eng="dve"):
        self.add(eng, lambda e: e.memset(ap, val), (), writes, cost=0.2)

    def recip(self, out, in_, reads, writes):
        self.add("dve", lambda e: e.reciprocal(out, in_), reads, writes, cost=0.15 + _fs(out) * 0.001)

    def scan(self, out, d0, d1, init, reads, writes):
        self.add("dve", lambda e: e.tensor_tensor_scan(out, d0, d1, init, ALU.mult, ALU.add), reads, writes,
                 cost=0.2 + _fs(out) * 0.002)

    def dma(self, out, in_, slot, reads, writes, eng="sp"):
        try:
            nbytes = int(out.nbytes())
        except Exception:
            nbytes = 1 << 18
        self.add(eng, lambda e: e.dma_start(out, in_), reads, writes, dma=slot, cost=2.0 + nbytes / 150e3)

    def emit(self):
        nc = self.nc
        real = []
        seg_of = []
        seg = 0
        last_w = {}
        readers = {}
        for o in self.ops:
            if o["kind"] == "barrier":
                seg += 1
                continue
            o["idx"] = len(real)
            real.append(o)
            o["seg"] = seg
            deps = set()
            for r in o["reads"]:
                if r in last_w:
                    deps.add(last_w[r])
            for w in o["writes"]:
                if w in last_w:
                    deps.add(last_w[w])
                deps.update(readers.get(w, ()))
            deps.discard(o["idx"])
            for w in o["writes"]:
                last_w[w] = o["idx"]
                readers[w] = []
            for r in o["reads"]:
                if r not in o["writes"]:
                    readers.setdefault(r, []).append(o["idx"])
            o["skey"] = ("dma", o["dma"]) if o["dma"] is not None else ("eng", o["eng"])
            o["deps"] = [d for d in deps if real[d]["seg"] == seg]
            o["signal"] = False
        nseg = seg + 1
        import heapq
        order = {e: [] for e in self.ENGS}
        seg_ops = [[] for _ in range(nseg)]
        for o in real:
            seg_ops[o["seg"]].append(o["idx"])
        fin = [0.0] * len(real)
        tnow = 0.0
        LAT = 0.15
        for sg in range(nseg):
            ids = seg_ops[sg]
            if not RESCHEDULE:
                for i in ids:
                    order[real[i]["eng"]].append(i)
                continue
            nd = {i: len(real[i]["deps"]) for i in ids}
            users = {i: [] for i in ids}
            for i in ids:
                for d in real[i]["deps"]:
                    users[d].append(i)
            ready = {e: [] for e in self.ENGS}
            rt = {}
            for i in ids:
                if nd[i] == 0:
                    rt[i] = tnow
                    heapq.heappush(ready[real[i]["eng"]], i)
            free = {e: tnow for e in self.ENGS}
            sp_list = [i for i in ids if real[i]["eng"] == "sp"]
            sp_pos = 0
            remaining = len(ids)
            while remaining:
                best = None
                for e in self.ENGS:
                    h = ready[e]
                    if not h:
                        continue
                    if e == "sp":
                        if sp_pos < len(sp_list) and nd[sp_list[sp_pos]] == 0:
                            i = sp_list[sp_pos]
                            st_ = max(free[e], rt[i])
                            cand = (st_, i, e)
                        else:
                            continue
                    else:
                        cands = heapq.nsmallest(6, h)
                        cand = None
                        for i in cands:
                            st_ = max(free[e], rt[i])
                            if cand is None or st_ < cand[0] - 1e-9:
                                cand = (st_, i, e)
                    if cand is not None and (best is None or cand[0] < best[0] - 1e-9 or
                                             (abs(cand[0] - best[0]) <= 1e-9 and cand[1] < best[1])):
                        best = cand
                if best is None:
                    raise RuntimeError("scheduler deadlock")
                st_, i, e = best
                if e == "sp":
                    sp_pos += 1
                    ready[e].remove(i)
                    heapq.heapify(ready[e])
                else:
                    ready[e].remove(i)
                    heapq.heapify(ready[e])
                o = real[i]
                if o["dma"] is not None:
                    free[e] = st_ + 0.1
                    fin[i] = st_ + o["cost"]
                else:
                    free[e] = st_ + o["cost"]
                    fin[i] = free[e]
                order[e].append(i)
                remaining -= 1
                for u in users[i]:
                    nd[u] -= 1
                    rt[u] = max(rt.get(u, tnow), fin[i] + LAT)
                    if nd[u] == 0:
                        heapq.heappush(ready[real[u]["eng"]], u)
            tnow = max([tnow] + [fin[i] for i in ids]) + 1.0
        self.est_us = tnow
        pos = {}
        for e in self.ENGS:
            for p, i in enumerate(order[e]):
                pos[i] = p
        dma_seq = {}
        for i in order["sp"] + [i for e in self.ENGS if e != "sp" for i in order[e]]:
            pass
        cnt = {}
        for e in self.ENGS:
            for i in order[e]:
                o = real[i]
                if o["dma"] is not None:
                    cnt[o["skey"]] = cnt.get(o["skey"], 0) + 16
                    o["sigval"] = cnt[o["skey"]]
                    o["dpos"] = cnt[o["skey"]]
        last_in_seg = [dict() for _ in range(nseg)]
        for e in self.ENGS:
            for i in order[e]:
                o = real[i]
                last_in_seg[o["seg"]][o["skey"]] = i
        first_seen = set()
        for e in self.ENGS:
            for i in order[e]:
                o = real[i]
                if o["seg"] > 0 and (e, o["seg"]) not in first_seen:
                    first_seen.add((e, o["seg"]))
                    for sgp in range(o["seg"]):
                        o["deps"] = list(o["deps"]) + list(last_in_seg[sgp].values())
        for o in real:
            keep = {}
            for d in o["deps"]:
                od = real[d]
                if od["dma"] is None and od["eng"] == o["eng"]:
                    if o["eng"] == "pe" or not SAME_ENGINE_SYNC:
                        continue
                sk = od["skey"]
                key = od["dpos"] if od["dma"] is not None else pos[d]
                if sk not in keep or keep[sk][0] < key:
                    keep[sk] = (key, d)
            o["deps"] = [v[1] for v in keep.values()]
            for d in o["deps"]:
                real[d]["signal"] = True
        for e in self.ENGS:
            c = 0
            for i in order[e]:
                o = real[i]
                if o["dma"] is None and o["signal"]:
                    c += 1
                    o["sigval"] = c
        skeys = sorted({o["skey"] for o in real}, key=str)
        with ExitStack() as st:
            sems = {}
            for i, sk in enumerate(skeys):
                sems[sk] = st.enter_context(nc.semaphore(f"sem{i}"))
            dma_final = {sk: cnt[sk] for sk in skeys if sk[0] == "dma"}
            self.sem_names = {f"sem{i}": sk for i, sk in enumerate(skeys)}

            def run(ename, eng):
                waited = {}
                for i in order[ename]:
                    o = real[i]
                    for d in o["deps"]:
                        od = real[d]
                        sk, val = od["skey"], od["sigval"]
                        if waited.get(sk, 0) < val:
                            eng.wait_ge(sems[sk], val)
                            waited[sk] = val
                    ins = o["fn"](eng)
                    if o["dma"] is not None:
                        ins.then_inc(sems[o["skey"]], 16)
                    elif o["signal"]:
                        ins.then_inc(sems[o["skey"]], 1)
                if ename == "sp":
                    for sk, val in dma_final.items():
                        if waited.get(sk, 0) < val:
                            eng.wait_ge(sems[sk], val)

            with nc.Block() as block:
                @block.tensor
                def _(e):
                    run("pe", e)

                @block.scalar
                def _(e):
                    run("act", e)

                @block.vector
                def _(e):
                    run("dve", e)

                @block.gpsimd
                def _(e):
                    run("pool", e)

                @block.sync
                def _(e):
                    run("sp", e)
        return len(real)


def host_consts(S):
    c = {}
    c["ones"] = np.ones((128, 128), np.float32)
    c["ident"] = np.eye(128, dtype=np.float32)
    s = np.arange(128)[:, None]
    t = np.arange(128)[None, :]
    c["bmask"] = ((s // 64 == t // 64) & (s <= t)).astype(np.float32)
    tri = (s <= t).astype(np.float32)
    c["tri2"] = np.concatenate([tri, tri], axis=1)
    cm = np.ones((128, TC), np.float32)
    cm[:, ::64] = 0.0
    c["cmask"] = cm
    half = 16
    inv_freq = (500000.0 ** (-np.arange(0, 32, 2, dtype=np.float32) / 32)).astype(np.float32)
    pos = np.arange(S, dtype=np.float32)
    ang = pos[None, :] * inv_freq[:, None]
    cosT = np.ones((128, S), np.float32)
    sinT = np.zeros((128, S), np.float32)
    cosT[0:16] = np.cos(ang)
    cosT[16:32] = np.cos(ang)
    sinT[0:16] = np.sin(ang)
    sinT[16:32] = np.sin(ang)
    c["cosT"] = cosT
    c["sinT"] = sinT
    pm = np.zeros((128, 128), np.float32)
    for m in range(16):
        pm[m + 16, m] = -1.0
        pm[m, m + 16] = 1.0
    c["pmat"] = pm
    return c


class Ctx:
    pass


def alloc_common(nc, st, S, kind):
    C = Ctx()
    C.S = S
    C.NT = S // TC
    sb = lambda name, shape, dt: st.enter_context(nc.sbuf_tensor("s_" + name, shape, dt))
    C.hT = sb("hT", [128, KC, S], BF16)
    C.wf = [sb(f"wf{i}", [128, KC, 128], F32) for i in range(2)]
    C.wb = [sb(f"wb{i}", [128, KC, 512], BF16) for i in range(2)]
    C.T = [sb(f"T{i}", [128, TC], F32) for i in range(14)]
    C.B = [sb(f"B{i}", [128, TC], BF16) for i in range(10)]
    C.ones_b = sb("ones_b", [128, 128], BF16)
    C.ident_b = sb("ident_b", [128, 128], BF16)
    C.cst_f = sb("cst_f", [128, 128], F32)
    C.gnorm = sb("gnorm", [128, KC], F32)
    C.ps = [st.enter_context(nc.psum_tensor(f"ps{i}", [128, 512], F32)) for i in range(8)]
    return C


def load_const_bf16(P, C, dst, src_dram, name):
    P.dma(C.cst_f[:], src_dram, slot="cst", reads=[("dram", name)], writes=["cst_f"])
    P.cp(dst[:], C.cst_f[:], reads=["cst_f"], writes=[name], eng="dve")


def emit_prologue(P, C, dr, n_part, have_xout, final_only=False):
    S, NT = C.S, C.NT
    P.dma(C.gnorm[:], dr["gnorm"], slot="gn", reads=[], writes=["gnorm"])
    stat_banks = [C.ps[4 + i] for i in range(NT)]
    for kc in range(KC):
        for tc in range(NT):
            sl = (kc * NT + tc) % 2
            xa = C.T[sl]
            xk = ("T", sl)
            cols = slice(tc * TC, (tc + 1) * TC)
            rows = slice(kc * 128, (kc + 1) * 128)
            P.dma(xa[:], dr["xT"][rows, cols], slot=("xa", sl), reads=([("xout", kc, tc)] if dr.get("x_dep") else []), writes=[xk])
            for pi in range(n_part):
                pa = C.T[2 + sl]
                pk = ("T", 2 + sl)
                P.dma(pa[:], dr["parts"][pi][rows, cols], slot=("pa", sl), reads=[("yp", pi, kc, tc)], writes=[pk])
                P.tt(xa[:], xa[:], pa[:], ALU.add, reads=[xk, pk], writes=[xk])
            if have_xout:
                P.dma(dr["xout"][rows, cols], xa[:], slot=("xo", sl), reads=[xk], writes=[("xout", kc, tc), "xout_all"])
            if final_only:
                continue
            xs = C.B[sl]
            P.act(xs[:], xa[:], AF.Square, reads=[xk], writes=[("B", sl)])
            P.mm(stat_banks[tc][:], C.ones_b[:], xs[:], start=(kc == 0), stop=(kc == KC - 1),
                 reads=[("B", sl), "ones_b"], writes=[("ps", 4 + tc)])
    if final_only:
        return
    for tc in range(NT):
        r = C.T[4 + tc]
        P.act(r[:], stat_banks[tc][:], AF.Sqrt, reads=[("ps", 4 + tc), "eps_norm"], writes=[("T", 4 + tc)],
              bias=C.eps_norm[:], scale=1.0 / D)
        P.recip(r[:], r[:], reads=[("T", 4 + tc)], writes=[("T", 4 + tc)])
    src = dr["xout"] if have_xout else dr["xT"]
    for kc in range(KC):
        for tc in range(NT):
            sl = (kc * NT + tc) % 2
            xa = C.T[sl]
            xk = ("T", sl)
            cols = slice(tc * TC, (tc + 1) * TC)
            rows = slice(kc * 128, (kc + 1) * 128)
            rd = [("xout", kc, tc)] if have_xout else []
            P.dma(xa[:], src[rows, cols], slot=("xa", sl), reads=rd, writes=[xk])
            P.stt(C.hT[:, kc, cols], xa[:], C.gnorm[:, kc:kc + 1], C.T[4 + tc][:], ALU.mult, ALU.mult,
                  reads=[xk, "gnorm", ("T", 4 + tc)], writes=[("hT", kc, tc)])


class WStream:
    def __init__(self, P, C):
        self.P, self.C = P, C
        self.n = 0

    def load_group(self, w_dram, col0, nslab, gslot, kcn=KC, name="w"):
        P, C = self.P, self.C
        for s in range(nslab):
            sl = self.n % 2
            self.n += 1
            src = w_dram[:, col0 + s * 128: col0 + (s + 1) * 128].rearrange("(k p) c -> p k c", p=128)
            P.dma(C.wf[sl][:, 0:kcn, :], src, slot=("wf", sl), reads=[], writes=[("wf", sl)])
            P.act(C.wb[gslot][:, 0:kcn, s * 128:(s + 1) * 128], C.wf[sl][:, 0:kcn, :], AF.Copy,
                  reads=[("wf", sl)], writes=[("wb", gslot, s)])


def inproj_fm(P, C, gslot, s, tc, bank):
    cols = slice(tc * TC, (tc + 1) * TC)
    for kc in range(KC):
        P.mm(C.ps[bank][:], C.wb[gslot][:, kc, s * 128:(s + 1) * 128], C.hT[:, kc, cols],
             start=(kc == 0), stop=(kc == KC - 1),
             reads=[("wb", gslot, s), ("hT", kc, tc)], writes=[("ps", bank)])


def inproj_tm(P, C, gslot, s0, ncol, tc, tt, out_ap, bank):
    t0 = tc * TC + tt * 128
    rd = [("wb", gslot, s0 + i) for i in range((ncol + 127) // 128)]
    for kc in range(KC):
        P.mm(out_ap, C.hT[:, kc, t0:t0 + 128], C.wb[gslot][:, kc, s0 * 128:s0 * 128 + ncol],
             start=(kc == 0), stop=(kc == KC - 1),
             reads=rd + [("hT", kc, tc)], writes=[("ps", bank)])


def emit_outproj(P, C, ws, dr, nfc):
    S, NT = C.S, C.NT
    for fc in range(nfc):
        for tc in range(NT):
            cols = slice(tc * TC, (tc + 1) * TC)
            P.dma(C.hT[:, fc, cols], dr["mixT"][fc * 128:(fc + 1) * 128, cols], slot="mixld",
                  reads=[("mixT", dr.get("mix_tag", 0), fc, tc)], writes=[("hT", fc, tc), "mixall"])
    for dc in range(KC):
        g = dc % 2
        ws.load_group(dr["w_out"], dc * 128, 1, g, kcn=nfc)
        for tc in range(NT):
            cols = slice(tc * TC, (tc + 1) * TC)
            bank = (dc * NT + tc) % 2
            for fc in range(nfc):
                P.mm(C.ps[bank][:], C.wb[g][:, fc, 0:128], C.hT[:, fc, cols],
                     start=(fc == 0), stop=(fc == nfc - 1),
                     reads=[("wb", g, 0), "mixall"], writes=[("ps", bank)])
            ysl = (dc * NT + tc) % 2
            yt = C.T[12 + ysl]
            P.act(yt[:], C.ps[bank][:], AF.Copy, reads=[("ps", bank)], writes=[("T", 12 + ysl)])
            P.dma(dr["yp"][dc * 128:(dc + 1) * 128, cols], yt[:], slot=("yst", ysl),
                  reads=[("T", 12 + ysl)], writes=[("yp", dr.get("yp_tag", 0), dc, tc)])


REC_GROUPS = ["rg", "hg", "hg", "rg", "hg", "hg", "rg", "hg", "hg", "rg", "hg", "hg"]


def emit_rec_layer(P, C, dr, do_outproj=True):
    S, NT = C.S, C.NT
    nc = P.nc
    T, B, ps = C.T, C.B, C.ps
    R = C.R
    P.dma(R.convw[:], dr["convw"], slot="p0", reads=[], writes=["convw"])
    P.dma(R.vec[:], dr["recvec"], slot="p1", reads=[], writes=["recvec"])
    P.dma(R.lbraw[:], dr["hglb"], slot="p2", reads=[], writes=["lbraw"])
    P.dma(R.onorm[:], dr["onorm"], slot="p3", reads=[], writes=["onorm"])
    P.dma(R.cmask[:], dr["cmask"], slot="p4", reads=[], writes=["cmask"])
    P.dma(R.bmask[:], dr["bmask"], slot="p5", reads=[], writes=["bmask"])
    P.act(R.sp[:], R.vec[:, :, 3], AF.Exp, reads=["recvec"], writes=["sp"], scale=-1.0)
    P.act(R.sp[:], R.sp[:], AF.Ln, reads=["sp", "one_c"], writes=["sp"], bias=C.one_c[:], scale=1.0)
    P.add("dve", lambda e: e.tensor_scalar_mul(R.cneg[:], R.sp[:], -8.0), ["sp"], ["cneg"])
    P.add("dve", lambda e: e.tensor_scalar_mul(R.cneg2[:], R.sp[:], -16.0), ["sp"], ["cneg2"])
    P.tt(R.lb[:], R.lbraw[:, :, 1], R.lbraw[:, :, 0], ALU.subtract, reads=["lbraw"], writes=["lb"])
    P.act(R.lb[:], R.lb[:], AF.Sigmoid, reads=["lb"], writes=["lb"])
    P.add("dve", lambda e: e.tensor_scalar_mul(R.lb[:], R.lb[:], float(dr["lbflag"])), ["lb"], ["lb"])
    P.ts(R.oml[:], R.lb[:], -1.0, 1.0, ALU.mult, ALU.add, reads=["lb"], writes=["oml"])
    P.memset(R.zero_f[:], 0.0, writes=["zero_f"], eng="dve")
    P.memset(R.zero_b[:], 0.0, writes=["zero_b"], eng="dve")
    P.add("dve", lambda e: e.tensor_scalar_mul(R.vech[:], R.vec[:], 0.5), ["recvec"], ["vech"])
    P.add("dve", lambda e: e.tensor_scalar_mul(R.chalf[:], R.sp[:], -4.0), ["sp"], ["chalf"])
    P.add("dve", lambda e: e.tensor_scalar_mul(R.omh[:], R.oml[:], 0.5), ["oml"], ["omh"])
    P.tt(R.lbp[:], R.omh[:], R.lb[:], ALU.add, reads=["omh", "lb"], writes=["lbp"])

    ws = WStream(P, C)
    groups = REC_GROUPS
    objs = []
    ia = ie = 0
    for gi, gk in enumerate(groups):
        if gk == "rg":
            objs.append(RGGroup(P, C, dr, gi % 2, ia))
            ia += 1
        else:
            objs.append(HGGroup(P, C, dr, gi % 2, ie))
            ie += 1
    seq = [(gi, tc) for gi in range(len(groups)) for tc in range(NT)]
    ws.load_group(dr["w_in"], 0, 4, 0)
    objs[0].inproj(0, 0)
    for k, (gi, tc) in enumerate(seq):
        if tc == 0 and gi + 1 < len(groups):
            ws.load_group(dr["w_in"], (gi + 1) * 512, 4, (gi + 1) % 2)
        if k + 1 < len(seq):
            objs[seq[k + 1][0]].inproj(seq[k + 1][1], (k + 1) % 2)
        objs[gi].mixer(tc, k % 2)
    if do_outproj:
        emit_outproj(P, C, ws, dr, 16)


_BK = [0]
_SL = [0]


def silu2(P, C, out_ap, out_key, bank):
    k = 4 + (_SL[0] % 2)
    _SL[0] += 1
    th = C.T[k]
    P.act(th[:], C.ps[bank][:], AF.Tanh, reads=[("ps", bank)], writes=[("T", k)], scale=0.5)
    P.stt(out_ap, th[:], 1.0, C.ps[bank][:], ALU.add, ALU.mult, reads=[("T", k), ("ps", bank)], writes=[out_key])


def next_bank():
    b = _BK[0] % 2
    _BK[0] += 1
    return b


class RGGroup:
    def __init__(self, P, C, dr, gslot, a):
        self.P, self.C, self.dr, self.gslot, self.a = P, C, dr, gslot, a

    def inproj(self, tc, par):
        P, C, dr, gslot, a = self.P, self.C, self.dr, self.gslot, self.a
        T, B, ps, R = C.T, C.B, C.ps, C.R
        if tc == 0:
            P.dma(R.wgf[:], dr["wgate"][a], slot="wg", reads=[], writes=["wgf"])
            P.act(R.wgb[:], R.wgf[:], AF.Copy, reads=["wgf"], writes=["wgb"])
        xr = R.xraw[par]
        for i in range(2):
            bank = next_bank()
            inproj_fm(P, C, gslot, i, tc, bank)
            if tc == 0:
                P.memset(xr[:, i, 0:3], 0.0, writes=[("xraw", par, i)], eng="dve")
            else:
                P.act(xr[:, i, 0:3], R.xraw[1 - par][:, i, TC:TC + 3], AF.Copy, reads=[("xraw", 1 - par, i)],
                      writes=[("xraw", par, i)])
            P.act(xr[:, i, 3:3 + TC], ps[bank][:], AF.Copy, reads=[("ps", bank)], writes=[("xraw", par, i)])
        for j in range(2):
            bank = next_bank()
            inproj_fm(P, C, gslot, 2 + j, tc, bank)
            silu2(P, C, B[par * 2 + j][:], ("B", par * 2 + j), bank)

    def mixer(self, tc, par):
        P, C, dr, gslot, a = self.P, self.C, self.dr, self.gslot, self.a
        T, B, ps, R = C.T, C.B, C.ps, C.R
        xr = R.xraw[par]
        for i in range(2):
            ch = 2 * a + i
            xc = T[6 + i]
            k = ("T", 6 + i)
            P.act(xc[:], xr[:, i, 3:3 + TC], AF.Identity, reads=[("xraw", par, i), "convw", "recvec"], writes=[k],
                  bias=R.vec[:, ch, 0:1], scale=R.convw[:, ch, 3:4])
            for tap in range(3):
                P.stt(xc[:], xr[:, i, tap:tap + TC], R.convw[:, ch, tap:tap + 1], xc[:], ALU.mult, ALU.add,
                      reads=[("xraw", par, i), "convw", k], writes=[k])
            P.act(B[6 + i][:], xc[:], AF.Copy, reads=[k], writes=[("B", 6 + i)])
        for j in range(2):
            ch = 2 * a + j
            for g in range(2):
                for i in range(2):
                    P.mm(ps[3 + g][:], R.wgb[:, g, i, j * 128:(j + 1) * 128], B[6 + i][:], start=(i == 0), stop=(i == 1),
                         reads=["wgb", ("B", 6 + i)], writes=[("ps", 3 + g)])
            r_, ig, aa, mm_, uu = T[8], T[9], T[10], T[11], T[12]
            P.act(r_[:], ps[3][:], AF.Tanh, reads=[("ps", 3), "vech"], writes=[("T", 8)], bias=R.vech[:, ch, 1:2], scale=0.5)
            P.act(ig[:], ps[4][:], AF.Tanh, reads=[("ps", 4), "vech"], writes=[("T", 9)], bias=R.vech[:, ch, 2:3], scale=0.5)
            P.act(aa[:], r_[:], AF.Exp, reads=[("T", 8), "chalf"], writes=[("T", 10)], scale=R.chalf[:, ch:ch + 1],
                  bias=R.chalf[:, ch:ch + 1])
            P.act(mm_[:], r_[:], AF.Exp, reads=[("T", 8), "cneg"], writes=[("T", 11)], scale=R.cneg[:, ch:ch + 1],
                  bias=R.cneg[:, ch:ch + 1])
            P.act(mm_[:], mm_[:], AF.Sqrt, reads=[("T", 11), "one_c"], writes=[("T", 11)], bias=C.one_c[:], scale=-1.0)
            P.stt(uu[:], ig[:], 1.0, T[6 + j][:], ALU.add, ALU.mult, reads=[("T", 9), ("T", 6 + j)], writes=[("T", 12)])
            P.stt(uu[:], uu[:], 0.5, mm_[:], ALU.mult, ALU.mult, reads=[("T", 12), ("T", 11)], writes=[("T", 12)])
            hcur = R.h[par][j]
            hk = ("h", par, j)
            if tc == 0:
                init = 0.0
                rd = []
            else:
                init = R.h[1 - par][j][:, TC - 1:TC]
                rd = [("h", 1 - par, j)]
            P.scan(hcur[:], aa[:], uu[:], init, reads=[("T", 10), ("T", 12)] + rd, writes=[hk])
            mo = B[8 + j]
            P.stt(mo[:], hcur[:], 0.5, B[par * 2 + j][:], ALU.mult, ALU.mult, reads=[hk, ("B", par * 2 + j)], writes=[("B", 8 + j)])
            fc = 2 * a + j
            P.dma(dr["mixT"][fc * 128:(fc + 1) * 128, tc * TC:(tc + 1) * TC], mo[:], slot=("mo", j),
                  reads=[("B", 8 + j)], writes=[("mixT", dr.get("mix_tag", 0), fc, tc)])


class HGGroup:
    def __init__(self, P, C, dr, gslot, e):
        self.P, self.C, self.dr, self.gslot, self.e = P, C, dr, gslot, e
        self.nS = 0

    def tiles(self, par):
        T, B = self.C.T, self.C.B
        qi, fi, si, vi = par, 2 + par, 2 * par, 2 * par + 1
        return (T[qi], ("T", qi)), (T[fi], ("T", fi)), (B[si], ("B", si)), (B[vi], ("B", vi))

    def inproj(self, tc, par):
        P, C, gslot = self.P, self.C, self.gslot
        ps = C.ps
        (qs, qk), (ff, fk), (sg, sk), (V, vk) = self.tiles(par)
        bank = next_bank()
        inproj_fm(P, C, gslot, 0, tc, bank)
        silu2(P, C, qs[:], qk, bank)
        bank = next_bank()
        inproj_fm(P, C, gslot, 1, tc, bank)
        silu2(P, C, sg[:], sk, bank)
        bank = next_bank()
        inproj_fm(P, C, gslot, 2, tc, bank)
        P.act(ff[:], ps[bank][:], AF.Tanh, reads=[("ps", bank)], writes=[fk], scale=0.5)
        bank = next_bank()
        for tt in range(4):
            inproj_tm(P, C, gslot, 3, 128, tc, tt, ps[bank][:, tt * 128:(tt + 1) * 128], bank)
        P.act(V[:], ps[bank][:], AF.Copy, reads=[("ps", bank)], writes=[vk])

    def mixer(self, tc, par):
        P, C, dr, e = self.P, self.C, self.dr, self.e
        T, B, ps, R = C.T, C.B, C.ps, C.R
        HG_SCALE = 128 ** -0.5
        (qs, qk), (ff, fk), (sg, sk), (V, vk) = self.tiles(par)
        lf, bb, eb, enb = T[8], T[9], T[10], T[11]
        P.ts(ff[:], ff[:], R.omh[:, e:e + 1], R.lbp[:, e:e + 1], ALU.mult, ALU.add, reads=[fk, "omh", "lbp"], writes=[fk])
        P.act(lf[:], ff[:], AF.Ln, reads=[fk], writes=[("T", 8)])
        P.scan(bb[:], R.cmask[:], lf[:], 0.0, reads=["cmask", ("T", 8)], writes=[("T", 9)])
        P.act(eb[:], bb[:], AF.Exp, reads=[("T", 9)], writes=[("T", 10)])
        P.act(enb[:], bb[:], AF.Exp, reads=[("T", 9)], writes=[("T", 11)], scale=-1.0)
        P.ts(ff[:], ff[:], -1.0, 1.0, ALU.mult, ALU.add, reads=[fk], writes=[fk])
        Qd, Kd, K2 = B[4], B[5], B[6]
        P.stt(Qd[:], qs[:], HG_SCALE * 0.5, eb[:], ALU.mult, ALU.mult, reads=[qk, ("T", 10)], writes=[("B", 4)])
        P.tt(Kd[:], ff[:], enb[:], ALU.mult, reads=[fk, ("T", 11)], writes=[("B", 5)])
        for c in range(8):
            cs = slice(c * 64, (c + 1) * 64)
            P.stt(K2[:, cs], ff[:, cs], eb[:, c * 64 + 63:c * 64 + 64], enb[:, cs], ALU.mult, ALU.mult,
                  reads=[fk, ("T", 10), ("T", 11)], writes=[("B", 6)])
        for tt in range(4):
            tsl = slice(tt * 128, (tt + 1) * 128)
            ap = tt % 2
            atb = 5 if ap == 0 else 3
            at_ps = ps[atb][:, 0:128]
            P.mm(at_ps, Kd[:, tsl], Qd[:, tsl], start=True, stop=True, reads=[("B", 5), ("B", 4)], writes=[("ps", atb)])
            atm = R.atm[ap]
            P.tt(atm[:], at_ps, R.bmask[:], ALU.mult, reads=[("ps", atb), "bmask"], writes=[("atm", ap)])
            tr_ps = ps[4][:].bitcast(BF16)[:, 0:128]
            P.tr(tr_ps, K2[:, tsl], C.ident_b[:], reads=[("B", 6), "ident_b"], writes=[("ps", 4)])
            k2t = R.k2t[ap]
            P.act(k2t[:], tr_ps, AF.Copy, reads=[("ps", 4)], writes=[("k2t", ap)])
            o_ps = ps[7][:, tsl]
            P.mm(o_ps, V[:, tsl], atm[:], start=True, stop=False, reads=[vk, ("atm", ap)], writes=[("ps", 7)])
            for h2 in range(2):
                nS = self.nS
                c = tt * 2 + h2
                if tc == 0 and c == 0:
                    sb_prev, sbk = R.zero_b, "zero_b"
                    s_prev, spk = R.zero_f, "zero_f"
                else:
                    sb_prev, sbk = R.Sb[(nS - 1) % 4], ("Sb", (nS - 1) % 4)
                    s_prev, spk = R.Sf[(nS - 1) % 4], ("Sf", (nS - 1) % 4)
                cg = slice(tt * 128 + h2 * 64, tt * 128 + (h2 + 1) * 64)
                P.mm(ps[7][:, cg], sb_prev[:], Qd[:, cg], start=False, stop=(h2 == 1),
                     reads=[sbk, ("B", 4)], writes=[("ps", 7)])
                ub = 6 if nS % 2 == 0 else 2
                u_ps = ps[ub][:, 0:128]
                P.mm(u_ps, k2t[h2 * 64:(h2 + 1) * 64, :], V[h2 * 64:(h2 + 1) * 64, tsl], start=True, stop=True,
                     reads=[("k2t", ap), vk], writes=[("ps", ub)])
                s_new = R.Sf[nS % 4]
                P.stt(s_new[:], s_prev[:], eb[:, c * 64 + 63:c * 64 + 64], u_ps, ALU.mult, ALU.add,
                      reads=[spk, ("T", 10), ("ps", ub)], writes=[("Sf", nS % 4)])
                P.act(R.Sb[nS % 4][:], s_new[:], AF.Copy, reads=[("Sf", nS % 4)], writes=[("Sb", nS % 4)])
                self.nS += 1
        osq = B[7]
        P.act(osq[:], ps[7][:], AF.Square, reads=[("ps", 7)], writes=[("B", 7)])
        P.mm(ps[4][:], C.ones_b[:], osq[:], start=True, stop=True, reads=["ones_b", ("B", 7)], writes=[("ps", 4)])
        rs = T[12]
        P.act(rs[:], ps[4][:], AF.Sqrt, reads=[("ps", 4), "eps_norm"], writes=[("T", 12)], bias=C.eps_norm[:], scale=1.0 / 128)
        P.recip(rs[:], rs[:], reads=[("T", 12)], writes=[("T", 12)])
        ot = T[13]
        P.stt(ot[:], ps[7][:], R.onorm[:, 0:1], rs[:], ALU.mult, ALU.mult, reads=[("ps", 7), "onorm", ("T", 12)], writes=[("T", 13)])
        mo = B[8 + (tc % 2)]
        P.stt(mo[:], ot[:], 0.5, sg[:], ALU.mult, ALU.mult, reads=[("T", 13), sk], writes=[("B", 8 + (tc % 2))])
        fc = 8 + e
        P.dma(dr["mixT"][fc * 128:(fc + 1) * 128, tc * TC:(tc + 1) * TC], mo[:], slot=("mo", tc % 2),
              reads=[("B", 8 + (tc % 2))], writes=[("mixT", dr.get("mix_tag", 0), fc, tc)])


def alloc_rec(nc, st, C, mx=None):
    R = Ctx()
    sb = lambda name, shape, dt: st.enter_context(nc.sbuf_tensor("s_" + name, shape, dt))
    R.convw = sb("convw", [128, 8, 4], F32)
    R.vec = sb("recvec", [128, 8, 4], F32)
    R.lbraw = sb("lbraw", [128, 8, 2], F32)
    R.onorm = sb("onorm", [128, 1], F32)
    if mx is None:
        R.cmask = sb("cmask", [128, TC], F32)
    R.bmask = sb("bmask", [128, 128], F32)
    R.sp = sb("sp", [128, 8], F32)
    R.cneg = sb("cneg", [128, 8], F32)
    R.cneg2 = sb("cneg2", [128, 8], F32)
    R.lb = sb("lb", [128, 8], F32)
    R.oml = sb("oml", [128, 8], F32)
    R.omh = sb("omh", [128, 8], F32)
    R.lbp = sb("lbp", [128, 8], F32)
    R.chalf = sb("chalf", [128, 8], F32)
    R.vech = sb("vech", [128, 8, 4], F32)
    R.zero_f = sb("zero_f", [128, 128], F32)
    R.zero_b = sb("zero_b", [128, 128], BF16)
    R.wgb = sb("wgb", [128, 2, 2, 256], BF16)
    if mx is None:
        R.wgf = sb("wgf", [128, 2, 2, 256], F32)
        R.xraw = [sb(f"xraw{i}", [128, 2, TC + 3], F32) for i in range(2)]
        R.h = [[sb(f"h{p}{j}", [128, TC], F32) for j in range(2)] for p in range(2)]
    else:
        XW = 2 * (TC + 3)
        R.xraw = [mx[:, i * XW:(i + 1) * XW].rearrange("p (a b) -> p a b", a=2) for i in range(2)]
        o = 2 * XW
        R.h = [[mx[:, o + (p * 2 + j) * TC: o + (p * 2 + j + 1) * TC] for j in range(2)] for p in range(2)]
        o += 4 * TC
        R.cmask = mx[:, o:o + TC]
        o += TC
        R.wgf = mx[:, o:o + 1024].rearrange("p (g i c) -> p g i c", g=2, i=2)
    R.atm = [sb(f"atm{i}", [128, 128], BF16) for i in range(2)]
    R.k2t = [sb(f"k2t{i}", [128, 128], BF16) for i in range(2)]
    R.Sf = [sb(f"Sf{i}", [128, 128], F32) for i in range(4)]
    R.Sb = [sb(f"Sb{i}", [128, 128], BF16) for i in range(4)]
    C.R = R


def emit_consts(P, C, dr):
    load_const_bf16(P, C, C.ones_b, dr["ones"], "ones_b")
    load_const_bf16(P, C, C.ident_b, dr["ident"], "ident_b")
    P.memset(C.eps_norm[:], NORM_EPS, writes=["eps_norm"])
    P.memset(C.one_c[:], 1.0, writes=["one_c"])


def build_rec_program(S, n_part, lbflag):
    nc = bass.Bass("TRN2", target_bir_lowering=False)
    di = lambda name, shape, dt=F32: nc.dram_tensor(name, shape, dt, kind="ExternalInput").ap()
    do = lambda name, shape, dt=F32: nc.dram_tensor(name, shape, dt, kind="ExternalOutput").ap()
    dr = {}
    dr["xT"] = di("xT", [D, S])
    dr["parts"] = [di(f"part{i}", [D, S]) for i in range(n_part)]
    dr["gnorm"] = di("gnorm", [128, KC])
    dr["w_in"] = di("w_in", [D, 6144])
    dr["w_out"] = di("w_out", [2048, D])
    dr["convw"] = di("convw", [128, 8, 4])
    dr["recvec"] = di("recvec", [128, 8, 4])
    dr["hglb"] = di("hglb", [128, 8, 2])
    dr["onorm"] = di("onorm", [128, 1])
    dr["wgate"] = di("wgate", [4, 128, 2, 2, 256])
    dr["cmask"] = di("cmask", [128, TC])
    dr["bmask"] = di("bmask", [128, 128])
    dr["ones"] = di("ones", [128, 128])
    dr["ident"] = di("ident", [128, 128])
    dr["lbflag"] = lbflag
    if n_part:
        dr["xout"] = do("xout", [D, S])
    dr["yp"] = do("yp", [D, S])
    dr["mixT"] = nc.dram_tensor("mixT", [2048, S], BF16, kind="Internal").ap()
    with ExitStack() as st:
        C = alloc_common(nc, st, S, "rec")
        C.eps_norm = st.enter_context(nc.sbuf_tensor("s_eps_norm", [128, 1], F32))
        C.one_c = st.enter_context(nc.sbuf_tensor("s_one_c", [128, 1], F32))
        alloc_rec(nc, st, C)
        P = Prog(nc)
        emit_consts(P, C, dr)
        emit_prologue(P, C, dr, n_part, bool(n_part))
        emit_rec_layer(P, C, dr)
        n = P.emit()
    return nc, n


def pvec(v):
    v = np.asarray(v, np.float32)
    return np.ascontiguousarray(v.reshape(-1, 128).T)


def rec_layout(inp, j, r):
    w_in = inp["rec_w_in"][j]
    w_out = inp["rec_w_out"][j]
    rg_heads = [4 * r + a for a in range(4)]
    hg_heads = [8 * r + e for e in range(8)]
    cols = []
    ia = ie = 0
    for gk in REC_GROUPS:
        if gk == "rg":
            hh = rg_heads[ia]; ia += 1
            cols.append(np.arange(256 * hh, 256 * hh + 256))
            cols.append(np.arange(2048 + 256 * hh, 2048 + 256 * hh + 256))
        else:
            e = hg_heads[ie]; ie += 1
            cols.append(np.arange(4096 + 128 * e, 4096 + 128 * e + 128))
            cols.append(np.arange(10240 + 128 * e, 10240 + 128 * e + 128))
            cols.append(np.arange(6144 + 128 * e, 6144 + 128 * e + 128))
            cols.append(np.arange(8192 + 128 * e, 8192 + 128 * e + 128))
    cols = np.concatenate(cols)
    rows = []
    for hh in rg_heads:
        rows.append(np.arange(256 * hh, 256 * hh + 256))
    for e in hg_heads:
        rows.append(np.arange(2048 + 128 * e, 2048 + 128 * e + 128))
    rows = np.concatenate(rows)
    ch = np.concatenate([np.arange(256 * hh, 256 * hh + 256) for hh in rg_heads])
    convw = np.ascontiguousarray(inp["rg_conv_w"][j][:, ch].T.reshape(8, 128, 4).transpose(1, 0, 2))
    vec = np.stack([inp["rg_conv_b"][j][ch], inp["rg_b_gate_a"][j][ch], inp["rg_b_gate_x"][j][ch],
                    inp["rg_lambda"][j][ch]], axis=-1)
    vec = np.ascontiguousarray(vec.reshape(8, 128, 4).transpose(1, 0, 2))
    hch = np.concatenate([np.arange(128 * e, 128 * e + 128) for e in hg_heads])
    lbr = np.stack([inp["hg_lb"][0][hch], inp["hg_lb"][j][hch]], axis=-1)
    lbr = np.ascontiguousarray(lbr.reshape(8, 128, 2).transpose(1, 0, 2))
    wg = np.stack([inp["rg_w_gate_a"][j][rg_heads], inp["rg_w_gate_x"][j][rg_heads]], axis=1)
    wg = wg.reshape(4, 2, 2, 128, 256).transpose(0, 3, 1, 2, 4)
    return dict(
        w_in=np.ascontiguousarray(w_in[:, cols]),
        w_out=np.ascontiguousarray(w_out[rows, :]),
        gnorm=pvec(inp["rec_norm"][j]),
        convw=convw.astype(np.float32), recvec=vec.astype(np.float32), hglb=lbr.astype(np.float32),
        onorm=np.ascontiguousarray(inp["hg_out_norm"][j].reshape(128, 1).astype(np.float32)),
        wgate=np.ascontiguousarray(wg.astype(np.float32)),
    )


def alloc_att(nc, st, C, mx=None):
    A = Ctx()
    S = C.S
    sb = lambda name, shape, dt: st.enter_context(nc.sbuf_tensor("s_" + name, shape, dt))
    if mx is None:
        A.KT = sb("KT", [128, 2, S], BF16)
        A.Vt = sb("Vt", [128, S // 128, 264], BF16)
    else:
        A.KT = mx[:, 0:S].bitcast(BF16).rearrange("p (a s) -> p a s", a=2)
        nv = (S // 128) * 132
        A.Vt = mx[:, S:S + nv].bitcast(BF16).rearrange("p (t c) -> p t c", c=264)
    A.cosT = sb("cosT", [128, S], F32)
    A.sinT = sb("sinT", [128, S], F32)
    A.pmat = sb("pmat", [128, 128], F32)
    A.ones_f = sb("ones_f", [128, 128], F32)
    A.qkg = sb("qkg", [128, 2], F32)
    A.subg = sb("subg", [128, 256], F32)
    A.lp = sb("lp", [128, 4], F32)
    A.pr = sb("pr", [128, 2], F32)
    A.ex = sb("ex", [128, 2], F32)
    A.neglam = sb("neglam", [128, 1], F32)
    A.tri2f = sb("tri2f", [128, 256], F32)
    A.tri2 = sb("tri2", [128, 256], BF16)
    A.pt = [sb(f"pt{i}", [128, 256], BF16) for i in range(2)]
    A.o = sb("o", [128, 256], F32)
    A.junk = sb("junk", [128, 256], F32)
    A.on = sb("on", [128, 256], BF16)
    A.sm = sb("sm", [128, 8], F32)
    A.eps_sub = sb("eps_sub", [128, 1], F32)
    C.A = A


def emit_att_layer(P, C, dr, lam_init, do_outproj=True):
    S, NT = C.S, C.NT
    T, B, ps, A = C.T, C.B, C.ps, C.A
    SCALE = 128 ** -0.5
    P.dma(A.cosT[:], dr["cosT"], slot="a0", reads=[], writes=["cosT"])
    P.dma(A.sinT[:], dr["sinT"], slot="a1", reads=[], writes=["sinT"])
    P.dma(A.pmat[:], dr["pmat"], slot="a2", reads=[], writes=["pmat"])
    P.dma(A.ones_f[:], dr["ones"], slot="a3", reads=[], writes=["ones_f"])
    P.dma(A.qkg[:], dr["qkg"], slot="a4", reads=[], writes=["qkg"])
    P.dma(A.subg[:], dr["subg"], slot="a5", reads=[], writes=["subg"])
    P.dma(A.lp[:], dr["lp"], slot="a6", reads=[], writes=["lp"])
    P.dma(A.tri2f[:], dr["tri2"], slot="a7", reads=[], writes=["tri2f"])
    P.cp(A.tri2[:], A.tri2f[:], reads=["tri2f"], writes=["tri2"], eng="dve")
    P.memset(A.eps_sub[:], SUBLN_EPS, writes=["eps_sub"], eng="dve")
    P.add("dve", lambda e: e.tensor_scalar_mul(A.subg[:], A.subg[:], 1.0 - lam_init), ["subg"], ["subg"])
    P.tt(A.pr[:, 0:1], A.lp[:, 0:1], A.lp[:, 1:2], ALU.mult, reads=["lp"], writes=["pr"])
    P.tt(A.pr[:, 1:2], A.lp[:, 2:3], A.lp[:, 3:4], ALU.mult, reads=["lp", "pr"], writes=["pr"])
    P.mm(ps[2][:, 0:2], A.ones_f[:], A.pr[:], start=True, stop=True, reads=["ones_f", "pr"], writes=[("ps", 2)])
    P.act(A.ex[:], ps[2][:, 0:2], AF.Exp, reads=[("ps", 2)], writes=["ex"])
    P.tt(A.neglam[:], A.ex[:, 1:2], A.ex[:, 0:1], ALU.subtract, reads=["ex"], writes=["neglam"])
    P.add("dve", lambda e: e.tensor_scalar_add(A.neglam[:], A.neglam[:], -lam_init), ["neglam"], ["neglam"])
    P.memset(A.Vt[:, :, 256:257], 1.0, writes=["Vt_ones"], eng="dve")

    ws = WStream(P, C)
    ws.load_group(dr["w_in"], 0, 4, 0)
    ngroups = 8
    bkc = [0]

    def nb():
        b = bkc[0] % 2
        bkc[0] += 1
        return b

    def norm_rope(bank, gcol, out_ap, out_keys, tc):
        cols = slice(tc * TC, (tc + 1) * TC)
        raw, rstd, t1, t2, sq = T[6], T[7], T[8], T[9], B[2]
        P.act(raw[:], ps[bank][:], AF.Copy, reads=[("ps", bank)], writes=[("T", 6)])
        P.act(sq[:], ps[bank][:], AF.Square, reads=[("ps", bank)], writes=[("B", 2)])
        P.mm(ps[2][:], C.ones_b[:], sq[:], start=True, stop=True, reads=["ones_b", ("B", 2)], writes=[("ps", 2)])
        P.act(rstd[:], ps[2][:], AF.Sqrt, reads=[("ps", 2), "eps_norm"], writes=[("T", 7)], bias=C.eps_norm[:], scale=1.0 / 128)
        P.recip(rstd[:], rstd[:], reads=[("T", 7)], writes=[("T", 7)])
        P.stt(raw[:], raw[:], A.qkg[:, gcol:gcol + 1], rstd[:], ALU.mult, ALU.mult, reads=[("T", 6), "qkg", ("T", 7)], writes=[("T", 6)])
        P.mm(ps[3][:], A.pmat[:], raw[:], start=True, stop=True, reads=["pmat", ("T", 6)], writes=[("ps", 3)])
        P.tt(t1[:], raw[:], A.cosT[:, cols], ALU.mult, reads=[("T", 6), "cosT"], writes=[("T", 8)])
        P.tt(t2[:], ps[3][:], A.sinT[:, cols], ALU.mult, reads=[("ps", 3), "sinT"], writes=[("T", 9)])
        P.tt(out_ap, t1[:], t2[:], ALU.add, reads=[("T", 8), ("T", 9)], writes=out_keys)

    for gi in range(ngroups):
        gslot = gi % 2
        a = gi // 2
        if gi + 1 < ngroups:
            ws.load_group(dr["w_in"], (gi + 1) * 512, 4, (gi + 1) % 2)
        if gi % 2 == 0:
            for tc in range(NT):
                cols = slice(tc * TC, (tc + 1) * TC)
                for i in range(2):
                    bank = nb()
                    inproj_fm(P, C, gslot, i, tc, bank)
                    norm_rope(bank, 1, A.KT[:, i, cols], [("KT", i, tc)], tc)
                for half in range(2):
                    bank = nb()
                    for t2_ in range(2):
                        tt = half * 2 + t2_
                        inproj_tm(P, C, gslot, 2, 256, tc, tt, ps[bank][:, t2_ * 256:(t2_ + 1) * 256], bank)
                    tile0 = tc * 4 + half * 2
                    P.act(A.Vt[:, tile0:tile0 + 2, 0:256], ps[bank][:].rearrange("p (a b) -> p a b", a=2), AF.Copy,
                          reads=[("ps", bank)], writes=[("Vt", tile0), ("Vt", tile0 + 1)])
        else:
            for tc in range(NT):
                cols = slice(tc * TC, (tc + 1) * TC)
                Qt = [B[4], B[5]]
                sgt = [B[6], B[7]]
                for i in range(2):
                    bank = nb()
                    inproj_fm(P, C, gslot, i, tc, bank)
                    norm_rope(bank, 0, Qt[i][:], [("B", 4 + i)], tc)
                for i in range(2):
                    bank = nb()
                    inproj_fm(P, C, gslot, 2 + i, tc, bank)
                    silu2(P, C, sgt[i][:], ("B", 6 + i), bank)
                for qt in range(4):
                    j = tc * 4 + qt
                    qs = slice(qt * 128, (qt + 1) * 128)

                    def emit_qk(i):
                        sb_ = 4 if i % 2 == 0 else 7
                        ks = slice(i * 128, (i + 1) * 128)
                        for m in range(2):
                            P.mm(ps[sb_][:, m * 128:(m + 1) * 128], A.KT[:, m, ks], Qt[m][:, qs], start=True, stop=True,
                                 reads=[("KT", m, i // 4), ("B", 4 + m)], writes=[("ps", sb_)])
                        pt = A.pt[i % 2]
                        P.act(pt[:], ps[sb_][:, 0:256], AF.Exp, reads=[("ps", sb_)], writes=[("pt", i % 2)], scale=SCALE)
                        if i == j:
                            P.tt(pt[:], pt[:], A.tri2[:], ALU.mult, reads=[("pt", i % 2), "tri2"], writes=[("pt", i % 2)])

                    def emit_pv(i):
                        pt = A.pt[i % 2]
                        for m in range(2):
                            P.mm(ps[5 + m][:, 0:257], pt[:, m * 128:(m + 1) * 128], A.Vt[:, i, 0:257],
                                 start=(i == 0), stop=(i == j),
                                 reads=[("pt", i % 2), ("Vt", i), "Vt_ones"], writes=[("ps", 5 + m)])

                    emit_qk(0)
                    for i in range(j + 1):
                        if i + 1 <= j:
                            emit_qk(i + 1)
                        emit_pv(i)
                    sm = A.sm
                    P.recip(sm[:, 0:1], ps[5][:, 256:257], reads=[("ps", 5)], writes=["sm"])
                    P.recip(sm[:, 1:2], ps[6][:, 256:257], reads=[("ps", 6), "sm"], writes=["sm"])
                    P.tt(sm[:, 2:3], sm[:, 1:2], A.neglam[:], ALU.mult, reads=["sm", "neglam"], writes=["sm"])
                    P.add("dve", lambda e, sm=sm: e.tensor_scalar_mul(A.o[:], ps[5][:, 0:256], sm[:, 0:1]), [("ps", 5), "sm"], ["o"])
                    P.stt(A.o[:], ps[6][:, 0:256], sm[:, 2:3], A.o[:], ALU.mult, ALU.add, reads=[("ps", 6), "sm", "o"], writes=["o"])
                    P.add("act", lambda e, sm=sm: e.activation(A.junk[:], A.o[:], AF.Square, accum_out=sm[:, 3:4]), ["o"], ["junk", "sm3"])
                    P.act(sm[:, 4:5], sm[:, 3:4], AF.Sqrt, reads=["sm3", "eps_sub"], writes=["sm4"], bias=A.eps_sub[:], scale=1.0 / 256)
                    P.recip(sm[:, 5:6], sm[:, 4:5], reads=["sm4"], writes=["sm5"])
                    P.stt(A.on[:], A.o[:], sm[:, 5:6], A.subg[:], ALU.mult, ALU.mult, reads=["o", "sm5", "subg"], writes=["on"])
                    trv = ps[3][:].bitcast(BF16)
                    for m in range(2):
                        P.tr(trv[:, m * 128:(m + 1) * 128], A.on[:, m * 128:(m + 1) * 128], C.ident_b[:],
                             reads=["on", "ident_b"], writes=[("ps", 3)])
                    for m in range(2):
                        P.stt(B[8 + m][:, qs], trv[:, m * 128:(m + 1) * 128], 0.5, sgt[m][:, qs], ALU.mult, ALU.mult,
                              reads=[("ps", 3), ("B", 6 + m)], writes=[("B", 8 + m)])
                for m in range(2):
                    fc = 2 * a + m
                    P.dma(dr["mixT"][fc * 128:(fc + 1) * 128, cols], B[8 + m][:], slot=("mo", m),
                          reads=[("B", 8 + m)], writes=[("mixT", dr.get("mix_tag", 0), fc, tc)])
    if do_outproj:
        emit_outproj(P, C, ws, dr, 8)


def build_att_program(S, n_part, layer):
    nc = bass.Bass("TRN2", target_bir_lowering=False)
    di = lambda name, shape, dt=F32: nc.dram_tensor(name, shape, dt, kind="ExternalInput").ap()
    do = lambda name, shape, dt=F32: nc.dram_tensor(name, shape, dt, kind="ExternalOutput").ap()
    dr = {}
    dr["xT"] = di("xT", [D, S])
    dr["parts"] = [di(f"part{i}", [D, S]) for i in range(n_part)]
    dr["gnorm"] = di("gnorm", [128, KC])
    dr["w_in"] = di("w_in", [D, 4096])
    dr["w_out"] = di("w_out", [1024, D])
    dr["cosT"] = di("cosT", [128, S])
    dr["sinT"] = di("sinT", [128, S])
    dr["pmat"] = di("pmat", [128, 128])
    dr["qkg"] = di("qkg", [128, 2])
    dr["subg"] = di("subg", [128, 256])
    dr["lp"] = di("lp", [128, 4])
    dr["tri2"] = di("tri2", [128, 256])
    dr["ones"] = di("ones", [128, 128])
    dr["ident"] = di("ident", [128, 128])
    if n_part:
        dr["xout"] = do("xout", [D, S])
    dr["yp"] = do("yp", [D, S])
    dr["mixT"] = nc.dram_tensor("mixT", [1024, S], BF16, kind="Internal").ap()
    lam_init = 0.8 - 0.6 * math.exp(-0.3 * layer)
    with ExitStack() as st:
        C = alloc_common(nc, st, S, "att")
        C.eps_norm = st.enter_context(nc.sbuf_tensor("s_eps_norm", [128, 1], F32))
        C.one_c = st.enter_context(nc.sbuf_tensor("s_one_c", [128, 1], F32))
        alloc_att(nc, st, C)
        P = Prog(nc)
        emit_consts(P, C, dr)
        emit_prologue(P, C, dr, n_part, bool(n_part))
        emit_att_layer(P, C, dr, lam_init)
        n = P.emit()
    return nc, n


def att_layout(inp, j, r):
    w_in = inp["att_w_in"][j]
    w_out = inp["att_w_out"][j]
    heads = [4 * r + a for a in range(4)]
    cols = []
    for h in heads:
        cols.append(np.arange(2048 + 256 * h, 2048 + 256 * h + 256))
        cols.append(np.arange(4096 + 256 * h, 4096 + 256 * h + 256))
        cols.append(np.arange(256 * h, 256 * h + 256))
        cols.append(np.arange(6144 + 256 * h, 6144 + 256 * h + 256))
    cols = np.concatenate(cols)
    rows = np.concatenate([np.arange(256 * h, 256 * h + 256) for h in heads])
    qkg = np.stack([inp["att_q_norm"][j], inp["att_k_norm"][j]], axis=-1).astype(np.float32)
    subg = np.ascontiguousarray(np.broadcast_to(inp["att_sub_norm"][j][None, :], (128, 256)).astype(np.float32))
    lp = np.ascontiguousarray(inp["att_lambda"][j].T.astype(np.float32))
    return dict(
        w_in=np.ascontiguousarray(w_in[:, cols]),
        w_out=np.ascontiguousarray(w_out[rows, :]),
        gnorm=pvec(inp["att_norm"][j]),
        qkg=np.ascontiguousarray(qkg), subg=subg, lp=lp,
    )


def build_final_program(S):
    nc = bass.Bass("TRN2", target_bir_lowering=False)
    di = lambda name, shape, dt=F32: nc.dram_tensor(name, shape, dt, kind="ExternalInput").ap()
    x = di("xT", [1024, S])
    p0 = di("part0", [1024, S])
    p1 = di("part1", [1024, S])
    out = nc.dram_tensor("xout", [1024, S], F32, kind="ExternalOutput").ap()
    with ExitStack() as st:
        tl = [[st.enter_context(nc.sbuf_tensor(f"s_f{k}{i}", [128, S], F32)) for i in range(2)] for k in range(3)]
        P = Prog(nc)
        for c in range(8):
            sl = c % 2
            rows = slice(c * 128, (c + 1) * 128)
            xa, pa, pb = tl[0][sl], tl[1][sl], tl[2][sl]
            P.dma(xa[:], x[rows, :], slot=("fx", sl), reads=[], writes=[("fx", sl)])
            P.dma(pa[:], p0[rows, :], slot=("fa", sl), reads=[], writes=[("fa", sl)])
            P.dma(pb[:], p1[rows, :], slot=("fb", sl), reads=[], writes=[("fb", sl)])
            P.tt(xa[:], xa[:], pa[:], ALU.add, reads=[("fx", sl), ("fa", sl)], writes=[("fx", sl)])
            P.tt(xa[:], xa[:], pb[:], ALU.add, reads=[("fx", sl), ("fb", sl)], writes=[("fx", sl)])
            P.dma(out[rows, :], xa[:], slot=("fo", sl), reads=[("fx", sl)], writes=[("out", c)])
        P.emit()
    return nc


REC_KEYS = ("w_in", "w_out", "gnorm", "convw", "recvec", "hglb", "onorm", "wgate")
ATT_KEYS = ("w_in", "w_out", "gnorm", "qkg", "subg", "lp")
CONST_KEYS = ("ones", "ident", "cmask", "bmask", "cosT", "sinT", "pmat", "tri2")
ACTIVE_CORES = (0, 1, 2, 3)
NCORES = 4


def build_fused_program(S, nlayers=4, nseq=1):
    nc = bass.Bass("TRN2", target_bir_lowering=False)
    di = lambda name, shape, dt=F32: nc.dram_tensor(name, shape, dt, kind="ExternalInput").ap()
    x_ins = [di("xT" if q == 0 else f"xT{q}", [D, S]) for q in range(nseq)]
    outs = [nc.dram_tensor("outT" if q == 0 else f"outT{q}", [D, S], F32, kind="ExternalOutput").ap() for q in range(nseq)]
    yp = [nc.dram_tensor(f"yp{i}", [D, S], F32, kind="Internal").ap() for i in range(2)]
    xb = [nc.dram_tensor(f"xbuf{i}", [D, S], F32, kind="Internal").ap() for i in range(2)]
    mixTs = [nc.dram_tensor(f"mixT{i}", [2048, S], BF16, kind="Internal").ap() for i in range(2)]
    cshape = dict(ones=[128, 128], ident=[128, 128], cmask=[128, TC], bmask=[128, 128], cosT=[128, S], sinT=[128, S],
                  pmat=[128, 128], tri2=[128, 256])
    cd = {k: di(k, cshape[k]) for k in CONST_KEYS}
    rshape = dict(w_in=[D, 6144], w_out=[2048, D], gnorm=[128, KC], convw=[128, 8, 4], recvec=[128, 8, 4], hglb=[128, 8, 2],
                  onorm=[128, 1], wgate=[4, 128, 2, 2, 256])
    ashape = dict(w_in=[D, 4096], w_out=[1024, D], gnorm=[128, KC], qkg=[128, 2], subg=[128, 256], lp=[128, 4])
    drs = []
    for L in range(nlayers):
        sh = rshape if L % 2 == 0 else ashape
        halves = []
        for hf in range(2):
            dr = {k: di(f"{k}{L}h{hf}", sh[k]) for k in sh if not (k == "gnorm" and hf == 1)}
            if hf == 1:
                dr["gnorm"] = halves[0]["gnorm"]
            dr.update(cd)
            dr["mixT"] = mixTs[hf] if L % 2 == 0 else mixTs[hf][0:1024, :]
            dr["mix_tag"] = hf
            dr["yp"] = yp[hf]
            dr["yp_tag"] = hf
            dr["lbflag"] = 0.0 if L == 0 else 1.0
            halves.append(dr)
        drs.append(halves)
    with ExitStack() as st:
        C = alloc_common(nc, st, S, "all")
        C.eps_norm = st.enter_context(nc.sbuf_tensor("s_eps_norm", [128, 1], F32))
        C.one_c = st.enter_context(nc.sbuf_tensor("s_one_c", [128, 1], F32))
        mx = st.enter_context(nc.sbuf_tensor("s_mx", [128, 5644], F32))
        alloc_rec(nc, st, C, mx)
        alloc_att(nc, st, C, mx)
        P = Prog(nc)
        emit_consts(P, C, cd)
        for q in range(nseq):
          x_in, out = x_ins[q], outs[q]
          for L in range(nlayers):
              pr = dict(gnorm=drs[L][0]["gnorm"])
              if L == 0:
                  pr["xT"] = x_in
                  pr["parts"] = []
              else:
                  pr["xT"] = x_in if L == 1 else xb[(L - 2) % 2]
                  pr["x_dep"] = L >= 2
                  pr["parts"] = yp
                  pr["xout"] = xb[(L - 1) % 2]
              emit_prologue(P, C, pr, len(pr["parts"]), L > 0)
              for hf in range(2):
                  if L % 2 == 0:
                      emit_rec_layer(P, C, drs[L][hf], do_outproj=False)
                  else:
                      emit_att_layer(P, C, drs[L][hf], 0.8 - 0.6 * math.exp(-0.3 * L), do_outproj=False)
              for hf in range(2):
                  emit_outproj(P, C, WStream(P, C), drs[L][hf], 16 if L % 2 == 0 else 8)
              P.barrier()
          pr = dict(xT=(x_in if nlayers == 1 else xb[(nlayers - 2) % 2]), x_dep=nlayers >= 2, parts=yp,
                    xout=out, gnorm=drs[0][0]["gnorm"])
          emit_prologue(P, C, pr, 2, True, final_only=True)
          P.barrier()
        n = P.emit()
    return nc, n


def fused_maps(inp, S=SEQ, nlayers=4, ncores=2, nseq=2):
    cs = host_consts(S)
    x = np.asarray(inp["x"], np.float32)
    wmap = {}
    for L in range(nlayers):
        j = L // 2
        for hf in range(2):
            lay = rec_layout(inp, j, hf) if L % 2 == 0 else att_layout(inp, j, hf)
            for k, v in lay.items():
                if k == "gnorm" and hf == 1:
                    continue
                wmap[f"{k}{L}h{hf}"] = v
    maps = []
    for c in range(ncores):
        m = {}
        for q in range(nseq):
            b = c * nseq + q
            m["xT" if q == 0 else f"xT{q}"] = np.ascontiguousarray(x[b][:S].T)
        m.update(wmap)
        for k in CONST_KEYS:
            m[k] = cs[k]
        maps.append(m)
    return maps


_PROGS = {}
NCORES = 2
NSEQ = 2


def kernel(**inp):
    inp = {k: np.asarray(v) for k, v in inp.items()}
    if "fused" not in _PROGS:
        _PROGS["fused"] = build_fused_program(SEQ, 4, NSEQ)[0]
    nc = _PROGS["fused"]
    maps = fused_maps(inp, SEQ, 4, NCORES, NSEQ)
    res = run_bass_kernel_spmd(nc, maps, core_ids=list(range(NCORES)))
    out = np.empty((BATCH, SEQ, D), np.float32)
    for c in range(NCORES):
        for q in range(NSEQ):
            out[c * NSEQ + q] = np.asarray(res.results[c]["outT" if q == 0 else f"outT{q}"]).T
    return out
```

```python
import math
from contextlib import ExitStack

import numpy as np
import concourse.bass as bass
import concourse.mybir as mybir
from concourse.bass_utils import run_bass_kernel_spmd

F32 = mybir.dt.float32
BF16 = mybir.dt.bfloat16
AF = mybir.ActivationFunctionType
ALU = mybir.AluOpType

D = 2048
KC = 16
SEQ = 2048
BATCH = 4
TC = 512
NORM_EPS = 1e-6
SUBLN_EPS = 1e-5
SAME_ENGINE_SYNC = True


def _fs(ap):
    try:
        return int(ap.free_size())
    except Exception:
        return 512


RESCHEDULE = True


class Prog:
    ENGS = ("pe", "act", "dve", "pool", "sp")

    def __init__(self, nc):
        self.nc = nc
        self.ops = []

    def add(self, eng, fn, reads=(), writes=(), dma=None, cost=0.5):
        self.ops.append(dict(kind="op", eng=eng, fn=fn, reads=tuple(reads), writes=tuple(writes), dma=dma, cost=cost))

    def barrier(self):
        self.ops.append(dict(kind="barrier"))

    def mm(self, out, lhsT, rhs, start, stop, reads, writes):
        self.add("pe", lambda e: e.matmul(out, lhsT, rhs, start=start, stop=stop), reads, writes,
                 cost=0.035 + _fs(rhs) / 2400.0)

    def tr(self, out, in_, ident, reads, writes):
        self.add("pe", lambda e: e.transpose(out, in_, ident), reads, writes, cost=0.1)

    def act(self, out, in_, func, reads, writes, bias=0.0, scale=1.0, eng="act"):
        self.add(eng, lambda e: e.activation(out, in_, func, bias=bias, scale=scale), reads, writes,
                 cost=0.2 + _fs(out) * 0.0008)

    def ts(self, out, in0, s1, s2, op0, op1, reads, writes, eng="dve"):
        self.add(eng, lambda e: e.tensor_scalar(out, in0, s1, s2, op0, op1), reads, writes, cost=0.15 + _fs(out) * 0.001)

    def stt(self, out, in0, scalar, in1, op0, op1, reads, writes, eng="dve"):
        self.add(eng, lambda e: e.scalar_tensor_tensor(out, in0, scalar, in1, op0, op1), reads, writes,
                 cost=0.2 + _fs(out) * 0.0011)

    def tt(self, out, in0, in1, op, reads, writes, eng="dve"):
        c = 0.15 + _fs(out) * (0.001 if eng == "dve" else 0.0022)
        self.add(eng, lambda e: e.tensor_tensor(out, in0, in1, op), reads, writes, cost=c)

    def cp(self, out, in_, reads, writes, eng="dve"):
        c = 0.15 + _fs(out) * 0.001
        self.add(eng, lambda e: e.tensor_copy(out, in_), reads, writes, cost=c)

    def memset(self, ap, val, writes, eng="dve"):
        self.add(eng, lambda e: e.memset(ap, val), (), writes, cost=0.2)

    def recip(self, out, in_, reads, writes):
        self.add("dve", lambda e: e.reciprocal(out, in_), reads, writes, cost=0.15 + _fs(out) * 0.001)

    def scan(self, out, d0, d1, init, reads, writes):
        self.add("dve", lambda e: e.tensor_tensor_scan(out, d0, d1, init, ALU.mult, ALU.add), reads, writes,
                 cost=0.2 + _fs(out) * 0.002)

    def dma(self, out, in_, slot, reads, writes, eng="sp"):
        try:
            nbytes = int(out.nbytes())
        except Exception:
            nbytes = 1 << 18
        self.add(eng, lambda e: e.dma_start(out, in_), reads, writes, dma=slot, cost=2.0 + nbytes / 150e3)

    def emit(self):
        nc = self.nc
        real = []
        seg_of = []
        seg = 0
        last_w = {}
        readers = {}
        for o in self.ops:
            if o["kind"] == "barrier":
                seg += 1
                continue
            o["idx"] = len(real)
            real.append(o)
            o["seg"] = seg
            deps = set()
            for r in o["reads"]:
                if r in last_w:
                    deps.add(last_w[r])
            for w in o["writes"]:
                if w in last_w:
                    deps.add(last_w[w])
                deps.update(readers.get(w, ()))
            deps.discard(o["idx"])
            for w in o["writes"]:
                last_w[w] = o["idx"]
                readers[w] = []
            for r in o["reads"]:
                if r not in o["writes"]:
                    readers.setdefault(r, []).append(o["idx"])
            o["skey"] = ("dma", o["dma"]) if o["dma"] is not None else ("eng", o["eng"])
            o["deps"] = [d for d in deps if real[d]["seg"] == seg]
            o["signal"] = False
        nseg = seg + 1
        import heapq
        order = {e: [] for e in self.ENGS}
        seg_ops = [[] for _ in range(nseg)]
        for o in real:
            seg_ops[o["seg"]].append(o["idx"])
        fin = [0.0] * len(real)
        tnow = 0.0
        LAT = 0.15
        for sg in range(nseg):
            ids = seg_ops[sg]
            if not RESCHEDULE:
                for i in ids:
                    order[real[i]["eng"]].append(i)
                continue
            nd = {i: len(real[i]["deps"]) for i in ids}
            users = {i: [] for i in ids}
            for i in ids:
                for d in real[i]["deps"]:
                    users[d].append(i)
            ready = {e: [] for e in self.ENGS}
            rt = {}
            for i in ids:
                if nd[i] == 0:
                    rt[i] = tnow
                    heapq.heappush(ready[real[i]["eng"]], i)
            free = {e: tnow for e in self.ENGS}
            sp_list = [i for i in ids if real[i]["eng"] == "sp"]
            sp_pos = 0
            remaining = len(ids)
            while remaining:
                best = None
                for e in self.ENGS:
                    h = ready[e]
                    if not h:
                        continue
                    if e == "sp":
                        if sp_pos < len(sp_list) and nd[sp_list[sp_pos]] == 0:
                            i = sp_list[sp_pos]
                            st_ = max(free[e], rt[i])
                            cand = (st_, i, e)
                        else:
                            continue
                    else:
                        cands = heapq.nsmallest(6, h)
                        cand = None
                        for i in cands:
                            st_ = max(free[e], rt[i])
                            if cand is None or st_ < cand[0] - 1e-9:
                                cand = (st_, i, e)
                    if cand is not None and (best is None or cand[0] < best[0] - 1e-9 or
                                             (abs(cand[0] - best[0]) <= 1e-9 and cand[1] < best[1])):
                        best = cand
                if best is None:
                    raise RuntimeError("scheduler deadlock")
                st_, i, e = best
                if e == "sp":
                    sp_pos += 1
                    ready[e].remove(i)
                    heapq.heapify(ready[e])
                else:
                    ready[e].remove(i)
                    heapq.heapify(ready[e])
                o = real[i]
                if o["dma"] is not None:
                    free[e] = st_ + 0.1
                    fin[i] = st_ + o["cost"]
                else:
                    free[e] = st_ + o["cost"]
                    fin[i] = free[e]
                order[e].append(i)
                remaining -= 1
                for u in users[i]:
                    nd[u] -= 1
                    rt[u] = max(rt.get(u, tnow), fin[i] + LAT)
                    if nd[u] == 0:
                        heapq.heappush(ready[real[u]["eng"]], u)
            tnow = max([tnow] + [fin[i] for i in ids]) + 1.0
        self.est_us = tnow
        pos = {}
        for e in self.ENGS:
            for p, i in enumerate(order[e]):
                pos[i] = p
        dma_seq = {}
        for i in order["sp"] + [i for e in self.ENGS if e != "sp" for i in order[e]]:
            pass
        cnt = {}
        for e in self.ENGS:
            for i in order[e]:
                o = real[i]
                if o["dma"] is not None:
                    cnt[o["skey"]] = cnt.get(o["skey"], 0) + 16
                    o["sigval"] = cnt[o["skey"]]
                    o["dpos"] = cnt[o["skey"]]
        last_in_seg = [dict() for _ in range(nseg)]
        for e in self.ENGS:
            for i in order[e]:
                o = real[i]
                last_in_seg[o["seg"]][o["skey"]] = i
        first_seen = set()
        for e in self.ENGS:
            for i in order[e]:
                o = real[i]
                if o["seg"] > 0 and (e, o["seg"]) not in first_seen:
                    first_seen.add((e, o["seg"]))
                    for sgp in range(o["seg"]):
                        o["deps"] = list(o["deps"]) + list(last_in_seg[sgp].values())
        for o in real:
            keep = {}
            for d in o["deps"]:
                od = real[d]
                if od["dma"] is None and od["eng"] == o["eng"]:
                    if o["eng"] == "pe" or not SAME_ENGINE_SYNC:
                        continue
                sk = od["skey"]
                key = od["dpos"] if od["dma"] is not None else pos[d]
                if sk not in keep or keep[sk][0] < key:
                    keep[sk] = (key, d)
            o["deps"] = [v[1] for v in keep.values()]
            for d in o["deps"]:
                real[d]["signal"] = True
        for e in self.ENGS:
            c = 0
            for i in order[e]:
                o = real[i]
                if o["dma"] is None and o["signal"]:
                    c += 1
                    o["sigval"] = c
        skeys = sorted({o["skey"] for o in real}, key=str)
        with ExitStack() as st:
            sems = {}
            for i, sk in enumerate(skeys):
                sems[sk] = st.enter_context(nc.semaphore(f"sem{i}"))
            dma_final = {sk: cnt[sk] for sk in skeys if sk[0] == "dma"}
            self.sem_names = {f"sem{i}": sk for i, sk in enumerate(skeys)}

            def run(ename, eng):
                waited = {}
                for i in order[ename]:
                    o = real[i]
                    for d in o["deps"]:
                        od = real[d]
                        sk, val = od["skey"], od["sigval"]
                        if waited.get(sk, 0) < val:
                            eng.wait_ge(sems[sk], val)
                            waited[sk] = val
                    ins = o["fn"](eng)
                    if o["dma"] is not None:
                        ins.then_inc(sems[o["skey"]], 16)
                    elif o["signal"]:
                        ins.then_inc(sems[o["skey"]], 1)
                if ename == "sp":
                    for sk, val in dma_final.items():
                        if waited.get(sk, 0) < val:
                            eng.wait_ge(sems[sk], val)

            with nc.Block() as block:
                @block.tensor
                def _(e):
                    run("pe", e)

                @block.scalar
                def _(e):
                    run("act", e)

                @block.vector
                def _(e):
                    run("dve", e)

                @block.gpsimd
                def _(e):
                    run("pool", e)

                @block.sync
                def _(e):
                    run("sp", e)
        return len(real)


def host_consts(S):
    c = {}
    c["ones"] = np.ones((128, 128), np.float32)
    c["ident"] = np.eye(128, dtype=np.float32)
    s = np.arange(128)[:, None]
    t = np.arange(128)[None, :]
    c["bmask"] = ((s // 64 == t // 64) & (s <= t)).astype(np.float32)
    tri = (s <= t).astype(np.float32)
    c["tri2"] = np.concatenate([tri, tri], axis=1)
    cm = np.ones((128, TC), np.float32)
    cm[:, ::64] = 0.0
    c["cmask"] = cm
    half = 16
    inv_freq = (500000.0 ** (-np.arange(0, 32, 2, dtype=np.float32) / 32)).astype(np.float32)
    pos = np.arange(S, dtype=np.float32)
    ang = pos[None, :] * inv_freq[:, None]
    cosT = np.ones((128, S), np.float32)
    sinT = np.zeros((128, S), np.float32)
    cosT[0:16] = np.cos(ang)
    cosT[16:32] = np.cos(ang)
    sinT[0:16] = np.sin(ang)
    sinT[16:32] = np.sin(ang)
    c["cosT"] = cosT
    c["sinT"] = sinT
    pm = np.zeros((128, 128), np.float32)
    for m in range(16):
        pm[m + 16, m] = -1.0
        pm[m, m + 16] = 1.0
    c["pmat"] = pm
    return c


class Ctx:
    pass


def alloc_common(nc, st, S, kind):
    C = Ctx()
    C.S = S
    C.NT = S // TC
    sb = lambda name, shape, dt: st.enter_context(nc.sbuf_tensor("s_" + name, shape, dt))
    C.hT = sb("hT", [128, KC, S], BF16)
    C.wf = [sb(f"wf{i}", [128, KC, 128], F32) for i in range(2)]
    C.wb = [sb(f"wb{i}", [128, KC, 512], BF16) for i in range(2)]
    C.T = [sb(f"T{i}", [128, TC], F32) for i in range(14)]
    C.B = [sb(f"B{i}", [128, TC], BF16) for i in range(12)]
    C.ones_b = sb("ones_b", [128, 128], BF16)
    C.ident_b = sb("ident_b", [128, 128], BF16)
    C.cst_f = sb("cst_f", [128, 128], F32)
    C.gnorm = sb("gnorm", [128, KC], F32)
    C.ps = [st.enter_context(nc.psum_tensor(f"ps{i}", [128, 512], F32)) for i in range(8)]
    return C


def load_const_bf16(P, C, dst, src_dram, name):
    P.dma(C.cst_f[:], src_dram, slot="cst", reads=[("dram", name)], writes=["cst_f"])
    P.cp(dst[:], C.cst_f[:], reads=["cst_f"], writes=[name], eng="dve")


def emit_prologue(P, C, dr, n_part, have_xout, final_only=False):
    S, NT = C.S, C.NT
    P.dma(C.gnorm[:], dr["gnorm"], slot="gn", reads=[], writes=["gnorm"])
    stat_banks = [C.ps[4 + i] for i in range(NT)]
    for kc in range(KC):
        for tc in range(NT):
            sl = (kc * NT + tc) % 2
            xa = C.T[sl]
            xk = ("T", sl)
            cols = slice(tc * TC, (tc + 1) * TC)
            rows = slice(kc * 128, (kc + 1) * 128)
            P.dma(xa[:], dr["xT"][rows, cols], slot=("xa", sl), reads=([("xout", kc, tc)] if dr.get("x_dep") else []), writes=[xk])
            for pi in range(n_part):
                pa = C.T[2 + sl]
                pk = ("T", 2 + sl)
                P.dma(pa[:], dr["parts"][pi][rows, cols], slot=("pa", sl), reads=[("yp", pi, kc, tc)], writes=[pk])
                P.tt(xa[:], xa[:], pa[:], ALU.add, reads=[xk, pk], writes=[xk])
            if have_xout:
                P.dma(dr["xout"][rows, cols], xa[:], slot=("xo", sl), reads=[xk], writes=[("xout", kc, tc), "xout_all"])
            if final_only:
                continue
            xs = C.B[sl]
            P.act(xs[:], xa[:], AF.Square, reads=[xk], writes=[("B", sl)])
            P.mm(stat_banks[tc][:], C.ones_b[:], xs[:], start=(kc == 0), stop=(kc == KC - 1),
                 reads=[("B", sl), "ones_b"], writes=[("ps", 4 + tc)])
    if final_only:
        return
    for tc in range(NT):
        r = C.T[4 + tc]
        rstd_act(P, r[:], ("T", 4 + tc), stat_banks[tc][:], [("ps", 4 + tc)], C.eps_norm[:], "eps_norm", 1.0 / D)
    src = dr["xout"] if have_xout else dr["xT"]
    for kc in range(KC):
        for tc in range(NT):
            sl = (kc * NT + tc) % 2
            xa = C.T[sl]
            xk = ("T", sl)
            cols = slice(tc * TC, (tc + 1) * TC)
            rows = slice(kc * 128, (kc + 1) * 128)
            rd = [("xout", kc, tc)] if have_xout else []
            P.dma(xa[:], src[rows, cols], slot=("xa", sl), reads=rd, writes=[xk])
            P.stt(C.hT[:, kc, cols], xa[:], C.gnorm[:, kc:kc + 1], C.T[4 + tc][:], ALU.mult, ALU.mult,
                  reads=[xk, "gnorm", ("T", 4 + tc)], writes=[("hT", kc, tc)])


class WStream:
    def __init__(self, P, C):
        self.P, self.C = P, C
        self.n = 0

    def load_group(self, w_dram, col0, nslab, gslot, kcn=KC, name="w"):
        P, C = self.P, self.C
        for s in range(nslab):
            sl = self.n % 2
            self.n += 1
            src = w_dram[:, col0 + s * 128: col0 + (s + 1) * 128].rearrange("(k p) c -> p k c", p=128)
            P.dma(C.wf[sl][:, 0:kcn, :], src, slot=("wf", sl), reads=[], writes=[("wf", sl)])
            P.act(C.wb[gslot][:, 0:kcn, s * 128:(s + 1) * 128], C.wf[sl][:, 0:kcn, :], AF.Copy,
                  reads=[("wf", sl)], writes=[("wb", gslot, s)])


def inproj_fm(P, C, gslot, s, tc, bank):
    cols = slice(tc * TC, (tc + 1) * TC)
    for kc in range(KC):
        P.mm(C.ps[bank][:], C.wb[gslot][:, kc, s * 128:(s + 1) * 128], C.hT[:, kc, cols],
             start=(kc == 0), stop=(kc == KC - 1),
             reads=[("wb", gslot, s), ("hT", kc, tc)], writes=[("ps", bank)])


def inproj_tm(P, C, gslot, s0, ncol, tc, tt, out_ap, bank):
    t0 = tc * TC + tt * 128
    rd = [("wb", gslot, s0 + i) for i in range((ncol + 127) // 128)]
    for kc in range(KC):
        P.mm(out_ap, C.hT[:, kc, t0:t0 + 128], C.wb[gslot][:, kc, s0 * 128:s0 * 128 + ncol],
             start=(kc == 0), stop=(kc == KC - 1),
             reads=rd + [("hT", kc, tc)], writes=[("ps", bank)])


def emit_outproj(P, C, ws, dr, nfc):
    S, NT = C.S, C.NT
    for fc in range(nfc):
        for tc in range(NT):
            cols = slice(tc * TC, (tc + 1) * TC)
            P.dma(C.hT[:, fc, cols], dr["mixT"][fc * 128:(fc + 1) * 128, cols], slot="mixld",
                  reads=[("mixT", dr.get("mix_tag", 0), fc, tc)], writes=[("hT", fc, tc), "mixall"])
    for dc in range(KC):
        g = dc % 2
        ws.load_group(dr["w_out"], dc * 128, 1, g, kcn=nfc)
        for tc in range(NT):
            cols = slice(tc * TC, (tc + 1) * TC)
            bank = (dc * NT + tc) % 2
            for fc in range(nfc):
                P.mm(C.ps[bank][:], C.wb[g][:, fc, 0:128], C.hT[:, fc, cols],
                     start=(fc == 0), stop=(fc == nfc - 1),
                     reads=[("wb", g, 0), "mixall"], writes=[("ps", bank)])
            ysl = (dc * NT + tc) % 2
            yt = C.T[12 + ysl]
            P.act(yt[:], C.ps[bank][:], AF.Copy, reads=[("ps", bank)], writes=[("T", 12 + ysl)])
            P.dma(dr["yp"][dc * 128:(dc + 1) * 128, cols], yt[:], slot=("yst", ysl),
                  reads=[("T", 12 + ysl)], writes=[("yp", dr.get("yp_tag", 0), dc, tc)])


REC_GROUPS = ["rg", "hg", "hg", "rg", "hg", "hg", "rg", "hg", "hg", "rg", "hg", "hg"]


def emit_rec_layer(P, C, dr, do_outproj=True):
    S, NT = C.S, C.NT
    nc = P.nc
    T, B, ps = C.T, C.B, C.ps
    R = C.R
    P.dma(R.convw[:], dr["convw"], slot="p0", reads=[], writes=["convw"])
    P.dma(R.vec[:], dr["recvec"], slot="p1", reads=[], writes=["recvec"])
    P.dma(R.lbraw[:], dr["hglb"], slot="p2", reads=[], writes=["lbraw"])
    P.dma(R.onorm[:], dr["onorm"], slot="p3", reads=[], writes=["onorm"])
    P.dma(R.cmask[:], dr["cmask"], slot="p4", reads=[], writes=["cmask"])
    P.dma(R.bmask[:], dr["bmask"], slot="p5", reads=[], writes=["bmask"])
    P.act(R.sp[:], R.vec[:, :, 3], AF.Exp, reads=["recvec"], writes=["sp"], scale=-1.0)
    P.act(R.sp[:], R.sp[:], AF.Ln, reads=["sp", "one_c"], writes=["sp"], bias=C.one_c[:], scale=1.0)
    P.add("dve", lambda e: e.tensor_scalar_mul(R.cneg[:], R.sp[:], -8.0), ["sp"], ["cneg"])
    P.add("dve", lambda e: e.tensor_scalar_mul(R.cneg2[:], R.sp[:], -16.0), ["sp"], ["cneg2"])
    P.tt(R.lb[:], R.lbraw[:, :, 1], R.lbraw[:, :, 0], ALU.subtract, reads=["lbraw"], writes=["lb"])
    P.act(R.lb[:], R.lb[:], AF.Sigmoid, reads=["lb"], writes=["lb"])
    P.add("dve", lambda e: e.tensor_scalar_mul(R.lb[:], R.lb[:], float(dr["lbflag"])), ["lb"], ["lb"])
    P.ts(R.oml[:], R.lb[:], -1.0, 1.0, ALU.mult, ALU.add, reads=["lb"], writes=["oml"])
    P.memset(R.zero_f[:], 0.0, writes=["zero_f"], eng="dve")
    P.memset(R.zero_b[:], 0.0, writes=["zero_b"], eng="dve")
    P.add("dve", lambda e: e.tensor_scalar_mul(R.vech[:], R.vec[:], 0.5), ["recvec"], ["vech"])
    P.add("dve", lambda e: e.tensor_scalar_mul(R.chalf[:], R.sp[:], -4.0), ["sp"], ["chalf"])
    P.add("dve", lambda e: e.tensor_scalar_mul(R.omh[:], R.oml[:], 0.5), ["oml"], ["omh"])
    P.tt(R.lbp[:], R.omh[:], R.lb[:], ALU.add, reads=["omh", "lb"], writes=["lbp"])

    ws = WStream(P, C)
    groups = REC_GROUPS
    objs = []
    ia = ie = 0
    for gi, gk in enumerate(groups):
        if gk == "rg":
            objs.append(RGGroup(P, C, dr, gi % 2, ia))
            ia += 1
        else:
            objs.append(HGGroup(P, C, dr, gi % 2, ie))
            ie += 1
    seq = [(gi, tc) for gi in range(len(groups)) for tc in range(NT)]
    ws.load_group(dr["w_in"], 0, 4, 0)
    objs[0].inproj(0, 0)
    for k, (gi, tc) in enumerate(seq):
        if tc == 0 and gi + 1 < len(groups):
            ws.load_group(dr["w_in"], (gi + 1) * 512, 4, (gi + 1) % 2)
        if k + 1 < len(seq):
            objs[seq[k + 1][0]].inproj(seq[k + 1][1], (k + 1) % 2)
        objs[gi].mixer(tc, k % 2)
    if do_outproj:
        emit_outproj(P, C, ws, dr, 16)


_BK = [0]
_SL = [0]


def rstd_act(P, out, out_key, in_, in_keys, eps_ap, eps_key, inv_n):
    P.act(out, in_, AF.Ln, reads=list(in_keys) + [eps_key], writes=[out_key], bias=eps_ap, scale=inv_n)
    P.act(out, out, AF.Exp, reads=[out_key], writes=[out_key], scale=-0.5)


def v_tokmajor(P, C, gslot, s, tc, bank, tmp_idx, out_ap, out_keys):
    inproj_fm(P, C, gslot, s, tc, bank)
    vT = C.B[tmp_idx]
    P.act(vT[:], C.ps[bank][:], AF.Copy, reads=[("ps", bank)], writes=[("B", tmp_idx)])
    pv = C.ps[bank][:].bitcast(BF16)
    for tt in range(4):
        P.tr(pv[:, tt * 128:(tt + 1) * 128], vT[:, tt * 128:(tt + 1) * 128], C.ident_b[:],
             reads=[("B", tmp_idx), "ident_b"], writes=[("ps", bank)])
    P.act(out_ap, pv[:, 0:512].rearrange("p (a b) -> p a b", a=4), AF.Copy, reads=[("ps", bank)], writes=out_keys)


def silu2(P, C, out_ap, out_key, bank):
    k = 4 + (_SL[0] % 2)
    _SL[0] += 1
    th = C.T[k]
    P.act(th[:], C.ps[bank][:], AF.Tanh, reads=[("ps", bank)], writes=[("T", k)], scale=0.5)
    P.stt(out_ap, th[:], 1.0, C.ps[bank][:], ALU.add, ALU.mult, reads=[("T", k), ("ps", bank)], writes=[out_key])


def next_bank():
    b = _BK[0] % 2
    _BK[0] += 1
    return b


class RGGroup:
    def __init__(self, P, C, dr, gslot, a):
        self.P, self.C, self.dr, self.gslot, self.a = P, C, dr, gslot, a

    def inproj(self, tc, par):
        P, C, dr, gslot, a = self.P, self.C, self.dr, self.gslot, self.a
        T, B, ps, R = C.T, C.B, C.ps, C.R
        if tc == 0:
            P.dma(R.wgf[:], dr["wgate"][a], slot="wg", reads=[], writes=["wgf"])
            P.act(R.wgb[:], R.wgf[:], AF.Copy, reads=["wgf"], writes=["wgb"])
        xr = R.xraw[par]
        for i in range(2):
            bank = next_bank()
            inproj_fm(P, C, gslot, i, tc, bank)
            if tc == 0:
                P.memset(xr[:, i, 0:3], 0.0, writes=[("xraw", par, i)], eng="dve")
            else:
                P.act(xr[:, i, 0:3], R.xraw[1 - par][:, i, TC:TC + 3], AF.Copy, reads=[("xraw", 1 - par, i)],
                      writes=[("xraw", par, i)])
            P.act(xr[:, i, 3:3 + TC], ps[bank][:], AF.Copy, reads=[("ps", bank)], writes=[("xraw", par, i)])
        for j in range(2):
            bank = next_bank()
            inproj_fm(P, C, gslot, 2 + j, tc, bank)
            silu2(P, C, B[par * 2 + j][:], ("B", par * 2 + j), bank)

    def mixer(self, tc, par):
        P, C, dr, gslot, a = self.P, self.C, self.dr, self.gslot, self.a
        T, B, ps, R = C.T, C.B, C.ps, C.R
        xr = R.xraw[par]
        for i in range(2):
            ch = 2 * a + i
            xc = T[6 + i]
            k = ("T", 6 + i)
            P.act(xc[:], xr[:, i, 3:3 + TC], AF.Identity, reads=[("xraw", par, i), "convw", "recvec"], writes=[k],
                  bias=R.vec[:, ch, 0:1], scale=R.convw[:, ch, 3:4])
            for tap in range(3):
                P.stt(xc[:], xr[:, i, tap:tap + TC], R.convw[:, ch, tap:tap + 1], xc[:], ALU.mult, ALU.add,
                      reads=[("xraw", par, i), "convw", k], writes=[k])
            P.act(B[6 + i][:], xc[:], AF.Copy, reads=[k], writes=[("B", 6 + i)])
        for j in range(2):
            ch = 2 * a + j
            for g in range(2):
                for i in range(2):
                    P.mm(ps[3 + g][:], R.wgb[:, g, i, j * 128:(j + 1) * 128], B[6 + i][:], start=(i == 0), stop=(i == 1),
                         reads=["wgb", ("B", 6 + i)], writes=[("ps", 3 + g)])
            r_, ig, aa, mm_, uu = T[8], T[9], T[10], T[11], T[12]
            P.act(r_[:], ps[3][:], AF.Tanh, reads=[("ps", 3), "vech"], writes=[("T", 8)], bias=R.vech[:, ch, 1:2], scale=0.5)
            P.act(ig[:], ps[4][:], AF.Tanh, reads=[("ps", 4), "vech"], writes=[("T", 9)], bias=R.vech[:, ch, 2:3], scale=0.5)
            P.act(aa[:], r_[:], AF.Exp, reads=[("T", 8), "chalf"], writes=[("T", 10)], scale=R.chalf[:, ch:ch + 1],
                  bias=R.chalf[:, ch:ch + 1])
            P.act(mm_[:], r_[:], AF.Exp, reads=[("T", 8), "cneg"], writes=[("T", 11)], scale=R.cneg[:, ch:ch + 1],
                  bias=R.cneg[:, ch:ch + 1])
            P.act(mm_[:], mm_[:], AF.Ln, reads=[("T", 11), "one_c"], writes=[("T", 11)], bias=C.one_c[:], scale=-1.0)
            P.act(mm_[:], mm_[:], AF.Exp, reads=[("T", 11)], writes=[("T", 11)], scale=0.5)
            P.stt(uu[:], ig[:], 1.0, T[6 + j][:], ALU.add, ALU.mult, reads=[("T", 9), ("T", 6 + j)], writes=[("T", 12)])
            P.stt(uu[:], uu[:], 0.5, mm_[:], ALU.mult, ALU.mult, reads=[("T", 12), ("T", 11)], writes=[("T", 12)])
            hcur = R.h[par][j]
            hk = ("h", par, j)
            if tc == 0:
                init = 0.0
                rd = []
            else:
                init = R.h[1 - par][j][:, TC - 1:TC]
                rd = [("h", 1 - par, j)]
            P.scan(hcur[:], aa[:], uu[:], init, reads=[("T", 10), ("T", 12)] + rd, writes=[hk])
            mo = B[8 + j]
            P.stt(mo[:], hcur[:], 0.5, B[par * 2 + j][:], ALU.mult, ALU.mult, reads=[hk, ("B", par * 2 + j)], writes=[("B", 8 + j)])
            fc = 2 * a + j
            P.dma(dr["mixT"][fc * 128:(fc + 1) * 128, tc * TC:(tc + 1) * TC], mo[:], slot=("mo", j),
                  reads=[("B", 8 + j)], writes=[("mixT", dr.get("mix_tag", 0), fc, tc)])


class HGGroup:
    def __init__(self, P, C, dr, gslot, e):
        self.P, self.C, self.dr, self.gslot, self.e = P, C, dr, gslot, e
        self.nS = 0

    def tiles(self, par):
        T, B = self.C.T, self.C.B
        qi, fi, si, vi = par, 2 + par, 2 * par, 2 * par + 1
        return (T[qi], ("T", qi)), (T[fi], ("T", fi)), (B[si], ("B", si)), (B[vi], ("B", vi))

    def inproj(self, tc, par):
        P, C, gslot = self.P, self.C, self.gslot
        ps = C.ps
        (qs, qk), (ff, fk), (sg, sk), (V, vk) = self.tiles(par)
        bank = next_bank()
        inproj_fm(P, C, gslot, 0, tc, bank)
        silu2(P, C, qs[:], qk, bank)
        bank = next_bank()
        inproj_fm(P, C, gslot, 1, tc, bank)
        silu2(P, C, sg[:], sk, bank)
        bank = next_bank()
        inproj_fm(P, C, gslot, 2, tc, bank)
        P.act(ff[:], ps[bank][:], AF.Tanh, reads=[("ps", bank)], writes=[fk], scale=0.5)
        bank = next_bank()
        v_tokmajor(P, C, gslot, 3, tc, bank, 10 + par, V[:].rearrange("p (a b) -> p a b", a=4), [vk])

    def mixer(self, tc, par):
        P, C, dr, e = self.P, self.C, self.dr, self.e
        T, B, ps, R = C.T, C.B, C.ps, C.R
        HG_SCALE = 128 ** -0.5
        (qs, qk), (ff, fk), (sg, sk), (V, vk) = self.tiles(par)
        lf, bb, eb, enb = T[8], T[9], T[10], T[11]
        P.ts(ff[:], ff[:], R.omh[:, e:e + 1], R.lbp[:, e:e + 1], ALU.mult, ALU.add, reads=[fk, "omh", "lbp"], writes=[fk])
        P.act(lf[:], ff[:], AF.Ln, reads=[fk], writes=[("T", 8)])
        P.scan(bb[:], R.cmask[:], lf[:], 0.0, reads=["cmask", ("T", 8)], writes=[("T", 9)])
        P.act(eb[:], bb[:], AF.Exp, reads=[("T", 9)], writes=[("T", 10)])
        P.act(enb[:], bb[:], AF.Exp, reads=[("T", 9)], writes=[("T", 11)], scale=-1.0)
        P.ts(ff[:], ff[:], -1.0, 1.0, ALU.mult, ALU.add, reads=[fk], writes=[fk])
        Qd, Kd, K2 = B[4], B[5], B[6]
        P.stt(Qd[:], qs[:], HG_SCALE * 0.5, eb[:], ALU.mult, ALU.mult, reads=[qk, ("T", 10)], writes=[("B", 4)])
        P.tt(Kd[:], ff[:], enb[:], ALU.mult, reads=[fk, ("T", 11)], writes=[("B", 5)])
        for c in range(8):
            cs = slice(c * 64, (c + 1) * 64)
            P.stt(K2[:, cs], ff[:, cs], eb[:, c * 64 + 63:c * 64 + 64], enb[:, cs], ALU.mult, ALU.mult,
                  reads=[fk, ("T", 10), ("T", 11)], writes=[("B", 6)])
        for tt in range(4):
            tsl = slice(tt * 128, (tt + 1) * 128)
            ap = tt % 2
            atb = 5 if ap == 0 else 3
            at_ps = ps[atb][:, 0:128]
            P.mm(at_ps, Kd[:, tsl], Qd[:, tsl], start=True, stop=True, reads=[("B", 5), ("B", 4)], writes=[("ps", atb)])
            atm = R.atm[ap]
            P.tt(atm[:], at_ps, R.bmask[:], ALU.mult, reads=[("ps", atb), "bmask"], writes=[("atm", ap)])
            tr_ps = ps[4][:].bitcast(BF16)[:, 0:128]
            P.tr(tr_ps, K2[:, tsl], C.ident_b[:], reads=[("B", 6), "ident_b"], writes=[("ps", 4)])
            k2t = R.k2t[ap]
            P.act(k2t[:], tr_ps, AF.Copy, reads=[("ps", 4)], writes=[("k2t", ap)])
            o_ps = ps[7][:, tsl]
            P.mm(o_ps, V[:, tsl], atm[:], start=True, stop=False, reads=[vk, ("atm", ap)], writes=[("ps", 7)])
            for h2 in range(2):
                nS = self.nS
                c = tt * 2 + h2
                if tc == 0 and c == 0:
                    sb_prev, sbk = R.zero_b, "zero_b"
                    s_prev, spk = R.zero_f, "zero_f"
                else:
                    sb_prev, sbk = R.Sb[(nS - 1) % 4], ("Sb", (nS - 1) % 4)
                    s_prev, spk = R.Sf[(nS - 1) % 4], ("Sf", (nS - 1) % 4)
                cg = slice(tt * 128 + h2 * 64, tt * 128 + (h2 + 1) * 64)
                P.mm(ps[7][:, cg], sb_prev[:], Qd[:, cg], start=False, stop=(h2 == 1),
                     reads=[sbk, ("B", 4)], writes=[("ps", 7)])
                ub = 6 if nS % 2 == 0 else 2
                u_ps = ps[ub][:, 0:128]
                P.mm(u_ps, k2t[h2 * 64:(h2 + 1) * 64, :], V[h2 * 64:(h2 + 1) * 64, tsl], start=True, stop=True,
                     reads=[("k2t", ap), vk], writes=[("ps", ub)])
                s_new = R.Sf[nS % 4]
                P.stt(s_new[:], s_prev[:], eb[:, c * 64 + 63:c * 64 + 64], u_ps, ALU.mult, ALU.add,
                      reads=[spk, ("T", 10), ("ps", ub)], writes=[("Sf", nS % 4)])
                P.act(R.Sb[nS % 4][:], s_new[:], AF.Copy, reads=[("Sf", nS % 4)], writes=[("Sb", nS % 4)])
                self.nS += 1
        osq = B[7]
        P.act(osq[:], ps[7][:], AF.Square, reads=[("ps", 7)], writes=[("B", 7)])
        P.mm(ps[4][:], C.ones_b[:], osq[:], start=True, stop=True, reads=["ones_b", ("B", 7)], writes=[("ps", 4)])
        rs = T[12]
        rstd_act(P, rs[:], ("T", 12), ps[4][:], [("ps", 4)], C.eps_norm[:], "eps_norm", 1.0 / 128)
        ot = T[13]
        P.stt(ot[:], ps[7][:], R.onorm[:, 0:1], rs[:], ALU.mult, ALU.mult, reads=[("ps", 7), "onorm", ("T", 12)], writes=[("T", 13)])
        mo = B[8 + (tc % 2)]
        P.stt(mo[:], ot[:], 0.5, sg[:], ALU.mult, ALU.mult, reads=[("T", 13), sk], writes=[("B", 8 + (tc % 2))])
        fc = 8 + e
        P.dma(dr["mixT"][fc * 128:(fc + 1) * 128, tc * TC:(tc + 1) * TC], mo[:], slot=("mo", tc % 2),
              reads=[("B", 8 + (tc % 2))], writes=[("mixT", dr.get("mix_tag", 0), fc, tc)])


def alloc_rec(nc, st, C, mx=None):
    R = Ctx()
    sb = lambda name, shape, dt: st.enter_context(nc.sbuf_tensor("s_" + name, shape, dt))
    R.convw = sb("convw", [128, 8, 4], F32)
    R.vec = sb("recvec", [128, 8, 4], F32)
    R.lbraw = sb("lbraw", [128, 8, 2], F32)
    R.onorm = sb("onorm", [128, 1], F32)
    if mx is None:
        R.cmask = sb("cmask", [128, TC], F32)
    R.bmask = sb("bmask", [128, 128], F32)
    R.sp = sb("sp", [128, 8], F32)
    R.cneg = sb("cneg", [128, 8], F32)
    R.cneg2 = sb("cneg2", [128, 8], F32)
    R.lb = sb("lb", [128, 8], F32)
    R.oml = sb("oml", [128, 8], F32)
    R.omh = sb("omh", [128, 8], F32)
    R.lbp = sb("lbp", [128, 8], F32)
    R.chalf = sb("chalf", [128, 8], F32)
    R.vech = sb("vech", [128, 8, 4], F32)
    R.zero_f = sb("zero_f", [128, 128], F32)
    R.zero_b = sb("zero_b", [128, 128], BF16)
    R.wgb = sb("wgb", [128, 2, 2, 256], BF16)
    if mx is None:
        R.wgf = sb("wgf", [128, 2, 2, 256], F32)
        R.xraw = [sb(f"xraw{i}", [128, 2, TC + 3], F32) for i in range(2)]
        R.h = [[sb(f"h{p}{j}", [128, TC], F32) for j in range(2)] for p in range(2)]
    else:
        XW = 2 * (TC + 3)
        R.xraw = [mx[:, i * XW:(i + 1) * XW].rearrange("p (a b) -> p a b", a=2) for i in range(2)]
        o = 2 * XW
        R.h = [[mx[:, o + (p * 2 + j) * TC: o + (p * 2 + j + 1) * TC] for j in range(2)] for p in range(2)]
        o += 4 * TC
        R.cmask = mx[:, o:o + TC]
        o += TC
        R.wgf = mx[:, o:o + 1024].rearrange("p (g i c) -> p g i c", g=2, i=2)
    R.atm = [sb(f"atm{i}", [128, 128], BF16) for i in range(2)]
    R.k2t = [sb(f"k2t{i}", [128, 128], BF16) for i in range(2)]
    R.Sf = [sb(f"Sf{i}", [128, 128], F32) for i in range(4)]
    R.Sb = [sb(f"Sb{i}", [128, 128], BF16) for i in range(4)]
    C.R = R


def emit_consts(P, C, dr):
    load_const_bf16(P, C, C.ones_b, dr["ones"], "ones_b")
    load_const_bf16(P, C, C.ident_b, dr["ident"], "ident_b")
    P.memset(C.eps_norm[:], NORM_EPS, writes=["eps_norm"])
    P.memset(C.one_c[:], 1.0, writes=["one_c"])


def build_rec_program(S, n_part, lbflag):
    nc = bass.Bass("TRN2", target_bir_lowering=False)
    di = lambda name, shape, dt=F32: nc.dram_tensor(name, shape, dt, kind="ExternalInput").ap()
    do = lambda name, shape, dt=F32: nc.dram_tensor(name, shape, dt, kind="ExternalOutput").ap()
    dr = {}
    dr["xT"] = di("xT", [D, S])
    dr["parts"] = [di(f"part{i}", [D, S]) for i in range(n_part)]
    dr["gnorm"] = di("gnorm", [128, KC])
    dr["w_in"] = di("w_in", [D, 6144])
    dr["w_out"] = di("w_out", [2048, D])
    dr["convw"] = di("convw", [128, 8, 4])
    dr["recvec"] = di("recvec", [128, 8, 4])
    dr["hglb"] = di("hglb", [128, 8, 2])
    dr["onorm"] = di("onorm", [128, 1])
    dr["wgate"] = di("wgate", [4, 128, 2, 2, 256])
    dr["cmask"] = di("cmask", [128, TC])
    dr["bmask"] = di("bmask", [128, 128])
    dr["ones"] = di("ones", [128, 128])
    dr["ident"] = di("ident", [128, 128])
    dr["lbflag"] = lbflag
    if n_part:
        dr["xout"] = do("xout", [D, S])
    dr["yp"] = do("yp", [D, S])
    dr["mixT"] = nc.dram_tensor("mixT", [2048, S], BF16, kind="Internal").ap()
    with ExitStack() as st:
        C = alloc_common(nc, st, S, "rec")
        C.eps_norm = st.enter_context(nc.sbuf_tensor("s_eps_norm", [128, 1], F32))
        C.one_c = st.enter_context(nc.sbuf_tensor("s_one_c", [128, 1], F32))
        alloc_rec(nc, st, C)
        P = Prog(nc)
        emit_consts(P, C, dr)
        emit_prologue(P, C, dr, n_part, bool(n_part))
        emit_rec_layer(P, C, dr)
        n = P.emit()
    return nc, n


def pvec(v):
    v = np.asarray(v, np.float32)
    return np.ascontiguousarray(v.reshape(-1, 128).T)


def rec_layout(inp, j, r):
    w_in = inp["rec_w_in"][j]
    w_out = inp["rec_w_out"][j]
    rg_heads = [4 * r + a for a in range(4)]
    hg_heads = [8 * r + e for e in range(8)]
    cols = []
    ia = ie = 0
    for gk in REC_GROUPS:
        if gk == "rg":
            hh = rg_heads[ia]; ia += 1
            cols.append(np.arange(256 * hh, 256 * hh + 256))
            cols.append(np.arange(2048 + 256 * hh, 2048 + 256 * hh + 256))
        else:
            e = hg_heads[ie]; ie += 1
            cols.append(np.arange(4096 + 128 * e, 4096 + 128 * e + 128))
            cols.append(np.arange(10240 + 128 * e, 10240 + 128 * e + 128))
            cols.append(np.arange(6144 + 128 * e, 6144 + 128 * e + 128))
            cols.append(np.arange(8192 + 128 * e, 8192 + 128 * e + 128))
    cols = np.concatenate(cols)
    rows = []
    for hh in rg_heads:
        rows.append(np.arange(256 * hh, 256 * hh + 256))
    for e in hg_heads:
        rows.append(np.arange(2048 + 128 * e, 2048 + 128 * e + 128))
    rows = np.concatenate(rows)
    ch = np.concatenate([np.arange(256 * hh, 256 * hh + 256) for hh in rg_heads])
    convw = np.ascontiguousarray(inp["rg_conv_w"][j][:, ch].T.reshape(8, 128, 4).transpose(1, 0, 2))
    vec = np.stack([inp["rg_conv_b"][j][ch], inp["rg_b_gate_a"][j][ch], inp["rg_b_gate_x"][j][ch],
                    inp["rg_lambda"][j][ch]], axis=-1)
    vec = np.ascontiguousarray(vec.reshape(8, 128, 4).transpose(1, 0, 2))
    hch = np.concatenate([np.arange(128 * e, 128 * e + 128) for e in hg_heads])
    lbr = np.stack([inp["hg_lb"][0][hch], inp["hg_lb"][j][hch]], axis=-1)
    lbr = np.ascontiguousarray(lbr.reshape(8, 128, 2).transpose(1, 0, 2))
    wg = np.stack([inp["rg_w_gate_a"][j][rg_heads], inp["rg_w_gate_x"][j][rg_heads]], axis=1)
    wg = wg.reshape(4, 2, 2, 128, 256).transpose(0, 3, 1, 2, 4)
    return dict(
        w_in=np.ascontiguousarray(w_in[:, cols]),
        w_out=np.ascontiguousarray(w_out[rows, :]),
        gnorm=pvec(inp["rec_norm"][j]),
        convw=convw.astype(np.float32), recvec=vec.astype(np.float32), hglb=lbr.astype(np.float32),
        onorm=np.ascontiguousarray(inp["hg_out_norm"][j].reshape(128, 1).astype(np.float32)),
        wgate=np.ascontiguousarray(wg.astype(np.float32)),
    )


def alloc_att(nc, st, C, mx=None):
    A = Ctx()
    S = C.S
    sb = lambda name, shape, dt: st.enter_context(nc.sbuf_tensor("s_" + name, shape, dt))
    if mx is None:
        A.KT = sb("KT", [128, 2, S], BF16)
        A.Vt = sb("Vt", [128, S // 128, 264], BF16)
    else:
        A.KT = mx[:, 0:S].bitcast(BF16).rearrange("p (a s) -> p a s", a=2)
        nv = (S // 128) * 132
        A.Vt = mx[:, S:S + nv].bitcast(BF16).rearrange("p (t c) -> p t c", c=264)
    A.cosT = sb("cosT", [128, S], F32)
    A.sinT = sb("sinT", [128, S], F32)
    A.pmat = sb("pmat", [128, 128], F32)
    A.ones_f = sb("ones_f", [128, 128], F32)
    A.qkg = sb("qkg", [128, 2], F32)
    A.subg = sb("subg", [128, 256], F32)
    A.lp = sb("lp", [128, 4], F32)
    A.pr = sb("pr", [128, 2], F32)
    A.ex = sb("ex", [128, 2], F32)
    A.neglam = sb("neglam", [128, 1], F32)
    A.tri2f = sb("tri2f", [128, 256], F32)
    A.tri2 = sb("tri2", [128, 256], BF16)
    A.pt = [sb(f"pt{i}", [128, 256], BF16) for i in range(2)]
    A.o = sb("o", [128, 256], F32)
    A.junk = sb("junk", [128, 256], F32)
    A.on = sb("on", [128, 256], BF16)
    A.sm = sb("sm", [128, 8], F32)
    A.eps_sub = sb("eps_sub", [128, 1], F32)
    C.A = A


def emit_att_layer(P, C, dr, lam_init, do_outproj=True):
    S, NT = C.S, C.NT
    T, B, ps, A = C.T, C.B, C.ps, C.A
    SCALE = 128 ** -0.5
    P.dma(A.cosT[:], dr["cosT"], slot="a0", reads=[], writes=["cosT"])
    P.dma(A.sinT[:], dr["sinT"], slot="a1", reads=[], writes=["sinT"])
    P.dma(A.pmat[:], dr["pmat"], slot="a2", reads=[], writes=["pmat"])
    P.dma(A.ones_f[:], dr["ones"], slot="a3", reads=[], writes=["ones_f"])
    P.dma(A.qkg[:], dr["qkg"], slot="a4", reads=[], writes=["qkg"])
    P.dma(A.subg[:], dr["subg"], slot="a5", reads=[], writes=["subg"])
    P.dma(A.lp[:], dr["lp"], slot="a6", reads=[], writes=["lp"])
    P.dma(A.tri2f[:], dr["tri2"], slot="a7", reads=[], writes=["tri2f"])
    P.cp(A.tri2[:], A.tri2f[:], reads=["tri2f"], writes=["tri2"], eng="dve")
    P.memset(A.eps_sub[:], SUBLN_EPS, writes=["eps_sub"], eng="dve")
    P.add("dve", lambda e: e.tensor_scalar_mul(A.subg[:], A.subg[:], 1.0 - lam_init), ["subg"], ["subg"])
    P.tt(A.pr[:, 0:1], A.lp[:, 0:1], A.lp[:, 1:2], ALU.mult, reads=["lp"], writes=["pr"])
    P.tt(A.pr[:, 1:2], A.lp[:, 2:3], A.lp[:, 3:4], ALU.mult, reads=["lp", "pr"], writes=["pr"])
    P.mm(ps[2][:, 0:2], A.ones_f[:], A.pr[:], start=True, stop=True, reads=["ones_f", "pr"], writes=[("ps", 2)])
    P.act(A.ex[:], ps[2][:, 0:2], AF.Exp, reads=[("ps", 2)], writes=["ex"])
    P.tt(A.neglam[:], A.ex[:, 1:2], A.ex[:, 0:1], ALU.subtract, reads=["ex"], writes=["neglam"])
    P.add("dve", lambda e: e.tensor_scalar_add(A.neglam[:], A.neglam[:], -lam_init), ["neglam"], ["neglam"])
    P.memset(A.Vt[:, :, 256:257], 1.0, writes=["Vt_ones"], eng="dve")

    ws = WStream(P, C)
    ws.load_group(dr["w_in"], 0, 4, 0)
    ngroups = 8
    bkc = [0]

    def nb():
        b = bkc[0] % 2
        bkc[0] += 1
        return b

    def norm_rope(bank, gcol, out_ap, out_keys, tc):
        cols = slice(tc * TC, (tc + 1) * TC)
        raw, rstd, t1, t2, sq = T[6], T[7], T[8], T[9], B[2]
        P.act(raw[:], ps[bank][:], AF.Copy, reads=[("ps", bank)], writes=[("T", 6)])
        P.act(sq[:], ps[bank][:], AF.Square, reads=[("ps", bank)], writes=[("B", 2)])
        P.mm(ps[2][:], C.ones_b[:], sq[:], start=True, stop=True, reads=["ones_b", ("B", 2)], writes=[("ps", 2)])
        rstd_act(P, rstd[:], ("T", 7), ps[2][:], [("ps", 2)], C.eps_norm[:], "eps_norm", 1.0 / 128)
        P.stt(raw[:], raw[:], A.qkg[:, gcol:gcol + 1], rstd[:], ALU.mult, ALU.mult, reads=[("T", 6), "qkg", ("T", 7)], writes=[("T", 6)])
        P.mm(ps[3][:], A.pmat[:], raw[:], start=True, stop=True, reads=["pmat", ("T", 6)], writes=[("ps", 3)])
        P.tt(t1[:], raw[:], A.cosT[:, cols], ALU.mult, reads=[("T", 6), "cosT"], writes=[("T", 8)])
        P.tt(t2[:], ps[3][:], A.sinT[:, cols], ALU.mult, reads=[("ps", 3), "sinT"], writes=[("T", 9)])
        P.tt(out_ap, t1[:], t2[:], ALU.add, reads=[("T", 8), ("T", 9)], writes=out_keys)

    for gi in range(ngroups):
        gslot = gi % 2
        a = gi // 2
        if gi + 1 < ngroups:
            ws.load_group(dr["w_in"], (gi + 1) * 512, 4, (gi + 1) % 2)
        if gi % 2 == 0:
            for tc in range(NT):
                cols = slice(tc * TC, (tc + 1) * TC)
                for i in range(2):
                    bank = nb()
                    inproj_fm(P, C, gslot, i, tc, bank)
                    norm_rope(bank, 1, A.KT[:, i, cols], [("KT", i, tc)], tc)
                for half in range(2):
                    bank = nb()
                    tile0 = tc * 4
                    v_tokmajor(P, C, gslot, 2 + half, tc, bank, 10 + half, A.Vt[:, tile0:tile0 + 4, half * 128:(half + 1) * 128],
                               [("Vt", tile0 + q_, half) for q_ in range(4)])
        else:
            for tc in range(NT):
                cols = slice(tc * TC, (tc + 1) * TC)
                Qt = [B[4], B[5]]
                sgt = [B[6], B[7]]
                for i in range(2):
                    bank = nb()
                    inproj_fm(P, C, gslot, i, tc, bank)
                    norm_rope(bank, 0, Qt[i][:], [("B", 4 + i)], tc)
                for i in range(2):
                    bank = nb()
                    inproj_fm(P, C, gslot, 2 + i, tc, bank)
                    silu2(P, C, sgt[i][:], ("B", 6 + i), bank)
                for qt in range(4):
                    j = tc * 4 + qt
                    qs = slice(qt * 128, (qt + 1) * 128)

                    def emit_qk(i):
                        sb_ = 4 if i % 2 == 0 else 7
                        ks = slice(i * 128, (i + 1) * 128)
                        for m in range(2):
                            P.mm(ps[sb_][:, m * 128:(m + 1) * 128], A.KT[:, m, ks], Qt[m][:, qs], start=True, stop=True,
                                 reads=[("KT", m, i // 4), ("B", 4 + m)], writes=[("ps", sb_)])
                        pt = A.pt[i % 2]
                        P.act(pt[:], ps[sb_][:, 0:256], AF.Exp, reads=[("ps", sb_)], writes=[("pt", i % 2)], scale=SCALE)
                        if i == j:
                            P.tt(pt[:], pt[:], A.tri2[:], ALU.mult, reads=[("pt", i % 2), "tri2"], writes=[("pt", i % 2)])

                    def emit_pv(i):
                        pt = A.pt[i % 2]
                        for m in range(2):
                            P.mm(ps[5 + m][:, 0:257], pt[:, m * 128:(m + 1) * 128], A.Vt[:, i, 0:257],
                                 start=(i == 0), stop=(i == j),
                                 reads=[("pt", i % 2), ("Vt", i, 0), ("Vt", i, 1), "Vt_ones"], writes=[("ps", 5 + m)])

                    emit_qk(0)
                    for i in range(j + 1):
                        if i + 1 <= j:
                            emit_qk(i + 1)
                        emit_pv(i)
                    sm = A.sm
                    P.recip(sm[:, 0:1], ps[5][:, 256:257], reads=[("ps", 5)], writes=["sm"])
                    P.recip(sm[:, 1:2], ps[6][:, 256:257], reads=[("ps", 6), "sm"], writes=["sm"])
                    P.tt(sm[:, 2:3], sm[:, 1:2], A.neglam[:], ALU.mult, reads=["sm", "neglam"], writes=["sm"])
                    P.add("dve", lambda e, sm=sm: e.tensor_scalar_mul(A.o[:], ps[5][:, 0:256], sm[:, 0:1]), [("ps", 5), "sm"], ["o"])
                    P.stt(A.o[:], ps[6][:, 0:256], sm[:, 2:3], A.o[:], ALU.mult, ALU.add, reads=[("ps", 6), "sm", "o"], writes=["o"])
                    P.add("act", lambda e, sm=sm: e.activation(A.junk[:], A.o[:], AF.Square, accum_out=sm[:, 3:4]), ["o"], ["junk", "sm3"])
                    rstd_act(P, sm[:, 5:6], "sm5", sm[:, 3:4], ["sm3"], A.eps_sub[:], "eps_sub", 1.0 / 256)
                    P.stt(A.on[:], A.o[:], sm[:, 5:6], A.subg[:], ALU.mult, ALU.mult, reads=["o", "sm5", "subg"], writes=["on"])
                    trv = ps[3][:].bitcast(BF16)
                    for m in range(2):
                        P.tr(trv[:, m * 128:(m + 1) * 128], A.on[:, m * 128:(m + 1) * 128], C.ident_b[:],
                             reads=["on", "ident_b"], writes=[("ps", 3)])
                    for m in range(2):
                        P.stt(B[8 + m][:, qs], trv[:, m * 128:(m + 1) * 128], 0.5, sgt[m][:, qs], ALU.mult, ALU.mult,
                              reads=[("ps", 3), ("B", 6 + m)], writes=[("B", 8 + m)])
                for m in range(2):
                    fc = 2 * a + m
                    P.dma(dr["mixT"][fc * 128:(fc + 1) * 128, cols], B[8 + m][:], slot=("mo", m),
                          reads=[("B", 8 + m)], writes=[("mixT", dr.get("mix_tag", 0), fc, tc)])
    if do_outproj:
        emit_outproj(P, C, ws, dr, 8)


def build_att_program(S, n_part, layer):
    nc = bass.Bass("TRN2", target_bir_lowering=False)
    di = lambda name, shape, dt=F32: nc.dram_tensor(name, shape, dt, kind="ExternalInput").ap()
    do = lambda name, shape, dt=F32: nc.dram_tensor(name, shape, dt, kind="ExternalOutput").ap()
    dr = {}
    dr["xT"] = di("xT", [D, S])
    dr["parts"] = [di(f"part{i}", [D, S]) for i in range(n_part)]
    dr["gnorm"] = di("gnorm", [128, KC])
    dr["w_in"] = di("w_in", [D, 4096])
    dr["w_out"] = di("w_out", [1024, D])
    dr["cosT"] = di("cosT", [128, S])
    dr["sinT"] = di("sinT", [128, S])
    dr["pmat"] = di("pmat", [128, 128])
    dr["qkg"] = di("qkg", [128, 2])
    dr["subg"] = di("subg", [128, 256])
    dr["lp"] = di("lp", [128, 4])
    dr["tri2"] = di("tri2", [128, 256])
    dr["ones"] = di("ones", [128, 128])
    dr["ident"] = di("ident", [128, 128])
    if n_part:
        dr["xout"] = do("xout", [D, S])
    dr["yp"] = do("yp", [D, S])
    dr["mixT"] = nc.dram_tensor("mixT", [1024, S], BF16, kind="Internal").ap()
    lam_init = 0.8 - 0.6 * math.exp(-0.3 * layer)
    with ExitStack() as st:
        C = alloc_common(nc, st, S, "att")
        C.eps_norm = st.enter_context(nc.sbuf_tensor("s_eps_norm", [128, 1], F32))
        C.one_c = st.enter_context(nc.sbuf_tensor("s_one_c", [128, 1], F32))
        alloc_att(nc, st, C)
        P = Prog(nc)
        emit_consts(P, C, dr)
        emit_prologue(P, C, dr, n_part, bool(n_part))
        emit_att_layer(P, C, dr, lam_init)
        n = P.emit()
    return nc, n


def att_layout(inp, j, r):
    w_in = inp["att_w_in"][j]
    w_out = inp["att_w_out"][j]
    heads = [4 * r + a for a in range(4)]
    cols = []
    for h in heads:
        cols.append(np.arange(2048 + 256 * h, 2048 + 256 * h + 256))
        cols.append(np.arange(4096 + 256 * h, 4096 + 256 * h + 256))
        cols.append(np.arange(256 * h, 256 * h + 256))
        cols.append(np.arange(6144 + 256 * h, 6144 + 256 * h + 256))
    cols = np.concatenate(cols)
    rows = np.concatenate([np.arange(256 * h, 256 * h + 256) for h in heads])
    qkg = np.stack([inp["att_q_norm"][j], inp["att_k_norm"][j]], axis=-1).astype(np.float32)
    subg = np.ascontiguousarray(np.broadcast_to(inp["att_sub_norm"][j][None, :], (128, 256)).astype(np.float32))
    lp = np.ascontiguousarray(inp["att_lambda"][j].T.astype(np.float32))
    return dict(
        w_in=np.ascontiguousarray(w_in[:, cols]),
        w_out=np.ascontiguousarray(w_out[rows, :]),
        gnorm=pvec(inp["att_norm"][j]),
        qkg=np.ascontiguousarray(qkg), subg=subg, lp=lp,
    )


def build_final_program(S):
    nc = bass.Bass("TRN2", target_bir_lowering=False)
    di = lambda name, shape, dt=F32: nc.dram_tensor(name, shape, dt, kind="ExternalInput").ap()
    x = di("xT", [1024, S])
    p0 = di("part0", [1024, S])
    p1 = di("part1", [1024, S])
    out = nc.dram_tensor("xout", [1024, S], F32, kind="ExternalOutput").ap()
    with ExitStack() as st:
        tl = [[st.enter_context(nc.sbuf_tensor(f"s_f{k}{i}", [128, S], F32)) for i in range(2)] for k in range(3)]
        P = Prog(nc)
        for c in range(8):
            sl = c % 2
            rows = slice(c * 128, (c + 1) * 128)
            xa, pa, pb = tl[0][sl], tl[1][sl], tl[2][sl]
            P.dma(xa[:], x[rows, :], slot=("fx", sl), reads=[], writes=[("fx", sl)])
            P.dma(pa[:], p0[rows, :], slot=("fa", sl), reads=[], writes=[("fa", sl)])
            P.dma(pb[:], p1[rows, :], slot=("fb", sl), reads=[], writes=[("fb", sl)])
            P.tt(xa[:], xa[:], pa[:], ALU.add, reads=[("fx", sl), ("fa", sl)], writes=[("fx", sl)])
            P.tt(xa[:], xa[:], pb[:], ALU.add, reads=[("fx", sl), ("fb", sl)], writes=[("fx", sl)])
            P.dma(out[rows, :], xa[:], slot=("fo", sl), reads=[("fx", sl)], writes=[("out", c)])
        P.emit()
    return nc


REC_KEYS = ("w_in", "w_out", "gnorm", "convw", "recvec", "hglb", "onorm", "wgate")
ATT_KEYS = ("w_in", "w_out", "gnorm", "qkg", "subg", "lp")
CONST_KEYS = ("ones", "ident", "cmask", "bmask", "cosT", "sinT", "pmat", "tri2")
ACTIVE_CORES = (0, 1, 2, 3)
NCORES = 4


def build_fused_program(S, nlayers=4, nseq=1):
    nc = bass.Bass("TRN2", target_bir_lowering=False)
    di = lambda name, shape, dt=F32: nc.dram_tensor(name, shape, dt, kind="ExternalInput").ap()
    x_ins = [di("xT" if q == 0 else f"xT{q}", [D, S]) for q in range(nseq)]
    outs = [nc.dram_tensor("outT" if q == 0 else f"outT{q}", [D, S], F32, kind="ExternalOutput").ap() for q in range(nseq)]
    yp = [nc.dram_tensor(f"yp{i}", [D, S], F32, kind="Internal").ap() for i in range(2)]
    xb = [nc.dram_tensor(f"xbuf{i}", [D, S], F32, kind="Internal").ap() for i in range(2)]
    mixTs = [nc.dram_tensor(f"mixT{i}", [2048, S], BF16, kind="Internal").ap() for i in range(2)]
    cshape = dict(ones=[128, 128], ident=[128, 128], cmask=[128, TC], bmask=[128, 128], cosT=[128, S], sinT=[128, S],
                  pmat=[128, 128], tri2=[128, 256])
    cd = {k: di(k, cshape[k]) for k in CONST_KEYS}
    rshape = dict(w_in=[D, 6144], w_out=[2048, D], gnorm=[128, KC], convw=[128, 8, 4], recvec=[128, 8, 4], hglb=[128, 8, 2],
                  onorm=[128, 1], wgate=[4, 128, 2, 2, 256])
    ashape = dict(w_in=[D, 4096], w_out=[1024, D], gnorm=[128, KC], qkg=[128, 2], subg=[128, 256], lp=[128, 4])
    drs = []
    for L in range(nlayers):
        sh = rshape if L % 2 == 0 else ashape
        halves = []
        for hf in range(2):
            dr = {k: di(f"{k}{L}h{hf}", sh[k]) for k in sh if not (k == "gnorm" and hf == 1)}
            if hf == 1:
                dr["gnorm"] = halves[0]["gnorm"]
            dr.update(cd)
            dr["mixT"] = mixTs[hf] if L % 2 == 0 else mixTs[hf][0:1024, :]
            dr["mix_tag"] = hf
            dr["yp"] = yp[hf]
            dr["yp_tag"] = hf
            dr["lbflag"] = 0.0 if L == 0 else 1.0
            halves.append(dr)
        drs.append(halves)
    with ExitStack() as st:
        C = alloc_common(nc, st, S, "all")
        C.eps_norm = st.enter_context(nc.sbuf_tensor("s_eps_norm", [128, 1], F32))
        C.one_c = st.enter_context(nc.sbuf_tensor("s_one_c", [128, 1], F32))
        mx = st.enter_context(nc.sbuf_tensor("s_mx", [128, 5644], F32))
        alloc_rec(nc, st, C, mx)
        alloc_att(nc, st, C, mx)
        P = Prog(nc)
        emit_consts(P, C, cd)
        for q in range(nseq):
          x_in, out = x_ins[q], outs[q]
          for L in range(nlayers):
              pr = dict(gnorm=drs[L][0]["gnorm"])
              if L == 0:
                  pr["xT"] = x_in
                  pr["parts"] = []
              else:
                  pr["xT"] = x_in if L == 1 else xb[(L - 2) % 2]
                  pr["x_dep"] = L >= 2
                  pr["parts"] = yp
                  pr["xout"] = xb[(L - 1) % 2]
              emit_prologue(P, C, pr, len(pr["parts"]), L > 0)
              for hf in range(2):
                  if L % 2 == 0:
                      emit_rec_layer(P, C, drs[L][hf], do_outproj=False)
                  else:
                      emit_att_layer(P, C, drs[L][hf], 0.8 - 0.6 * math.exp(-0.3 * L), do_outproj=False)
              for hf in range(2):
                  emit_outproj(P, C, WStream(P, C), drs[L][hf], 16 if L % 2 == 0 else 8)
              P.barrier()
          pr = dict(xT=(x_in if nlayers == 1 else xb[(nlayers - 2) % 2]), x_dep=nlayers >= 2, parts=yp,
                    xout=out, gnorm=drs[0][0]["gnorm"])
          emit_prologue(P, C, pr, 2, True, final_only=True)
          P.barrier()
        n = P.emit()
    return nc, n


def fused_maps(inp, S=SEQ, nlayers=4, ncores=2, nseq=2):
    cs = host_consts(S)
    x = np.asarray(inp["x"], np.float32)
    wmap = {}
    for L in range(nlayers):
        j = L // 2
        for hf in range(2):
            lay = rec_layout(inp, j, hf) if L % 2 == 0 else att_layout(inp, j, hf)
            for k, v in lay.items():
                if k == "gnorm" and hf == 1:
                    continue
                wmap[f"{k}{L}h{hf}"] = v
    maps = []
    for c in range(ncores):
        m = {}
        for q in range(nseq):
            b = c * nseq + q
            m["xT" if q == 0 else f"xT{q}"] = np.ascontiguousarray(x[b][:S].T)
        m.update(wmap)
        for k in CONST_KEYS:
            m[k] = cs[k]
        maps.append(m)
    return maps


_PROGS = {}
NCORES = 2
NSEQ = 2


def kernel(**inp):
    inp = {k: np.asarray(v) for k, v in inp.items()}
    if "fused" not in _PROGS:
        _PROGS["fused"] = build_fused_program(SEQ, 4, NSEQ)[0]
    nc = _PROGS["fused"]
    maps = fused_maps(inp, SEQ, 4, NCORES, NSEQ)
    res = run_bass_kernel_spmd(nc, maps, core_ids=list(range(NCORES)))
    out = np.empty((BATCH, SEQ, D), np.float32)
    for c in range(NCORES):
        for q in range(NSEQ):
            out[c * NSEQ + q] = np.asarray(res.results[c]["outT" if q == 0 else f"outT{q}"]).T
    return out
```

```python
import math
from contextlib import ExitStack

import numpy as np
import concourse.bass as bass
import concourse.mybir as mybir
from concourse.bass_utils import run_bass_kernel_spmd

F32 = mybir.dt.float32
BF16 = mybir.dt.bfloat16
AF = mybir.ActivationFunctionType
ALU = mybir.AluOpType

D = 2048
KC = 16
SEQ = 2048
BATCH = 4
TC = 512
NORM_EPS = 1e-6
SUBLN_EPS = 1e-5
SAME_ENGINE_SYNC = True


def _fs(ap):
    try:
        return int(ap.free_size())
    except Exception:
        return 512


RESCHEDULE = True


class Prog:
    ENGS = ("pe", "act", "dve", "pool", "sp")

    def __init__(self, nc):
        self.nc = nc
        self.ops = []

    def add(self, eng, fn, reads=(), writes=(), dma=None, cost=0.5):
        self.ops.append(dict(kind="op", eng=eng, fn=fn, reads=tuple(reads), writes=tuple(writes), dma=dma, cost=cost))

    def barrier(self):
        self.ops.append(dict(kind="barrier"))

    def mm(self, out, lhsT, rhs, start, stop, reads, writes):
        self.add("pe", lambda e: e.matmul(out, lhsT, rhs, start=start, stop=stop), reads, writes,
                 cost=0.035 + _fs(rhs) / 2400.0)

    def tr(self, out, in_, ident, reads, writes):
        self.add("pe", lambda e: e.transpose(out, in_, ident), reads, writes, cost=0.1)

    def act(self, out, in_, func, reads, writes, bias=0.0, scale=1.0, eng="act"):
        self.add(eng, lambda e: e.activation(out, in_, func, bias=bias, scale=scale), reads, writes,
                 cost=0.2 + _fs(out) * 0.0008)
        self.ops[-1]["tset"] = "tanh" if func == AF.Tanh else ("ln" if func == AF.Ln else None)

    def ts(self, out, in0, s1, s2, op0, op1, reads, writes, eng="dve"):
        self.add(eng, lambda e: e.tensor_scalar(out, in0, s1, s2, op0, op1), reads, writes, cost=0.15 + _fs(out) * 0.001)

    def stt(self, out, in0, scalar, in1, op0, op1, reads, writes, eng="dve"):
        self.add(eng, lambda e: e.scalar_tensor_tensor(out, in0, scalar, in1, op0, op1), reads, writes,
                 cost=0.2 + _fs(out) * 0.0011)

    def tt(self, out, in0, in1, op, reads, writes, eng="dve"):
        c = 0.15 + _fs(out) * (0.001 if eng == "dve" else 0.0022)
        self.add(eng, lambda e: e.tensor_tensor(out, in0, in1, op), reads, writes, cost=c)

    def cp(self, out, in_, reads, writes, eng="dve"):
        c = 0.15 + _fs(out) * 0.001
        self.add(eng, lambda e: e.tensor_copy(out, in_), reads, writes, cost=c)

    def memset(self, ap, val, writes, eng="dve"):
        self.add(eng, lambda e: e.memset(ap, val), (), writes, cost=0.2)

    def recip(self, out, in_, reads, writes):
        self.add("dve", lambda e: e.reciprocal(out, in_), reads, writes, cost=0.15 + _fs(out) * 0.001)

    def scan(self, out, d0, d1, init, reads, writes):
        self.add("dve", lambda e: e.tensor_tensor_scan(out, d0, d1, init, ALU.mult, ALU.add), reads, writes,
                 cost=0.2 + _fs(out) * 0.002)

    def dma(self, out, in_, slot, reads, writes, eng="sp"):
        try:
            nbytes = int(out.nbytes())
        except Exception:
            nbytes = 1 << 18
        self.add(eng, lambda e: e.dma_start(out, in_), reads, writes, dma=slot, cost=2.0 + nbytes / 150e3)

    def emit(self):
        nc = self.nc
        real = []
        seg_of = []
        seg = 0
        last_w = {}
        readers = {}
        for o in self.ops:
            if o["kind"] == "barrier":
                seg += 1
                continue
            o["idx"] = len(real)
            real.append(o)
            o["seg"] = seg
            deps = set()
            for r in o["reads"]:
                if r in last_w:
                    deps.add(last_w[r])
            for w in o["writes"]:
                if w in last_w:
                    deps.add(last_w[w])
                deps.update(readers.get(w, ()))
            deps.discard(o["idx"])
            for w in o["writes"]:
                last_w[w] = o["idx"]
                readers[w] = []
            for r in o["reads"]:
                if r not in o["writes"]:
                    readers.setdefault(r, []).append(o["idx"])
            o["skey"] = ("dma", o["dma"]) if o["dma"] is not None else ("eng", o["eng"])
            o["deps"] = [d for d in deps if real[d]["seg"] == seg]
            o["signal"] = False
        nseg = seg + 1
        import heapq
        order = {e: [] for e in self.ENGS}
        seg_ops = [[] for _ in range(nseg)]
        for o in real:
            seg_ops[o["seg"]].append(o["idx"])
        fin = [0.0] * len(real)
        tnow = 0.0
        LAT = 0.15
        for sg in range(nseg):
            ids = seg_ops[sg]
            if not RESCHEDULE:
                for i in ids:
                    order[real[i]["eng"]].append(i)
                continue
            nd = {i: len(real[i]["deps"]) for i in ids}
            users = {i: [] for i in ids}
            for i in ids:
                for d in real[i]["deps"]:
                    users[d].append(i)
            ready = {e: [] for e in self.ENGS}
            rt = {}
            for i in ids:
                if nd[i] == 0:
                    rt[i] = tnow
                    heapq.heappush(ready[real[i]["eng"]], i)
            free = {e: tnow for e in self.ENGS}
            cur_set = [None]
            sp_list = [i for i in ids if real[i]["eng"] == "sp"]
            sp_pos = 0
            remaining = len(ids)
            while remaining:
                best = None
                for e in self.ENGS:
                    h = ready[e]
                    if not h:
                        continue
                    if e == "sp":
                        if sp_pos < len(sp_list) and nd[sp_list[sp_pos]] == 0:
                            i = sp_list[sp_pos]
                            st_ = max(free[e], rt[i])
                            cand = (st_, i, e)
                        else:
                            continue
                    else:
                        cands = heapq.nsmallest(12 if e == "act" else 6, h)
                        cand = None
                        for i in cands:
                            st_ = max(free[e], rt[i])
                            if e == "act":
                                ts_ = real[i].get("tset")
                                if ts_ is not None and cur_set[0] is not None and ts_ != cur_set[0]:
                                    st_ += 1.3
                            if cand is None or st_ < cand[0] - 1e-9:
                                cand = (st_, i, e)
                    if cand is not None and (best is None or cand[0] < best[0] - 1e-9 or
                                             (abs(cand[0] - best[0]) <= 1e-9 and cand[1] < best[1])):
                        best = cand
                if best is None:
                    raise RuntimeError("scheduler deadlock")
                st_, i, e = best
                if e == "sp":
                    sp_pos += 1
                    ready[e].remove(i)
                    heapq.heapify(ready[e])
                else:
                    ready[e].remove(i)
                    heapq.heapify(ready[e])
                o = real[i]
                if e == "act" and o.get("tset") is not None:
                    cur_set[0] = o["tset"]
                if o["dma"] is not None:
                    free[e] = st_ + 0.1
                    fin[i] = st_ + o["cost"]
                else:
                    free[e] = st_ + o["cost"]
                    fin[i] = free[e]
                order[e].append(i)
                remaining -= 1
                for u in users[i]:
                    nd[u] -= 1
                    rt[u] = max(rt.get(u, tnow), fin[i] + LAT)
                    if nd[u] == 0:
                        heapq.heappush(ready[real[u]["eng"]], u)
            tnow = max([tnow] + [fin[i] for i in ids]) + 1.0
        self.est_us = tnow
        pos = {}
        for e in self.ENGS:
            for p, i in enumerate(order[e]):
                pos[i] = p
        dma_seq = {}
        for i in order["sp"] + [i for e in self.ENGS if e != "sp" for i in order[e]]:
            pass
        cnt = {}
        for e in self.ENGS:
            for i in order[e]:
                o = real[i]
                if o["dma"] is not None:
                    cnt[o["skey"]] = cnt.get(o["skey"], 0) + 16
                    o["sigval"] = cnt[o["skey"]]
                    o["dpos"] = cnt[o["skey"]]
        last_in_seg = [dict() for _ in range(nseg)]
        for e in self.ENGS:
            for i in order[e]:
                o = real[i]
                last_in_seg[o["seg"]][o["skey"]] = i
        first_seen = set()
        for e in self.ENGS:
            for i in order[e]:
                o = real[i]
                if o["seg"] > 0 and (e, o["seg"]) not in first_seen:
                    first_seen.add((e, o["seg"]))
                    for sgp in range(o["seg"]):
                        o["deps"] = list(o["deps"]) + list(last_in_seg[sgp].values())
        for o in real:
            keep = {}
            for d in o["deps"]:
                od = real[d]
                if od["dma"] is None and od["eng"] == o["eng"]:
                    if o["eng"] == "pe" or not SAME_ENGINE_SYNC:
                        continue
                sk = od["skey"]
                key = od["dpos"] if od["dma"] is not None else pos[d]
                if sk not in keep or keep[sk][0] < key:
                    keep[sk] = (key, d)
            o["deps"] = [v[1] for v in keep.values()]
            for d in o["deps"]:
                real[d]["signal"] = True
        for e in self.ENGS:
            c = 0
            for i in order[e]:
                o = real[i]
                if o["dma"] is None and o["signal"]:
                    c += 1
                    o["sigval"] = c
        skeys = sorted({o["skey"] for o in real}, key=str)
        with ExitStack() as st:
            sems = {}
            for i, sk in enumerate(skeys):
                sems[sk] = st.enter_context(nc.semaphore(f"sem{i}"))
            dma_final = {sk: cnt[sk] for sk in skeys if sk[0] == "dma"}
            self.sem_names = {f"sem{i}": sk for i, sk in enumerate(skeys)}

            def run(ename, eng):
                waited = {}
                for i in order[ename]:
                    o = real[i]
                    for d in o["deps"]:
                        od = real[d]
                        sk, val = od["skey"], od["sigval"]
                        if waited.get(sk, 0) < val:
                            eng.wait_ge(sems[sk], val)
                            waited[sk] = val
                    ins = o["fn"](eng)
                    if o["dma"] is not None:
                        ins.then_inc(sems[o["skey"]], 16)
                    elif o["signal"]:
                        ins.then_inc(sems[o["skey"]], 1)
                if ename == "sp":
                    for sk, val in dma_final.items():
                        if waited.get(sk, 0) < val:
                            eng.wait_ge(sems[sk], val)

            with nc.Block() as block:
                @block.tensor
                def _(e):
                    run("pe", e)

                @block.scalar
                def _(e):
                    run("act", e)

                @block.vector
                def _(e):
                    run("dve", e)

                @block.gpsimd
                def _(e):
                    run("pool", e)

                @block.sync
                def _(e):
                    run("sp", e)
        return len(real)


def host_consts(S):
    c = {}
    c["ones"] = np.ones((128, 128), np.float32)
    c["ident"] = np.eye(128, dtype=np.float32)
    s = np.arange(128)[:, None]
    t = np.arange(128)[None, :]
    c["bmask"] = ((s // 64 == t // 64) & (s <= t)).astype(np.float32)
    tri = (s <= t).astype(np.float32)
    c["tri2"] = np.concatenate([tri, tri], axis=1)
    cm = np.ones((128, TC), np.float32)
    cm[:, ::64] = 0.0
    c["cmask"] = cm
    half = 16
    inv_freq = (500000.0 ** (-np.arange(0, 32, 2, dtype=np.float32) / 32)).astype(np.float32)
    pos = np.arange(S, dtype=np.float32)
    ang = pos[None, :] * inv_freq[:, None]
    cosT = np.ones((128, S), np.float32)
    sinT = np.zeros((128, S), np.float32)
    cosT[0:16] = np.cos(ang)
    cosT[16:32] = np.cos(ang)
    sinT[0:16] = np.sin(ang)
    sinT[16:32] = np.sin(ang)
    c["cosT"] = cosT
    c["sinT"] = sinT
    pm = np.zeros((128, 128), np.float32)
    for m in range(16):
        pm[m + 16, m] = -1.0
        pm[m, m + 16] = 1.0
    c["pmat"] = pm
    return c


class Ctx:
    pass


def alloc_common(nc, st, S, kind):
    C = Ctx()
    C.S = S
    C.NT = S // TC
    sb = lambda name, shape, dt: st.enter_context(nc.sbuf_tensor("s_" + name, shape, dt))
    C.hT = sb("hT", [128, KC, S], BF16)
    C.wf = [sb(f"wf{i}", [128, KC, 128], F32) for i in range(2)]
    C.wb = [sb(f"wb{i}", [128, KC, 512], BF16) for i in range(2)]
    C.T = [sb(f"T{i}", [128, TC], F32) for i in range(14)]
    C.B = [sb(f"B{i}", [128, TC], BF16) for i in range(12)]
    C.ones_b = sb("ones_b", [128, 128], BF16)
    C.ident_b = sb("ident_b", [128, 128], BF16)
    C.cst_f = sb("cst_f", [128, 128], F32)
    C.gnorm = sb("gnorm", [128, KC], F32)
    C.ps = [st.enter_context(nc.psum_tensor(f"ps{i}", [128, 512], F32)) for i in range(8)]
    return C


def load_const_bf16(P, C, dst, src_dram, name):
    P.dma(C.cst_f[:], src_dram, slot="cst", reads=[("dram", name)], writes=["cst_f"])
    P.cp(dst[:], C.cst_f[:], reads=["cst_f"], writes=[name], eng="dve")


def emit_prologue(P, C, dr, n_part, have_xout, final_only=False):
    S, NT = C.S, C.NT
    P.dma(C.gnorm[:], dr["gnorm"], slot="gn", reads=[], writes=["gnorm"])
    stat_banks = [C.ps[4 + i] for i in range(NT)]
    for kc in range(KC):
        for tc in range(NT):
            sl = (kc * NT + tc) % 4
            xa = C.T[sl]
            xk = ("T", sl)
            cols = slice(tc * TC, (tc + 1) * TC)
            rows = slice(kc * 128, (kc + 1) * 128)
            P.dma(xa[:], dr["xT"][rows, cols], slot=("xa", sl), reads=([("xout", kc, tc)] if dr.get("x_dep") else []), writes=[xk])
            for pi in range(n_part):
                pa = C.T[8 + sl]
                pk = ("T", 8 + sl)
                P.dma(pa[:], dr["parts"][pi][rows, cols], slot=("pa", sl), reads=[("yp", pi, kc, tc)], writes=[pk])
                P.tt(xa[:], xa[:], pa[:], ALU.add, reads=[xk, pk], writes=[xk])
            if have_xout:
                P.dma(dr["xout"][rows, cols], xa[:], slot=("xo", sl), reads=[xk], writes=[("xout", kc, tc), "xout_all"])
            if final_only:
                continue
            xs = C.B[sl]
            P.act(xs[:], xa[:], AF.Square, reads=[xk], writes=[("B", sl)])
            P.mm(stat_banks[tc][:], C.ones_b[:], xs[:], start=(kc == 0), stop=(kc == KC - 1),
                 reads=[("B", sl), "ones_b"], writes=[("ps", 4 + tc)])
    if final_only:
        return
    for tc in range(NT):
        r = C.T[4 + tc]
        rstd_act(P, r[:], ("T", 4 + tc), stat_banks[tc][:], [("ps", 4 + tc)], C.eps_norm[:], "eps_norm", 1.0 / D)
    src = dr["xout"] if have_xout else dr["xT"]
    for kc in range(KC):
        for tc in range(NT):
            sl = (kc * NT + tc) % 4
            xa = C.T[sl]
            xk = ("T", sl)
            cols = slice(tc * TC, (tc + 1) * TC)
            rows = slice(kc * 128, (kc + 1) * 128)
            rd = [("xout", kc, tc)] if have_xout else []
            P.dma(xa[:], src[rows, cols], slot=("xa", sl), reads=rd, writes=[xk])
            P.stt(C.hT[:, kc, cols], xa[:], C.gnorm[:, kc:kc + 1], C.T[4 + tc][:], ALU.mult, ALU.mult,
                  reads=[xk, "gnorm", ("T", 4 + tc)], writes=[("hT", kc, tc)])


class WStream:
    def __init__(self, P, C):
        self.P, self.C = P, C
        self.n = 0

    def load_group(self, w_dram, col0, nslab, gslot, kcn=KC, name="w"):
        P, C = self.P, self.C
        ncol = nslab * 128
        for q in range(kcn // 4):
            sl = self.n % 2
            self.n += 1
            src = w_dram[q * 512:(q + 1) * 512, col0:col0 + ncol].rearrange("(k p) c -> p k c", p=128)
            stage = C.wf[sl][:].rearrange("p k c -> p (k c)")[:, 0:4 * ncol].rearrange("p (k c) -> p k c", k=4)
            P.dma(stage, src, slot=("wf", sl), reads=[], writes=[("wf", sl)])
            P.act(C.wb[gslot][:, 4 * q:4 * q + 4, 0:ncol], stage, AF.Copy,
                  reads=[("wf", sl)], writes=[("wb", gslot, q)])


def inproj_fm(P, C, gslot, s, tc, bank):
    cols = slice(tc * TC, (tc + 1) * TC)
    for kc in range(KC):
        P.mm(C.ps[bank][:], C.wb[gslot][:, kc, s * 128:(s + 1) * 128], C.hT[:, kc, cols],
             start=(kc == 0), stop=(kc == KC - 1),
             reads=[("wb", gslot, kc // 4), ("hT", kc, tc)], writes=[("ps", bank)])


def inproj_tm(P, C, gslot, s0, ncol, tc, tt, out_ap, bank):
    t0 = tc * TC + tt * 128
    rd = [("wb", gslot, q) for q in range(4)]
    for kc in range(KC):
        P.mm(out_ap, C.hT[:, kc, t0:t0 + 128], C.wb[gslot][:, kc, s0 * 128:s0 * 128 + ncol],
             start=(kc == 0), stop=(kc == KC - 1),
             reads=rd + [("hT", kc, tc)], writes=[("ps", bank)])


def emit_outproj(P, C, ws, dr, nfc):
    S, NT = C.S, C.NT
    fo = dr.get("fc_off", 0)
    ws.load_group(dr["w_out"], 0, 4, 0, kcn=nfc)
    for tc in range(NT):
        cols = slice(tc * TC, (tc + 1) * TC)
        for fc in range(nfc):
            P.dma(C.hT[:, fo + fc, cols], dr["mixT"][fc * 128:(fc + 1) * 128, cols], slot=("mixld", fo, tc),
                  reads=[("mixT", dr.get("mix_tag", 0), fc, tc)], writes=[("hT", fo + fc, tc), ("mixall", fo, tc)])
    for g4 in range(4):
        g = g4 % 2
        if g4 + 1 < 4:
            ws.load_group(dr["w_out"], (g4 + 1) * 512, 4, (g4 + 1) % 2, kcn=nfc)
        for s_ in range(4):
            dc = g4 * 4 + s_
            for tc in range(NT):
                cols = slice(tc * TC, (tc + 1) * TC)
                bank = (dc * NT + tc) % 2
                for fc in range(nfc):
                    P.mm(C.ps[bank][:], C.wb[g][:, fc, s_ * 128:(s_ + 1) * 128], C.hT[:, fo + fc, cols],
                         start=(fc == 0), stop=(fc == nfc - 1),
                         reads=[("wb", g, fc // 4), ("mixall", fo, tc)], writes=[("ps", bank)])
                ysl = (dc * NT + tc) % 2
                yt = C.T[12 + ysl]
                P.act(yt[:], C.ps[bank][:], AF.Copy, reads=[("ps", bank)], writes=[("T", 12 + ysl)])
                P.dma(dr["yp"][dc * 128:(dc + 1) * 128, cols], yt[:], slot=("yst", ysl),
                      reads=[("T", 12 + ysl)], writes=[("yp", dr.get("yp_tag", 0), dc, tc)])


REC_GROUPS = ["rg", "hg", "hg", "rg", "hg", "hg", "rg", "hg", "hg", "rg", "hg", "hg"]


def emit_rec_layer(P, C, dr, do_outproj=True):
    S, NT = C.S, C.NT
    nc = P.nc
    T, B, ps = C.T, C.B, C.ps
    R = C.R
    P.dma(R.convw[:], dr["convw"], slot="p0", reads=[], writes=["convw"])
    P.dma(R.vec[:], dr["recvec"], slot="p1", reads=[], writes=["recvec"])
    P.dma(R.lbraw[:], dr["hglb"], slot="p2", reads=[], writes=["lbraw"])
    P.dma(R.onorm[:], dr["onorm"], slot="p3", reads=[], writes=["onorm"])
    P.dma(R.cmask[:], dr["cmask"], slot="p4", reads=[], writes=["cmask"])
    P.dma(R.bmask[:], dr["bmask"], slot="p5", reads=[], writes=["bmask"])
    P.act(R.sp[:], R.vec[:, :, 3], AF.Exp, reads=["recvec"], writes=["sp"], scale=-1.0)
    P.act(R.sp[:], R.sp[:], AF.Ln, reads=["sp", "one_c"], writes=["sp"], bias=C.one_c[:], scale=1.0)
    P.add("dve", lambda e: e.tensor_scalar_mul(R.cneg[:], R.sp[:], -8.0), ["sp"], ["cneg"])
    P.add("dve", lambda e: e.tensor_scalar_mul(R.cneg2[:], R.sp[:], -16.0), ["sp"], ["cneg2"])
    P.tt(R.lb[:], R.lbraw[:, :, 1], R.lbraw[:, :, 0], ALU.subtract, reads=["lbraw"], writes=["lb"])
    P.act(R.lb[:], R.lb[:], AF.Sigmoid, reads=["lb"], writes=["lb"])
    P.add("dve", lambda e: e.tensor_scalar_mul(R.lb[:], R.lb[:], float(dr["lbflag"])), ["lb"], ["lb"])
    P.ts(R.oml[:], R.lb[:], -1.0, 1.0, ALU.mult, ALU.add, reads=["lb"], writes=["oml"])
    P.memset(R.zero_f[:], 0.0, writes=["zero_f"], eng="dve")
    P.memset(R.zero_b[:], 0.0, writes=["zero_b"], eng="dve")
    P.add("dve", lambda e: e.tensor_scalar_mul(R.vech[:], R.vec[:], 0.5), ["recvec"], ["vech"])
    P.add("dve", lambda e: e.tensor_scalar_mul(R.chalf[:], R.sp[:], -4.0), ["sp"], ["chalf"])
    P.add("dve", lambda e: e.tensor_scalar_mul(R.omh[:], R.oml[:], 0.5), ["oml"], ["omh"])
    P.tt(R.lbp[:], R.omh[:], R.lb[:], ALU.add, reads=["omh", "lb"], writes=["lbp"])

    ws = WStream(P, C)
    groups = REC_GROUPS
    objs = []
    ia = ie = 0
    for gi, gk in enumerate(groups):
        if gk == "rg":
            objs.append(RGGroup(P, C, dr, gi % 2, ia))
            ia += 1
        else:
            objs.append(HGGroup(P, C, dr, gi % 2, ie))
            ie += 1
    seq = [(gi, tc) for gi in range(len(groups)) for tc in range(NT)]
    ws.load_group(dr["w_in"], 0, 4, 0)
    objs[0].inproj(0, 0)
    for k, (gi, tc) in enumerate(seq):
        if tc == 0 and gi + 1 < len(groups):
            ws.load_group(dr["w_in"], (gi + 1) * 512, 4, (gi + 1) % 2)
        if k + 1 < len(seq):
            objs[seq[k + 1][0]].inproj(seq[k + 1][1], (k + 1) % 2)
        objs[gi].mixer(tc, k % 2)
    if do_outproj:
        emit_outproj(P, C, ws, dr, 16)


_BK = [0]
_SL = [0]


def rstd_act(P, out, out_key, in_, in_keys, eps_ap, eps_key, inv_n):
    P.act(out, in_, AF.Ln, reads=list(in_keys) + [eps_key], writes=[out_key], bias=eps_ap, scale=inv_n)
    P.act(out, out, AF.Exp, reads=[out_key], writes=[out_key], scale=-0.5)


def v_tokmajor(P, C, gslot, s, tc, bank, tmp_idx, out_ap, out_keys):
    inproj_fm(P, C, gslot, s, tc, bank)
    vT = C.B[tmp_idx]
    P.act(vT[:], C.ps[bank][:], AF.Copy, reads=[("ps", bank)], writes=[("B", tmp_idx)])
    pv = C.ps[bank][:].bitcast(BF16)
    for tt in range(4):
        P.tr(pv[:, tt * 128:(tt + 1) * 128], vT[:, tt * 128:(tt + 1) * 128], C.ident_b[:],
             reads=[("B", tmp_idx), "ident_b"], writes=[("ps", bank)])
    P.cp(out_ap, pv[:, 0:512].rearrange("p (a b) -> p a b", a=4), reads=[("ps", bank)], writes=out_keys, eng="dve")


def silu2(P, C, out_ap, out_key, bank):
    k = 4 + (_SL[0] % 2)
    _SL[0] += 1
    th = C.T[k]
    P.act(th[:], C.ps[bank][:], AF.Tanh, reads=[("ps", bank)], writes=[("T", k)], scale=0.5)
    P.stt(out_ap, th[:], 1.0, C.ps[bank][:], ALU.add, ALU.mult, reads=[("T", k), ("ps", bank)], writes=[out_key])


def next_bank():
    b = _BK[0] % 2
    _BK[0] += 1
    return b


class RGGroup:
    def __init__(self, P, C, dr, gslot, a):
        self.P, self.C, self.dr, self.gslot, self.a = P, C, dr, gslot, a

    def inproj(self, tc, par):
        P, C, dr, gslot, a = self.P, self.C, self.dr, self.gslot, self.a
        T, B, ps, R = C.T, C.B, C.ps, C.R
        if tc == 0:
            P.dma(R.wgf[:], dr["wgate"][a], slot="wg", reads=[], writes=["wgf"])
            P.act(R.wgb[:], R.wgf[:], AF.Copy, reads=["wgf"], writes=["wgb"])
        xr = R.xraw[par]
        for i in range(2):
            bank = next_bank()
            inproj_fm(P, C, gslot, i, tc, bank)
            if tc == 0:
                P.memset(xr[:, i, 0:3], 0.0, writes=[("xraw", par, i)], eng="dve")
            else:
                P.act(xr[:, i, 0:3], R.xraw[1 - par][:, i, TC:TC + 3], AF.Copy, reads=[("xraw", 1 - par, i)],
                      writes=[("xraw", par, i)])
            P.act(xr[:, i, 3:3 + TC], ps[bank][:], AF.Copy, reads=[("ps", bank)], writes=[("xraw", par, i)])
        for j in range(2):
            bank = next_bank()
            inproj_fm(P, C, gslot, 2 + j, tc, bank)
            silu2(P, C, B[par * 2 + j][:], ("B", par * 2 + j), bank)

    def mixer(self, tc, par):
        P, C, dr, gslot, a = self.P, self.C, self.dr, self.gslot, self.a
        T, B, ps, R = C.T, C.B, C.ps, C.R
        xr = R.xraw[par]
        for i in range(2):
            ch = 2 * a + i
            xc = T[6 + i]
            k = ("T", 6 + i)
            P.act(xc[:], xr[:, i, 3:3 + TC], AF.Identity, reads=[("xraw", par, i), "convw", "recvec"], writes=[k],
                  bias=R.vec[:, ch, 0:1], scale=R.convw[:, ch, 3:4])
            for tap in range(3):
                P.stt(xc[:], xr[:, i, tap:tap + TC], R.convw[:, ch, tap:tap + 1], xc[:], ALU.mult, ALU.add,
                      reads=[("xraw", par, i), "convw", k], writes=[k])
            P.act(B[6 + i][:], xc[:], AF.Copy, reads=[k], writes=[("B", 6 + i)])
        for j in range(2):
            ch = 2 * a + j
            for g in range(2):
                for i in range(2):
                    P.mm(ps[3 + g][:], R.wgb[:, g, i, j * 128:(j + 1) * 128], B[6 + i][:], start=(i == 0), stop=(i == 1),
                         reads=["wgb", ("B", 6 + i)], writes=[("ps", 3 + g)])
            r_, ig, aa, mm_, uu = T[8], T[9], T[10], T[11], T[12]
            P.act(r_[:], ps[3][:], AF.Tanh, reads=[("ps", 3), "vech"], writes=[("T", 8)], bias=R.vech[:, ch, 1:2], scale=0.5)
            P.act(ig[:], ps[4][:], AF.Tanh, reads=[("ps", 4), "vech"], writes=[("T", 9)], bias=R.vech[:, ch, 2:3], scale=0.5)
            P.act(aa[:], r_[:], AF.Exp, reads=[("T", 8), "chalf"], writes=[("T", 10)], scale=R.chalf[:, ch:ch + 1],
                  bias=R.chalf[:, ch:ch + 1])
            P.act(mm_[:], r_[:], AF.Exp, reads=[("T", 8), "cneg"], writes=[("T", 11)], scale=R.cneg[:, ch:ch + 1],
                  bias=R.cneg[:, ch:ch + 1])
            P.act(mm_[:], mm_[:], AF.Ln, reads=[("T", 11), "one_c"], writes=[("T", 11)], bias=C.one_c[:], scale=-1.0)
            P.act(mm_[:], mm_[:], AF.Exp, reads=[("T", 11)], writes=[("T", 11)], scale=0.5)
            P.stt(uu[:], ig[:], 1.0, T[6 + j][:], ALU.add, ALU.mult, reads=[("T", 9), ("T", 6 + j)], writes=[("T", 12)])
            P.stt(uu[:], uu[:], 0.5, mm_[:], ALU.mult, ALU.mult, reads=[("T", 12), ("T", 11)], writes=[("T", 12)])
            hcur = R.h[par][j]
            hk = ("h", par, j)
            if tc == 0:
                init = 0.0
                rd = []
            else:
                init = R.h[1 - par][j][:, TC - 1:TC]
                rd = [("h", 1 - par, j)]
            P.scan(hcur[:], aa[:], uu[:], init, reads=[("T", 10), ("T", 12)] + rd, writes=[hk])
            mo = B[8 + j]
            P.stt(mo[:], hcur[:], 0.5, B[par * 2 + j][:], ALU.mult, ALU.mult, reads=[hk, ("B", par * 2 + j)], writes=[("B", 8 + j)])
            fc = 2 * a + j
            P.dma(dr["mixT"][fc * 128:(fc + 1) * 128, tc * TC:(tc + 1) * TC], mo[:], slot=("mo", j),
                  reads=[("B", 8 + j)], writes=[("mixT", dr.get("mix_tag", 0), fc, tc)])


class HGGroup:
    def __init__(self, P, C, dr, gslot, e):
        self.P, self.C, self.dr, self.gslot, self.e = P, C, dr, gslot, e
        self.nS = 0

    def tiles(self, par):
        T, B = self.C.T, self.C.B
        qi, fi, si, vi = par, 2 + par, 2 * par, 2 * par + 1
        return (T[qi], ("T", qi)), (T[fi], ("T", fi)), (B[si], ("B", si)), (B[vi], ("B", vi))

    def inproj(self, tc, par):
        P, C, gslot = self.P, self.C, self.gslot
        ps = C.ps
        (qs, qk), (ff, fk), (sg, sk), (V, vk) = self.tiles(par)
        bank = next_bank()
        inproj_fm(P, C, gslot, 0, tc, bank)
        silu2(P, C, qs[:], qk, bank)
        bank = next_bank()
        inproj_fm(P, C, gslot, 1, tc, bank)
        silu2(P, C, sg[:], sk, bank)
        bank = next_bank()
        inproj_fm(P, C, gslot, 2, tc, bank)
        P.act(ff[:], ps[bank][:], AF.Tanh, reads=[("ps", bank)], writes=[fk], scale=0.5)
        bank = next_bank()
        v_tokmajor(P, C, gslot, 3, tc, bank, 10 + par, V[:].rearrange("p (a b) -> p a b", a=4), [vk])

    def mixer(self, tc, par):
        P, C, dr, e = self.P, self.C, self.dr, self.e
        T, B, ps, R = C.T, C.B, C.ps, C.R
        HG_SCALE = 128 ** -0.5
        (qs, qk), (ff, fk), (sg, sk), (V, vk) = self.tiles(par)
        lf, bb, eb, enb = T[8], T[9], T[10], T[11]
        P.ts(ff[:], ff[:], R.omh[:, e:e + 1], R.lbp[:, e:e + 1], ALU.mult, ALU.add, reads=[fk, "omh", "lbp"], writes=[fk])
        P.act(lf[:], ff[:], AF.Ln, reads=[fk], writes=[("T", 8)])
        P.scan(bb[:], R.cmask[:], lf[:], 0.0, reads=["cmask", ("T", 8)], writes=[("T", 9)])
        P.act(eb[:], bb[:], AF.Exp, reads=[("T", 9)], writes=[("T", 10)])
        P.act(enb[:], bb[:], AF.Exp, reads=[("T", 9)], writes=[("T", 11)], scale=-1.0)
        P.ts(ff[:], ff[:], -1.0, 1.0, ALU.mult, ALU.add, reads=[fk], writes=[fk])
        Qd, Kd, K2 = B[4], B[5], B[6]
        P.stt(Qd[:], qs[:], HG_SCALE * 0.5, eb[:], ALU.mult, ALU.mult, reads=[qk, ("T", 10)], writes=[("B", 4)])
        P.tt(Kd[:], ff[:], enb[:], ALU.mult, reads=[fk, ("T", 11)], writes=[("B", 5)])
        for c in range(8):
            cs = slice(c * 64, (c + 1) * 64)
            P.stt(K2[:, cs], ff[:, cs], eb[:, c * 64 + 63:c * 64 + 64], enb[:, cs], ALU.mult, ALU.mult,
                  reads=[fk, ("T", 10), ("T", 11)], writes=[("B", 6)])
        for tt in range(4):
            tsl = slice(tt * 128, (tt + 1) * 128)
            ap = tt % 2
            atb = 5 if ap == 0 else 3
            at_ps = ps[atb][:, 0:128]
            P.mm(at_ps, Kd[:, tsl], Qd[:, tsl], start=True, stop=True, reads=[("B", 5), ("B", 4)], writes=[("ps", atb)])
            atm = R.atm[ap]
            P.tt(atm[:], at_ps, R.bmask[:], ALU.mult, reads=[("ps", atb), "bmask"], writes=[("atm", ap)])
            tr_ps = ps[4][:].bitcast(BF16)[:, 0:128]
            P.tr(tr_ps, K2[:, tsl], C.ident_b[:], reads=[("B", 6), "ident_b"], writes=[("ps", 4)])
            k2t = R.k2t[ap]
            P.cp(k2t[:], tr_ps, reads=[("ps", 4)], writes=[("k2t", ap)], eng="dve")
            o_ps = ps[7][:, tsl]
            P.mm(o_ps, V[:, tsl], atm[:], start=True, stop=False, reads=[vk, ("atm", ap)], writes=[("ps", 7)])
            for h2 in range(2):
                nS = self.nS
                c = tt * 2 + h2
                if tc == 0 and c == 0:
                    sb_prev, sbk = R.zero_b, "zero_b"
                    s_prev, spk = R.zero_f, "zero_f"
                else:
                    sb_prev, sbk = R.Sb[(nS - 1) % 4], ("Sb", (nS - 1) % 4)
                    s_prev, spk = R.Sf[(nS - 1) % 4], ("Sf", (nS - 1) % 4)
                cg = slice(tt * 128 + h2 * 64, tt * 128 + (h2 + 1) * 64)
                P.mm(ps[7][:, cg], sb_prev[:], Qd[:, cg], start=False, stop=(h2 == 1),
                     reads=[sbk, ("B", 4)], writes=[("ps", 7)])
                ub = 6 if nS % 2 == 0 else 2
                u_ps = ps[ub][:, 0:128]
                P.mm(u_ps, k2t[h2 * 64:(h2 + 1) * 64, :], V[h2 * 64:(h2 + 1) * 64, tsl], start=True, stop=True,
                     reads=[("k2t", ap), vk], writes=[("ps", ub)])
                s_new = R.Sf[nS % 4]
                P.stt(s_new[:], s_prev[:], eb[:, c * 64 + 63:c * 64 + 64], u_ps, ALU.mult, ALU.add,
                      reads=[spk, ("T", 10), ("ps", ub)], writes=[("Sf", nS % 4)])
                P.cp(R.Sb[nS % 4][:], s_new[:], reads=[("Sf", nS % 4)], writes=[("Sb", nS % 4)], eng="pool")
                self.nS += 1
        osq = B[7]
        P.act(osq[:], ps[7][:], AF.Square, reads=[("ps", 7)], writes=[("B", 7)])
        P.mm(ps[4][:], C.ones_b[:], osq[:], start=True, stop=True, reads=["ones_b", ("B", 7)], writes=[("ps", 4)])
        rs = T[12]
        rstd_act(P, rs[:], ("T", 12), ps[4][:], [("ps", 4)], C.eps_norm[:], "eps_norm", 1.0 / 128)
        ot = T[13]
        P.stt(ot[:], ps[7][:], R.onorm[:, 0:1], rs[:], ALU.mult, ALU.mult, reads=[("ps", 7), "onorm", ("T", 12)], writes=[("T", 13)])
        mo = B[8 + (tc % 2)]
        P.stt(mo[:], ot[:], 0.5, sg[:], ALU.mult, ALU.mult, reads=[("T", 13), sk], writes=[("B", 8 + (tc % 2))])
        fc = 8 + e
        P.dma(dr["mixT"][fc * 128:(fc + 1) * 128, tc * TC:(tc + 1) * TC], mo[:], slot=("mo", tc % 2),
              reads=[("B", 8 + (tc % 2))], writes=[("mixT", dr.get("mix_tag", 0), fc, tc)])


def alloc_rec(nc, st, C, mx=None):
    R = Ctx()
    sb = lambda name, shape, dt: st.enter_context(nc.sbuf_tensor("s_" + name, shape, dt))
    R.convw = sb("convw", [128, 8, 4], F32)
    R.vec = sb("recvec", [128, 8, 4], F32)
    R.lbraw = sb("lbraw", [128, 8, 2], F32)
    R.onorm = sb("onorm", [128, 1], F32)
    if mx is None:
        R.cmask = sb("cmask", [128, TC], F32)
    R.bmask = sb("bmask", [128, 128], F32)
    R.sp = sb("sp", [128, 8], F32)
    R.cneg = sb("cneg", [128, 8], F32)
    R.cneg2 = sb("cneg2", [128, 8], F32)
    R.lb = sb("lb", [128, 8], F32)
    R.oml = sb("oml", [128, 8], F32)
    R.omh = sb("omh", [128, 8], F32)
    R.lbp = sb("lbp", [128, 8], F32)
    R.chalf = sb("chalf", [128, 8], F32)
    R.vech = sb("vech", [128, 8, 4], F32)
    R.zero_f = sb("zero_f", [128, 128], F32)
    R.zero_b = sb("zero_b", [128, 128], BF16)
    R.wgb = sb("wgb", [128, 2, 2, 256], BF16)
    if mx is None:
        R.wgf = sb("wgf", [128, 2, 2, 256], F32)
        R.xraw = [sb(f"xraw{i}", [128, 2, TC + 3], F32) for i in range(2)]
        R.h = [[sb(f"h{p}{j}", [128, TC], F32) for j in range(2)] for p in range(2)]
    else:
        XW = 2 * (TC + 3)
        R.xraw = [mx[:, i * XW:(i + 1) * XW].rearrange("p (a b) -> p a b", a=2) for i in range(2)]
        o = 2 * XW
        R.h = [[mx[:, o + (p * 2 + j) * TC: o + (p * 2 + j + 1) * TC] for j in range(2)] for p in range(2)]
        o += 4 * TC
        R.cmask = mx[:, o:o + TC]
        o += TC
        R.wgf = mx[:, o:o + 1024].rearrange("p (g i c) -> p g i c", g=2, i=2)
    R.atm = [sb(f"atm{i}", [128, 128], BF16) for i in range(2)]
    R.k2t = [sb(f"k2t{i}", [128, 128], BF16) for i in range(2)]
    R.Sf = [sb(f"Sf{i}", [128, 128], F32) for i in range(4)]
    R.Sb = [sb(f"Sb{i}", [128, 128], BF16) for i in range(4)]
    C.R = R


def emit_consts(P, C, dr):
    load_const_bf16(P, C, C.ones_b, dr["ones"], "ones_b")
    load_const_bf16(P, C, C.ident_b, dr["ident"], "ident_b")
    P.memset(C.eps_norm[:], NORM_EPS, writes=["eps_norm"])
    P.memset(C.one_c[:], 1.0, writes=["one_c"])


def build_rec_program(S, n_part, lbflag):
    nc = bass.Bass("TRN2", target_bir_lowering=False)
    di = lambda name, shape, dt=F32: nc.dram_tensor(name, shape, dt, kind="ExternalInput").ap()
    do = lambda name, shape, dt=F32: nc.dram_tensor(name, shape, dt, kind="ExternalOutput").ap()
    dr = {}
    dr["xT"] = di("xT", [D, S])
    dr["parts"] = [di(f"part{i}", [D, S]) for i in range(n_part)]
    dr["gnorm"] = di("gnorm", [128, KC])
    dr["w_in"] = di("w_in", [D, 6144])
    dr["w_out"] = di("w_out", [2048, D])
    dr["convw"] = di("convw", [128, 8, 4])
    dr["recvec"] = di("recvec", [128, 8, 4])
    dr["hglb"] = di("hglb", [128, 8, 2])
    dr["onorm"] = di("onorm", [128, 1])
    dr["wgate"] = di("wgate", [4, 128, 2, 2, 256])
    dr["cmask"] = di("cmask", [128, TC])
    dr["bmask"] = di("bmask", [128, 128])
    dr["ones"] = di("ones", [128, 128])
    dr["ident"] = di("ident", [128, 128])
    dr["lbflag"] = lbflag
    if n_part:
        dr["xout"] = do("xout", [D, S])
    dr["yp"] = do("yp", [D, S])
    dr["mixT"] = nc.dram_tensor("mixT", [2048, S], BF16, kind="Internal").ap()
    with ExitStack() as st:
        C = alloc_common(nc, st, S, "rec")
        C.eps_norm = st.enter_context(nc.sbuf_tensor("s_eps_norm", [128, 1], F32))
        C.one_c = st.enter_context(nc.sbuf_tensor("s_one_c", [128, 1], F32))
        alloc_rec(nc, st, C)
        P = Prog(nc)
        emit_consts(P, C, dr)
        emit_prologue(P, C, dr, n_part, bool(n_part))
        emit_rec_layer(P, C, dr)
        n = P.emit()
    return nc, n


def pvec(v):
    v = np.asarray(v, np.float32)
    return np.ascontiguousarray(v.reshape(-1, 128).T)


def rec_layout(inp, j, r):
    w_in = inp["rec_w_in"][j]
    w_out = inp["rec_w_out"][j]
    rg_heads = [4 * r + a for a in range(4)]
    hg_heads = [8 * r + e for e in range(8)]
    cols = []
    ia = ie = 0
    for gk in REC_GROUPS:
        if gk == "rg":
            hh = rg_heads[ia]; ia += 1
            cols.append(np.arange(256 * hh, 256 * hh + 256))
            cols.append(np.arange(2048 + 256 * hh, 2048 + 256 * hh + 256))
        else:
            e = hg_heads[ie]; ie += 1
            cols.append(np.arange(4096 + 128 * e, 4096 + 128 * e + 128))
            cols.append(np.arange(10240 + 128 * e, 10240 + 128 * e + 128))
            cols.append(np.arange(6144 + 128 * e, 6144 + 128 * e + 128))
            cols.append(np.arange(8192 + 128 * e, 8192 + 128 * e + 128))
    cols = np.concatenate(cols)
    rows = []
    for hh in rg_heads:
        rows.append(np.arange(256 * hh, 256 * hh + 256))
    for e in hg_heads:
        rows.append(np.arange(2048 + 128 * e, 2048 + 128 * e + 128))
    rows = np.concatenate(rows)
    ch = np.concatenate([np.arange(256 * hh, 256 * hh + 256) for hh in rg_heads])
    convw = np.ascontiguousarray(inp["rg_conv_w"][j][:, ch].T.reshape(8, 128, 4).transpose(1, 0, 2))
    vec = np.stack([inp["rg_conv_b"][j][ch], inp["rg_b_gate_a"][j][ch], inp["rg_b_gate_x"][j][ch],
                    inp["rg_lambda"][j][ch]], axis=-1)
    vec = np.ascontiguousarray(vec.reshape(8, 128, 4).transpose(1, 0, 2))
    hch = np.concatenate([np.arange(128 * e, 128 * e + 128) for e in hg_heads])
    lbr = np.stack([inp["hg_lb"][0][hch], inp["hg_lb"][j][hch]], axis=-1)
    lbr = np.ascontiguousarray(lbr.reshape(8, 128, 2).transpose(1, 0, 2))
    wg = np.stack([inp["rg_w_gate_a"][j][rg_heads], inp["rg_w_gate_x"][j][rg_heads]], axis=1)
    wg = wg.reshape(4, 2, 2, 128, 256).transpose(0, 3, 1, 2, 4)
    return dict(
        w_in=np.ascontiguousarray(w_in[:, cols]),
        w_out=np.ascontiguousarray(w_out[rows, :]),
        gnorm=pvec(inp["rec_norm"][j]),
        convw=convw.astype(np.float32), recvec=vec.astype(np.float32), hglb=lbr.astype(np.float32),
        onorm=np.ascontiguousarray(inp["hg_out_norm"][j].reshape(128, 1).astype(np.float32)),
        wgate=np.ascontiguousarray(wg.astype(np.float32)),
    )


def alloc_att(nc, st, C, mx=None):
    A = Ctx()
    S = C.S
    sb = lambda name, shape, dt: st.enter_context(nc.sbuf_tensor("s_" + name, shape, dt))
    if mx is None:
        A.KT = sb("KT", [128, 2, S], BF16)
        A.Vt = sb("Vt", [128, S // 128, 264], BF16)
    else:
        A.KT = mx[:, 0:S].bitcast(BF16).rearrange("p (a s) -> p a s", a=2)
        nv = (S // 128) * 132
        A.Vt = mx[:, S:S + nv].bitcast(BF16).rearrange("p (t c) -> p t c", c=264)
    A.cosT = sb("cosT", [128, S], F32)
    A.sinT = sb("sinT", [128, S], F32)
    A.pmat = sb("pmat", [128, 128], F32)
    A.ones_f = sb("ones_f", [128, 128], F32)
    A.qkg = sb("qkg", [128, 2], F32)
    A.subg = sb("subg", [128, 256], F32)
    A.lp = sb("lp", [128, 4], F32)
    A.pr = sb("pr", [128, 2], F32)
    A.ex = sb("ex", [128, 2], F32)
    A.neglam = sb("neglam", [128, 1], F32)
    A.tri2f = sb("tri2f", [128, 256], F32)
    A.tri2 = sb("tri2", [128, 256], BF16)
    A.pt = [sb(f"pt{i}", [128, 256], BF16) for i in range(2)]
    A.o = sb("o", [128, 256], F32)
    A.junk = sb("junk", [128, 256], F32)
    A.on = sb("on", [128, 256], BF16)
    A.sm = sb("sm", [128, 8], F32)
    A.eps_sub = sb("eps_sub", [128, 1], F32)
    C.A = A


def emit_att_layer(P, C, dr, lam_init, do_outproj=True):
    S, NT = C.S, C.NT
    T, B, ps, A = C.T, C.B, C.ps, C.A
    SCALE = 128 ** -0.5
    P.dma(A.cosT[:], dr["cosT"], slot="a0", reads=[], writes=["cosT"])
    P.dma(A.sinT[:], dr["sinT"], slot="a1", reads=[], writes=["sinT"])
    P.dma(A.pmat[:], dr["pmat"], slot="a2", reads=[], writes=["pmat"])
    P.dma(A.ones_f[:], dr["ones"], slot="a3", reads=[], writes=["ones_f"])
    P.dma(A.qkg[:], dr["qkg"], slot="a4", reads=[], writes=["qkg"])
    P.dma(A.subg[:], dr["subg"], slot="a5", reads=[], writes=["subg"])
    P.dma(A.lp[:], dr["lp"], slot="a6", reads=[], writes=["lp"])
    P.dma(A.tri2f[:], dr["tri2"], slot="a7", reads=[], writes=["tri2f"])
    P.cp(A.tri2[:], A.tri2f[:], reads=["tri2f"], writes=["tri2"], eng="dve")
    P.memset(A.eps_sub[:], SUBLN_EPS, writes=["eps_sub"], eng="dve")
    P.add("dve", lambda e: e.tensor_scalar_mul(A.subg[:], A.subg[:], 1.0 - lam_init), ["subg"], ["subg"])
    P.tt(A.pr[:, 0:1], A.lp[:, 0:1], A.lp[:, 1:2], ALU.mult, reads=["lp"], writes=["pr"])
    P.tt(A.pr[:, 1:2], A.lp[:, 2:3], A.lp[:, 3:4], ALU.mult, reads=["lp", "pr"], writes=["pr"])
    P.mm(ps[2][:, 0:2], A.ones_f[:], A.pr[:], start=True, stop=True, reads=["ones_f", "pr"], writes=[("ps", 2)])
    P.act(A.ex[:], ps[2][:, 0:2], AF.Exp, reads=[("ps", 2)], writes=["ex"])
    P.tt(A.neglam[:], A.ex[:, 1:2], A.ex[:, 0:1], ALU.subtract, reads=["ex"], writes=["neglam"])
    P.add("dve", lambda e: e.tensor_scalar_add(A.neglam[:], A.neglam[:], -lam_init), ["neglam"], ["neglam"])
    P.memset(A.Vt[:, :, 256:257], 1.0, writes=["Vt_ones"], eng="dve")

    ws = WStream(P, C)
    ws.load_group(dr["w_in"], 0, 4, 0)
    ngroups = 8
    bkc = [0]

    def nb():
        b = bkc[0] % 2
        bkc[0] += 1
        return b

    def norm_rope(bank, gcol, out_ap, out_keys, tc):
        cols = slice(tc * TC, (tc + 1) * TC)
        raw, rstd, t1, t2, sq = T[6], T[7], T[8], T[9], B[2]
        P.act(raw[:], ps[bank][:], AF.Copy, reads=[("ps", bank)], writes=[("T", 6)])
        P.act(sq[:], ps[bank][:], AF.Square, reads=[("ps", bank)], writes=[("B", 2)])
        P.mm(ps[2][:], C.ones_b[:], sq[:], start=True, stop=True, reads=["ones_b", ("B", 2)], writes=[("ps", 2)])
        rstd_act(P, rstd[:], ("T", 7), ps[2][:], [("ps", 2)], C.eps_norm[:], "eps_norm", 1.0 / 128)
        P.stt(raw[:], raw[:], A.qkg[:, gcol:gcol + 1], rstd[:], ALU.mult, ALU.mult, reads=[("T", 6), "qkg", ("T", 7)], writes=[("T", 6)])
        P.mm(ps[3][:], A.pmat[:], raw[:], start=True, stop=True, reads=["pmat", ("T", 6)], writes=[("ps", 3)])
        P.tt(t1[:], raw[:], A.cosT[:, cols], ALU.mult, reads=[("T", 6), "cosT"], writes=[("T", 8)])
        P.tt(t2[:], ps[3][:], A.sinT[:, cols], ALU.mult, reads=[("ps", 3), "sinT"], writes=[("T", 9)])
        P.tt(out_ap, t1[:], t2[:], ALU.add, reads=[("T", 8), ("T", 9)], writes=out_keys)

    for gi in range(ngroups):
        gslot = gi % 2
        a = gi // 2
        if gi + 1 < ngroups:
            ws.load_group(dr["w_in"], (gi + 1) * 512, 4, (gi + 1) % 2)
        if gi % 2 == 0:
            for tc in range(NT):
                cols = slice(tc * TC, (tc + 1) * TC)
                for i in range(2):
                    bank = nb()
                    inproj_fm(P, C, gslot, i, tc, bank)
                    norm_rope(bank, 1, A.KT[:, i, cols], [("KT", i, tc)], tc)
                for half in range(2):
                    bank = nb()
                    tile0 = tc * 4
                    v_tokmajor(P, C, gslot, 2 + half, tc, bank, 10 + half, A.Vt[:, tile0:tile0 + 4, half * 128:(half + 1) * 128],
                               [("Vt", tile0 + q_, half) for q_ in range(4)])
        else:
            for tc in range(NT):
                cols = slice(tc * TC, (tc + 1) * TC)
                Qt = [B[4], B[5]]
                sgt = [B[6], B[7]]
                for i in range(2):
                    bank = nb()
                    inproj_fm(P, C, gslot, i, tc, bank)
                    norm_rope(bank, 0, Qt[i][:], [("B", 4 + i)], tc)
                for i in range(2):
                    bank = nb()
                    inproj_fm(P, C, gslot, 2 + i, tc, bank)
                    silu2(P, C, sgt[i][:], ("B", 6 + i), bank)
                for qt in range(4):
                    j = tc * 4 + qt
                    qs = slice(qt * 128, (qt + 1) * 128)

                    def emit_qk(i):
                        sb_ = 4 if i % 2 == 0 else 7
                        ks = slice(i * 128, (i + 1) * 128)
                        for m in range(2):
                            P.mm(ps[sb_][:, m * 128:(m + 1) * 128], A.KT[:, m, ks], Qt[m][:, qs], start=True, stop=True,
                                 reads=[("KT", m, i // 4), ("B", 4 + m)], writes=[("ps", sb_)])
                        pt = A.pt[i % 2]
                        P.act(pt[:], ps[sb_][:, 0:256], AF.Exp, reads=[("ps", sb_)], writes=[("pt", i % 2)], scale=SCALE)
                        if i == j:
                            P.tt(pt[:], pt[:], A.tri2[:], ALU.mult, reads=[("pt", i % 2), "tri2"], writes=[("pt", i % 2)])

                    def emit_pv(i):
                        pt = A.pt[i % 2]
                        for m in range(2):
                            P.mm(ps[5 + m][:, 0:257], pt[:, m * 128:(m + 1) * 128], A.Vt[:, i, 0:257],
                                 start=(i == 0), stop=(i == j),
                                 reads=[("pt", i % 2), ("Vt", i, 0), ("Vt", i, 1), "Vt_ones"], writes=[("ps", 5 + m)])

                    emit_qk(0)
                    for i in range(j + 1):
                        if i + 1 <= j:
                            emit_qk(i + 1)
                        emit_pv(i)
                    sm = A.sm
                    P.recip(sm[:, 0:1], ps[5][:, 256:257], reads=[("ps", 5)], writes=["sm"])
                    P.recip(sm[:, 1:2], ps[6][:, 256:257], reads=[("ps", 6), "sm"], writes=["sm"])
                    P.tt(sm[:, 2:3], sm[:, 1:2], A.neglam[:], ALU.mult, reads=["sm", "neglam"], writes=["sm"])
                    P.add("dve", lambda e, sm=sm: e.tensor_scalar_mul(A.o[:], ps[5][:, 0:256], sm[:, 0:1]), [("ps", 5), "sm"], ["o"])
                    P.stt(A.o[:], ps[6][:, 0:256], sm[:, 2:3], A.o[:], ALU.mult, ALU.add, reads=[("ps", 6), "sm", "o"], writes=["o"])
                    P.add("act", lambda e, sm=sm: e.activation(A.junk[:], A.o[:], AF.Square, accum_out=sm[:, 3:4]), ["o"], ["junk", "sm3"])
                    rstd_act(P, sm[:, 5:6], "sm5", sm[:, 3:4], ["sm3"], A.eps_sub[:], "eps_sub", 1.0 / 256)
                    P.stt(A.on[:], A.o[:], sm[:, 5:6], A.subg[:], ALU.mult, ALU.mult, reads=["o", "sm5", "subg"], writes=["on"])
                    trv = ps[3][:].bitcast(BF16)
                    for m in range(2):
                        P.tr(trv[:, m * 128:(m + 1) * 128], A.on[:, m * 128:(m + 1) * 128], C.ident_b[:],
                             reads=["on", "ident_b"], writes=[("ps", 3)])
                    for m in range(2):
                        P.stt(B[8 + m][:, qs], trv[:, m * 128:(m + 1) * 128], 0.5, sgt[m][:, qs], ALU.mult, ALU.mult,
                              reads=[("ps", 3), ("B", 6 + m)], writes=[("B", 8 + m)])
                for m in range(2):
                    fc = 2 * a + m
                    P.dma(dr["mixT"][fc * 128:(fc + 1) * 128, cols], B[8 + m][:], slot=("mo", m),
                          reads=[("B", 8 + m)], writes=[("mixT", dr.get("mix_tag", 0), fc, tc)])
    if do_outproj:
        emit_outproj(P, C, ws, dr, 8)


def build_att_program(S, n_part, layer):
    nc = bass.Bass("TRN2", target_bir_lowering=False)
    di = lambda name, shape, dt=F32: nc.dram_tensor(name, shape, dt, kind="ExternalInput").ap()
    do = lambda name, shape, dt=F32: nc.dram_tensor(name, shape, dt, kind="ExternalOutput").ap()
    dr = {}
    dr["xT"] = di("xT", [D, S])
    dr["parts"] = [di(f"part{i}", [D, S]) for i in range(n_part)]
    dr["gnorm"] = di("gnorm", [128, KC])
    dr["w_in"] = di("w_in", [D, 4096])
    dr["w_out"] = di("w_out", [1024, D])
    dr["cosT"] = di("cosT", [128, S])
    dr["sinT"] = di("sinT", [128, S])
    dr["pmat"] = di("pmat", [128, 128])
    dr["qkg"] = di("qkg", [128, 2])
    dr["subg"] = di("subg", [128, 256])
    dr["lp"] = di("lp", [128, 4])
    dr["tri2"] = di("tri2", [128, 256])
    dr["ones"] = di("ones", [128, 128])
    dr["ident"] = di("ident", [128, 128])
    if n_part:
        dr["xout"] = do("xout", [D, S])
    dr["yp"] = do("yp", [D, S])
    dr["mixT"] = nc.dram_tensor("mixT", [1024, S], BF16, kind="Internal").ap()
    lam_init = 0.8 - 0.6 * math.exp(-0.3 * layer)
    with ExitStack() as st:
        C = alloc_common(nc, st, S, "att")
        C.eps_norm = st.enter_context(nc.sbuf_tensor("s_eps_norm", [128, 1], F32))
        C.one_c = st.enter_context(nc.sbuf_tensor("s_one_c", [128, 1], F32))
        alloc_att(nc, st, C)
        P = Prog(nc)
        emit_consts(P, C, dr)
        emit_prologue(P, C, dr, n_part, bool(n_part))
        emit_att_layer(P, C, dr, lam_init)
        n = P.emit()
    return nc, n


def att_layout(inp, j, r):
    w_in = inp["att_w_in"][j]
    w_out = inp["att_w_out"][j]
    heads = [4 * r + a for a in range(4)]
    cols = []
    for h in heads:
        cols.append(np.arange(2048 + 256 * h, 2048 + 256 * h + 256))
        cols.append(np.arange(4096 + 256 * h, 4096 + 256 * h + 256))
        cols.append(np.arange(256 * h, 256 * h + 256))
        cols.append(np.arange(6144 + 256 * h, 6144 + 256 * h + 256))
    cols = np.concatenate(cols)
    rows = np.concatenate([np.arange(256 * h, 256 * h + 256) for h in heads])
    qkg = np.stack([inp["att_q_norm"][j], inp["att_k_norm"][j]], axis=-1).astype(np.float32)
    subg = np.ascontiguousarray(np.broadcast_to(inp["att_sub_norm"][j][None, :], (128, 256)).astype(np.float32))
    lp = np.ascontiguousarray(inp["att_lambda"][j].T.astype(np.float32))
    return dict(
        w_in=np.ascontiguousarray(w_in[:, cols]),
        w_out=np.ascontiguousarray(w_out[rows, :]),
        gnorm=pvec(inp["att_norm"][j]),
        qkg=np.ascontiguousarray(qkg), subg=subg, lp=lp,
    )


def build_final_program(S):
    nc = bass.Bass("TRN2", target_bir_lowering=False)
    di = lambda name, shape, dt=F32: nc.dram_tensor(name, shape, dt, kind="ExternalInput").ap()
    x = di("xT", [1024, S])
    p0 = di("part0", [1024, S])
    p1 = di("part1", [1024, S])
    out = nc.dram_tensor("xout", [1024, S], F32, kind="ExternalOutput").ap()
    with ExitStack() as st:
        tl = [[st.enter_context(nc.sbuf_tensor(f"s_f{k}{i}", [128, S], F32)) for i in range(2)] for k in range(3)]
        P = Prog(nc)
        for c in range(8):
            sl = c % 2
            rows = slice(c * 128, (c + 1) * 128)
            xa, pa, pb = tl[0][sl], tl[1][sl], tl[2][sl]
            P.dma(xa[:], x[rows, :], slot=("fx", sl), reads=[], writes=[("fx", sl)])
            P.dma(pa[:], p0[rows, :], slot=("fa", sl), reads=[], writes=[("fa", sl)])
            P.dma(pb[:], p1[rows, :], slot=("fb", sl), reads=[], writes=[("fb", sl)])
            P.tt(xa[:], xa[:], pa[:], ALU.add, reads=[("fx", sl), ("fa", sl)], writes=[("fx", sl)])
            P.tt(xa[:], xa[:], pb[:], ALU.add, reads=[("fx", sl), ("fb", sl)], writes=[("fx", sl)])
            P.dma(out[rows, :], xa[:], slot=("fo", sl), reads=[("fx", sl)], writes=[("out", c)])
        P.emit()
    return nc


REC_KEYS = ("w_in", "w_out", "gnorm", "convw", "recvec", "hglb", "onorm", "wgate")
ATT_KEYS = ("w_in", "w_out", "gnorm", "qkg", "subg", "lp")
CONST_KEYS = ("ones", "ident", "cmask", "bmask", "cosT", "sinT", "pmat", "tri2")
ACTIVE_CORES = (0, 1, 2, 3)
NCORES = 4


def build_fused_program(S, nlayers=4, nseq=1):
    nc = bass.Bass("TRN2", target_bir_lowering=False)
    di = lambda name, shape, dt=F32: nc.dram_tensor(name, shape, dt, kind="ExternalInput").ap()
    x_ins = [di("xT" if q == 0 else f"xT{q}", [D, S]) for q in range(nseq)]
    outs = [nc.dram_tensor("outT" if q == 0 else f"outT{q}", [D, S], F32, kind="ExternalOutput").ap() for q in range(nseq)]
    yp = [nc.dram_tensor(f"yp{i}", [D, S], F32, kind="Internal").ap() for i in range(2)]
    xb = [nc.dram_tensor(f"xbuf{i}", [D, S], F32, kind="Internal").ap() for i in range(2)]
    mixTs = [nc.dram_tensor(f"mixT{i}", [2048, S], BF16, kind="Internal").ap() for i in range(2)]
    cshape = dict(ones=[128, 128], ident=[128, 128], cmask=[128, TC], bmask=[128, 128], cosT=[128, S], sinT=[128, S],
                  pmat=[128, 128], tri2=[128, 256])
    cd = {k: di(k, cshape[k]) for k in CONST_KEYS}
    rshape = dict(w_in=[D, 6144], w_out=[2048, D], gnorm=[128, KC], convw=[128, 8, 4], recvec=[128, 8, 4], hglb=[128, 8, 2],
                  onorm=[128, 1], wgate=[4, 128, 2, 2, 256])
    ashape = dict(w_in=[D, 4096], w_out=[1024, D], gnorm=[128, KC], qkg=[128, 2], subg=[128, 256], lp=[128, 4])
    drs = []
    for L in range(nlayers):
        sh = rshape if L % 2 == 0 else ashape
        halves = []
        for hf in range(2):
            dr = {k: di(f"{k}{L}h{hf}", sh[k]) for k in sh if not (k == "gnorm" and hf == 1)}
            if hf == 1:
                dr["gnorm"] = halves[0]["gnorm"]
            dr.update(cd)
            dr["mixT"] = mixTs[hf] if L % 2 == 0 else mixTs[hf][0:1024, :]
            dr["mix_tag"] = hf
            dr["yp"] = yp[hf]
            dr["yp_tag"] = hf
            dr["fc_off"] = 8 * hf if L % 2 == 1 else 0
            dr["lbflag"] = 0.0 if L == 0 else 1.0
            halves.append(dr)
        drs.append(halves)
    with ExitStack() as st:
        C = alloc_common(nc, st, S, "all")
        C.eps_norm = st.enter_context(nc.sbuf_tensor("s_eps_norm", [128, 1], F32))
        C.one_c = st.enter_context(nc.sbuf_tensor("s_one_c", [128, 1], F32))
        mx = st.enter_context(nc.sbuf_tensor("s_mx", [128, 5644], F32))
        alloc_rec(nc, st, C, mx)
        alloc_att(nc, st, C, mx)
        P = Prog(nc)
        emit_consts(P, C, cd)
        for q in range(nseq):
          x_in, out = x_ins[q], outs[q]
          for L in range(nlayers):
              pr = dict(gnorm=drs[L][0]["gnorm"])
              if L == 0:
                  pr["xT"] = x_in
                  pr["parts"] = []
              else:
                  pr["xT"] = x_in if L == 1 else xb[(L - 2) % 2]
                  pr["x_dep"] = L >= 2
                  pr["parts"] = yp
                  pr["xout"] = xb[(L - 1) % 2]
              emit_prologue(P, C, pr, len(pr["parts"]), L > 0)
              for hf in range(2):
                  if L % 2 == 0:
                      emit_rec_layer(P, C, drs[L][hf], do_outproj=False)
                  else:
                      emit_att_layer(P, C, drs[L][hf], 0.8 - 0.6 * math.exp(-0.3 * L), do_outproj=False)
              for hf in range(2):
                  emit_outproj(P, C, WStream(P, C), drs[L][hf], 16 if L % 2 == 0 else 8)
              P.barrier()
          pr = dict(xT=(x_in if nlayers == 1 else xb[(nlayers - 2) % 2]), x_dep=nlayers >= 2, parts=yp,
                    xout=out, gnorm=drs[0][0]["gnorm"])
          emit_prologue(P, C, pr, 2, True, final_only=True)
          P.barrier()
        n = P.emit()
    return nc, n


def fused_maps(inp, S=SEQ, nlayers=4, ncores=2, nseq=2):
    cs = host_consts(S)
    x = np.asarray(inp["x"], np.float32)
    wmap = {}
    for L in range(nlayers):
        j = L // 2
        for hf in range(2):
            lay = rec_layout(inp, j, hf) if L % 2 == 0 else att_layout(inp, j, hf)
            for k, v in lay.items():
                if k == "gnorm" and hf == 1:
                    continue
                wmap[f"{k}{L}h{hf}"] = v
    maps = []
    for c in range(ncores):
        m = {}
        for q in range(nseq):
            b = c * nseq + q
            m["xT" if q == 0 else f"xT{q}"] = np.ascontiguousarray(x[b][:S].T)
        m.update(wmap)
        for k in CONST_KEYS:
            m[k] = cs[k]
        maps.append(m)
    return maps


_PROGS = {}
NCORES = 2
NSEQ = 2


def kernel(**inp):
    inp = {k: np.asarray(v) for k, v in inp.items()}
    if "fused" not in _PROGS:
        _PROGS["fused"] = build_fused_program(SEQ, 4, NSEQ)[0]
    nc = _PROGS["fused"]
    maps = fused_maps(inp, SEQ, 4, NCORES, NSEQ)
    res = run_bass_kernel_spmd(nc, maps, core_ids=list(range(NCORES)))
    out = np.empty((BATCH, SEQ, D), np.float32)
    for c in range(NCORES):
        for q in range(NSEQ):
            out[c * NSEQ + q] = np.asarray(res.results[c]["outT" if q == 0 else f"outT{q}"]).T
    return out
```

```python
import math
from contextlib import ExitStack

import numpy as np
import concourse.bass as bass
import concourse.mybir as mybir
from concourse.bass_utils import run_bass_kernel_spmd

F32 = mybir.dt.float32
BF16 = mybir.dt.bfloat16
AF = mybir.ActivationFunctionType
ALU = mybir.AluOpType

D = 2048
KC = 16
SEQ = 2048
BATCH = 4
TC = 512
NORM_EPS = 1e-6
SUBLN_EPS = 1e-5
SAME_ENGINE_SYNC = True


def _fs(ap):
    try:
        return int(ap.free_size())
    except Exception:
        return 512


RESCHEDULE = True


class Prog:
    ENGS = ("pe", "act", "dve", "pool", "sp")

    def __init__(self, nc):
        self.nc = nc
        self.ops = []

    def add(self, eng, fn, reads=(), writes=(), dma=None, cost=0.5):
        self.ops.append(dict(kind="op", eng=eng, fn=fn, reads=tuple(reads), writes=tuple(writes), dma=dma, cost=cost))

    def barrier(self):
        self.ops.append(dict(kind="barrier"))

    def mm(self, out, lhsT, rhs, start, stop, reads, writes):
        self.add("pe", lambda e: e.matmul(out, lhsT, rhs, start=start, stop=stop), reads, writes,
                 cost=0.035 + _fs(rhs) / 2400.0)

    def tr(self, out, in_, ident, reads, writes):
        self.add("pe", lambda e: e.transpose(out, in_, ident), reads, writes, cost=0.1)

    def act(self, out, in_, func, reads, writes, bias=0.0, scale=1.0, eng="act"):
        self.add(eng, lambda e: e.activation(out, in_, func, bias=bias, scale=scale), reads, writes,
                 cost=0.2 + _fs(out) * 0.0008)
        self.ops[-1]["tset"] = "tanh" if func == AF.Tanh else ("ln" if func == AF.Ln else None)

    def ts(self, out, in0, s1, s2, op0, op1, reads, writes, eng="dve"):
        self.add(eng, lambda e: e.tensor_scalar(out, in0, s1, s2, op0, op1), reads, writes, cost=0.15 + _fs(out) * 0.001)

    def stt(self, out, in0, scalar, in1, op0, op1, reads, writes, eng="dve"):
        self.add(eng, lambda e: e.scalar_tensor_tensor(out, in0, scalar, in1, op0, op1), reads, writes,
                 cost=0.2 + _fs(out) * 0.0011)

    def tt(self, out, in0, in1, op, reads, writes, eng="dve"):
        c = 0.15 + _fs(out) * (0.001 if eng == "dve" else 0.0022)
        self.add(eng, lambda e: e.tensor_tensor(out, in0, in1, op), reads, writes, cost=c)

    def cp(self, out, in_, reads, writes, eng="dve"):
        c = 0.15 + _fs(out) * 0.001
        self.add(eng, lambda e: e.tensor_copy(out, in_), reads, writes, cost=c)

    def memset(self, ap, val, writes, eng="dve"):
        self.add(eng, lambda e: e.memset(ap, val), (), writes, cost=0.2)

    def recip(self, out, in_, reads, writes):
        self.add("dve", lambda e: e.reciprocal(out, in_), reads, writes, cost=0.15 + _fs(out) * 0.001)

    def scan(self, out, d0, d1, init, reads, writes):
        self.add("dve", lambda e: e.tensor_tensor_scan(out, d0, d1, init, ALU.mult, ALU.add), reads, writes,
                 cost=0.2 + _fs(out) * 0.002)

    def dma(self, out, in_, slot, reads, writes, eng="sp"):
        try:
            nbytes = int(out.nbytes())
        except Exception:
            nbytes = 1 << 18
        self.add(eng, lambda e: e.dma_start(out, in_), reads, writes, dma=slot, cost=2.0 + nbytes / 150e3)

    def emit(self):
        nc = self.nc
        real = []
        seg_of = []
        seg = 0
        last_w = {}
        readers = {}
        for o in self.ops:
            if o["kind"] == "barrier":
                seg += 1
                continue
            o["idx"] = len(real)
            real.append(o)
            o["seg"] = seg
            deps = set()
            for r in o["reads"]:
                if r in last_w:
                    deps.add(last_w[r])
            for w in o["writes"]:
                if w in last_w:
                    deps.add(last_w[w])
                deps.update(readers.get(w, ()))
            deps.discard(o["idx"])
            for w in o["writes"]:
                last_w[w] = o["idx"]
                readers[w] = []
            for r in o["reads"]:
                if r not in o["writes"]:
                    readers.setdefault(r, []).append(o["idx"])
            o["skey"] = ("dma", o["dma"]) if o["dma"] is not None else ("eng", o["eng"])
            o["deps"] = [d for d in deps if real[d]["seg"] == seg]
            o["signal"] = False
        nseg = seg + 1
        import heapq
        order = {e: [] for e in self.ENGS}
        seg_ops = [[] for _ in range(nseg)]
        for o in real:
            seg_ops[o["seg"]].append(o["idx"])
        fin = [0.0] * len(real)
        tnow = 0.0
        LAT = 0.15
        for sg in range(nseg):
            ids = seg_ops[sg]
            if not RESCHEDULE:
                for i in ids:
                    order[real[i]["eng"]].append(i)
                continue
            nd = {i: len(real[i]["deps"]) for i in ids}
            users = {i: [] for i in ids}
            for i in ids:
                for d in real[i]["deps"]:
                    users[d].append(i)
            ready = {e: [] for e in self.ENGS}
            rt = {}
            for i in ids:
                if nd[i] == 0:
                    rt[i] = tnow
                    heapq.heappush(ready[real[i]["eng"]], i)
            free = {e: tnow for e in self.ENGS}
            cur_set = [None]
            sp_list = [i for i in ids if real[i]["eng"] == "sp"]
            sp_pos = 0
            remaining = len(ids)
            while remaining:
                best = None
                for e in self.ENGS:
                    h = ready[e]
                    if not h:
                        continue
                    if e == "sp":
                        if sp_pos < len(sp_list) and nd[sp_list[sp_pos]] == 0:
                            i = sp_list[sp_pos]
                            st_ = max(free[e], rt[i])
                            cand = (st_, i, e)
                        else:
                            continue
                    else:
                        cands = heapq.nsmallest(12 if e == "act" else 6, h)
                        cand = None
                        for i in cands:
                            st_ = max(free[e], rt[i])
                            if e == "act":
                                ts_ = real[i].get("tset")
                                if ts_ is not None and cur_set[0] is not None and ts_ != cur_set[0]:
                                    st_ += 1.3
                            if cand is None or st_ < cand[0] - 1e-9:
                                cand = (st_, i, e)
                    if cand is not None and (best is None or cand[0] < best[0] - 1e-9 or
                                             (abs(cand[0] - best[0]) <= 1e-9 and cand[1] < best[1])):
                        best = cand
                if best is None:
                    raise RuntimeError("scheduler deadlock")
                st_, i, e = best
                if e == "sp":
                    sp_pos += 1
                    ready[e].remove(i)
                    heapq.heapify(ready[e])
                else:
                    ready[e].remove(i)
                    heapq.heapify(ready[e])
                o = real[i]
                if e == "act" and o.get("tset") is not None:
                    cur_set[0] = o["tset"]
                if o["dma"] is not None:
                    free[e] = st_ + 0.1
                    fin[i] = st_ + o["cost"]
                else:
                    free[e] = st_ + o["cost"]
                    fin[i] = free[e]
                order[e].append(i)
                remaining -= 1
                for u in users[i]:
                    nd[u] -= 1
                    rt[u] = max(rt.get(u, tnow), fin[i] + LAT)
                    if nd[u] == 0:
                        heapq.heappush(ready[real[u]["eng"]], u)
            tnow = max([tnow] + [fin[i] for i in ids]) + 1.0
        self.est_us = tnow
        pos = {}
        for e in self.ENGS:
            for p, i in enumerate(order[e]):
                pos[i] = p
        dma_seq = {}
        for i in order["sp"] + [i for e in self.ENGS if e != "sp" for i in order[e]]:
            pass
        cnt = {}
        for e in self.ENGS:
            for i in order[e]:
                o = real[i]
                if o["dma"] is not None:
                    cnt[o["skey"]] = cnt.get(o["skey"], 0) + 16
                    o["sigval"] = cnt[o["skey"]]
                    o["dpos"] = cnt[o["skey"]]
        last_in_seg = [dict() for _ in range(nseg)]
        for e in self.ENGS:
            for i in order[e]:
                o = real[i]
                last_in_seg[o["seg"]][o["skey"]] = i
        first_seen = set()
        for e in self.ENGS:
            for i in order[e]:
                o = real[i]
                if o["seg"] > 0 and (e, o["seg"]) not in first_seen:
                    first_seen.add((e, o["seg"]))
                    for sgp in range(o["seg"]):
                        o["deps"] = list(o["deps"]) + list(last_in_seg[sgp].values())
        for o in real:
            keep = {}
            for d in o["deps"]:
                od = real[d]
                if od["dma"] is None and od["eng"] == o["eng"]:
                    if o["eng"] == "pe" or not SAME_ENGINE_SYNC:
                        continue
                sk = od["skey"]
                key = od["dpos"] if od["dma"] is not None else pos[d]
                if sk not in keep or keep[sk][0] < key:
                    keep[sk] = (key, d)
            o["deps"] = [v[1] for v in keep.values()]
            for d in o["deps"]:
                real[d]["signal"] = True
        for e in self.ENGS:
            c = 0
            for i in order[e]:
                o = real[i]
                if o["dma"] is None and o["signal"]:
                    c += 1
                    o["sigval"] = c
        skeys = sorted({o["skey"] for o in real}, key=str)
        with ExitStack() as st:
            sems = {}
            for i, sk in enumerate(skeys):
                sems[sk] = st.enter_context(nc.semaphore(f"sem{i}"))
            dma_final = {sk: cnt[sk] for sk in skeys if sk[0] == "dma"}
            self.sem_names = {f"sem{i}": sk for i, sk in enumerate(skeys)}

            def run(ename, eng):
                waited = {}
                for i in order[ename]:
                    o = real[i]
                    for d in o["deps"]:
                        od = real[d]
                        sk, val = od["skey"], od["sigval"]
                        if waited.get(sk, 0) < val:
                            eng.wait_ge(sems[sk], val)
                            waited[sk] = val
                    ins = o["fn"](eng)
                    if o["dma"] is not None:
                        ins.then_inc(sems[o["skey"]], 16)
                    elif o["signal"]:
                        ins.then_inc(sems[o["skey"]], 1)
                if ename == "sp":
                    for sk, val in dma_final.items():
                        if waited.get(sk, 0) < val:
                            eng.wait_ge(sems[sk], val)

            with nc.Block() as block:
                @block.tensor
                def _(e):
                    run("pe", e)

                @block.scalar
                def _(e):
                    run("act", e)

                @block.vector
                def _(e):
                    run("dve", e)

                @block.gpsimd
                def _(e):
                    run("pool", e)

                @block.sync
                def _(e):
                    run("sp", e)
        return len(real)


def host_consts(S):
    c = {}
    c["ones"] = np.ones((128, 128), np.float32)
    c["ident"] = np.eye(128, dtype=np.float32)
    s = np.arange(128)[:, None]
    t = np.arange(128)[None, :]
    c["bmask"] = ((s // 64 == t // 64) & (s <= t)).astype(np.float32)
    tri = (s <= t).astype(np.float32)
    c["tri2"] = np.concatenate([tri, tri], axis=1)
    cm = np.ones((128, TC), np.float32)
    cm[:, ::64] = 0.0
    c["cmask"] = cm
    half = 16
    inv_freq = (500000.0 ** (-np.arange(0, 32, 2, dtype=np.float32) / 32)).astype(np.float32)
    pos = np.arange(S, dtype=np.float32)
    ang = pos[None, :] * inv_freq[:, None]
    cosT = np.ones((128, S), np.float32)
    sinT = np.zeros((128, S), np.float32)
    cosT[0:16] = np.cos(ang)
    cosT[16:32] = np.cos(ang)
    sinT[0:16] = np.sin(ang)
    sinT[16:32] = np.sin(ang)
    c["cosT"] = cosT
    c["sinT"] = sinT
    pm = np.zeros((128, 128), np.float32)
    for m in range(16):
        pm[m + 16, m] = -1.0
        pm[m, m + 16] = 1.0
    c["pmat"] = pm
    return c


class Ctx:
    pass


def alloc_common(nc, st, S, kind):
    C = Ctx()
    C.S = S
    C.NT = S // TC
    sb = lambda name, shape, dt: st.enter_context(nc.sbuf_tensor("s_" + name, shape, dt))
    C.hT = sb("hT", [128, KC, S], BF16)
    C.wf = [sb(f"wf{i}", [128, KC, 128], F32) for i in range(2)]
    C.wb = [sb(f"wb{i}", [128, KC, 512], BF16) for i in range(2)]
    C.T = [sb(f"T{i}", [128, TC], F32) for i in range(14)]
    C.B = [sb(f"B{i}", [128, TC], BF16) for i in range(12)]
    C.ones_b = sb("ones_b", [128, 128], BF16)
    C.ident_b = sb("ident_b", [128, 128], BF16)
    C.cst_f = sb("cst_f", [128, 128], F32)
    C.gnorm = sb("gnorm", [128, KC], F32)
    C.ps = [st.enter_context(nc.psum_tensor(f"ps{i}", [128, 512], F32)) for i in range(8)]
    return C


def load_const_bf16(P, C, dst, src_dram, name):
    P.dma(C.cst_f[:], src_dram, slot="cst", reads=[("dram", name)], writes=["cst_f"])
    P.cp(dst[:], C.cst_f[:], reads=["cst_f"], writes=[name], eng="dve")


def emit_prologue(P, C, dr, n_part, have_xout, final_only=False):
    S, NT = C.S, C.NT
    P.dma(C.gnorm[:], dr["gnorm"], slot="gn", reads=[], writes=["gnorm"])
    stat_banks = [C.ps[4 + i] for i in range(NT)]
    for kc in range(KC):
        for tc in range(NT):
            sl = (kc * NT + tc) % 4
            xa = C.T[sl]
            xk = ("T", sl)
            cols = slice(tc * TC, (tc + 1) * TC)
            rows = slice(kc * 128, (kc + 1) * 128)
            P.dma(xa[:], dr["xT"][rows, cols], slot=("xa", sl), reads=([("xout", kc, tc)] if dr.get("x_dep") else []), writes=[xk])
            for pi in range(n_part):
                pa = C.T[8 + sl]
                pk = ("T", 8 + sl)
                P.dma(pa[:], dr["parts"][pi][rows, cols], slot=("pa", sl), reads=[("yp", pi, kc, tc)], writes=[pk])
                P.tt(xa[:], xa[:], pa[:], ALU.add, reads=[xk, pk], writes=[xk])
            if have_xout:
                P.dma(dr["xout"][rows, cols], xa[:], slot=("xo", sl), reads=[xk], writes=[("xout", kc, tc), "xout_all"])
            if final_only:
                continue
            xs = C.B[sl]
            P.act(xs[:], xa[:], AF.Square, reads=[xk], writes=[("B", sl)])
            P.mm(stat_banks[tc][:], C.ones_b[:], xs[:], start=(kc == 0), stop=(kc == KC - 1),
                 reads=[("B", sl), "ones_b"], writes=[("ps", 4 + tc)])
    if final_only:
        return
    for tc in range(NT):
        r = C.T[4 + tc]
        rstd_act(P, r[:], ("T", 4 + tc), stat_banks[tc][:], [("ps", 4 + tc)], C.eps_norm[:], "eps_norm", 1.0 / D)
    src = dr["xout"] if have_xout else dr["xT"]
    for kc in range(KC):
        for tc in range(NT):
            sl = (kc * NT + tc) % 4
            xa = C.T[sl]
            xk = ("T", sl)
            cols = slice(tc * TC, (tc + 1) * TC)
            rows = slice(kc * 128, (kc + 1) * 128)
            rd = [("xout", kc, tc)] if have_xout else []
            P.dma(xa[:], src[rows, cols], slot=("xa", sl), reads=rd, writes=[xk])
            P.stt(C.hT[:, kc, cols], xa[:], C.gnorm[:, kc:kc + 1], C.T[4 + tc][:], ALU.mult, ALU.mult,
                  reads=[xk, "gnorm", ("T", 4 + tc)], writes=[("hT", kc, tc)])


class WStream:
    def __init__(self, P, C):
        self.P, self.C = P, C
        self.n = 0

    def load_group(self, w_dram, col0, nslab, gslot, kcn=KC, name="w"):
        P, C = self.P, self.C
        ncol = nslab * 128
        for q in range(kcn // 4):
            sl = self.n % 2
            self.n += 1
            src = w_dram[q * 512:(q + 1) * 512, col0:col0 + ncol].rearrange("(k p) c -> p k c", p=128)
            stage = C.wf[sl][:].rearrange("p k c -> p (k c)")[:, 0:4 * ncol].rearrange("p (k c) -> p k c", k=4)
            P.dma(stage, src, slot=("wf", sl), reads=[], writes=[("wf", sl)])
            P.act(C.wb[gslot][:, 4 * q:4 * q + 4, 0:ncol], stage, AF.Copy,
                  reads=[("wf", sl)], writes=[("wb", gslot, q)])


def inproj_fm(P, C, gslot, s, tc, bank):
    cols = slice(tc * TC, (tc + 1) * TC)
    for kc in range(KC):
        P.mm(C.ps[bank][:], C.wb[gslot][:, kc, s * 128:(s + 1) * 128], C.hT[:, kc, cols],
             start=(kc == 0), stop=(kc == KC - 1),
             reads=[("wb", gslot, kc // 4), ("hT", kc, tc)], writes=[("ps", bank)])


def inproj_tm(P, C, gslot, s0, ncol, tc, tt, out_ap, bank):
    t0 = tc * TC + tt * 128
    rd = [("wb", gslot, q) for q in range(4)]
    for kc in range(KC):
        P.mm(out_ap, C.hT[:, kc, t0:t0 + 128], C.wb[gslot][:, kc, s0 * 128:s0 * 128 + ncol],
             start=(kc == 0), stop=(kc == KC - 1),
             reads=rd + [("hT", kc, tc)], writes=[("ps", bank)])


def emit_outproj(P, C, ws, dr, nfc):
    S, NT = C.S, C.NT
    fo = dr.get("fc_off", 0)
    ws.load_group(dr["w_out"], 0, 4, 0, kcn=nfc)
    for tc in range(NT):
        cols = slice(tc * TC, (tc + 1) * TC)
        for fc in range(nfc):
            P.dma(C.hT[:, fo + fc, cols], dr["mixT"][fc * 128:(fc + 1) * 128, cols], slot=("mixld", fo, tc),
                  reads=[("mixT", dr.get("mix_tag", 0), fc, tc)], writes=[("hT", fo + fc, tc), ("mixall", fo, tc)],
                  eng="act")
    for g4 in range(4):
        g = g4 % 2
        if g4 + 1 < 4:
            ws.load_group(dr["w_out"], (g4 + 1) * 512, 4, (g4 + 1) % 2, kcn=nfc)
        for s_ in range(4):
            dc = g4 * 4 + s_
            for tc in range(NT):
                cols = slice(tc * TC, (tc + 1) * TC)
                bank = (dc * NT + tc) % 2
                for fc in range(nfc):
                    P.mm(C.ps[bank][:], C.wb[g][:, fc, s_ * 128:(s_ + 1) * 128], C.hT[:, fo + fc, cols],
                         start=(fc == 0), stop=(fc == nfc - 1),
                         reads=[("wb", g, fc // 4), ("mixall", fo, tc), ("hT", fo + fc, tc)], writes=[("ps", bank)])
                ysl = (dc * NT + tc) % 2
                yt = C.T[12 + ysl]
                P.act(yt[:], C.ps[bank][:], AF.Copy, reads=[("ps", bank)], writes=[("T", 12 + ysl)])
                P.dma(dr["yp"][dc * 128:(dc + 1) * 128, cols], yt[:], slot=("yst", ysl),
                      reads=[("T", 12 + ysl)], writes=[("yp", dr.get("yp_tag", 0), dc, tc)])


REC_GROUPS = ["rg", "hg", "hg", "rg", "hg", "hg", "rg", "hg", "hg", "rg", "hg", "hg"]


def emit_rec_layer(P, C, dr, do_outproj=True):
    S, NT = C.S, C.NT
    nc = P.nc
    T, B, ps = C.T, C.B, C.ps
    R = C.R
    P.dma(R.convw[:], dr["convw"], slot="p0", reads=[], writes=["convw"])
    P.dma(R.vec[:], dr["recvec"], slot="p1", reads=[], writes=["recvec"])
    P.dma(R.lbraw[:], dr["hglb"], slot="p2", reads=[], writes=["lbraw"])
    P.dma(R.onorm[:], dr["onorm"], slot="p3", reads=[], writes=["onorm"])
    P.dma(R.cmask[:], dr["cmask"], slot="p4", reads=[], writes=["cmask"])
    P.dma(R.bmask[:], dr["bmask"], slot="p5", reads=[], writes=["bmask"])
    P.act(R.sp[:], R.vec[:, :, 3], AF.Exp, reads=["recvec"], writes=["sp"], scale=-1.0)
    P.act(R.sp[:], R.sp[:], AF.Ln, reads=["sp", "one_c"], writes=["sp"], bias=C.one_c[:], scale=1.0)
    P.add("dve", lambda e: e.tensor_scalar_mul(R.cneg[:], R.sp[:], -8.0), ["sp"], ["cneg"])
    P.add("dve", lambda e: e.tensor_scalar_mul(R.cneg2[:], R.sp[:], -16.0), ["sp"], ["cneg2"])
    P.tt(R.lb[:], R.lbraw[:, :, 1], R.lbraw[:, :, 0], ALU.subtract, reads=["lbraw"], writes=["lb"])
    P.act(R.lb[:], R.lb[:], AF.Sigmoid, reads=["lb"], writes=["lb"])
    P.add("dve", lambda e: e.tensor_scalar_mul(R.lb[:], R.lb[:], float(dr["lbflag"])), ["lb"], ["lb"])
    P.ts(R.oml[:], R.lb[:], -1.0, 1.0, ALU.mult, ALU.add, reads=["lb"], writes=["oml"])
    P.memset(R.zero_f[:], 0.0, writes=["zero_f"], eng="dve")
    P.memset(R.zero_b[:], 0.0, writes=["zero_b"], eng="dve")
    P.add("dve", lambda e: e.tensor_scalar_mul(R.vech[:], R.vec[:], 0.5), ["recvec"], ["vech"])
    P.add("dve", lambda e: e.tensor_scalar_mul(R.chalf[:], R.sp[:], -4.0), ["sp"], ["chalf"])
    P.add("dve", lambda e: e.tensor_scalar_mul(R.omh[:], R.oml[:], 0.5), ["oml"], ["omh"])
    P.tt(R.lbp[:], R.omh[:], R.lb[:], ALU.add, reads=["omh", "lb"], writes=["lbp"])

    ws = WStream(P, C)
    groups = REC_GROUPS
    objs = []
    ia = ie = 0
    for gi, gk in enumerate(groups):
        if gk == "rg":
            objs.append(RGGroup(P, C, dr, gi % 2, ia))
            ia += 1
        else:
            objs.append(HGGroup(P, C, dr, gi % 2, ie))
            ie += 1
    seq = [(gi, tc) for gi in range(len(groups)) for tc in range(NT)]
    ws.load_group(dr["w_in"], 0, 4, 0)
    objs[0].inproj(0, 0)
    for k, (gi, tc) in enumerate(seq):
        if tc == 0 and gi + 1 < len(groups):
            ws.load_group(dr["w_in"], (gi + 1) * 512, 4, (gi + 1) % 2)
        if k + 1 < len(seq):
            objs[seq[k + 1][0]].inproj(seq[k + 1][1], (k + 1) % 2)
        objs[gi].mixer(tc, k % 2)
    if do_outproj:
        emit_outproj(P, C, ws, dr, 16)


_BK = [0]
_SL = [0]


def rstd_act(P, out, out_key, in_, in_keys, eps_ap, eps_key, inv_n):
    P.act(out, in_, AF.Ln, reads=list(in_keys) + [eps_key], writes=[out_key], bias=eps_ap, scale=inv_n)
    P.act(out, out, AF.Exp, reads=[out_key], writes=[out_key], scale=-0.5)


def v_tokmajor(P, C, gslot, s, tc, bank, tmp_idx, out_ap, out_keys):
    inproj_fm(P, C, gslot, s, tc, bank)
    vT = C.B[tmp_idx]
    P.act(vT[:], C.ps[bank][:], AF.Copy, reads=[("ps", bank)], writes=[("B", tmp_idx)])
    pv = C.ps[bank][:].bitcast(BF16)
    for tt in range(4):
        P.tr(pv[:, tt * 128:(tt + 1) * 128], vT[:, tt * 128:(tt + 1) * 128], C.ident_b[:],
             reads=[("B", tmp_idx), "ident_b"], writes=[("ps", bank)])
    P.cp(out_ap, pv[:, 0:512].rearrange("p (a b) -> p a b", a=4), reads=[("ps", bank)], writes=out_keys, eng="dve")


def silu2(P, C, out_ap, out_key, bank):
    k = 4 + (_SL[0] % 2)
    _SL[0] += 1
    th = C.T[k]
    P.act(th[:], C.ps[bank][:], AF.Tanh, reads=[("ps", bank)], writes=[("T", k)], scale=0.5)
    P.stt(out_ap, th[:], 1.0, C.ps[bank][:], ALU.add, ALU.mult, reads=[("T", k), ("ps", bank)], writes=[out_key])


def next_bank():
    b = _BK[0] % 2
    _BK[0] += 1
    return b


class RGGroup:
    def __init__(self, P, C, dr, gslot, a):
        self.P, self.C, self.dr, self.gslot, self.a = P, C, dr, gslot, a

    def inproj(self, tc, par):
        P, C, dr, gslot, a = self.P, self.C, self.dr, self.gslot, self.a
        T, B, ps, R = C.T, C.B, C.ps, C.R
        if tc == 0:
            P.dma(R.wgf[:], dr["wgate"][a], slot="wg", reads=[], writes=["wgf"])
            P.act(R.wgb[:], R.wgf[:], AF.Copy, reads=["wgf"], writes=["wgb"])
        xr = R.xraw[par]
        for i in range(2):
            bank = next_bank()
            inproj_fm(P, C, gslot, i, tc, bank)
            if tc == 0:
                P.memset(xr[:, i, 0:3], 0.0, writes=[("xraw", par, i)], eng="dve")
            else:
                P.act(xr[:, i, 0:3], R.xraw[1 - par][:, i, TC:TC + 3], AF.Copy, reads=[("xraw", 1 - par, i)],
                      writes=[("xraw", par, i)])
            P.act(xr[:, i, 3:3 + TC], ps[bank][:], AF.Copy, reads=[("ps", bank)], writes=[("xraw", par, i)])
        for j in range(2):
            bank = next_bank()
            inproj_fm(P, C, gslot, 2 + j, tc, bank)
            silu2(P, C, B[par * 2 + j][:], ("B", par * 2 + j), bank)

    def mixer(self, tc, par):
        P, C, dr, gslot, a = self.P, self.C, self.dr, self.gslot, self.a
        T, B, ps, R = C.T, C.B, C.ps, C.R
        xr = R.xraw[par]
        for i in range(2):
            ch = 2 * a + i
            xc = T[6 + i]
            k = ("T", 6 + i)
            P.act(xc[:], xr[:, i, 3:3 + TC], AF.Identity, reads=[("xraw", par, i), "convw", "recvec"], writes=[k],
                  bias=R.vec[:, ch, 0:1], scale=R.convw[:, ch, 3:4])
            for tap in range(3):
                P.stt(xc[:], xr[:, i, tap:tap + TC], R.convw[:, ch, tap:tap + 1], xc[:], ALU.mult, ALU.add,
                      reads=[("xraw", par, i), "convw", k], writes=[k])
            P.act(B[6 + i][:], xc[:], AF.Copy, reads=[k], writes=[("B", 6 + i)])
        for j in range(2):
            ch = 2 * a + j
            for g in range(2):
                for i in range(2):
                    P.mm(ps[3 + g][:], R.wgb[:, g, i, j * 128:(j + 1) * 128], B[6 + i][:], start=(i == 0), stop=(i == 1),
                         reads=["wgb", ("B", 6 + i)], writes=[("ps", 3 + g)])
            r_, ig, aa, mm_, uu = T[8], T[9], T[10], T[11], T[12]
            P.act(r_[:], ps[3][:], AF.Tanh, reads=[("ps", 3), "vech"], writes=[("T", 8)], bias=R.vech[:, ch, 1:2], scale=0.5)
            P.act(ig[:], ps[4][:], AF.Tanh, reads=[("ps", 4), "vech"], writes=[("T", 9)], bias=R.vech[:, ch, 2:3], scale=0.5)
            P.act(aa[:], r_[:], AF.Exp, reads=[("T", 8), "chalf"], writes=[("T", 10)], scale=R.chalf[:, ch:ch + 1],
                  bias=R.chalf[:, ch:ch + 1])
            P.act(mm_[:], r_[:], AF.Exp, reads=[("T", 8), "cneg"], writes=[("T", 11)], scale=R.cneg[:, ch:ch + 1],
                  bias=R.cneg[:, ch:ch + 1])
            P.act(mm_[:], mm_[:], AF.Ln, reads=[("T", 11), "one_c"], writes=[("T", 11)], bias=C.one_c[:], scale=-1.0)
            P.act(mm_[:], mm_[:], AF.Exp, reads=[("T", 11)], writes=[("T", 11)], scale=0.5)
            P.stt(uu[:], ig[:], 1.0, T[6 + j][:], ALU.add, ALU.mult, reads=[("T", 9), ("T", 6 + j)], writes=[("T", 12)])
            P.stt(uu[:], uu[:], 0.5, mm_[:], ALU.mult, ALU.mult, reads=[("T", 12), ("T", 11)], writes=[("T", 12)])
            hcur = R.h[par][j]
            hk = ("h", par, j)
            if tc == 0:
                init = 0.0
                rd = []
            else:
                init = R.h[1 - par][j][:, TC - 1:TC]
                rd = [("h", 1 - par, j)]
            P.scan(hcur[:], aa[:], uu[:], init, reads=[("T", 10), ("T", 12)] + rd, writes=[hk])
            mo = B[8 + j]
            P.stt(mo[:], hcur[:], 0.5, B[par * 2 + j][:], ALU.mult, ALU.mult, reads=[hk, ("B", par * 2 + j)], writes=[("B", 8 + j)])
            fc = 2 * a + j
            P.dma(dr["mixT"][fc * 128:(fc + 1) * 128, tc * TC:(tc + 1) * TC], mo[:], slot=("mo", j),
                  reads=[("B", 8 + j)], writes=[("mixT", dr.get("mix_tag", 0), fc, tc)])


class HGGroup:
    def __init__(self, P, C, dr, gslot, e):
        self.P, self.C, self.dr, self.gslot, self.e = P, C, dr, gslot, e
        self.nS = 0

    def tiles(self, par):
        T, B = self.C.T, self.C.B
        qi, fi, si, vi = par, 2 + par, 2 * par, 2 * par + 1
        return (T[qi], ("T", qi)), (T[fi], ("T", fi)), (B[si], ("B", si)), (B[vi], ("B", vi))

    def inproj(self, tc, par):
        P, C, gslot = self.P, self.C, self.gslot
        ps = C.ps
        (qs, qk), (ff, fk), (sg, sk), (V, vk) = self.tiles(par)
        bank = next_bank()
        inproj_fm(P, C, gslot, 0, tc, bank)
        silu2(P, C, qs[:], qk, bank)
        bank = next_bank()
        inproj_fm(P, C, gslot, 1, tc, bank)
        silu2(P, C, sg[:], sk, bank)
        bank = next_bank()
        inproj_fm(P, C, gslot, 2, tc, bank)
        P.act(ff[:], ps[bank][:], AF.Tanh, reads=[("ps", bank)], writes=[fk], scale=0.5)
        bank = next_bank()
        v_tokmajor(P, C, gslot, 3, tc, bank, 10 + par, V[:].rearrange("p (a b) -> p a b", a=4), [vk])

    def mixer(self, tc, par):
        P, C, dr, e = self.P, self.C, self.dr, self.e
        T, B, ps, R = C.T, C.B, C.ps, C.R
        HG_SCALE = 128 ** -0.5
        (qs, qk), (ff, fk), (sg, sk), (V, vk) = self.tiles(par)
        lf, bb, eb, enb = T[8], T[9], T[10], T[11]
        P.ts(ff[:], ff[:], R.omh[:, e:e + 1], R.lbp[:, e:e + 1], ALU.mult, ALU.add, reads=[fk, "omh", "lbp"], writes=[fk])
        P.act(lf[:], ff[:], AF.Ln, reads=[fk], writes=[("T", 8)])
        P.scan(bb[:], R.cmask[:], lf[:], 0.0, reads=["cmask", ("T", 8)], writes=[("T", 9)])
        P.act(eb[:], bb[:], AF.Exp, reads=[("T", 9)], writes=[("T", 10)])
        P.act(enb[:], bb[:], AF.Exp, reads=[("T", 9)], writes=[("T", 11)], scale=-1.0)
        P.ts(ff[:], ff[:], -1.0, 1.0, ALU.mult, ALU.add, reads=[fk], writes=[fk])
        Qd, Kd, K2 = B[4], B[5], B[6]
        P.stt(Qd[:], qs[:], HG_SCALE * 0.5, eb[:], ALU.mult, ALU.mult, reads=[qk, ("T", 10)], writes=[("B", 4)])
        P.tt(Kd[:], ff[:], enb[:], ALU.mult, reads=[fk, ("T", 11)], writes=[("B", 5)])
        for c in range(8):
            cs = slice(c * 64, (c + 1) * 64)
            P.stt(K2[:, cs], ff[:, cs], eb[:, c * 64 + 63:c * 64 + 64], enb[:, cs], ALU.mult, ALU.mult,
                  reads=[fk, ("T", 10), ("T", 11)], writes=[("B", 6)])
        for tt in range(4):
            tsl = slice(tt * 128, (tt + 1) * 128)
            ap = tt % 2
            atb = 5 if ap == 0 else 3
            at_ps = ps[atb][:, 0:128]
            P.mm(at_ps, Kd[:, tsl], Qd[:, tsl], start=True, stop=True, reads=[("B", 5), ("B", 4)], writes=[("ps", atb)])
            atm = R.atm[ap]
            P.tt(atm[:], at_ps, R.bmask[:], ALU.mult, reads=[("ps", atb), "bmask"], writes=[("atm", ap)])
            tr_ps = ps[4][:].bitcast(BF16)[:, 0:128]
            P.tr(tr_ps, K2[:, tsl], C.ident_b[:], reads=[("B", 6), "ident_b"], writes=[("ps", 4)])
            k2t = R.k2t[ap]
            P.cp(k2t[:], tr_ps, reads=[("ps", 4)], writes=[("k2t", ap)], eng="dve")
            o_ps = ps[7][:, tsl]
            P.mm(o_ps, V[:, tsl], atm[:], start=True, stop=False, reads=[vk, ("atm", ap)], writes=[("ps", 7)])
            for h2 in range(2):
                nS = self.nS
                c = tt * 2 + h2
                if tc == 0 and c == 0:
                    sb_prev, sbk = R.zero_b, "zero_b"
                    s_prev, spk = R.zero_f, "zero_f"
                else:
                    sb_prev, sbk = R.Sb[(nS - 1) % 4], ("Sb", (nS - 1) % 4)
                    s_prev, spk = R.Sf[(nS - 1) % 4], ("Sf", (nS - 1) % 4)
                cg = slice(tt * 128 + h2 * 64, tt * 128 + (h2 + 1) * 64)
                P.mm(ps[7][:, cg], sb_prev[:], Qd[:, cg], start=False, stop=(h2 == 1),
                     reads=[sbk, ("B", 4)], writes=[("ps", 7)])
                ub = 6 if nS % 2 == 0 else 2
                u_ps = ps[ub][:, 0:128]
                P.mm(u_ps, k2t[h2 * 64:(h2 + 1) * 64, :], V[h2 * 64:(h2 + 1) * 64, tsl], start=True, stop=True,
                     reads=[("k2t", ap), vk], writes=[("ps", ub)])
                s_new = R.Sf[nS % 4]
                P.stt(s_new[:], s_prev[:], eb[:, c * 64 + 63:c * 64 + 64], u_ps, ALU.mult, ALU.add,
                      reads=[spk, ("T", 10), ("ps", ub)], writes=[("Sf", nS % 4)])
                P.cp(R.Sb[nS % 4][:], s_new[:], reads=[("Sf", nS % 4)], writes=[("Sb", nS % 4)], eng="dve")
                self.nS += 1
        osq = B[7]
        P.act(osq[:], ps[7][:], AF.Square, reads=[("ps", 7)], writes=[("B", 7)])
        P.mm(ps[4][:], C.ones_b[:], osq[:], start=True, stop=True, reads=["ones_b", ("B", 7)], writes=[("ps", 4)])
        rs = T[12]
        rstd_act(P, rs[:], ("T", 12), ps[4][:], [("ps", 4)], C.eps_norm[:], "eps_norm", 1.0 / 128)
        ot = T[13]
        P.stt(ot[:], ps[7][:], R.onorm[:, 0:1], rs[:], ALU.mult, ALU.mult, reads=[("ps", 7), "onorm", ("T", 12)], writes=[("T", 13)])
        mo = B[8 + (tc % 2)]
        P.stt(mo[:], ot[:], 0.5, sg[:], ALU.mult, ALU.mult, reads=[("T", 13), sk], writes=[("B", 8 + (tc % 2))])
        fc = 8 + e
        P.dma(dr["mixT"][fc * 128:(fc + 1) * 128, tc * TC:(tc + 1) * TC], mo[:], slot=("mo", tc % 2),
              reads=[("B", 8 + (tc % 2))], writes=[("mixT", dr.get("mix_tag", 0), fc, tc)])


def alloc_rec(nc, st, C, mx=None):
    R = Ctx()
    sb = lambda name, shape, dt: st.enter_context(nc.sbuf_tensor("s_" + name, shape, dt))
    R.convw = sb("convw", [128, 8, 4], F32)
    R.vec = sb("recvec", [128, 8, 4], F32)
    R.lbraw = sb("lbraw", [128, 8, 2], F32)
    R.onorm = sb("onorm", [128, 1], F32)
    if mx is None:
        R.cmask = sb("cmask", [128, TC], F32)
    R.bmask = sb("bmask", [128, 128], F32)
    R.sp = sb("sp", [128, 8], F32)
    R.cneg = sb("cneg", [128, 8], F32)
    R.cneg2 = sb("cneg2", [128, 8], F32)
    R.lb = sb("lb", [128, 8], F32)
    R.oml = sb("oml", [128, 8], F32)
    R.omh = sb("omh", [128, 8], F32)
    R.lbp = sb("lbp", [128, 8], F32)
    R.chalf = sb("chalf", [128, 8], F32)
    R.vech = sb("vech", [128, 8, 4], F32)
    R.zero_f = sb("zero_f", [128, 128], F32)
    R.zero_b = sb("zero_b", [128, 128], BF16)
    R.wgb = sb("wgb", [128, 2, 2, 256], BF16)
    if mx is None:
        R.wgf = sb("wgf", [128, 2, 2, 256], F32)
        R.xraw = [sb(f"xraw{i}", [128, 2, TC + 3], F32) for i in range(2)]
        R.h = [[sb(f"h{p}{j}", [128, TC], F32) for j in range(2)] for p in range(2)]
    else:
        XW = 2 * (TC + 3)
        R.xraw = [mx[:, i * XW:(i + 1) * XW].rearrange("p (a b) -> p a b", a=2) for i in range(2)]
        o = 2 * XW
        R.h = [[mx[:, o + (p * 2 + j) * TC: o + (p * 2 + j + 1) * TC] for j in range(2)] for p in range(2)]
        o += 4 * TC
        R.cmask = mx[:, o:o + TC]
        o += TC
        R.wgf = mx[:, o:o + 1024].rearrange("p (g i c) -> p g i c", g=2, i=2)
    R.atm = [sb(f"atm{i}", [128, 128], BF16) for i in range(2)]
    R.k2t = [sb(f"k2t{i}", [128, 128], BF16) for i in range(2)]
    R.Sf = [sb(f"Sf{i}", [128, 128], F32) for i in range(4)]
    R.Sb = [sb(f"Sb{i}", [128, 128], BF16) for i in range(4)]
    C.R = R


def emit_consts(P, C, dr):
    load_const_bf16(P, C, C.ones_b, dr["ones"], "ones_b")
    load_const_bf16(P, C, C.ident_b, dr["ident"], "ident_b")
    P.memset(C.eps_norm[:], NORM_EPS, writes=["eps_norm"])
    P.memset(C.one_c[:], 1.0, writes=["one_c"])


def build_rec_program(S, n_part, lbflag):
    nc = bass.Bass("TRN2", target_bir_lowering=False)
    di = lambda name, shape, dt=F32: nc.dram_tensor(name, shape, dt, kind="ExternalInput").ap()
    do = lambda name, shape, dt=F32: nc.dram_tensor(name, shape, dt, kind="ExternalOutput").ap()
    dr = {}
    dr["xT"] = di("xT", [D, S])
    dr["parts"] = [di(f"part{i}", [D, S]) for i in range(n_part)]
    dr["gnorm"] = di("gnorm", [128, KC])
    dr["w_in"] = di("w_in", [D, 6144])
    dr["w_out"] = di("w_out", [2048, D])
    dr["convw"] = di("convw", [128, 8, 4])
    dr["recvec"] = di("recvec", [128, 8, 4])
    dr["hglb"] = di("hglb", [128, 8, 2])
    dr["onorm"] = di("onorm", [128, 1])
    dr["wgate"] = di("wgate", [4, 128, 2, 2, 256])
    dr["cmask"] = di("cmask", [128, TC])
    dr["bmask"] = di("bmask", [128, 128])
    dr["ones"] = di("ones", [128, 128])
    dr["ident"] = di("ident", [128, 128])
    dr["lbflag"] = lbflag
    if n_part:
        dr["xout"] = do("xout", [D, S])
    dr["yp"] = do("yp", [D, S])
    dr["mixT"] = nc.dram_tensor("mixT", [2048, S], BF16, kind="Internal").ap()
    with ExitStack() as st:
        C = alloc_common(nc, st, S, "rec")
        C.eps_norm = st.enter_context(nc.sbuf_tensor("s_eps_norm", [128, 1], F32))
        C.one_c = st.enter_context(nc.sbuf_tensor("s_one_c", [128, 1], F32))
        alloc_rec(nc, st, C)
        P = Prog(nc)
        emit_consts(P, C, dr)
        emit_prologue(P, C, dr, n_part, bool(n_part))
        emit_rec_layer(P, C, dr)
        n = P.emit()
    return nc, n


def pvec(v):
    v = np.asarray(v, np.float32)
    return np.ascontiguousarray(v.reshape(-1, 128).T)


def rec_layout(inp, j, r):
    w_in = inp["rec_w_in"][j]
    w_out = inp["rec_w_out"][j]
    rg_heads = [4 * r + a for a in range(4)]
    hg_heads = [8 * r + e for e in range(8)]
    cols = []
    ia = ie = 0
    for gk in REC_GROUPS:
        if gk == "rg":
            hh = rg_heads[ia]; ia += 1
            cols.append(np.arange(256 * hh, 256 * hh + 256))
            cols.append(np.arange(2048 + 256 * hh, 2048 + 256 * hh + 256))
        else:
            e = hg_heads[ie]; ie += 1
            cols.append(np.arange(4096 + 128 * e, 4096 + 128 * e + 128))
            cols.append(np.arange(10240 + 128 * e, 10240 + 128 * e + 128))
            cols.append(np.arange(6144 + 128 * e, 6144 + 128 * e + 128))
            cols.append(np.arange(8192 + 128 * e, 8192 + 128 * e + 128))
    cols = np.concatenate(cols)
    rows = []
    for hh in rg_heads:
        rows.append(np.arange(256 * hh, 256 * hh + 256))
    for e in hg_heads:
        rows.append(np.arange(2048 + 128 * e, 2048 + 128 * e + 128))
    rows = np.concatenate(rows)
    ch = np.concatenate([np.arange(256 * hh, 256 * hh + 256) for hh in rg_heads])
    convw = np.ascontiguousarray(inp["rg_conv_w"][j][:, ch].T.reshape(8, 128, 4).transpose(1, 0, 2))
    vec = np.stack([inp["rg_conv_b"][j][ch], inp["rg_b_gate_a"][j][ch], inp["rg_b_gate_x"][j][ch],
                    inp["rg_lambda"][j][ch]], axis=-1)
    vec = np.ascontiguousarray(vec.reshape(8, 128, 4).transpose(1, 0, 2))
    hch = np.concatenate([np.arange(128 * e, 128 * e + 128) for e in hg_heads])
    lbr = np.stack([inp["hg_lb"][0][hch], inp["hg_lb"][j][hch]], axis=-1)
    lbr = np.ascontiguousarray(lbr.reshape(8, 128, 2).transpose(1, 0, 2))
    wg = np.stack([inp["rg_w_gate_a"][j][rg_heads], inp["rg_w_gate_x"][j][rg_heads]], axis=1)
    wg = wg.reshape(4, 2, 2, 128, 256).transpose(0, 3, 1, 2, 4)
    return dict(
        w_in=np.ascontiguousarray(w_in[:, cols]),
        w_out=np.ascontiguousarray(w_out[rows, :]),
        gnorm=pvec(inp["rec_norm"][j]),
        convw=convw.astype(np.float32), recvec=vec.astype(np.float32), hglb=lbr.astype(np.float32),
        onorm=np.ascontiguousarray(inp["hg_out_norm"][j].reshape(128, 1).astype(np.float32)),
        wgate=np.ascontiguousarray(wg.astype(np.float32)),
    )


def alloc_att(nc, st, C, mx=None):
    A = Ctx()
    S = C.S
    sb = lambda name, shape, dt: st.enter_context(nc.sbuf_tensor("s_" + name, shape, dt))
    if mx is None:
        A.KT = sb("KT", [128, 2, S], BF16)
        A.Vt = sb("Vt", [128, S // 128, 264], BF16)
    else:
        A.KT = mx[:, 0:S].bitcast(BF16).rearrange("p (a s) -> p a s", a=2)
        nv = (S // 128) * 132
        A.Vt = mx[:, S:S + nv].bitcast(BF16).rearrange("p (t c) -> p t c", c=264)
    A.cosT = sb("cosT", [128, S], F32)
    A.sinT = sb("sinT", [128, S], F32)
    A.pmat = sb("pmat", [128, 128], F32)
    A.ones_f = sb("ones_f", [128, 128], F32)
    A.qkg = sb("qkg", [128, 2], F32)
    A.subg = sb("subg", [128, 256], F32)
    A.lp = sb("lp", [128, 4], F32)
    A.pr = sb("pr", [128, 2], F32)
    A.ex = sb("ex", [128, 2], F32)
    A.neglam = sb("neglam", [128, 1], F32)
    A.tri2f = sb("tri2f", [128, 256], F32)
    A.tri2 = sb("tri2", [128, 256], BF16)
    A.pt = [sb(f"pt{i}", [128, 256], BF16) for i in range(2)]
    A.o = sb("o", [128, 256], F32)
    A.junk = sb("junk", [128, 256], F32)
    A.on = sb("on", [128, 256], BF16)
    A.sm = sb("sm", [128, 8], F32)
    A.eps_sub = sb("eps_sub", [128, 1], F32)
    C.A = A


def emit_att_layer(P, C, dr, lam_init, do_outproj=True):
    S, NT = C.S, C.NT
    T, B, ps, A = C.T, C.B, C.ps, C.A
    SCALE = 128 ** -0.5
    P.dma(A.cosT[:], dr["cosT"], slot="a0", reads=[], writes=["cosT"])
    P.dma(A.sinT[:], dr["sinT"], slot="a1", reads=[], writes=["sinT"])
    P.dma(A.pmat[:], dr["pmat"], slot="a2", reads=[], writes=["pmat"])
    P.dma(A.ones_f[:], dr["ones"], slot="a3", reads=[], writes=["ones_f"])
    P.dma(A.qkg[:], dr["qkg"], slot="a4", reads=[], writes=["qkg"])
    P.dma(A.subg[:], dr["subg"], slot="a5", reads=[], writes=["subg"])
    P.dma(A.lp[:], dr["lp"], slot="a6", reads=[], writes=["lp"])
    P.dma(A.tri2f[:], dr["tri2"], slot="a7", reads=[], writes=["tri2f"])
    P.cp(A.tri2[:], A.tri2f[:], reads=["tri2f"], writes=["tri2"], eng="dve")
    P.memset(A.eps_sub[:], SUBLN_EPS, writes=["eps_sub"], eng="dve")
    P.add("dve", lambda e: e.tensor_scalar_mul(A.subg[:], A.subg[:], 1.0 - lam_init), ["subg"], ["subg"])
    P.tt(A.pr[:, 0:1], A.lp[:, 0:1], A.lp[:, 1:2], ALU.mult, reads=["lp"], writes=["pr"])
    P.tt(A.pr[:, 1:2], A.lp[:, 2:3], A.lp[:, 3:4], ALU.mult, reads=["lp", "pr"], writes=["pr"])
    P.mm(ps[2][:, 0:2], A.ones_f[:], A.pr[:], start=True, stop=True, reads=["ones_f", "pr"], writes=[("ps", 2)])
    P.act(A.ex[:], ps[2][:, 0:2], AF.Exp, reads=[("ps", 2)], writes=["ex"])
    P.tt(A.neglam[:], A.ex[:, 1:2], A.ex[:, 0:1], ALU.subtract, reads=["ex"], writes=["neglam"])
    P.add("dve", lambda e: e.tensor_scalar_add(A.neglam[:], A.neglam[:], -lam_init), ["neglam"], ["neglam"])
    P.memset(A.Vt[:, :, 256:257], 1.0, writes=["Vt_ones"], eng="dve")

    ws = WStream(P, C)
    ws.load_group(dr["w_in"], 0, 4, 0)
    ngroups = 8
    bkc = [0]

    def nb():
        b = bkc[0] % 2
        bkc[0] += 1
        return b

    def norm_rope(bank, gcol, out_ap, out_keys, tc):
        cols = slice(tc * TC, (tc + 1) * TC)
        raw, rstd, t1, t2, sq = T[6], T[7], T[8], T[9], B[2]
        P.act(raw[:], ps[bank][:], AF.Copy, reads=[("ps", bank)], writes=[("T", 6)])
        P.act(sq[:], ps[bank][:], AF.Square, reads=[("ps", bank)], writes=[("B", 2)])
        P.mm(ps[2][:], C.ones_b[:], sq[:], start=True, stop=True, reads=["ones_b", ("B", 2)], writes=[("ps", 2)])
        rstd_act(P, rstd[:], ("T", 7), ps[2][:], [("ps", 2)], C.eps_norm[:], "eps_norm", 1.0 / 128)
        P.stt(raw[:], raw[:], A.qkg[:, gcol:gcol + 1], rstd[:], ALU.mult, ALU.mult, reads=[("T", 6), "qkg", ("T", 7)], writes=[("T", 6)])
        P.mm(ps[3][:], A.pmat[:], raw[:], start=True, stop=True, reads=["pmat", ("T", 6)], writes=[("ps", 3)])
        P.tt(t1[:], raw[:], A.cosT[:, cols], ALU.mult, reads=[("T", 6), "cosT"], writes=[("T", 8)])
        P.tt(t2[:], ps[3][:], A.sinT[:, cols], ALU.mult, reads=[("ps", 3), "sinT"], writes=[("T", 9)])
        P.tt(out_ap, t1[:], t2[:], ALU.add, reads=[("T", 8), ("T", 9)], writes=out_keys)

    for gi in range(ngroups):
        gslot = gi % 2
        a = gi // 2
        if gi + 1 < ngroups:
            ws.load_group(dr["w_in"], (gi + 1) * 512, 4, (gi + 1) % 2)
        if gi % 2 == 0:
            for tc in range(NT):
                cols = slice(tc * TC, (tc + 1) * TC)
                for i in range(2):
                    bank = nb()
                    inproj_fm(P, C, gslot, i, tc, bank)
                    norm_rope(bank, 1, A.KT[:, i, cols], [("KT", i, tc)], tc)
                for half in range(2):
                    bank = nb()
                    tile0 = tc * 4
                    v_tokmajor(P, C, gslot, 2 + half, tc, bank, 10 + half, A.Vt[:, tile0:tile0 + 4, half * 128:(half + 1) * 128],
                               [("Vt", tile0 + q_, half) for q_ in range(4)])
        else:
            for tc in range(NT):
                cols = slice(tc * TC, (tc + 1) * TC)
                Qt = [B[4], B[5]]
                sgt = [B[6], B[7]]
                for i in range(2):
                    bank = nb()
                    inproj_fm(P, C, gslot, i, tc, bank)
                    norm_rope(bank, 0, Qt[i][:], [("B", 4 + i)], tc)
                for i in range(2):
                    bank = nb()
                    inproj_fm(P, C, gslot, 2 + i, tc, bank)
                    silu2(P, C, sgt[i][:], ("B", 6 + i), bank)
                for qt in range(4):
                    j = tc * 4 + qt
                    qs = slice(qt * 128, (qt + 1) * 128)

                    def emit_qk(i):
                        sb_ = 4 if i % 2 == 0 else 7
                        ks = slice(i * 128, (i + 1) * 128)
                        for m in range(2):
                            P.mm(ps[sb_][:, m * 128:(m + 1) * 128], A.KT[:, m, ks], Qt[m][:, qs], start=True, stop=True,
                                 reads=[("KT", m, i // 4), ("B", 4 + m)], writes=[("ps", sb_)])
                        pt = A.pt[i % 2]
                        P.act(pt[:], ps[sb_][:, 0:256], AF.Exp, reads=[("ps", sb_)], writes=[("pt", i % 2)], scale=SCALE)
                        if i == j:
                            P.tt(pt[:], pt[:], A.tri2[:], ALU.mult, reads=[("pt", i % 2), "tri2"], writes=[("pt", i % 2)])

                    def emit_pv(i):
                        pt = A.pt[i % 2]
                        for m in range(2):
                            P.mm(ps[5 + m][:, 0:257], pt[:, m * 128:(m + 1) * 128], A.Vt[:, i, 0:257],
                                 start=(i == 0), stop=(i == j),
                                 reads=[("pt", i % 2), ("Vt", i, 0), ("Vt", i, 1), "Vt_ones"], writes=[("ps", 5 + m)])

                    emit_qk(0)
                    for i in range(j + 1):
                        if i + 1 <= j:
                            emit_qk(i + 1)
                        emit_pv(i)
                    sm = A.sm
                    P.recip(sm[:, 0:1], ps[5][:, 256:257], reads=[("ps", 5)], writes=["sm"])
                    P.recip(sm[:, 1:2], ps[6][:, 256:257], reads=[("ps", 6), "sm"], writes=["sm"])
                    P.tt(sm[:, 2:3], sm[:, 1:2], A.neglam[:], ALU.mult, reads=["sm", "neglam"], writes=["sm"])
                    P.add("dve", lambda e, sm=sm: e.tensor_scalar_mul(A.o[:], ps[5][:, 0:256], sm[:, 0:1]), [("ps", 5), "sm"], ["o"])
                    P.stt(A.o[:], ps[6][:, 0:256], sm[:, 2:3], A.o[:], ALU.mult, ALU.add, reads=[("ps", 6), "sm", "o"], writes=["o"])
                    P.add("act", lambda e, sm=sm: e.activation(A.junk[:], A.o[:], AF.Square, accum_out=sm[:, 3:4]), ["o"], ["junk", "sm3"])
                    rstd_act(P, sm[:, 5:6], "sm5", sm[:, 3:4], ["sm3"], A.eps_sub[:], "eps_sub", 1.0 / 256)
                    P.stt(A.on[:], A.o[:], sm[:, 5:6], A.subg[:], ALU.mult, ALU.mult, reads=["o", "sm5", "subg"], writes=["on"])
                    trv = ps[3][:].bitcast(BF16)
                    for m in range(2):
                        P.tr(trv[:, m * 128:(m + 1) * 128], A.on[:, m * 128:(m + 1) * 128], C.ident_b[:],
                             reads=["on", "ident_b"], writes=[("ps", 3)])
                    for m in range(2):
                        P.stt(B[8 + m][:, qs], trv[:, m * 128:(m + 1) * 128], 0.5, sgt[m][:, qs], ALU.mult, ALU.mult,
                              reads=[("ps", 3), ("B", 6 + m)], writes=[("B", 8 + m)])
                for m in range(2):
                    fc = 2 * a + m
                    P.dma(dr["mixT"][fc * 128:(fc + 1) * 128, cols], B[8 + m][:], slot=("mo", m),
                          reads=[("B", 8 + m)], writes=[("mixT", dr.get("mix_tag", 0), fc, tc)])
    if do_outproj:
        emit_outproj(P, C, ws, dr, 8)


def build_att_program(S, n_part, layer):
    nc = bass.Bass("TRN2", target_bir_lowering=False)
    di = lambda name, shape, dt=F32: nc.dram_tensor(name, shape, dt, kind="ExternalInput").ap()
    do = lambda name, shape, dt=F32: nc.dram_tensor(name, shape, dt, kind="ExternalOutput").ap()
    dr = {}
    dr["xT"] = di("xT", [D, S])
    dr["parts"] = [di(f"part{i}", [D, S]) for i in range(n_part)]
    dr["gnorm"] = di("gnorm", [128, KC])
    dr["w_in"] = di("w_in", [D, 4096])
    dr["w_out"] = di("w_out", [1024, D])
    dr["cosT"] = di("cosT", [128, S])
    dr["sinT"] = di("sinT", [128, S])
    dr["pmat"] = di("pmat", [128, 128])
    dr["qkg"] = di("qkg", [128, 2])
    dr["subg"] = di("subg", [128, 256])
    dr["lp"] = di("lp", [128, 4])
    dr["tri2"] = di("tri2", [128, 256])
    dr["ones"] = di("ones", [128, 128])
    dr["ident"] = di("ident", [128, 128])
    if n_part:
        dr["xout"] = do("xout", [D, S])
    dr["yp"] = do("yp", [D, S])
    dr["mixT"] = nc.dram_tensor("mixT", [1024, S], BF16, kind="Internal").ap()
    lam_init = 0.8 - 0.6 * math.exp(-0.3 * layer)
    with ExitStack() as st:
        C = alloc_common(nc, st, S, "att")
        C.eps_norm = st.enter_context(nc.sbuf_tensor("s_eps_norm", [128, 1], F32))
        C.one_c = st.enter_context(nc.sbuf_tensor("s_one_c", [128, 1], F32))
        alloc_att(nc, st, C)
        P = Prog(nc)
        emit_consts(P, C, dr)
        emit_prologue(P, C, dr, n_part, bool(n_part))
        emit_att_layer(P, C, dr, lam_init)
        n = P.emit()
    return nc, n


def att_layout(inp, j, r):
    w_in = inp["att_w_in"][j]
    w_out = inp["att_w_out"][j]
    heads = [4 * r + a for a in range(4)]
    cols = []
    for h in heads:
        cols.append(np.arange(2048 + 256 * h, 2048 + 256 * h + 256))
        cols.append(np.arange(4096 + 256 * h, 4096 + 256 * h + 256))
        cols.append(np.arange(256 * h, 256 * h + 256))
        cols.append(np.arange(6144 + 256 * h, 6144 + 256 * h + 256))
    cols = np.concatenate(cols)
    rows = np.concatenate([np.arange(256 * h, 256 * h + 256) for h in heads])
    qkg = np.stack([inp["att_q_norm"][j], inp["att_k_norm"][j]], axis=-1).astype(np.float32)
    subg = np.ascontiguousarray(np.broadcast_to(inp["att_sub_norm"][j][None, :], (128, 256)).astype(np.float32))
    lp = np.ascontiguousarray(inp["att_lambda"][j].T.astype(np.float32))
    return dict(
        w_in=np.ascontiguousarray(w_in[:, cols]),
        w_out=np.ascontiguousarray(w_out[rows, :]),
        gnorm=pvec(inp["att_norm"][j]),
        qkg=np.ascontiguousarray(qkg), subg=subg, lp=lp,
    )


def build_final_program(S):
    nc = bass.Bass("TRN2", target_bir_lowering=False)
    di = lambda name, shape, dt=F32: nc.dram_tensor(name, shape, dt, kind="ExternalInput").ap()
    x = di("xT", [1024, S])
    p0 = di("part0", [1024, S])
    p1 = di("part1", [1024, S])
    out = nc.dram_tensor("xout", [1024, S], F32, kind="ExternalOutput").ap()
    with ExitStack() as st:
        tl = [[st.enter_context(nc.sbuf_tensor(f"s_f{k}{i}", [128, S], F32)) for i in range(2)] for k in range(3)]
        P = Prog(nc)
        for c in range(8):
            sl = c % 2
            rows = slice(c * 128, (c + 1) * 128)
            xa, pa, pb = tl[0][sl], tl[1][sl], tl[2][sl]
            P.dma(xa[:], x[rows, :], slot=("fx", sl), reads=[], writes=[("fx", sl)])
            P.dma(pa[:], p0[rows, :], slot=("fa", sl), reads=[], writes=[("fa", sl)])
            P.dma(pb[:], p1[rows, :], slot=("fb", sl), reads=[], writes=[("fb", sl)])
            P.tt(xa[:], xa[:], pa[:], ALU.add, reads=[("fx", sl), ("fa", sl)], writes=[("fx", sl)])
            P.tt(xa[:], xa[:], pb[:], ALU.add, reads=[("fx", sl), ("fb", sl)], writes=[("fx", sl)])
            P.dma(out[rows, :], xa[:], slot=("fo", sl), reads=[("fx", sl)], writes=[("out", c)])
        P.emit()
    return nc


REC_KEYS = ("w_in", "w_out", "gnorm", "convw", "recvec", "hglb", "onorm", "wgate")
ATT_KEYS = ("w_in", "w_out", "gnorm", "qkg", "subg", "lp")
CONST_KEYS = ("ones", "ident", "cmask", "bmask", "cosT", "sinT", "pmat", "tri2")
ACTIVE_CORES = (0, 1, 2, 3)
NCORES = 4


def build_fused_program(S, nlayers=4, nseq=1):
    nc = bass.Bass("TRN2", target_bir_lowering=False)
    di = lambda name, shape, dt=F32: nc.dram_tensor(name, shape, dt, kind="ExternalInput").ap()
    x_ins = [di("xT" if q == 0 else f"xT{q}", [D, S]) for q in range(nseq)]
    outs = [nc.dram_tensor("outT" if q == 0 else f"outT{q}", [D, S], F32, kind="ExternalOutput").ap() for q in range(nseq)]
    yp = [nc.dram_tensor(f"yp{i}", [D, S], F32, kind="Internal").ap() for i in range(2)]
    xb = [nc.dram_tensor(f"xbuf{i}", [D, S], F32, kind="Internal").ap() for i in range(2)]
    mixTs = [nc.dram_tensor(f"mixT{i}", [2048, S], BF16, kind="Internal").ap() for i in range(2)]
    cshape = dict(ones=[128, 128], ident=[128, 128], cmask=[128, TC], bmask=[128, 128], cosT=[128, S], sinT=[128, S],
                  pmat=[128, 128], tri2=[128, 256])
    cd = {k: di(k, cshape[k]) for k in CONST_KEYS}
    rshape = dict(w_in=[D, 6144], w_out=[2048, D], gnorm=[128, KC], convw=[128, 8, 4], recvec=[128, 8, 4], hglb=[128, 8, 2],
                  onorm=[128, 1], wgate=[4, 128, 2, 2, 256])
    ashape = dict(w_in=[D, 4096], w_out=[1024, D], gnorm=[128, KC], qkg=[128, 2], subg=[128, 256], lp=[128, 4])
    drs = []
    for L in range(nlayers):
        sh = rshape if L % 2 == 0 else ashape
        halves = []
        for hf in range(2):
            dr = {k: di(f"{k}{L}h{hf}", sh[k]) for k in sh if not (k == "gnorm" and hf == 1)}
            if hf == 1:
                dr["gnorm"] = halves[0]["gnorm"]
            dr.update(cd)
            dr["mixT"] = mixTs[hf] if L % 2 == 0 else mixTs[hf][0:1024, :]
            dr["mix_tag"] = hf
            dr["yp"] = yp[hf]
            dr["yp_tag"] = hf
            dr["fc_off"] = 8 * hf if L % 2 == 1 else 0
            dr["lbflag"] = 0.0 if L == 0 else 1.0
            halves.append(dr)
        drs.append(halves)
    with ExitStack() as st:
        C = alloc_common(nc, st, S, "all")
        C.eps_norm = st.enter_context(nc.sbuf_tensor("s_eps_norm", [128, 1], F32))
        C.one_c = st.enter_context(nc.sbuf_tensor("s_one_c", [128, 1], F32))
        mx = st.enter_context(nc.sbuf_tensor("s_mx", [128, 5644], F32))
        alloc_rec(nc, st, C, mx)
        alloc_att(nc, st, C, mx)
        P = Prog(nc)
        emit_consts(P, C, cd)
        for q in range(nseq):
          x_in, out = x_ins[q], outs[q]
          for L in range(nlayers):
              pr = dict(gnorm=drs[L][0]["gnorm"])
              if L == 0:
                  pr["xT"] = x_in
                  pr["parts"] = []
              else:
                  pr["xT"] = x_in if L == 1 else xb[(L - 2) % 2]
                  pr["x_dep"] = L >= 2
                  pr["parts"] = yp
                  pr["xout"] = xb[(L - 1) % 2]
              emit_prologue(P, C, pr, len(pr["parts"]), L > 0)
              P.barrier()
              for hf in range(2):
                  if L % 2 == 0:
                      emit_rec_layer(P, C, drs[L][hf], do_outproj=False)
                  else:
                      emit_att_layer(P, C, drs[L][hf], 0.8 - 0.6 * math.exp(-0.3 * L), do_outproj=False)
              for hf in range(2):
                  emit_outproj(P, C, WStream(P, C), drs[L][hf], 16 if L % 2 == 0 else 8)
          pr = dict(xT=(x_in if nlayers == 1 else xb[(nlayers - 2) % 2]), x_dep=nlayers >= 2, parts=yp,
                    xout=out, gnorm=drs[0][0]["gnorm"])
          emit_prologue(P, C, pr, 2, True, final_only=True)
          P.barrier()
        n = P.emit()
    return nc, n


def fused_maps(inp, S=SEQ, nlayers=4, ncores=2, nseq=2):
    cs = host_consts(S)
    x = np.asarray(inp["x"], np.float32)
    wmap = {}
    for L in range(nlayers):
        j = L // 2
        for hf in range(2):
            lay = rec_layout(inp, j, hf) if L % 2 == 0 else att_layout(inp, j, hf)
            for k, v in lay.items():
                if k == "gnorm" and hf == 1:
                    continue
                wmap[f"{k}{L}h{hf}"] = v
    maps = []
    for c in range(ncores):
        m = {}
        for q in range(nseq):
            b = c * nseq + q
            m["xT" if q == 0 else f"xT{q}"] = np.ascontiguousarray(x[b][:S].T)
        m.update(wmap)
        for k in CONST_KEYS:
            m[k] = cs[k]
        maps.append(m)
    return maps


_PROGS = {}
NCORES = 2
NSEQ = 2


def kernel(**inp):
    inp = {k: np.asarray(v) for k, v in inp.items()}
    if "fused" not in _PROGS:
        _PROGS["fused"] = build_fused_program(SEQ, 4, NSEQ)[0]
    nc = _PROGS["fused"]
    maps = fused_maps(inp, SEQ, 4, NCORES, NSEQ)
    res = run_bass_kernel_spmd(nc, maps, core_ids=list(range(NCORES)))
    out = np.empty((BATCH, SEQ, D), np.float32)
    for c in range(NCORES):
        for q in range(NSEQ):
            out[c * NSEQ + q] = np.asarray(res.results[c]["outT" if q == 0 else f"outT{q}"]).T
    return out
```

```python
import math
from contextlib import ExitStack

import numpy as np
import concourse.bass as bass
import concourse.mybir as mybir
from concourse.bass_utils import run_bass_kernel_spmd

F32 = mybir.dt.float32
BF16 = mybir.dt.bfloat16
AF = mybir.ActivationFunctionType
ALU = mybir.AluOpType

D = 2048
KC = 16
SEQ = 2048
BATCH = 4
TC = 512
NORM_EPS = 1e-6
SUBLN_EPS = 1e-5
SAME_ENGINE_SYNC = True


def _fs(ap):
    try:
        return int(ap.free_size())
    except Exception:
        return 512


RESCHEDULE = True


class Prog:
    ENGS = ("pe", "act", "dve", "pool", "sp")

    def __init__(self, nc):
        self.nc = nc
        self.ops = []

    def add(self, eng, fn, reads=(), writes=(), dma=None, cost=0.5):
        self.ops.append(dict(kind="op", eng=eng, fn=fn, reads=tuple(reads), writes=tuple(writes), dma=dma, cost=cost))

    def barrier(self):
        self.ops.append(dict(kind="barrier"))

    def mm(self, out, lhsT, rhs, start, stop, reads, writes):
        self.add("pe", lambda e: e.matmul(out, lhsT, rhs, start=start, stop=stop), reads, writes,
                 cost=0.035 + _fs(rhs) / 2400.0)

    def tr(self, out, in_, ident, reads, writes):
        self.add("pe", lambda e: e.transpose(out, in_, ident), reads, writes, cost=0.1)

    def act(self, out, in_, func, reads, writes, bias=0.0, scale=1.0, eng="act"):
        self.add(eng, lambda e: e.activation(out, in_, func, bias=bias, scale=scale), reads, writes,
                 cost=0.2 + _fs(out) * 0.0008)
        self.ops[-1]["tset"] = "tanh" if func == AF.Tanh else ("ln" if func == AF.Ln else None)

    def ts(self, out, in0, s1, s2, op0, op1, reads, writes, eng="dve"):
        self.add(eng, lambda e: e.tensor_scalar(out, in0, s1, s2, op0, op1), reads, writes, cost=0.15 + _fs(out) * 0.001)

    def stt(self, out, in0, scalar, in1, op0, op1, reads, writes, eng="dve"):
        self.add(eng, lambda e: e.scalar_tensor_tensor(out, in0, scalar, in1, op0, op1), reads, writes,
                 cost=0.2 + _fs(out) * 0.0011)

    def tt(self, out, in0, in1, op, reads, writes, eng="dve"):
        c = 0.15 + _fs(out) * (0.001 if eng == "dve" else 0.0022)
        self.add(eng, lambda e: e.tensor_tensor(out, in0, in1, op), reads, writes, cost=c)

    def cp(self, out, in_, reads, writes, eng="dve"):
        c = 0.15 + _fs(out) * 0.001
        self.add(eng, lambda e: e.tensor_copy(out, in_), reads, writes, cost=c)

    def memset(self, ap, val, writes, eng="dve"):
        self.add(eng, lambda e: e.memset(ap, val), (), writes, cost=0.2)

    def recip(self, out, in_, reads, writes):
        self.add("dve", lambda e: e.reciprocal(out, in_), reads, writes, cost=0.15 + _fs(out) * 0.001)

    def scan(self, out, d0, d1, init, reads, writes):
        self.add("dve", lambda e: e.tensor_tensor_scan(out, d0, d1, init, ALU.mult, ALU.add), reads, writes,
                 cost=0.2 + _fs(out) * 0.002)

    def dma(self, out, in_, slot, reads, writes, eng="sp"):
        try:
            nbytes = int(out.nbytes())
        except Exception:
            nbytes = 1 << 18
        self.add(eng, lambda e: e.dma_start(out, in_), reads, writes, dma=slot, cost=2.0 + nbytes / 150e3)

    def emit(self):
        nc = self.nc
        real = []
        seg_of = []
        seg = 0
        last_w = {}
        readers = {}
        for o in self.ops:
            if o["kind"] == "barrier":
                seg += 1
                continue
            o["idx"] = len(real)
            real.append(o)
            o["seg"] = seg
            deps = set()
            for r in o["reads"]:
                if r in last_w:
                    deps.add(last_w[r])
            for w in o["writes"]:
                if w in last_w:
                    deps.add(last_w[w])
                deps.update(readers.get(w, ()))
            deps.discard(o["idx"])
            for w in o["writes"]:
                last_w[w] = o["idx"]
                readers[w] = []
            for r in o["reads"]:
                if r not in o["writes"]:
                    readers.setdefault(r, []).append(o["idx"])
            o["skey"] = ("dma", o["dma"]) if o["dma"] is not None else ("eng", o["eng"])
            o["deps"] = [d for d in deps if real[d]["seg"] == seg]
            o["signal"] = False
        nseg = seg + 1
        import heapq
        order = {e: [] for e in self.ENGS}
        seg_ops = [[] for _ in range(nseg)]
        for o in real:
            seg_ops[o["seg"]].append(o["idx"])
        fin = [0.0] * len(real)
        tnow = 0.0
        LAT = 0.15
        for sg in range(nseg):
            ids = seg_ops[sg]
            if not RESCHEDULE:
                for i in ids:
                    order[real[i]["eng"]].append(i)
                continue
            nd = {i: len(real[i]["deps"]) for i in ids}
            users = {i: [] for i in ids}
            for i in ids:
                for d in real[i]["deps"]:
                    users[d].append(i)
            ready = {e: [] for e in self.ENGS}
            rt = {}
            for i in ids:
                if nd[i] == 0:
                    rt[i] = tnow
                    heapq.heappush(ready[real[i]["eng"]], i)
            free = {e: tnow for e in self.ENGS}
            cur_set = [None]
            sp_list = [i for i in ids if real[i]["eng"] == "sp"]
            sp_pos = 0
            remaining = len(ids)
            while remaining:
                best = None
                for e in self.ENGS:
                    h = ready[e]
                    if not h:
                        continue
                    if e == "sp":
                        if sp_pos < len(sp_list) and nd[sp_list[sp_pos]] == 0:
                            i = sp_list[sp_pos]
                            st_ = max(free[e], rt[i])
                            cand = (st_, i, e)
                        else:
                            continue
                    else:
                        cands = heapq.nsmallest(12 if e == "act" else 6, h)
                        cand = None
                        for i in cands:
                            st_ = max(free[e], rt[i])
                            if e == "act":
                                ts_ = real[i].get("tset")
                                if ts_ is not None and cur_set[0] is not None and ts_ != cur_set[0]:
                                    st_ += 1.3
                            if cand is None or st_ < cand[0] - 1e-9:
                                cand = (st_, i, e)
                    if cand is not None and (best is None or cand[0] < best[0] - 1e-9 or
                                             (abs(cand[0] - best[0]) <= 1e-9 and cand[1] < best[1])):
                        best = cand
                if best is None:
                    raise RuntimeError("scheduler deadlock")
                st_, i, e = best
                if e == "sp":
                    sp_pos += 1
                    ready[e].remove(i)
                    heapq.heapify(ready[e])
                else:
                    ready[e].remove(i)
                    heapq.heapify(ready[e])
                o = real[i]
                if e == "act" and o.get("tset") is not None:
                    cur_set[0] = o["tset"]
                if o["dma"] is not None:
                    free[e] = st_ + 0.1
                    fin[i] = st_ + o["cost"]
                else:
                    free[e] = st_ + o["cost"]
                    fin[i] = free[e]
                order[e].append(i)
                remaining -= 1
                for u in users[i]:
                    nd[u] -= 1
                    rt[u] = max(rt.get(u, tnow), fin[i] + LAT)
                    if nd[u] == 0:
                        heapq.heappush(ready[real[u]["eng"]], u)
            tnow = max([tnow] + [fin[i] for i in ids]) + 1.0
        self.est_us = tnow
        pos = {}
        for e in self.ENGS:
            for p, i in enumerate(order[e]):
                pos[i] = p
        dma_seq = {}
        for i in order["sp"] + [i for e in self.ENGS if e != "sp" for i in order[e]]:
            pass
        cnt = {}
        for e in self.ENGS:
            for i in order[e]:
                o = real[i]
                if o["dma"] is not None:
                    cnt[o["skey"]] = cnt.get(o["skey"], 0) + 16
                    o["sigval"] = cnt[o["skey"]]
                    o["dpos"] = cnt[o["skey"]]
        last_in_seg = [dict() for _ in range(nseg)]
        for e in self.ENGS:
            for i in order[e]:
                o = real[i]
                last_in_seg[o["seg"]][o["skey"]] = i
        first_seen = set()
        for e in self.ENGS:
            for i in order[e]:
                o = real[i]
                if o["seg"] > 0 and (e, o["seg"]) not in first_seen:
                    first_seen.add((e, o["seg"]))
                    for sgp in range(o["seg"]):
                        o["deps"] = list(o["deps"]) + list(last_in_seg[sgp].values())
        for o in real:
            keep = {}
            for d in o["deps"]:
                od = real[d]
                if od["dma"] is None and od["eng"] == o["eng"]:
                    if o["eng"] == "pe" or not SAME_ENGINE_SYNC:
                        continue
                sk = od["skey"]
                key = od["dpos"] if od["dma"] is not None else pos[d]
                if sk not in keep or keep[sk][0] < key:
                    keep[sk] = (key, d)
            o["deps"] = [v[1] for v in keep.values()]
            for d in o["deps"]:
                real[d]["signal"] = True
        for e in self.ENGS:
            c = 0
            for i in order[e]:
                o = real[i]
                if o["dma"] is None and o["signal"]:
                    c += 1
                    o["sigval"] = c
        skeys = sorted({o["skey"] for o in real}, key=str)
        with ExitStack() as st:
            sems = {}
            for i, sk in enumerate(skeys):
                sems[sk] = st.enter_context(nc.semaphore(f"sem{i}"))
            dma_final = {sk: cnt[sk] for sk in skeys if sk[0] == "dma"}
            self.sem_names = {f"sem{i}": sk for i, sk in enumerate(skeys)}

            def run(ename, eng):
                waited = {}
                for i in order[ename]:
                    o = real[i]
                    for d in o["deps"]:
                        od = real[d]
                        sk, val = od["skey"], od["sigval"]
                        if waited.get(sk, 0) < val:
                            eng.wait_ge(sems[sk], val)
                            waited[sk] = val
                    ins = o["fn"](eng)
                    if o["dma"] is not None:
                        ins.then_inc(sems[o["skey"]], 16)
                    elif o["signal"]:
                        ins.then_inc(sems[o["skey"]], 1)
                if ename == "sp":
                    for sk, val in dma_final.items():
                        if waited.get(sk, 0) < val:
                            eng.wait_ge(sems[sk], val)

            with nc.Block() as block:
                @block.tensor
                def _(e):
                    run("pe", e)

                @block.scalar
                def _(e):
                    run("act", e)

                @block.vector
                def _(e):
                    run("dve", e)

                @block.gpsimd
                def _(e):
                    run("pool", e)

                @block.sync
                def _(e):
                    run("sp", e)
        return len(real)


def host_consts(S):
    c = {}
    c["ones"] = np.ones((128, 128), np.float32)
    c["ident"] = np.eye(128, dtype=np.float32)
    s = np.arange(128)[:, None]
    t = np.arange(128)[None, :]
    c["bmask"] = ((s // 64 == t // 64) & (s <= t)).astype(np.float32)
    tri = (s <= t).astype(np.float32)
    c["tri2"] = np.concatenate([tri, tri], axis=1)
    cm = np.ones((128, TC), np.float32)
    cm[:, ::64] = 0.0
    c["cmask"] = cm
    half = 16
    inv_freq = (500000.0 ** (-np.arange(0, 32, 2, dtype=np.float32) / 32)).astype(np.float32)
    pos = np.arange(S, dtype=np.float32)
    ang = pos[None, :] * inv_freq[:, None]
    cosT = np.ones((128, S), np.float32)
    sinT = np.zeros((128, S), np.float32)
    cosT[0:16] = np.cos(ang)
    cosT[16:32] = np.cos(ang)
    sinT[0:16] = np.sin(ang)
    sinT[16:32] = np.sin(ang)
    c["cosT"] = cosT
    c["sinT"] = sinT
    pm = np.zeros((128, 128), np.float32)
    for m in range(16):
        pm[m + 16, m] = -1.0
        pm[m, m + 16] = 1.0
    c["pmat"] = pm
    return c


class Ctx:
    pass


def alloc_common(nc, st, S, kind):
    C = Ctx()
    C.S = S
    C.NT = S // TC
    sb = lambda name, shape, dt: st.enter_context(nc.sbuf_tensor("s_" + name, shape, dt))
    C.hT = sb("hT", [128, KC, S], BF16)
    C.wf = [sb(f"wf{i}", [128, KC, 128], F32) for i in range(2)]
    C.wb = [sb(f"wb{i}", [128, KC, 512], BF16) for i in range(2)]
    C.T = [sb(f"T{i}", [128, TC], F32) for i in range(14)]
    C.B = [sb(f"B{i}", [128, TC], BF16) for i in range(12)]
    C.ones_b = sb("ones_b", [128, 128], BF16)
    C.ident_b = sb("ident_b", [128, 128], BF16)
    C.cst_f = sb("cst_f", [128, 128], F32)
    C.gnorm = sb("gnorm", [128, KC], F32)
    C.ps = [st.enter_context(nc.psum_tensor(f"ps{i}", [128, 512], F32)) for i in range(8)]
    return C


def load_const_bf16(P, C, dst, src_dram, name):
    P.dma(C.cst_f[:], src_dram, slot="cst", reads=[("dram", name)], writes=["cst_f"])
    P.cp(dst[:], C.cst_f[:], reads=["cst_f"], writes=[name], eng="dve")


def emit_prologue(P, C, dr, n_part, have_xout, final_only=False, stats_ready=False):
    S, NT = C.S, C.NT
    P.dma(C.gnorm[:], dr["gnorm"], slot="gn", reads=[], writes=["gnorm"])
    stat_banks = [C.ps[4 + i] for i in range(NT)]
    for kc in range(KC if not stats_ready else 0):
        for tc in range(NT):
            sl = (kc * NT + tc) % 4
            xa = C.T[sl]
            xk = ("T", sl)
            cols = slice(tc * TC, (tc + 1) * TC)
            rows = slice(kc * 128, (kc + 1) * 128)
            P.dma(xa[:], dr["xT"][rows, cols], slot=("xa", sl), reads=([("xout", kc, tc)] if dr.get("x_dep") else []), writes=[xk])
            for pi in range(n_part):
                pa = C.T[8 + sl]
                pk = ("T", 8 + sl)
                P.dma(pa[:], dr["parts"][pi][rows, cols], slot=("pa", sl), reads=[("yp", pi, kc, tc)], writes=[pk])
                P.tt(xa[:], xa[:], pa[:], ALU.add, reads=[xk, pk], writes=[xk])
            if have_xout:
                P.dma(dr["xout"][rows, cols], xa[:], slot=("xo", sl), reads=[xk], writes=[("xout", kc, tc), "xout_all"])
            if final_only:
                continue
            xs = C.B[sl]
            P.act(xs[:], xa[:], AF.Square, reads=[xk], writes=[("B", sl)])
            P.mm(stat_banks[tc][:], C.ones_b[:], xs[:], start=(kc == 0), stop=(kc == KC - 1),
                 reads=[("B", sl), "ones_b"], writes=[("ps", 4 + tc)])
    if final_only:
        return
    for tc in range(NT):
        r = C.T[4 + tc]
        rstd_act(P, r[:], ("T", 4 + tc), stat_banks[tc][:], [("ps", 4 + tc)], C.eps_norm[:], "eps_norm", 1.0 / D)
    src = dr["xout"] if have_xout else dr["xT"]
    for kc in range(KC):
        for tc in range(NT):
            sl = (kc * NT + tc) % 4
            xa = C.T[sl]
            xk = ("T", sl)
            cols = slice(tc * TC, (tc + 1) * TC)
            rows = slice(kc * 128, (kc + 1) * 128)
            rd = [("xout", kc, tc)] if (have_xout or stats_ready) else []
            P.dma(xa[:], src[rows, cols], slot=("xa", sl), reads=rd, writes=[xk])
            P.stt(C.hT[:, kc, cols], xa[:], C.gnorm[:, kc:kc + 1], C.T[4 + tc][:], ALU.mult, ALU.mult,
                  reads=[xk, "gnorm", ("T", 4 + tc)], writes=[("hT", kc, tc)])


class WStream:
    def __init__(self, P, C):
        self.P, self.C = P, C
        self.n = 0

    def load_group(self, w_dram, col0, nslab, gslot, kcn=KC, name="w"):
        P, C = self.P, self.C
        ncol = nslab * 128
        for q in range(kcn // 4):
            sl = self.n % 2
            self.n += 1
            src = w_dram[q * 512:(q + 1) * 512, col0:col0 + ncol].rearrange("(k p) c -> p k c", p=128)
            stage = C.wf[sl][:].rearrange("p k c -> p (k c)")[:, 0:4 * ncol].rearrange("p (k c) -> p k c", k=4)
            P.dma(stage, src, slot=("wf", sl), reads=[], writes=[("wf", sl)])
            P.act(C.wb[gslot][:, 4 * q:4 * q + 4, 0:ncol], stage, AF.Copy,
                  reads=[("wf", sl)], writes=[("wb", gslot, q)])


def inproj_fm(P, C, gslot, s, tc, bank):
    cols = slice(tc * TC, (tc + 1) * TC)
    for kc in range(KC):
        P.mm(C.ps[bank][:], C.wb[gslot][:, kc, s * 128:(s + 1) * 128], C.hT[:, kc, cols],
             start=(kc == 0), stop=(kc == KC - 1),
             reads=[("wb", gslot, kc // 4), ("hT", kc, tc)], writes=[("ps", bank)])


def inproj_tm(P, C, gslot, s0, ncol, tc, tt, out_ap, bank):
    t0 = tc * TC + tt * 128
    rd = [("wb", gslot, q) for q in range(4)]
    for kc in range(KC):
        P.mm(out_ap, C.hT[:, kc, t0:t0 + 128], C.wb[gslot][:, kc, s0 * 128:s0 * 128 + ncol],
             start=(kc == 0), stop=(kc == KC - 1),
             reads=rd + [("hT", kc, tc)], writes=[("ps", bank)])


def emit_outproj(P, C, ws, dr, nfc):
    S, NT = C.S, C.NT
    fo = dr.get("fc_off", 0)
    ws.load_group(dr["w_out"], 0, 4, 0, kcn=nfc)
    for tc in range(NT):
        cols = slice(tc * TC, (tc + 1) * TC)
        for fc in range(nfc):
            P.dma(C.hT[:, fo + fc, cols], dr["mixT"][fc * 128:(fc + 1) * 128, cols], slot=("mixld", fo, tc),
                  reads=[("mixT", dr.get("mix_tag", 0), fc, tc)], writes=[("hT", fo + fc, tc), ("mixall", fo, tc)],
                  eng="act")
    for g4 in range(4):
        g = g4 % 2
        if g4 + 1 < 4:
            ws.load_group(dr["w_out"], (g4 + 1) * 512, 4, (g4 + 1) % 2, kcn=nfc)
        for s_ in range(4):
            dc = g4 * 4 + s_
            for tc in range(NT):
                cols = slice(tc * TC, (tc + 1) * TC)
                bank = (dc * NT + tc) % 2
                for fc in range(nfc):
                    P.mm(C.ps[bank][:], C.wb[g][:, fc, s_ * 128:(s_ + 1) * 128], C.hT[:, fo + fc, cols],
                         start=(fc == 0), stop=(fc == nfc - 1),
                         reads=[("wb", g, fc // 4), ("mixall", fo, tc), ("hT", fo + fc, tc)], writes=[("ps", bank)])
                ysl = (dc * NT + tc) % 2
                yt = C.T[12 + ysl]
                rows = slice(dc * 128, (dc + 1) * 128)
                ev = dr.get("evac")
                if ev is None:
                    P.act(yt[:], C.ps[bank][:], AF.Copy, reads=[("ps", bank)], writes=[("T", 12 + ysl)])
                    P.dma(dr["yp"][rows, cols], yt[:], slot=("yst", ysl),
                          reads=[("T", 12 + ysl)], writes=[("yp", dr.get("yp_tag", 0), dc, tc)])
                else:
                    k4 = (dc * NT + tc) % 4
                    xin = C.T[8 + k4]
                    P.dma(xin[:], ev["src"][rows, cols], slot=("oin", k4), reads=ev["src_keys"](dc, tc), writes=[("T", 8 + k4)])
                    P.tt(yt[:], xin[:], C.ps[bank][:], ALU.add, reads=[("T", 8 + k4), ("ps", bank)], writes=[("T", 12 + ysl)])
                    P.dma(ev["dst"][rows, cols], yt[:], slot=("yst", ysl),
                          reads=[("T", 12 + ysl)], writes=ev["dst_keys"](dc, tc))
                    if ev.get("stats"):
                        sq = C.B[k4]
                        P.act(sq[:], yt[:], AF.Square, reads=[("T", 12 + ysl)], writes=[("B", k4)])
                        P.mm(C.ps[4 + tc][:], C.ones_b[:], sq[:], start=(dc == 0), stop=(dc == KC - 1),
                             reads=[("B", k4), "ones_b"], writes=[("ps", 4 + tc)])


REC_GROUPS = ["rg", "hg", "hg", "rg", "hg", "hg", "rg", "hg", "hg", "rg", "hg", "hg"]


def emit_rec_layer(P, C, dr, do_outproj=True):
    S, NT = C.S, C.NT
    nc = P.nc
    T, B, ps = C.T, C.B, C.ps
    R = C.R
    P.dma(R.convw[:], dr["convw"], slot="p0", reads=[], writes=["convw"])
    P.dma(R.vec[:], dr["recvec"], slot="p1", reads=[], writes=["recvec"])
    P.dma(R.lbraw[:], dr["hglb"], slot="p2", reads=[], writes=["lbraw"])
    P.dma(R.onorm[:], dr["onorm"], slot="p3", reads=[], writes=["onorm"])
    P.dma(R.cmask[:], dr["cmask"], slot="p4", reads=[], writes=["cmask"])
    P.dma(R.bmask[:], dr["bmask"], slot="p5", reads=[], writes=["bmask"])
    P.act(R.sp[:], R.vec[:, :, 3], AF.Exp, reads=["recvec"], writes=["sp"], scale=-1.0)
    P.act(R.sp[:], R.sp[:], AF.Ln, reads=["sp", "one_c"], writes=["sp"], bias=C.one_c[:], scale=1.0)
    P.add("dve", lambda e: e.tensor_scalar_mul(R.cneg[:], R.sp[:], -8.0), ["sp"], ["cneg"])
    P.add("dve", lambda e: e.tensor_scalar_mul(R.cneg2[:], R.sp[:], -16.0), ["sp"], ["cneg2"])
    P.tt(R.lb[:], R.lbraw[:, :, 1], R.lbraw[:, :, 0], ALU.subtract, reads=["lbraw"], writes=["lb"])
    P.act(R.lb[:], R.lb[:], AF.Sigmoid, reads=["lb"], writes=["lb"])
    P.add("dve", lambda e: e.tensor_scalar_mul(R.lb[:], R.lb[:], float(dr["lbflag"])), ["lb"], ["lb"])
    P.ts(R.oml[:], R.lb[:], -1.0, 1.0, ALU.mult, ALU.add, reads=["lb"], writes=["oml"])
    P.memset(R.zero_f[:], 0.0, writes=["zero_f"], eng="dve")
    P.memset(R.zero_b[:], 0.0, writes=["zero_b"], eng="dve")
    P.add("dve", lambda e: e.tensor_scalar_mul(R.vech[:], R.vec[:], 0.5), ["recvec"], ["vech"])
    P.add("dve", lambda e: e.tensor_scalar_mul(R.chalf[:], R.sp[:], -4.0), ["sp"], ["chalf"])
    P.add("dve", lambda e: e.tensor_scalar_mul(R.omh[:], R.oml[:], 0.5), ["oml"], ["omh"])
    P.tt(R.lbp[:], R.omh[:], R.lb[:], ALU.add, reads=["omh", "lb"], writes=["lbp"])

    ws = WStream(P, C)
    groups = REC_GROUPS
    objs = []
    ia = ie = 0
    for gi, gk in enumerate(groups):
        if gk == "rg":
            objs.append(RGGroup(P, C, dr, gi % 2, ia))
            ia += 1
        else:
            objs.append(HGGroup(P, C, dr, gi % 2, ie))
            ie += 1
    seq = [(gi, tc) for gi in range(len(groups)) for tc in range(NT)]
    ws.load_group(dr["w_in"], 0, 4, 0)
    objs[0].inproj(0, 0)
    for k, (gi, tc) in enumerate(seq):
        if tc == 0 and gi + 1 < len(groups):
            ws.load_group(dr["w_in"], (gi + 1) * 512, 4, (gi + 1) % 2)
        if k + 1 < len(seq):
            objs[seq[k + 1][0]].inproj(seq[k + 1][1], (k + 1) % 2)
        objs[gi].mixer(tc, k % 2)
    if do_outproj:
        emit_outproj(P, C, ws, dr, 16)


_BK = [0]
_SL = [0]


def rstd_act(P, out, out_key, in_, in_keys, eps_ap, eps_key, inv_n):
    P.act(out, in_, AF.Ln, reads=list(in_keys) + [eps_key], writes=[out_key], bias=eps_ap, scale=inv_n)
    P.act(out, out, AF.Exp, reads=[out_key], writes=[out_key], scale=-0.5)


def v_tokmajor(P, C, gslot, s, tc, bank, tmp_idx, out_ap, out_keys):
    inproj_fm(P, C, gslot, s, tc, bank)
    vT = C.B[tmp_idx]
    P.act(vT[:], C.ps[bank][:], AF.Copy, reads=[("ps", bank)], writes=[("B", tmp_idx)])
    pv = C.ps[bank][:].bitcast(BF16)
    for tt in range(4):
        P.tr(pv[:, tt * 128:(tt + 1) * 128], vT[:, tt * 128:(tt + 1) * 128], C.ident_b[:],
             reads=[("B", tmp_idx), "ident_b"], writes=[("ps", bank)])
    P.cp(out_ap, pv[:, 0:512].rearrange("p (a b) -> p a b", a=4), reads=[("ps", bank)], writes=out_keys, eng="dve")


def silu2(P, C, out_ap, out_key, bank):
    k = 4 + (_SL[0] % 2)
    _SL[0] += 1
    th = C.T[k]
    P.act(th[:], C.ps[bank][:], AF.Tanh, reads=[("ps", bank)], writes=[("T", k)], scale=0.5)
    P.stt(out_ap, th[:], 1.0, C.ps[bank][:], ALU.add, ALU.mult, reads=[("T", k), ("ps", bank)], writes=[out_key])


def next_bank():
    b = _BK[0] % 2
    _BK[0] += 1
    return b


class RGGroup:
    def __init__(self, P, C, dr, gslot, a):
        self.P, self.C, self.dr, self.gslot, self.a = P, C, dr, gslot, a

    def inproj(self, tc, par):
        P, C, dr, gslot, a = self.P, self.C, self.dr, self.gslot, self.a
        T, B, ps, R = C.T, C.B, C.ps, C.R
        if tc == 0:
            P.dma(R.wgf[:], dr["wgate"][a], slot="wg", reads=[], writes=["wgf"])
            P.act(R.wgb[:], R.wgf[:], AF.Copy, reads=["wgf"], writes=["wgb"])
        xr = R.xraw[par]
        for i in range(2):
            bank = next_bank()
            inproj_fm(P, C, gslot, i, tc, bank)
            if tc == 0:
                P.memset(xr[:, i, 0:3], 0.0, writes=[("xraw", par, i)], eng="dve")
            else:
                P.act(xr[:, i, 0:3], R.xraw[1 - par][:, i, TC:TC + 3], AF.Copy, reads=[("xraw", 1 - par, i)],
                      writes=[("xraw", par, i)])
            P.act(xr[:, i, 3:3 + TC], ps[bank][:], AF.Copy, reads=[("ps", bank)], writes=[("xraw", par, i)])
        for j in range(2):
            bank = next_bank()
            inproj_fm(P, C, gslot, 2 + j, tc, bank)
            silu2(P, C, B[par * 2 + j][:], ("B", par * 2 + j), bank)

    def mixer(self, tc, par):
        P, C, dr, gslot, a = self.P, self.C, self.dr, self.gslot, self.a
        T, B, ps, R = C.T, C.B, C.ps, C.R
        xr = R.xraw[par]
        for i in range(2):
            ch = 2 * a + i
            xc = T[6 + i]
            k = ("T", 6 + i)
            P.act(xc[:], xr[:, i, 3:3 + TC], AF.Identity, reads=[("xraw", par, i), "convw", "recvec"], writes=[k],
                  bias=R.vec[:, ch, 0:1], scale=R.convw[:, ch, 3:4])
            for tap in range(3):
                P.stt(xc[:], xr[:, i, tap:tap + TC], R.convw[:, ch, tap:tap + 1], xc[:], ALU.mult, ALU.add,
                      reads=[("xraw", par, i), "convw", k], writes=[k])
            P.act(B[6 + i][:], xc[:], AF.Copy, reads=[k], writes=[("B", 6 + i)])
        for j in range(2):
            ch = 2 * a + j
            for g in range(2):
                for i in range(2):
                    P.mm(ps[3 + g][:], R.wgb[:, g, i, j * 128:(j + 1) * 128], B[6 + i][:], start=(i == 0), stop=(i == 1),
                         reads=["wgb", ("B", 6 + i)], writes=[("ps", 3 + g)])
            r_, ig, aa, mm_, uu = T[8], T[9], T[10], T[11], T[12]
            P.act(r_[:], ps[3][:], AF.Tanh, reads=[("ps", 3), "vech"], writes=[("T", 8)], bias=R.vech[:, ch, 1:2], scale=0.5)
            P.act(ig[:], ps[4][:], AF.Tanh, reads=[("ps", 4), "vech"], writes=[("T", 9)], bias=R.vech[:, ch, 2:3], scale=0.5)
            P.act(aa[:], r_[:], AF.Exp, reads=[("T", 8), "chalf"], writes=[("T", 10)], scale=R.chalf[:, ch:ch + 1],
                  bias=R.chalf[:, ch:ch + 1])
            P.act(mm_[:], r_[:], AF.Exp, reads=[("T", 8), "cneg"], writes=[("T", 11)], scale=R.cneg[:, ch:ch + 1],
                  bias=R.cneg[:, ch:ch + 1])
            P.act(mm_[:], mm_[:], AF.Ln, reads=[("T", 11), "one_c"], writes=[("T", 11)], bias=C.one_c[:], scale=-1.0)
            P.act(mm_[:], mm_[:], AF.Exp, reads=[("T", 11)], writes=[("T", 11)], scale=0.5)
            P.stt(uu[:], ig[:], 1.0, T[6 + j][:], ALU.add, ALU.mult, reads=[("T", 9), ("T", 6 + j)], writes=[("T", 12)])
            P.stt(uu[:], uu[:], 0.5, mm_[:], ALU.mult, ALU.mult, reads=[("T", 12), ("T", 11)], writes=[("T", 12)])
            hcur = R.h[par][j]
            hk = ("h", par, j)
            if tc == 0:
                init = 0.0
                rd = []
            else:
                init = R.h[1 - par][j][:, TC - 1:TC]
                rd = [("h", 1 - par, j)]
            P.scan(hcur[:], aa[:], uu[:], init, reads=[("T", 10), ("T", 12)] + rd, writes=[hk])
            mo = B[8 + j]
            P.stt(mo[:], hcur[:], 0.5, B[par * 2 + j][:], ALU.mult, ALU.mult, reads=[hk, ("B", par * 2 + j)], writes=[("B", 8 + j)])
            fc = 2 * a + j
            P.dma(dr["mixT"][fc * 128:(fc + 1) * 128, tc * TC:(tc + 1) * TC], mo[:], slot=("mo", j),
                  reads=[("B", 8 + j)], writes=[("mixT", dr.get("mix_tag", 0), fc, tc)])


class HGGroup:
    def __init__(self, P, C, dr, gslot, e):
        self.P, self.C, self.dr, self.gslot, self.e = P, C, dr, gslot, e
        self.nS = 0

    def tiles(self, par):
        T, B = self.C.T, self.C.B
        qi, fi, si, vi = par, 2 + par, 2 * par, 2 * par + 1
        return (T[qi], ("T", qi)), (T[fi], ("T", fi)), (B[si], ("B", si)), (B[vi], ("B", vi))

    def inproj(self, tc, par):
        P, C, gslot = self.P, self.C, self.gslot
        ps = C.ps
        (qs, qk), (ff, fk), (sg, sk), (V, vk) = self.tiles(par)
        bank = next_bank()
        inproj_fm(P, C, gslot, 0, tc, bank)
        silu2(P, C, qs[:], qk, bank)
        bank = next_bank()
        inproj_fm(P, C, gslot, 1, tc, bank)
        silu2(P, C, sg[:], sk, bank)
        bank = next_bank()
        inproj_fm(P, C, gslot, 2, tc, bank)
        P.act(ff[:], ps[bank][:], AF.Tanh, reads=[("ps", bank)], writes=[fk], scale=0.5)
        bank = next_bank()
        v_tokmajor(P, C, gslot, 3, tc, bank, 10 + par, V[:].rearrange("p (a b) -> p a b", a=4), [vk])

    def mixer(self, tc, par):
        P, C, dr, e = self.P, self.C, self.dr, self.e
        T, B, ps, R = C.T, C.B, C.ps, C.R
        HG_SCALE = 128 ** -0.5
        (qs, qk), (ff, fk), (sg, sk), (V, vk) = self.tiles(par)
        lf, bb, eb, enb = T[8], T[9], T[10], T[11]
        P.ts(ff[:], ff[:], R.omh[:, e:e + 1], R.lbp[:, e:e + 1], ALU.mult, ALU.add, reads=[fk, "omh", "lbp"], writes=[fk])
        P.act(lf[:], ff[:], AF.Ln, reads=[fk], writes=[("T", 8)])
        P.scan(bb[:], R.cmask[:], lf[:], 0.0, reads=["cmask", ("T", 8)], writes=[("T", 9)])
        P.act(eb[:], bb[:], AF.Exp, reads=[("T", 9)], writes=[("T", 10)])
        P.act(enb[:], bb[:], AF.Exp, reads=[("T", 9)], writes=[("T", 11)], scale=-1.0)
        P.ts(ff[:], ff[:], -1.0, 1.0, ALU.mult, ALU.add, reads=[fk], writes=[fk])
        Qd, Kd, K2 = B[4], B[5], B[6]
        P.stt(Qd[:], qs[:], HG_SCALE * 0.5, eb[:], ALU.mult, ALU.mult, reads=[qk, ("T", 10)], writes=[("B", 4)])
        P.tt(Kd[:], ff[:], enb[:], ALU.mult, reads=[fk, ("T", 11)], writes=[("B", 5)])
        for c in range(8):
            cs = slice(c * 64, (c + 1) * 64)
            P.stt(K2[:, cs], ff[:, cs], eb[:, c * 64 + 63:c * 64 + 64], enb[:, cs], ALU.mult, ALU.mult,
                  reads=[fk, ("T", 10), ("T", 11)], writes=[("B", 6)])
        for tt in range(4):
            tsl = slice(tt * 128, (tt + 1) * 128)
            ap = tt % 2
            atb = 5 if ap == 0 else 3
            at_ps = ps[atb][:, 0:128]
            P.mm(at_ps, Kd[:, tsl], Qd[:, tsl], start=True, stop=True, reads=[("B", 5), ("B", 4)], writes=[("ps", atb)])
            atm = R.atm[ap]
            P.tt(atm[:], at_ps, R.bmask[:], ALU.mult, reads=[("ps", atb), "bmask"], writes=[("atm", ap)])
            tr_ps = ps[4][:].bitcast(BF16)[:, 0:128]
            P.tr(tr_ps, K2[:, tsl], C.ident_b[:], reads=[("B", 6), "ident_b"], writes=[("ps", 4)])
            k2t = R.k2t[ap]
            P.cp(k2t[:], tr_ps, reads=[("ps", 4)], writes=[("k2t", ap)], eng="dve")
            o_ps = ps[7][:, tsl]
            P.mm(o_ps, V[:, tsl], atm[:], start=True, stop=False, reads=[vk, ("atm", ap)], writes=[("ps", 7)])
            for h2 in range(2):
                nS = self.nS
                c = tt * 2 + h2
                if tc == 0 and c == 0:
                    sb_prev, sbk = R.zero_b, "zero_b"
                    s_prev, spk = R.zero_f, "zero_f"
                else:
                    sb_prev, sbk = R.Sb[(nS - 1) % 4], ("Sb", (nS - 1) % 4)
                    s_prev, spk = R.Sf[(nS - 1) % 4], ("Sf", (nS - 1) % 4)
                cg = slice(tt * 128 + h2 * 64, tt * 128 + (h2 + 1) * 64)
                P.mm(ps[7][:, cg], sb_prev[:], Qd[:, cg], start=False, stop=(h2 == 1),
                     reads=[sbk, ("B", 4)], writes=[("ps", 7)])
                ub = 6 if nS % 2 == 0 else 2
                u_ps = ps[ub][:, 0:128]
                P.mm(u_ps, k2t[h2 * 64:(h2 + 1) * 64, :], V[h2 * 64:(h2 + 1) * 64, tsl], start=True, stop=True,
                     reads=[("k2t", ap), vk], writes=[("ps", ub)])
                s_new = R.Sf[nS % 4]
                P.stt(s_new[:], s_prev[:], eb[:, c * 64 + 63:c * 64 + 64], u_ps, ALU.mult, ALU.add,
                      reads=[spk, ("T", 10), ("ps", ub)], writes=[("Sf", nS % 4)])
                P.cp(R.Sb[nS % 4][:], s_new[:], reads=[("Sf", nS % 4)], writes=[("Sb", nS % 4)], eng="dve")
                self.nS += 1
        osq = B[7]
        P.act(osq[:], ps[7][:], AF.Square, reads=[("ps", 7)], writes=[("B", 7)])
        P.mm(ps[4][:], C.ones_b[:], osq[:], start=True, stop=True, reads=["ones_b", ("B", 7)], writes=[("ps", 4)])
        rs = T[12]
        rstd_act(P, rs[:], ("T", 12), ps[4][:], [("ps", 4)], C.eps_norm[:], "eps_norm", 1.0 / 128)
        ot = T[13]
        P.stt(ot[:], ps[7][:], R.onorm[:, 0:1], rs[:], ALU.mult, ALU.mult, reads=[("ps", 7), "onorm", ("T", 12)], writes=[("T", 13)])
        mo = B[8 + (tc % 2)]
        P.stt(mo[:], ot[:], 0.5, sg[:], ALU.mult, ALU.mult, reads=[("T", 13), sk], writes=[("B", 8 + (tc % 2))])
        fc = 8 + e
        P.dma(dr["mixT"][fc * 128:(fc + 1) * 128, tc * TC:(tc + 1) * TC], mo[:], slot=("mo", tc % 2),
              reads=[("B", 8 + (tc % 2))], writes=[("mixT", dr.get("mix_tag", 0), fc, tc)])


def alloc_rec(nc, st, C, mx=None):
    R = Ctx()
    sb = lambda name, shape, dt: st.enter_context(nc.sbuf_tensor("s_" + name, shape, dt))
    R.convw = sb("convw", [128, 8, 4], F32)
    R.vec = sb("recvec", [128, 8, 4], F32)
    R.lbraw = sb("lbraw", [128, 8, 2], F32)
    R.onorm = sb("onorm", [128, 1], F32)
    if mx is None:
        R.cmask = sb("cmask", [128, TC], F32)
    R.bmask = sb("bmask", [128, 128], F32)
    R.sp = sb("sp", [128, 8], F32)
    R.cneg = sb("cneg", [128, 8], F32)
    R.cneg2 = sb("cneg2", [128, 8], F32)
    R.lb = sb("lb", [128, 8], F32)
    R.oml = sb("oml", [128, 8], F32)
    R.omh = sb("omh", [128, 8], F32)
    R.lbp = sb("lbp", [128, 8], F32)
    R.chalf = sb("chalf", [128, 8], F32)
    R.vech = sb("vech", [128, 8, 4], F32)
    R.zero_f = sb("zero_f", [128, 128], F32)
    R.zero_b = sb("zero_b", [128, 128], BF16)
    R.wgb = sb("wgb", [128, 2, 2, 256], BF16)
    if mx is None:
        R.wgf = sb("wgf", [128, 2, 2, 256], F32)
        R.xraw = [sb(f"xraw{i}", [128, 2, TC + 3], F32) for i in range(2)]
        R.h = [[sb(f"h{p}{j}", [128, TC], F32) for j in range(2)] for p in range(2)]
    else:
        XW = 2 * (TC + 3)
        R.xraw = [mx[:, i * XW:(i + 1) * XW].rearrange("p (a b) -> p a b", a=2) for i in range(2)]
        o = 2 * XW
        R.h = [[mx[:, o + (p * 2 + j) * TC: o + (p * 2 + j + 1) * TC] for j in range(2)] for p in range(2)]
        o += 4 * TC
        R.cmask = mx[:, o:o + TC]
        o += TC
        R.wgf = mx[:, o:o + 1024].rearrange("p (g i c) -> p g i c", g=2, i=2)
    R.atm = [sb(f"atm{i}", [128, 128], BF16) for i in range(2)]
    R.k2t = [sb(f"k2t{i}", [128, 128], BF16) for i in range(2)]
    R.Sf = [sb(f"Sf{i}", [128, 128], F32) for i in range(4)]
    R.Sb = [sb(f"Sb{i}", [128, 128], BF16) for i in range(4)]
    C.R = R


def emit_consts(P, C, dr):
    load_const_bf16(P, C, C.ones_b, dr["ones"], "ones_b")
    load_const_bf16(P, C, C.ident_b, dr["ident"], "ident_b")
    P.memset(C.eps_norm[:], NORM_EPS, writes=["eps_norm"])
    P.memset(C.one_c[:], 1.0, writes=["one_c"])


def build_rec_program(S, n_part, lbflag):
    nc = bass.Bass("TRN2", target_bir_lowering=False)
    di = lambda name, shape, dt=F32: nc.dram_tensor(name, shape, dt, kind="ExternalInput").ap()
    do = lambda name, shape, dt=F32: nc.dram_tensor(name, shape, dt, kind="ExternalOutput").ap()
    dr = {}
    dr["xT"] = di("xT", [D, S])
    dr["parts"] = [di(f"part{i}", [D, S]) for i in range(n_part)]
    dr["gnorm"] = di("gnorm", [128, KC])
    dr["w_in"] = di("w_in", [D, 6144])
    dr["w_out"] = di("w_out", [2048, D])
    dr["convw"] = di("convw", [128, 8, 4])
    dr["recvec"] = di("recvec", [128, 8, 4])
    dr["hglb"] = di("hglb", [128, 8, 2])
    dr["onorm"] = di("onorm", [128, 1])
    dr["wgate"] = di("wgate", [4, 128, 2, 2, 256])
    dr["cmask"] = di("cmask", [128, TC])
    dr["bmask"] = di("bmask", [128, 128])
    dr["ones"] = di("ones", [128, 128])
    dr["ident"] = di("ident", [128, 128])
    dr["lbflag"] = lbflag
    if n_part:
        dr["xout"] = do("xout", [D, S])
    dr["yp"] = do("yp", [D, S])
    dr["mixT"] = nc.dram_tensor("mixT", [2048, S], BF16, kind="Internal").ap()
    with ExitStack() as st:
        C = alloc_common(nc, st, S, "rec")
        C.eps_norm = st.enter_context(nc.sbuf_tensor("s_eps_norm", [128, 1], F32))
        C.one_c = st.enter_context(nc.sbuf_tensor("s_one_c", [128, 1], F32))
        alloc_rec(nc, st, C)
        P = Prog(nc)
        emit_consts(P, C, dr)
        emit_prologue(P, C, dr, n_part, bool(n_part))
        emit_rec_layer(P, C, dr)
        n = P.emit()
    return nc, n


def pvec(v):
    v = np.asarray(v, np.float32)
    return np.ascontiguousarray(v.reshape(-1, 128).T)


def rec_layout(inp, j, r):
    w_in = inp["rec_w_in"][j]
    w_out = inp["rec_w_out"][j]
    rg_heads = [4 * r + a for a in range(4)]
    hg_heads = [8 * r + e for e in range(8)]
    cols = []
    ia = ie = 0
    for gk in REC_GROUPS:
        if gk == "rg":
            hh = rg_heads[ia]; ia += 1
            cols.append(np.arange(256 * hh, 256 * hh + 256))
            cols.append(np.arange(2048 + 256 * hh, 2048 + 256 * hh + 256))
        else:
            e = hg_heads[ie]; ie += 1
            cols.append(np.arange(4096 + 128 * e, 4096 + 128 * e + 128))
            cols.append(np.arange(10240 + 128 * e, 10240 + 128 * e + 128))
            cols.append(np.arange(6144 + 128 * e, 6144 + 128 * e + 128))
            cols.append(np.arange(8192 + 128 * e, 8192 + 128 * e + 128))
    cols = np.concatenate(cols)
    rows = []
    for hh in rg_heads:
        rows.append(np.arange(256 * hh, 256 * hh + 256))
    for e in hg_heads:
        rows.append(np.arange(2048 + 128 * e, 2048 + 128 * e + 128))
    rows = np.concatenate(rows)
    ch = np.concatenate([np.arange(256 * hh, 256 * hh + 256) for hh in rg_heads])
    convw = np.ascontiguousarray(inp["rg_conv_w"][j][:, ch].T.reshape(8, 128, 4).transpose(1, 0, 2))
    vec = np.stack([inp["rg_conv_b"][j][ch], inp["rg_b_gate_a"][j][ch], inp["rg_b_gate_x"][j][ch],
                    inp["rg_lambda"][j][ch]], axis=-1)
    vec = np.ascontiguousarray(vec.reshape(8, 128, 4).transpose(1, 0, 2))
    hch = np.concatenate([np.arange(128 * e, 128 * e + 128) for e in hg_heads])
    lbr = np.stack([inp["hg_lb"][0][hch], inp["hg_lb"][j][hch]], axis=-1)
    lbr = np.ascontiguousarray(lbr.reshape(8, 128, 2).transpose(1, 0, 2))
    wg = np.stack([inp["rg_w_gate_a"][j][rg_heads], inp["rg_w_gate_x"][j][rg_heads]], axis=1)
    wg = wg.reshape(4, 2, 2, 128, 256).transpose(0, 3, 1, 2, 4)
    return dict(
        w_in=np.ascontiguousarray(w_in[:, cols]),
        w_out=np.ascontiguousarray(w_out[rows, :]),
        gnorm=pvec(inp["rec_norm"][j]),
        convw=convw.astype(np.float32), recvec=vec.astype(np.float32), hglb=lbr.astype(np.float32),
        onorm=np.ascontiguousarray(inp["hg_out_norm"][j].reshape(128, 1).astype(np.float32)),
        wgate=np.ascontiguousarray(wg.astype(np.float32)),
    )


def alloc_att(nc, st, C, mx=None):
    A = Ctx()
    S = C.S
    sb = lambda name, shape, dt: st.enter_context(nc.sbuf_tensor("s_" + name, shape, dt))
    if mx is None:
        A.KT = sb("KT", [128, 2, S], BF16)
        A.Vt = sb("Vt", [128, S // 128, 264], BF16)
    else:
        A.KT = mx[:, 0:S].bitcast(BF16).rearrange("p (a s) -> p a s", a=2)
        nv = (S // 128) * 132
        A.Vt = mx[:, S:S + nv].bitcast(BF16).rearrange("p (t c) -> p t c", c=264)
    A.cosT = sb("cosT", [128, S], F32)
    A.sinT = sb("sinT", [128, S], F32)
    A.pmat = sb("pmat", [128, 128], F32)
    A.ones_f = sb("ones_f", [128, 128], F32)
    A.qkg = sb("qkg", [128, 2], F32)
    A.subg = sb("subg", [128, 256], F32)
    A.lp = sb("lp", [128, 4], F32)
    A.pr = sb("pr", [128, 2], F32)
    A.ex = sb("ex", [128, 2], F32)
    A.neglam = sb("neglam", [128, 1], F32)
    A.tri2f = sb("tri2f", [128, 256], F32)
    A.tri2 = sb("tri2", [128, 256], BF16)
    A.pt = [sb(f"pt{i}", [128, 256], BF16) for i in range(2)]
    A.o = sb("o", [128, 256], F32)
    A.junk = sb("junk", [128, 256], F32)
    A.on = sb("on", [128, 256], BF16)
    A.sm = sb("sm", [128, 8], F32)
    A.eps_sub = sb("eps_sub", [128, 1], F32)
    C.A = A


def emit_att_layer(P, C, dr, lam_init, do_outproj=True):
    S, NT = C.S, C.NT
    T, B, ps, A = C.T, C.B, C.ps, C.A
    SCALE = 128 ** -0.5
    P.dma(A.cosT[:], dr["cosT"], slot="a0", reads=[], writes=["cosT"])
    P.dma(A.sinT[:], dr["sinT"], slot="a1", reads=[], writes=["sinT"])
    P.dma(A.pmat[:], dr["pmat"], slot="a2", reads=[], writes=["pmat"])
    P.dma(A.ones_f[:], dr["ones"], slot="a3", reads=[], writes=["ones_f"])
    P.dma(A.qkg[:], dr["qkg"], slot="a4", reads=[], writes=["qkg"])
    P.dma(A.subg[:], dr["subg"], slot="a5", reads=[], writes=["subg"])
    P.dma(A.lp[:], dr["lp"], slot="a6", reads=[], writes=["lp"])
    P.dma(A.tri2f[:], dr["tri2"], slot="a7", reads=[], writes=["tri2f"])
    P.cp(A.tri2[:], A.tri2f[:], reads=["tri2f"], writes=["tri2"], eng="dve")
    P.memset(A.eps_sub[:], SUBLN_EPS, writes=["eps_sub"], eng="dve")
    P.add("dve", lambda e: e.tensor_scalar_mul(A.subg[:], A.subg[:], 1.0 - lam_init), ["subg"], ["subg"])
    P.tt(A.pr[:, 0:1], A.lp[:, 0:1], A.lp[:, 1:2], ALU.mult, reads=["lp"], writes=["pr"])
    P.tt(A.pr[:, 1:2], A.lp[:, 2:3], A.lp[:, 3:4], ALU.mult, reads=["lp", "pr"], writes=["pr"])
    P.mm(ps[2][:, 0:2], A.ones_f[:], A.pr[:], start=True, stop=True, reads=["ones_f", "pr"], writes=[("ps", 2)])
    P.act(A.ex[:], ps[2][:, 0:2], AF.Exp, reads=[("ps", 2)], writes=["ex"])
    P.tt(A.neglam[:], A.ex[:, 1:2], A.ex[:, 0:1], ALU.subtract, reads=["ex"], writes=["neglam"])
    P.add("dve", lambda e: e.tensor_scalar_add(A.neglam[:], A.neglam[:], -lam_init), ["neglam"], ["neglam"])
    P.memset(A.Vt[:, :, 256:257], 1.0, writes=["Vt_ones"], eng="dve")

    ws = WStream(P, C)
    ws.load_group(dr["w_in"], 0, 4, 0)
    ngroups = 8
    bkc = [0]

    def nb():
        b = bkc[0] % 2
        bkc[0] += 1
        return b

    def norm_rope(bank, gcol, out_ap, out_keys, tc):
        cols = slice(tc * TC, (tc + 1) * TC)
        raw, rstd, t1, t2, sq = T[6], T[7], T[8], T[9], B[2]
        P.act(raw[:], ps[bank][:], AF.Copy, reads=[("ps", bank)], writes=[("T", 6)])
        P.act(sq[:], ps[bank][:], AF.Square, reads=[("ps", bank)], writes=[("B", 2)])
        P.mm(ps[2][:], C.ones_b[:], sq[:], start=True, stop=True, reads=["ones_b", ("B", 2)], writes=[("ps", 2)])
        rstd_act(P, rstd[:], ("T", 7), ps[2][:], [("ps", 2)], C.eps_norm[:], "eps_norm", 1.0 / 128)
        P.stt(raw[:], raw[:], A.qkg[:, gcol:gcol + 1], rstd[:], ALU.mult, ALU.mult, reads=[("T", 6), "qkg", ("T", 7)], writes=[("T", 6)])
        P.mm(ps[3][:], A.pmat[:], raw[:], start=True, stop=True, reads=["pmat", ("T", 6)], writes=[("ps", 3)])
        P.tt(t1[:], raw[:], A.cosT[:, cols], ALU.mult, reads=[("T", 6), "cosT"], writes=[("T", 8)])
        P.tt(t2[:], ps[3][:], A.sinT[:, cols], ALU.mult, reads=[("ps", 3), "sinT"], writes=[("T", 9)])
        P.tt(out_ap, t1[:], t2[:], ALU.add, reads=[("T", 8), ("T", 9)], writes=out_keys)

    for gi in range(ngroups):
        gslot = gi % 2
        a = gi // 2
        if gi + 1 < ngroups:
            ws.load_group(dr["w_in"], (gi + 1) * 512, 4, (gi + 1) % 2)
        if gi % 2 == 0:
            for tc in range(NT):
                cols = slice(tc * TC, (tc + 1) * TC)
                for i in range(2):
                    bank = nb()
                    inproj_fm(P, C, gslot, i, tc, bank)
                    norm_rope(bank, 1, A.KT[:, i, cols], [("KT", i, tc)], tc)
                for half in range(2):
                    bank = nb()
                    tile0 = tc * 4
                    v_tokmajor(P, C, gslot, 2 + half, tc, bank, 10 + half, A.Vt[:, tile0:tile0 + 4, half * 128:(half + 1) * 128],
                               [("Vt", tile0 + q_, half) for q_ in range(4)])
        else:
            for tc in range(NT):
                cols = slice(tc * TC, (tc + 1) * TC)
                Qt = [B[4], B[5]]
                sgt = [B[6], B[7]]
                for i in range(2):
                    bank = nb()
                    inproj_fm(P, C, gslot, i, tc, bank)
                    norm_rope(bank, 0, Qt[i][:], [("B", 4 + i)], tc)
                for i in range(2):
                    bank = nb()
                    inproj_fm(P, C, gslot, 2 + i, tc, bank)
                    silu2(P, C, sgt[i][:], ("B", 6 + i), bank)
                for qt in range(4):
                    j = tc * 4 + qt
                    qs = slice(qt * 128, (qt + 1) * 128)

                    def emit_qk(i):
                        sb_ = 4 if i % 2 == 0 else 7
                        ks = slice(i * 128, (i + 1) * 128)
                        for m in range(2):
                            P.mm(ps[sb_][:, m * 128:(m + 1) * 128], A.KT[:, m, ks], Qt[m][:, qs], start=True, stop=True,
                                 reads=[("KT", m, i // 4), ("B", 4 + m)], writes=[("ps", sb_)])
                        pt = A.pt[i % 2]
                        P.act(pt[:], ps[sb_][:, 0:256], AF.Exp, reads=[("ps", sb_)], writes=[("pt", i % 2)], scale=SCALE)
                        if i == j:
                            P.tt(pt[:], pt[:], A.tri2[:], ALU.mult, reads=[("pt", i % 2), "tri2"], writes=[("pt", i % 2)])

                    def emit_pv(i):
                        pt = A.pt[i % 2]
                        for m in range(2):
                            P.mm(ps[5 + m][:, 0:257], pt[:, m * 128:(m + 1) * 128], A.Vt[:, i, 0:257],
                                 start=(i == 0), stop=(i == j),
                                 reads=[("pt", i % 2), ("Vt", i, 0), ("Vt", i, 1), "Vt_ones"], writes=[("ps", 5 + m)])

                    emit_qk(0)
                    for i in range(j + 1):
                        if i + 1 <= j:
                            emit_qk(i + 1)
                        emit_pv(i)
                    sm = A.sm
                    P.recip(sm[:, 0:1], ps[5][:, 256:257], reads=[("ps", 5)], writes=["sm"])
                    P.recip(sm[:, 1:2], ps[6][:, 256:257], reads=[("ps", 6), "sm"], writes=["sm"])
                    P.tt(sm[:, 2:3], sm[:, 1:2], A.neglam[:], ALU.mult, reads=["sm", "neglam"], writes=["sm"])
                    P.add("dve", lambda e, sm=sm: e.tensor_scalar_mul(A.o[:], ps[5][:, 0:256], sm[:, 0:1]), [("ps", 5), "sm"], ["o"])
                    P.stt(A.o[:], ps[6][:, 0:256], sm[:, 2:3], A.o[:], ALU.mult, ALU.add, reads=[("ps", 6), "sm", "o"], writes=["o"])
                    P.add("act", lambda e, sm=sm: e.activation(A.junk[:], A.o[:], AF.Square, accum_out=sm[:, 3:4]), ["o"], ["junk", "sm3"])
                    rstd_act(P, sm[:, 5:6], "sm5", sm[:, 3:4], ["sm3"], A.eps_sub[:], "eps_sub", 1.0 / 256)
                    P.stt(A.on[:], A.o[:], sm[:, 5:6], A.subg[:], ALU.mult, ALU.mult, reads=["o", "sm5", "subg"], writes=["on"])
                    trv = ps[3][:].bitcast(BF16)
                    for m in range(2):
                        P.tr(trv[:, m * 128:(m + 1) * 128], A.on[:, m * 128:(m + 1) * 128], C.ident_b[:],
                             reads=["on", "ident_b"], writes=[("ps", 3)])
                    for m in range(2):
                        P.stt(B[8 + m][:, qs], trv[:, m * 128:(m + 1) * 128], 0.5, sgt[m][:, qs], ALU.mult, ALU.mult,
                              reads=[("ps", 3), ("B", 6 + m)], writes=[("B", 8 + m)])
                for m in range(2):
                    fc = 2 * a + m
                    P.dma(dr["mixT"][fc * 128:(fc + 1) * 128, cols], B[8 + m][:], slot=("mo", m),
                          reads=[("B", 8 + m)], writes=[("mixT", dr.get("mix_tag", 0), fc, tc)])
    if do_outproj:
        emit_outproj(P, C, ws, dr, 8)


def build_att_program(S, n_part, layer):
    nc = bass.Bass("TRN2", target_bir_lowering=False)
    di = lambda name, shape, dt=F32: nc.dram_tensor(name, shape, dt, kind="ExternalInput").ap()
    do = lambda name, shape, dt=F32: nc.dram_tensor(name, shape, dt, kind="ExternalOutput").ap()
    dr = {}
    dr["xT"] = di("xT", [D, S])
    dr["parts"] = [di(f"part{i}", [D, S]) for i in range(n_part)]
    dr["gnorm"] = di("gnorm", [128, KC])
    dr["w_in"] = di("w_in", [D, 4096])
    dr["w_out"] = di("w_out", [1024, D])
    dr["cosT"] = di("cosT", [128, S])
    dr["sinT"] = di("sinT", [128, S])
    dr["pmat"] = di("pmat", [128, 128])
    dr["qkg"] = di("qkg", [128, 2])
    dr["subg"] = di("subg", [128, 256])
    dr["lp"] = di("lp", [128, 4])
    dr["tri2"] = di("tri2", [128, 256])
    dr["ones"] = di("ones", [128, 128])
    dr["ident"] = di("ident", [128, 128])
    if n_part:
        dr["xout"] = do("xout", [D, S])
    dr["yp"] = do("yp", [D, S])
    dr["mixT"] = nc.dram_tensor("mixT", [1024, S], BF16, kind="Internal").ap()
    lam_init = 0.8 - 0.6 * math.exp(-0.3 * layer)
    with ExitStack() as st:
        C = alloc_common(nc, st, S, "att")
        C.eps_norm = st.enter_context(nc.sbuf_tensor("s_eps_norm", [128, 1], F32))
        C.one_c = st.enter_context(nc.sbuf_tensor("s_one_c", [128, 1], F32))
        alloc_att(nc, st, C)
        P = Prog(nc)
        emit_consts(P, C, dr)
        emit_prologue(P, C, dr, n_part, bool(n_part))
        emit_att_layer(P, C, dr, lam_init)
        n = P.emit()
    return nc, n


def att_layout(inp, j, r):
    w_in = inp["att_w_in"][j]
    w_out = inp["att_w_out"][j]
    heads = [4 * r + a for a in range(4)]
    cols = []
    for h in heads:
        cols.append(np.arange(2048 + 256 * h, 2048 + 256 * h + 256))
        cols.append(np.arange(4096 + 256 * h, 4096 + 256 * h + 256))
        cols.append(np.arange(256 * h, 256 * h + 256))
        cols.append(np.arange(6144 + 256 * h, 6144 + 256 * h + 256))
    cols = np.concatenate(cols)
    rows = np.concatenate([np.arange(256 * h, 256 * h + 256) for h in heads])
    qkg = np.stack([inp["att_q_norm"][j], inp["att_k_norm"][j]], axis=-1).astype(np.float32)
    subg = np.ascontiguousarray(np.broadcast_to(inp["att_sub_norm"][j][None, :], (128, 256)).astype(np.float32))
    lp = np.ascontiguousarray(inp["att_lambda"][j].T.astype(np.float32))
    return dict(
        w_in=np.ascontiguousarray(w_in[:, cols]),
        w_out=np.ascontiguousarray(w_out[rows, :]),
        gnorm=pvec(inp["att_norm"][j]),
        qkg=np.ascontiguousarray(qkg), subg=subg, lp=lp,
    )


def build_final_program(S):
    nc = bass.Bass("TRN2", target_bir_lowering=False)
    di = lambda name, shape, dt=F32: nc.dram_tensor(name, shape, dt, kind="ExternalInput").ap()
    x = di("xT", [1024, S])
    p0 = di("part0", [1024, S])
    p1 = di("part1", [1024, S])
    out = nc.dram_tensor("xout", [1024, S], F32, kind="ExternalOutput").ap()
    with ExitStack() as st:
        tl = [[st.enter_context(nc.sbuf_tensor(f"s_f{k}{i}", [128, S], F32)) for i in range(2)] for k in range(3)]
        P = Prog(nc)
        for c in range(8):
            sl = c % 2
            rows = slice(c * 128, (c + 1) * 128)
            xa, pa, pb = tl[0][sl], tl[1][sl], tl[2][sl]
            P.dma(xa[:], x[rows, :], slot=("fx", sl), reads=[], writes=[("fx", sl)])
            P.dma(pa[:], p0[rows, :], slot=("fa", sl), reads=[], writes=[("fa", sl)])
            P.dma(pb[:], p1[rows, :], slot=("fb", sl), reads=[], writes=[("fb", sl)])
            P.tt(xa[:], xa[:], pa[:], ALU.add, reads=[("fx", sl), ("fa", sl)], writes=[("fx", sl)])
            P.tt(xa[:], xa[:], pb[:], ALU.add, reads=[("fx", sl), ("fb", sl)], writes=[("fx", sl)])
            P.dma(out[rows, :], xa[:], slot=("fo", sl), reads=[("fx", sl)], writes=[("out", c)])
        P.emit()
    return nc


REC_KEYS = ("w_in", "w_out", "gnorm", "convw", "recvec", "hglb", "onorm", "wgate")
ATT_KEYS = ("w_in", "w_out", "gnorm", "qkg", "subg", "lp")
CONST_KEYS = ("ones", "ident", "cmask", "bmask", "cosT", "sinT", "pmat", "tri2")
ACTIVE_CORES = (0, 1, 2, 3)
NCORES = 4


def build_fused_program(S, nlayers=4, nseq=1):
    nc = bass.Bass("TRN2", target_bir_lowering=False)
    di = lambda name, shape, dt=F32: nc.dram_tensor(name, shape, dt, kind="ExternalInput").ap()
    x_ins = [di("xT" if q == 0 else f"xT{q}", [D, S]) for q in range(nseq)]
    outs = [nc.dram_tensor("outT" if q == 0 else f"outT{q}", [D, S], F32, kind="ExternalOutput").ap() for q in range(nseq)]
    yp = [nc.dram_tensor(f"yp{i}", [D, S], F32, kind="Internal").ap() for i in range(2)]
    xb = [nc.dram_tensor(f"xbuf{i}", [D, S], F32, kind="Internal").ap() for i in range(2)]
    mixTs = [nc.dram_tensor(f"mixT{i}", [2048, S], BF16, kind="Internal").ap() for i in range(2)]
    cshape = dict(ones=[128, 128], ident=[128, 128], cmask=[128, TC], bmask=[128, 128], cosT=[128, S], sinT=[128, S],
                  pmat=[128, 128], tri2=[128, 256])
    cd = {k: di(k, cshape[k]) for k in CONST_KEYS}
    rshape = dict(w_in=[D, 6144], w_out=[2048, D], gnorm=[128, KC], convw=[128, 8, 4], recvec=[128, 8, 4], hglb=[128, 8, 2],
                  onorm=[128, 1], wgate=[4, 128, 2, 2, 256])
    ashape = dict(w_in=[D, 4096], w_out=[1024, D], gnorm=[128, KC], qkg=[128, 2], subg=[128, 256], lp=[128, 4])
    drs = []
    for L in range(nlayers):
        sh = rshape if L % 2 == 0 else ashape
        halves = []
        for hf in range(2):
            dr = {k: di(f"{k}{L}h{hf}", sh[k]) for k in sh if not (k == "gnorm" and hf == 1)}
            if hf == 1:
                dr["gnorm"] = halves[0]["gnorm"]
            dr.update(cd)
            dr["mixT"] = mixTs[hf] if L % 2 == 0 else mixTs[hf][0:1024, :]
            dr["mix_tag"] = hf
            dr["yp"] = yp[hf]
            dr["yp_tag"] = hf
            dr["fc_off"] = 8 * hf if L % 2 == 1 else 0
            dr["lbflag"] = 0.0 if L == 0 else 1.0
            halves.append(dr)
        drs.append(halves)
    with ExitStack() as st:
        C = alloc_common(nc, st, S, "all")
        C.eps_norm = st.enter_context(nc.sbuf_tensor("s_eps_norm", [128, 1], F32))
        C.one_c = st.enter_context(nc.sbuf_tensor("s_one_c", [128, 1], F32))
        mx = st.enter_context(nc.sbuf_tensor("s_mx", [128, 5644], F32))
        alloc_rec(nc, st, C, mx)
        alloc_att(nc, st, C, mx)
        P = Prog(nc)
        emit_consts(P, C, cd)
        for q in range(nseq):
          x_in, out = x_ins[q], outs[q]
          for L in range(nlayers):
              x_cur = x_in if L == 0 else xb[(L - 1) % 2]
              x_next = out if L == nlayers - 1 else xb[L % 2]
              pr = dict(gnorm=drs[L][0]["gnorm"], xT=x_cur, parts=[])
              emit_prologue(P, C, pr, 0, False, stats_ready=(L > 0))
              P.barrier()
              for hf in range(2):
                  if L % 2 == 0:
                      emit_rec_layer(P, C, drs[L][hf], do_outproj=False)
                  else:
                      emit_att_layer(P, C, drs[L][hf], 0.8 - 0.6 * math.exp(-0.3 * L), do_outproj=False)
              xk = (lambda dc, tc: [("xout", dc, tc)])
              yk = (lambda dc, tc: [("yp", 0, dc, tc)])
              drs[L][0]["evac"] = dict(src=x_cur, src_keys=(xk if L > 0 else (lambda dc, tc: [])), dst=yp[0], dst_keys=yk)
              drs[L][1]["evac"] = dict(src=yp[0], src_keys=yk, dst=x_next, dst_keys=xk, stats=(L < nlayers - 1))
              for hf in range(2):
                  emit_outproj(P, C, WStream(P, C), drs[L][hf], 16 if L % 2 == 0 else 8)
          P.barrier()
        n = P.emit()
    return nc, n


def fused_maps(inp, S=SEQ, nlayers=4, ncores=2, nseq=2):
    cs = host_consts(S)
    x = np.asarray(inp["x"], np.float32)
    wmap = {}
    for L in range(nlayers):
        j = L // 2
        for hf in range(2):
            lay = rec_layout(inp, j, hf) if L % 2 == 0 else att_layout(inp, j, hf)
            for k, v in lay.items():
                if k == "gnorm" and hf == 1:
                    continue
                wmap[f"{k}{L}h{hf}"] = v
    maps = []
    for c in range(ncores):
        m = {}
        for q in range(nseq):
            b = c * nseq + q
            m["xT" if q == 0 else f"xT{q}"] = np.ascontiguousarray(x[b][:S].T)
        m.update(wmap)
        for k in CONST_KEYS:
            m[k] = cs[k]
        maps.append(m)
    return maps


_PROGS = {}
NCORES = 2
NSEQ = 2


def kernel(**inp):
    inp = {k: np.asarray(v) for k, v in inp.items()}
    if "fused" not in _PROGS:
        _PROGS["fused"] = build_fused_program(SEQ, 4, NSEQ)[0]
    nc = _PROGS["fused"]
    maps = fused_maps(inp, SEQ, 4, NCORES, NSEQ)
    res = run_bass_kernel_spmd(nc, maps, core_ids=list(range(NCORES)))
    out = np.empty((BATCH, SEQ, D), np.float32)
    for c in range(NCORES):
        for q in range(NSEQ):
            out[c * NSEQ + q] = np.asarray(res.results[c]["outT" if q == 0 else f"outT{q}"]).T
    return out
```

```python
import math
from contextlib import ExitStack

import numpy as np
import concourse.bass as bass
import concourse.mybir as mybir
from concourse.bass_utils import run_bass_kernel_spmd

F32 = mybir.dt.float32
BF16 = mybir.dt.bfloat16
AF = mybir.ActivationFunctionType
ALU = mybir.AluOpType

D = 2048
KC = 16
SEQ = 2048
BATCH = 4
TC = 512
NORM_EPS = 1e-6
SUBLN_EPS = 1e-5
SAME_ENGINE_SYNC = True


def _fs(ap):
    try:
        return int(ap.free_size())
    except Exception:
        return 512


RESCHEDULE = True


class Prog:
    ENGS = ("pe", "act", "dve", "pool", "sp")

    def __init__(self, nc):
        self.nc = nc
        self.ops = []

    def add(self, eng, fn, reads=(), writes=(), dma=None, cost=0.5):
        self.ops.append(dict(kind="op", eng=eng, fn=fn, reads=tuple(reads), writes=tuple(writes), dma=dma, cost=cost))

    def barrier(self):
        self.ops.append(dict(kind="barrier"))

    def mm(self, out, lhsT, rhs, start, stop, reads, writes):
        self.add("pe", lambda e: e.matmul(out, lhsT, rhs, start=start, stop=stop), reads, writes,
                 cost=0.035 + _fs(rhs) / 2400.0)

    def tr(self, out, in_, ident, reads, writes):
        self.add("pe", lambda e: e.transpose(out, in_, ident), reads, writes, cost=0.1)

    def act(self, out, in_, func, reads, writes, bias=0.0, scale=1.0, eng="act"):
        self.add(eng, lambda e: e.activation(out, in_, func, bias=bias, scale=scale), reads, writes,
                 cost=0.2 + _fs(out) * 0.0008)
        self.ops[-1]["tset"] = "tanh" if func == AF.Tanh else ("ln" if func == AF.Ln else None)

    def ts(self, out, in0, s1, s2, op0, op1, reads, writes, eng="dve"):
        self.add(eng, lambda e: e.tensor_scalar(out, in0, s1, s2, op0, op1), reads, writes, cost=0.15 + _fs(out) * 0.001)

    def stt(self, out, in0, scalar, in1, op0, op1, reads, writes, eng="dve"):
        self.add(eng, lambda e: e.scalar_tensor_tensor(out, in0, scalar, in1, op0, op1), reads, writes,
                 cost=0.2 + _fs(out) * 0.0011)

    def tt(self, out, in0, in1, op, reads, writes, eng="dve"):
        c = 0.15 + _fs(out) * (0.001 if eng == "dve" else 0.0022)
        self.add(eng, lambda e: e.tensor_tensor(out, in0, in1, op), reads, writes, cost=c)

    def cp(self, out, in_, reads, writes, eng="dve"):
        c = 0.15 + _fs(out) * 0.001
        self.add(eng, lambda e: e.tensor_copy(out, in_), reads, writes, cost=c)

    def memset(self, ap, val, writes, eng="dve"):
        self.add(eng, lambda e: e.memset(ap, val), (), writes, cost=0.2)

    def recip(self, out, in_, reads, writes):
        self.add("dve", lambda e: e.reciprocal(out, in_), reads, writes, cost=0.15 + _fs(out) * 0.001)

    def scan(self, out, d0, d1, init, reads, writes):
        self.add("dve", lambda e: e.tensor_tensor_scan(out, d0, d1, init, ALU.mult, ALU.add), reads, writes,
                 cost=0.2 + _fs(out) * 0.002)

    def dma(self, out, in_, slot, reads, writes, eng="sp"):
        try:
            nbytes = int(out.nbytes())
        except Exception:
            nbytes = 1 << 18
        self.add(eng, lambda e: e.dma_start(out, in_), reads, writes, dma=slot, cost=2.0 + nbytes / 150e3)

    def emit(self):
        nc = self.nc
        real = []
        seg_of = []
        seg = 0
        last_w = {}
        readers = {}
        for o in self.ops:
            if o["kind"] == "barrier":
                seg += 1
                continue
            o["idx"] = len(real)
            real.append(o)
            o["seg"] = seg
            deps = set()
            for r in o["reads"]:
                if r in last_w:
                    deps.add(last_w[r])
            for w in o["writes"]:
                if w in last_w:
                    deps.add(last_w[w])
                deps.update(readers.get(w, ()))
            deps.discard(o["idx"])
            for w in o["writes"]:
                last_w[w] = o["idx"]
                readers[w] = []
            for r in o["reads"]:
                if r not in o["writes"]:
                    readers.setdefault(r, []).append(o["idx"])
            o["skey"] = ("dma", o["dma"]) if o["dma"] is not None else ("eng", o["eng"])
            o["deps"] = [d for d in deps if real[d]["seg"] == seg]
            o["signal"] = False
        nseg = seg + 1
        import heapq
        order = {e: [] for e in self.ENGS}
        seg_ops = [[] for _ in range(nseg)]
        for o in real:
            seg_ops[o["seg"]].append(o["idx"])
        fin = [0.0] * len(real)
        tnow = 0.0
        LAT = 0.15
        for sg in range(nseg):
            ids = seg_ops[sg]
            if not RESCHEDULE:
                for i in ids:
                    order[real[i]["eng"]].append(i)
                continue
            nd = {i: len(real[i]["deps"]) for i in ids}
            users = {i: [] for i in ids}
            for i in ids:
                for d in real[i]["deps"]:
                    users[d].append(i)
            ready = {e: [] for e in self.ENGS}
            rt = {}
            for i in ids:
                if nd[i] == 0:
                    rt[i] = tnow
                    heapq.heappush(ready[real[i]["eng"]], i)
            free = {e: tnow for e in self.ENGS}
            cur_set = [None]
            sp_list = [i for i in ids if real[i]["eng"] == "sp"]
            sp_pos = 0
            remaining = len(ids)
            while remaining:
                best = None
                for e in self.ENGS:
                    h = ready[e]
                    if not h:
                        continue
                    if e == "sp":
                        if sp_pos < len(sp_list) and nd[sp_list[sp_pos]] == 0:
                            i = sp_list[sp_pos]
                            st_ = max(free[e], rt[i])
                            cand = (st_, i, e)
                        else:
                            continue
                    else:
                        cands = heapq.nsmallest(12 if e == "act" else 6, h)
                        cand = None
                        for i in cands:
                            st_ = max(free[e], rt[i])
                            if e == "act":
                                ts_ = real[i].get("tset")
                                if ts_ is not None and cur_set[0] is not None and ts_ != cur_set[0]:
                                    st_ += 1.3
                            if cand is None or st_ < cand[0] - 1e-9:
                                cand = (st_, i, e)
                    if cand is not None and (best is None or cand[0] < best[0] - 1e-9 or
                                             (abs(cand[0] - best[0]) <= 1e-9 and cand[1] < best[1])):
                        best = cand
                if best is None:
                    raise RuntimeError("scheduler deadlock")
                st_, i, e = best
                if e == "sp":
                    sp_pos += 1
                    ready[e].remove(i)
                    heapq.heapify(ready[e])
                else:
                    ready[e].remove(i)
                    heapq.heapify(ready[e])
                o = real[i]
                if e == "act" and o.get("tset") is not None:
                    cur_set[0] = o["tset"]
                if o["dma"] is not None:
                    free[e] = st_ + 0.1
                    fin[i] = st_ + o["cost"]
                else:
                    free[e] = st_ + o["cost"]
                    fin[i] = free[e]
                order[e].append(i)
                remaining -= 1
                for u in users[i]:
                    nd[u] -= 1
                    rt[u] = max(rt.get(u, tnow), fin[i] + LAT)
                    if nd[u] == 0:
                        heapq.heappush(ready[real[u]["eng"]], u)
            tnow = max([tnow] + [fin[i] for i in ids]) + 1.0
        self.est_us = tnow
        pos = {}
        for e in self.ENGS:
            for p, i in enumerate(order[e]):
                pos[i] = p
        dma_seq = {}
        for i in order["sp"] + [i for e in self.ENGS if e != "sp" for i in order[e]]:
            pass
        cnt = {}
        for e in self.ENGS:
            for i in order[e]:
                o = real[i]
                if o["dma"] is not None:
                    cnt[o["skey"]] = cnt.get(o["skey"], 0) + 16
                    o["sigval"] = cnt[o["skey"]]
                    o["dpos"] = cnt[o["skey"]]
        last_in_seg = [dict() for _ in range(nseg)]
        for e in self.ENGS:
            for i in order[e]:
                o = real[i]
                last_in_seg[o["seg"]][o["skey"]] = i
        first_seen = set()
        for e in self.ENGS:
            for i in order[e]:
                o = real[i]
                if o["seg"] > 0 and (e, o["seg"]) not in first_seen:
                    first_seen.add((e, o["seg"]))
                    for sgp in range(o["seg"]):
                        o["deps"] = list(o["deps"]) + list(last_in_seg[sgp].values())
        for o in real:
            keep = {}
            for d in o["deps"]:
                od = real[d]
                if od["dma"] is None and od["eng"] == o["eng"]:
                    if o["eng"] == "pe" or not SAME_ENGINE_SYNC:
                        continue
                sk = od["skey"]
                key = od["dpos"] if od["dma"] is not None else pos[d]
                if sk not in keep or keep[sk][0] < key:
                    keep[sk] = (key, d)
            o["deps"] = [v[1] for v in keep.values()]
            for d in o["deps"]:
                real[d]["signal"] = True
        for e in self.ENGS:
            c = 0
            for i in order[e]:
                o = real[i]
                if o["dma"] is None and o["signal"]:
                    c += 1
                    o["sigval"] = c
        skeys = sorted({o["skey"] for o in real}, key=str)
        with ExitStack() as st:
            sems = {}
            for i, sk in enumerate(skeys):
                sems[sk] = st.enter_context(nc.semaphore(f"sem{i}"))
            dma_final = {sk: cnt[sk] for sk in skeys if sk[0] == "dma"}
            self.sem_names = {f"sem{i}": sk for i, sk in enumerate(skeys)}

            def run(ename, eng):
                waited = {}
                for i in order[ename]:
                    o = real[i]
                    for d in o["deps"]:
                        od = real[d]
                        sk, val = od["skey"], od["sigval"]
                        if waited.get(sk, 0) < val:
                            eng.wait_ge(sems[sk], val)
                            waited[sk] = val
                    ins = o["fn"](eng)
                    if o["dma"] is not None:
                        ins.then_inc(sems[o["skey"]], 16)
                    elif o["signal"]:
                        ins.then_inc(sems[o["skey"]], 1)
                if ename == "sp":
                    for sk, val in dma_final.items():
                        if waited.get(sk, 0) < val:
                            eng.wait_ge(sems[sk], val)

            with nc.Block() as block:
                @block.tensor
                def _(e):
                    run("pe", e)

                @block.scalar
                def _(e):
                    run("act", e)

                @block.vector
                def _(e):
                    run("dve", e)

                @block.gpsimd
                def _(e):
                    run("pool", e)

                @block.sync
                def _(e):
                    run("sp", e)
        return len(real)


def host_consts(S):
    c = {}
    c["ones"] = np.ones((128, 128), np.float32)
    c["ident"] = np.eye(128, dtype=np.float32)
    s = np.arange(128)[:, None]
    t = np.arange(128)[None, :]
    c["bmask"] = ((s // 64 == t // 64) & (s <= t)).astype(np.float32)
    tri = (s <= t).astype(np.float32)
    c["tri2"] = np.concatenate([tri, tri], axis=1)
    cm = np.ones((128, TC), np.float32)
    cm[:, ::64] = 0.0
    c["cmask"] = cm
    half = 16
    inv_freq = (500000.0 ** (-np.arange(0, 32, 2, dtype=np.float32) / 32)).astype(np.float32)
    pos = np.arange(S, dtype=np.float32)
    ang = pos[None, :] * inv_freq[:, None]
    cosT = np.ones((128, S), np.float32)
    sinT = np.zeros((128, S), np.float32)
    cosT[0:16] = np.cos(ang)
    cosT[16:32] = np.cos(ang)
    sinT[0:16] = np.sin(ang)
    sinT[16:32] = np.sin(ang)
    c["cosT"] = cosT
    c["sinT"] = sinT
    pm = np.zeros((128, 128), np.float32)
    for m in range(16):
        pm[m + 16, m] = -1.0
        pm[m, m + 16] = 1.0
    c["pmat"] = pm
    return c


class Ctx:
    pass


def alloc_common(nc, st, S, kind):
    C = Ctx()
    C.S = S
    C.NT = S // TC
    sb = lambda name, shape, dt: st.enter_context(nc.sbuf_tensor("s_" + name, shape, dt))
    C.hT = sb("hT", [128, KC, S], BF16)
    C.wf = [sb(f"wf{i}", [128, KC, 128], F32) for i in range(2)]
    C.wb = [sb(f"wb{i}", [128, KC, 512], BF16) for i in range(2)]
    C.T = [sb(f"T{i}", [128, TC], F32) for i in range(14)]
    C.B = [sb(f"B{i}", [128, TC], BF16) for i in range(12)]
    C.ones_b = sb("ones_b", [128, 128], BF16)
    C.ident_b = sb("ident_b", [128, 128], BF16)
    C.cst_f = sb("cst_f", [128, 128], F32)
    C.gnorm = sb("gnorm", [128, KC], F32)
    C.ps = [st.enter_context(nc.psum_tensor(f"ps{i}", [128, 512], F32)) for i in range(8)]
    return C


def load_const_bf16(P, C, dst, src_dram, name):
    P.dma(C.cst_f[:], src_dram, slot="cst", reads=[("dram", name)], writes=["cst_f"])
    P.cp(dst[:], C.cst_f[:], reads=["cst_f"], writes=[name], eng="dve")


def emit_prologue(P, C, dr, n_part, have_xout, final_only=False, stats_ready=False):
    S, NT = C.S, C.NT
    P.dma(C.gnorm[:], dr["gnorm"], slot="gn", reads=[], writes=["gnorm"])
    stat_banks = [C.ps[4 + i] for i in range(NT)]
    for kc in range(KC if not stats_ready else 0):
        for tc in range(NT):
            sl = (kc * NT + tc) % 4
            xa = C.T[sl]
            xk = ("T", sl)
            cols = slice(tc * TC, (tc + 1) * TC)
            rows = slice(kc * 128, (kc + 1) * 128)
            P.dma(xa[:], dr["xT"][rows, cols], slot=("xa", sl), reads=([("xout", kc, tc)] if dr.get("x_dep") else []), writes=[xk])
            for pi in range(n_part):
                pa = C.T[8 + sl]
                pk = ("T", 8 + sl)
                P.dma(pa[:], dr["parts"][pi][rows, cols], slot=("pa", sl), reads=[("yp", pi, kc, tc)], writes=[pk])
                P.tt(xa[:], xa[:], pa[:], ALU.add, reads=[xk, pk], writes=[xk])
            if have_xout:
                P.dma(dr["xout"][rows, cols], xa[:], slot=("xo", sl), reads=[xk], writes=[("xout", kc, tc), "xout_all"])
            if final_only:
                continue
            xs = C.B[sl]
            P.act(xs[:], xa[:], AF.Square, reads=[xk], writes=[("B", sl)])
            P.mm(stat_banks[tc][:], C.ones_b[:], xs[:], start=(kc == 0), stop=(kc == KC - 1),
                 reads=[("B", sl), "ones_b"], writes=[("ps", 4 + tc)])
    if final_only:
        return
    for tc in range(NT):
        r = C.T[4 + tc]
        rstd_act(P, r[:], ("T", 4 + tc), stat_banks[tc][:], [("ps", 4 + tc)], C.eps_norm[:], "eps_norm", 1.0 / D)
    src = dr["xout"] if have_xout else dr["xT"]
    for kc in range(KC):
        for tc in range(NT):
            sl = (kc * NT + tc) % 4
            xa = C.T[sl]
            xk = ("T", sl)
            cols = slice(tc * TC, (tc + 1) * TC)
            rows = slice(kc * 128, (kc + 1) * 128)
            rd = [("xout", kc, tc)] if (have_xout or stats_ready) else []
            P.dma(xa[:], src[rows, cols], slot=("xa", sl), reads=rd, writes=[xk])
            P.stt(C.hT[:, kc, cols], xa[:], C.gnorm[:, kc:kc + 1], C.T[4 + tc][:], ALU.mult, ALU.mult,
                  reads=[xk, "gnorm", ("T", 4 + tc)], writes=[("hT", kc, tc)])


class WStream:
    def __init__(self, P, C):
        self.P, self.C = P, C
        self.n = 0

    def load_group(self, w_dram, col0, nslab, gslot, kcn=KC, name="w", koff=0):
        P, C = self.P, self.C
        ncol = nslab * 128
        for q in range(kcn // 4):
            sl = self.n % 2
            self.n += 1
            src = w_dram[q * 512:(q + 1) * 512, col0:col0 + ncol].rearrange("(k p) c -> p k c", p=128)
            stage = C.wf[sl][:].rearrange("p k c -> p (k c)")[:, 0:4 * ncol].rearrange("p (k c) -> p k c", k=4)
            P.dma(stage, src, slot=("wf", sl), reads=[], writes=[("wf", sl)])
            P.act(C.wb[gslot][:, koff + 4 * q:koff + 4 * q + 4, 0:ncol], stage, AF.Copy,
                  reads=[("wf", sl)], writes=[("wb", gslot, koff // 4 + q)])


def inproj_fm(P, C, gslot, s, tc, bank):
    cols = slice(tc * TC, (tc + 1) * TC)
    for kc in range(KC):
        P.mm(C.ps[bank][:], C.wb[gslot][:, kc, s * 128:(s + 1) * 128], C.hT[:, kc, cols],
             start=(kc == 0), stop=(kc == KC - 1),
             reads=[("wb", gslot, kc // 4), ("hT", kc, tc)], writes=[("ps", bank)])


def inproj_tm(P, C, gslot, s0, ncol, tc, tt, out_ap, bank):
    t0 = tc * TC + tt * 128
    rd = [("wb", gslot, q) for q in range(4)]
    for kc in range(KC):
        P.mm(out_ap, C.hT[:, kc, t0:t0 + 128], C.wb[gslot][:, kc, s0 * 128:s0 * 128 + ncol],
             start=(kc == 0), stop=(kc == KC - 1),
             reads=rd + [("hT", kc, tc)], writes=[("ps", bank)])


def emit_outproj(P, C, ws, dr, nfc):
    S, NT = C.S, C.NT
    fo = dr.get("fc_off", 0)
    ws.load_group(dr["w_out"], 0, 4, 0, kcn=nfc)
    for tc in range(NT):
        cols = slice(tc * TC, (tc + 1) * TC)
        for fc in range(nfc):
            P.dma(C.hT[:, fo + fc, cols], dr["mixT"][fc * 128:(fc + 1) * 128, cols], slot=("mixld", fo, tc),
                  reads=[("mixT", dr.get("mix_tag", 0), fc, tc)], writes=[("hT", fo + fc, tc), ("mixall", fo, tc)],
                  eng="act")
    for g4 in range(4):
        g = g4 % 2
        if g4 + 1 < 4:
            ws.load_group(dr["w_out"], (g4 + 1) * 512, 4, (g4 + 1) % 2, kcn=nfc)
        for s_ in range(4):
            dc = g4 * 4 + s_
            for tc in range(NT):
                cols = slice(tc * TC, (tc + 1) * TC)
                bank = (dc * NT + tc) % 2
                for fc in range(nfc):
                    P.mm(C.ps[bank][:], C.wb[g][:, fc, s_ * 128:(s_ + 1) * 128], C.hT[:, fo + fc, cols],
                         start=(fc == 0), stop=(fc == nfc - 1),
                         reads=[("wb", g, fc // 4), ("mixall", fo, tc), ("hT", fo + fc, tc)], writes=[("ps", bank)])
                ysl = (dc * NT + tc) % 2
                yt = C.T[12 + ysl]
                rows = slice(dc * 128, (dc + 1) * 128)
                ev = dr.get("evac")
                if ev is None:
                    P.act(yt[:], C.ps[bank][:], AF.Copy, reads=[("ps", bank)], writes=[("T", 12 + ysl)])
                    P.dma(dr["yp"][rows, cols], yt[:], slot=("yst", ysl),
                          reads=[("T", 12 + ysl)], writes=[("yp", dr.get("yp_tag", 0), dc, tc)])
                else:
                    k4 = (dc * NT + tc) % 4
                    xin = C.T[8 + k4]
                    P.dma(xin[:], ev["src"][rows, cols], slot=("oin", k4), reads=ev["src_keys"](dc, tc), writes=[("T", 8 + k4)])
                    P.tt(yt[:], xin[:], C.ps[bank][:], ALU.add, reads=[("T", 8 + k4), ("ps", bank)], writes=[("T", 12 + ysl)])
                    P.dma(ev["dst"][rows, cols], yt[:], slot=("yst", ysl),
                          reads=[("T", 12 + ysl)], writes=ev["dst_keys"](dc, tc))
                    if ev.get("stats"):
                        sq = C.B[k4]
                        P.act(sq[:], yt[:], AF.Square, reads=[("T", 12 + ysl)], writes=[("B", k4)])
                        P.mm(C.ps[4 + tc][:], C.ones_b[:], sq[:], start=(dc == 0), stop=(dc == KC - 1),
                             reads=[("B", k4), "ones_b"], writes=[("ps", 4 + tc)])


def emit_outproj_pair(P, C, ws, dr0, dr1, ev):
    S, NT = C.S, C.NT
    halves = (dr0, dr1)
    for hf in range(2):
        ws.load_group(halves[hf]["w_out"], 0, 4, 0, kcn=8, koff=8 * hf)
    for tc in range(NT):
        cols = slice(tc * TC, (tc + 1) * TC)
        for hf in range(2):
            fo = 8 * hf
            for fc in range(8):
                P.dma(C.hT[:, fo + fc, cols], halves[hf]["mixT"][fc * 128:(fc + 1) * 128, cols], slot=("mixld", fo, tc),
                      reads=[("mixT", hf, fc, tc)], writes=[("hT", fo + fc, tc), ("mixall", fo, tc)], eng="act")
    for g4 in range(4):
        g = g4 % 2
        if g4 + 1 < 4:
            for hf in range(2):
                ws.load_group(halves[hf]["w_out"], (g4 + 1) * 512, 4, (g4 + 1) % 2, kcn=8, koff=8 * hf)
        for s_ in range(4):
            dc = g4 * 4 + s_
            rows = slice(dc * 128, (dc + 1) * 128)
            for tc in range(NT):
                cols = slice(tc * TC, (tc + 1) * TC)
                bank = (dc * NT + tc) % 2
                for fc in range(16):
                    P.mm(C.ps[bank][:], C.wb[g][:, fc, s_ * 128:(s_ + 1) * 128], C.hT[:, fc, cols],
                         start=(fc == 0), stop=(fc == 15),
                         reads=[("wb", g, fc // 4), ("mixall", 8 * (fc // 8), tc), ("hT", fc, tc)], writes=[("ps", bank)])
                ysl = (dc * NT + tc) % 2
                yt = C.T[12 + ysl]
                k4 = (dc * NT + tc) % 4
                xin = C.T[8 + k4]
                P.dma(xin[:], ev["src"][rows, cols], slot=("oin", k4), reads=ev["src_keys"](dc, tc), writes=[("T", 8 + k4)])
                P.tt(yt[:], xin[:], C.ps[bank][:], ALU.add, reads=[("T", 8 + k4), ("ps", bank)], writes=[("T", 12 + ysl)])
                P.dma(ev["dst"][rows, cols], yt[:], slot=("yst", ysl), reads=[("T", 12 + ysl)], writes=ev["dst_keys"](dc, tc))
                if ev.get("stats"):
                    sq = C.B[k4]
                    P.act(sq[:], yt[:], AF.Square, reads=[("T", 12 + ysl)], writes=[("B", k4)])
                    P.mm(C.ps[4 + tc][:], C.ones_b[:], sq[:], start=(dc == 0), stop=(dc == KC - 1),
                         reads=[("B", k4), "ones_b"], writes=[("ps", 4 + tc)])


REC_GROUPS = ["rg", "hg", "hg", "rg", "hg", "hg", "rg", "hg", "hg", "rg", "hg", "hg"]


def emit_rec_layer(P, C, dr, do_outproj=True):
    S, NT = C.S, C.NT
    nc = P.nc
    T, B, ps = C.T, C.B, C.ps
    R = C.R
    P.dma(R.convw[:], dr["convw"], slot="p0", reads=[], writes=["convw"])
    P.dma(R.vec[:], dr["recvec"], slot="p1", reads=[], writes=["recvec"])
    P.dma(R.lbraw[:], dr["hglb"], slot="p2", reads=[], writes=["lbraw"])
    P.dma(R.onorm[:], dr["onorm"], slot="p3", reads=[], writes=["onorm"])
    P.dma(R.cmask[:], dr["cmask"], slot="p4", reads=[], writes=["cmask"])
    P.dma(R.bmask[:], dr["bmask"], slot="p5", reads=[], writes=["bmask"])
    P.act(R.sp[:], R.vec[:, :, 3], AF.Exp, reads=["recvec"], writes=["sp"], scale=-1.0)
    P.act(R.sp[:], R.sp[:], AF.Ln, reads=["sp", "one_c"], writes=["sp"], bias=C.one_c[:], scale=1.0)
    P.add("dve", lambda e: e.tensor_scalar_mul(R.cneg[:], R.sp[:], -8.0), ["sp"], ["cneg"])
    P.add("dve", lambda e: e.tensor_scalar_mul(R.cneg2[:], R.sp[:], -16.0), ["sp"], ["cneg2"])
    P.tt(R.lb[:], R.lbraw[:, :, 1], R.lbraw[:, :, 0], ALU.subtract, reads=["lbraw"], writes=["lb"])
    P.act(R.lb[:], R.lb[:], AF.Sigmoid, reads=["lb"], writes=["lb"])
    P.add("dve", lambda e: e.tensor_scalar_mul(R.lb[:], R.lb[:], float(dr["lbflag"])), ["lb"], ["lb"])
    P.ts(R.oml[:], R.lb[:], -1.0, 1.0, ALU.mult, ALU.add, reads=["lb"], writes=["oml"])
    P.memset(R.zero_f[:], 0.0, writes=["zero_f"], eng="dve")
    P.memset(R.zero_b[:], 0.0, writes=["zero_b"], eng="dve")
    P.add("dve", lambda e: e.tensor_scalar_mul(R.vech[:], R.vec[:], 0.5), ["recvec"], ["vech"])
    P.add("dve", lambda e: e.tensor_scalar_mul(R.chalf[:], R.sp[:], -4.0), ["sp"], ["chalf"])
    P.add("dve", lambda e: e.tensor_scalar_mul(R.omh[:], R.oml[:], 0.5), ["oml"], ["omh"])
    P.tt(R.lbp[:], R.omh[:], R.lb[:], ALU.add, reads=["omh", "lb"], writes=["lbp"])

    ws = WStream(P, C)
    groups = REC_GROUPS
    objs = []
    ia = ie = 0
    for gi, gk in enumerate(groups):
        if gk == "rg":
            objs.append(RGGroup(P, C, dr, gi % 2, ia))
            ia += 1
        else:
            objs.append(HGGroup(P, C, dr, gi % 2, ie))
            ie += 1
    seq = [(gi, tc) for gi in range(len(groups)) for tc in range(NT)]
    ws.load_group(dr["w_in"], 0, 4, 0)
    objs[0].inproj(0, 0)
    for k, (gi, tc) in enumerate(seq):
        if tc == 0 and gi + 1 < len(groups):
            ws.load_group(dr["w_in"], (gi + 1) * 512, 4, (gi + 1) % 2)
        if k + 1 < len(seq):
            objs[seq[k + 1][0]].inproj(seq[k + 1][1], (k + 1) % 2)
        objs[gi].mixer(tc, k % 2)
    if do_outproj:
        emit_outproj(P, C, ws, dr, 16)


_BK = [0]
_SL = [0]


def rstd_act(P, out, out_key, in_, in_keys, eps_ap, eps_key, inv_n):
    P.act(out, in_, AF.Ln, reads=list(in_keys) + [eps_key], writes=[out_key], bias=eps_ap, scale=inv_n)
    P.act(out, out, AF.Exp, reads=[out_key], writes=[out_key], scale=-0.5)


def v_tokmajor(P, C, gslot, s, tc, bank, tmp_idx, out_ap, out_keys):
    inproj_fm(P, C, gslot, s, tc, bank)
    vT = C.B[tmp_idx]
    P.act(vT[:], C.ps[bank][:], AF.Copy, reads=[("ps", bank)], writes=[("B", tmp_idx)])
    pv = C.ps[bank][:].bitcast(BF16)
    for tt in range(4):
        P.tr(pv[:, tt * 128:(tt + 1) * 128], vT[:, tt * 128:(tt + 1) * 128], C.ident_b[:],
             reads=[("B", tmp_idx), "ident_b"], writes=[("ps", bank)])
    P.cp(out_ap, pv[:, 0:512].rearrange("p (a b) -> p a b", a=4), reads=[("ps", bank)], writes=out_keys, eng="dve")


def silu2(P, C, out_ap, out_key, bank):
    k = 4 + (_SL[0] % 2)
    _SL[0] += 1
    th = C.T[k]
    P.act(th[:], C.ps[bank][:], AF.Tanh, reads=[("ps", bank)], writes=[("T", k)], scale=0.5)
    P.stt(out_ap, th[:], 1.0, C.ps[bank][:], ALU.add, ALU.mult, reads=[("T", k), ("ps", bank)], writes=[out_key])


def next_bank():
    b = _BK[0] % 2
    _BK[0] += 1
    return b


class RGGroup:
    def __init__(self, P, C, dr, gslot, a):
        self.P, self.C, self.dr, self.gslot, self.a = P, C, dr, gslot, a

    def inproj(self, tc, par):
        P, C, dr, gslot, a = self.P, self.C, self.dr, self.gslot, self.a
        T, B, ps, R = C.T, C.B, C.ps, C.R
        if tc == 0:
            P.dma(R.wgf[:], dr["wgate"][a], slot="wg", reads=[], writes=["wgf"])
            P.act(R.wgb[:], R.wgf[:], AF.Copy, reads=["wgf"], writes=["wgb"])
        xr = R.xraw[par]
        for i in range(2):
            bank = next_bank()
            inproj_fm(P, C, gslot, i, tc, bank)
            if tc == 0:
                P.memset(xr[:, i, 0:3], 0.0, writes=[("xraw", par, i)], eng="dve")
            else:
                P.act(xr[:, i, 0:3], R.xraw[1 - par][:, i, TC:TC + 3], AF.Copy, reads=[("xraw", 1 - par, i)],
                      writes=[("xraw", par, i)])
            P.act(xr[:, i, 3:3 + TC], ps[bank][:], AF.Copy, reads=[("ps", bank)], writes=[("xraw", par, i)])
        for j in range(2):
            bank = next_bank()
            inproj_fm(P, C, gslot, 2 + j, tc, bank)
            silu2(P, C, B[par * 2 + j][:], ("B", par * 2 + j), bank)

    def mixer(self, tc, par):
        P, C, dr, gslot, a = self.P, self.C, self.dr, self.gslot, self.a
        T, B, ps, R = C.T, C.B, C.ps, C.R
        xr = R.xraw[par]
        for i in range(2):
            ch = 2 * a + i
            xc = T[6 + i]
            k = ("T", 6 + i)
            P.act(xc[:], xr[:, i, 3:3 + TC], AF.Identity, reads=[("xraw", par, i), "convw", "recvec"], writes=[k],
                  bias=R.vec[:, ch, 0:1], scale=R.convw[:, ch, 3:4])
            for tap in range(3):
                P.stt(xc[:], xr[:, i, tap:tap + TC], R.convw[:, ch, tap:tap + 1], xc[:], ALU.mult, ALU.add,
                      reads=[("xraw", par, i), "convw", k], writes=[k])
            P.act(B[6 + i][:], xc[:], AF.Copy, reads=[k], writes=[("B", 6 + i)])
        for j in range(2):
            ch = 2 * a + j
            for g in range(2):
                for i in range(2):
                    P.mm(ps[3 + g][:], R.wgb[:, g, i, j * 128:(j + 1) * 128], B[6 + i][:], start=(i == 0), stop=(i == 1),
                         reads=["wgb", ("B", 6 + i)], writes=[("ps", 3 + g)])
            r_, ig, aa, mm_, uu = T[8], T[9], T[10], T[11], T[12]
            P.act(r_[:], ps[3][:], AF.Tanh, reads=[("ps", 3), "vech"], writes=[("T", 8)], bias=R.vech[:, ch, 1:2], scale=0.5)
            P.act(ig[:], ps[4][:], AF.Tanh, reads=[("ps", 4), "vech"], writes=[("T", 9)], bias=R.vech[:, ch, 2:3], scale=0.5)
            P.act(aa[:], r_[:], AF.Exp, reads=[("T", 8), "chalf"], writes=[("T", 10)], scale=R.chalf[:, ch:ch + 1],
                  bias=R.chalf[:, ch:ch + 1])
            P.act(mm_[:], r_[:], AF.Exp, reads=[("T", 8), "cneg"], writes=[("T", 11)], scale=R.cneg[:, ch:ch + 1],
                  bias=R.cneg[:, ch:ch + 1])
            P.act(mm_[:], mm_[:], AF.Ln, reads=[("T", 11), "one_c"], writes=[("T", 11)], bias=C.one_c[:], scale=-1.0)
            P.act(mm_[:], mm_[:], AF.Exp, reads=[("T", 11)], writes=[("T", 11)], scale=0.5)
            P.stt(uu[:], ig[:], 1.0, T[6 + j][:], ALU.add, ALU.mult, reads=[("T", 9), ("T", 6 + j)], writes=[("T", 12)])
            P.stt(uu[:], uu[:], 0.5, mm_[:], ALU.mult, ALU.mult, reads=[("T", 12), ("T", 11)], writes=[("T", 12)])
            hcur = R.h[par][j]
            hk = ("h", par, j)
            if tc == 0:
                init = 0.0
                rd = []
            else:
                init = R.h[1 - par][j][:, TC - 1:TC]
                rd = [("h", 1 - par, j)]
            P.scan(hcur[:], aa[:], uu[:], init, reads=[("T", 10), ("T", 12)] + rd, writes=[hk])
            mo = B[8 + j]
            P.stt(mo[:], hcur[:], 0.5, B[par * 2 + j][:], ALU.mult, ALU.mult, reads=[hk, ("B", par * 2 + j)], writes=[("B", 8 + j)])
            fc = 2 * a + j
            P.dma(dr["mixT"][fc * 128:(fc + 1) * 128, tc * TC:(tc + 1) * TC], mo[:], slot=("mo", j),
                  reads=[("B", 8 + j)], writes=[("mixT", dr.get("mix_tag", 0), fc, tc)])


class HGGroup:
    def __init__(self, P, C, dr, gslot, e):
        self.P, self.C, self.dr, self.gslot, self.e = P, C, dr, gslot, e
        self.nS = 0

    def tiles(self, par):
        T, B = self.C.T, self.C.B
        qi, fi, si, vi = par, 2 + par, 2 * par, 2 * par + 1
        return (T[qi], ("T", qi)), (T[fi], ("T", fi)), (B[si], ("B", si)), (B[vi], ("B", vi))

    def inproj(self, tc, par):
        P, C, gslot = self.P, self.C, self.gslot
        ps = C.ps
        (qs, qk), (ff, fk), (sg, sk), (V, vk) = self.tiles(par)
        bank = next_bank()
        inproj_fm(P, C, gslot, 0, tc, bank)
        silu2(P, C, qs[:], qk, bank)
        bank = next_bank()
        inproj_fm(P, C, gslot, 1, tc, bank)
        silu2(P, C, sg[:], sk, bank)
        bank = next_bank()
        inproj_fm(P, C, gslot, 2, tc, bank)
        P.act(ff[:], ps[bank][:], AF.Tanh, reads=[("ps", bank)], writes=[fk], scale=0.5)
        bank = next_bank()
        v_tokmajor(P, C, gslot, 3, tc, bank, 10 + par, V[:].rearrange("p (a b) -> p a b", a=4), [vk])

    def mixer(self, tc, par):
        P, C, dr, e = self.P, self.C, self.dr, self.e
        T, B, ps, R = C.T, C.B, C.ps, C.R
        HG_SCALE = 128 ** -0.5
        (qs, qk), (ff, fk), (sg, sk), (V, vk) = self.tiles(par)
        lf, bb, eb, enb = T[8], T[9], T[10], T[11]
        P.ts(ff[:], ff[:], R.omh[:, e:e + 1], R.lbp[:, e:e + 1], ALU.mult, ALU.add, reads=[fk, "omh", "lbp"], writes=[fk])
        P.act(lf[:], ff[:], AF.Ln, reads=[fk], writes=[("T", 8)])
        P.scan(bb[:], R.cmask[:], lf[:], 0.0, reads=["cmask", ("T", 8)], writes=[("T", 9)])
        P.act(eb[:], bb[:], AF.Exp, reads=[("T", 9)], writes=[("T", 10)])
        P.act(enb[:], bb[:], AF.Exp, reads=[("T", 9)], writes=[("T", 11)], scale=-1.0)
        P.ts(ff[:], ff[:], -1.0, 1.0, ALU.mult, ALU.add, reads=[fk], writes=[fk])
        Qd, Kd, K2 = B[4], B[5], B[6]
        P.stt(Qd[:], qs[:], HG_SCALE * 0.5, eb[:], ALU.mult, ALU.mult, reads=[qk, ("T", 10)], writes=[("B", 4)])
        P.tt(Kd[:], ff[:], enb[:], ALU.mult, reads=[fk, ("T", 11)], writes=[("B", 5)])
        for c in range(8):
            cs = slice(c * 64, (c + 1) * 64)
            P.stt(K2[:, cs], ff[:, cs], eb[:, c * 64 + 63:c * 64 + 64], enb[:, cs], ALU.mult, ALU.mult,
                  reads=[fk, ("T", 10), ("T", 11)], writes=[("B", 6)])
        for tt in range(4):
            tsl = slice(tt * 128, (tt + 1) * 128)
            ap = tt % 2
            atb = 5 if ap == 0 else 3
            at_ps = ps[atb][:, 0:128]
            P.mm(at_ps, Kd[:, tsl], Qd[:, tsl], start=True, stop=True, reads=[("B", 5), ("B", 4)], writes=[("ps", atb)])
            atm = R.atm[ap]
            P.tt(atm[:], at_ps, R.bmask[:], ALU.mult, reads=[("ps", atb), "bmask"], writes=[("atm", ap)])
            tr_ps = ps[4][:].bitcast(BF16)[:, 0:128]
            P.tr(tr_ps, K2[:, tsl], C.ident_b[:], reads=[("B", 6), "ident_b"], writes=[("ps", 4)])
            k2t = R.k2t[ap]
            P.cp(k2t[:], tr_ps, reads=[("ps", 4)], writes=[("k2t", ap)], eng="dve")
            o_ps = ps[7][:, tsl]
            P.mm(o_ps, V[:, tsl], atm[:], start=True, stop=False, reads=[vk, ("atm", ap)], writes=[("ps", 7)])
            for h2 in range(2):
                nS = self.nS
                c = tt * 2 + h2
                if tc == 0 and c == 0:
                    sb_prev, sbk = R.zero_b, "zero_b"
                    s_prev, spk = R.zero_f, "zero_f"
                else:
                    sb_prev, sbk = R.Sb[(nS - 1) % 4], ("Sb", (nS - 1) % 4)
                    s_prev, spk = R.Sf[(nS - 1) % 4], ("Sf", (nS - 1) % 4)
                cg = slice(tt * 128 + h2 * 64, tt * 128 + (h2 + 1) * 64)
                P.mm(ps[7][:, cg], sb_prev[:], Qd[:, cg], start=False, stop=(h2 == 1),
                     reads=[sbk, ("B", 4)], writes=[("ps", 7)])
                ub = 6 if nS % 2 == 0 else 2
                u_ps = ps[ub][:, 0:128]
                P.mm(u_ps, k2t[h2 * 64:(h2 + 1) * 64, :], V[h2 * 64:(h2 + 1) * 64, tsl], start=True, stop=True,
                     reads=[("k2t", ap), vk], writes=[("ps", ub)])
                s_new = R.Sf[nS % 4]
                P.stt(s_new[:], s_prev[:], eb[:, c * 64 + 63:c * 64 + 64], u_ps, ALU.mult, ALU.add,
                      reads=[spk, ("T", 10), ("ps", ub)], writes=[("Sf", nS % 4)])
                P.cp(R.Sb[nS % 4][:], s_new[:], reads=[("Sf", nS % 4)], writes=[("Sb", nS % 4)], eng="dve")
                self.nS += 1
        osq = B[7]
        P.act(osq[:], ps[7][:], AF.Square, reads=[("ps", 7)], writes=[("B", 7)])
        P.mm(ps[4][:], C.ones_b[:], osq[:], start=True, stop=True, reads=["ones_b", ("B", 7)], writes=[("ps", 4)])
        rs = T[12]
        rstd_act(P, rs[:], ("T", 12), ps[4][:], [("ps", 4)], C.eps_norm[:], "eps_norm", 1.0 / 128)
        ot = T[13]
        P.stt(ot[:], ps[7][:], R.onorm[:, 0:1], rs[:], ALU.mult, ALU.mult, reads=[("ps", 7), "onorm", ("T", 12)], writes=[("T", 13)])
        mo = B[8 + (tc % 2)]
        P.stt(mo[:], ot[:], 0.5, sg[:], ALU.mult, ALU.mult, reads=[("T", 13), sk], writes=[("B", 8 + (tc % 2))])
        fc = 8 + e
        P.dma(dr["mixT"][fc * 128:(fc + 1) * 128, tc * TC:(tc + 1) * TC], mo[:], slot=("mo", tc % 2),
              reads=[("B", 8 + (tc % 2))], writes=[("mixT", dr.get("mix_tag", 0), fc, tc)])


def alloc_rec(nc, st, C, mx=None):
    R = Ctx()
    sb = lambda name, shape, dt: st.enter_context(nc.sbuf_tensor("s_" + name, shape, dt))
    R.convw = sb("convw", [128, 8, 4], F32)
    R.vec = sb("recvec", [128, 8, 4], F32)
    R.lbraw = sb("lbraw", [128, 8, 2], F32)
    R.onorm = sb("onorm", [128, 1], F32)
    if mx is None:
        R.cmask = sb("cmask", [128, TC], F32)
    R.bmask = sb("bmask", [128, 128], F32)
    R.sp = sb("sp", [128, 8], F32)
    R.cneg = sb("cneg", [128, 8], F32)
    R.cneg2 = sb("cneg2", [128, 8], F32)
    R.lb = sb("lb", [128, 8], F32)
    R.oml = sb("oml", [128, 8], F32)
    R.omh = sb("omh", [128, 8], F32)
    R.lbp = sb("lbp", [128, 8], F32)
    R.chalf = sb("chalf", [128, 8], F32)
    R.vech = sb("vech", [128, 8, 4], F32)
    R.zero_f = sb("zero_f", [128, 128], F32)
    R.zero_b = sb("zero_b", [128, 128], BF16)
    R.wgb = sb("wgb", [128, 2, 2, 256], BF16)
    if mx is None:
        R.wgf = sb("wgf", [128, 2, 2, 256], F32)
        R.xraw = [sb(f"xraw{i}", [128, 2, TC + 3], F32) for i in range(2)]
        R.h = [[sb(f"h{p}{j}", [128, TC], F32) for j in range(2)] for p in range(2)]
    else:
        XW = 2 * (TC + 3)
        R.xraw = [mx[:, i * XW:(i + 1) * XW].rearrange("p (a b) -> p a b", a=2) for i in range(2)]
        o = 2 * XW
        R.h = [[mx[:, o + (p * 2 + j) * TC: o + (p * 2 + j + 1) * TC] for j in range(2)] for p in range(2)]
        o += 4 * TC
        R.cmask = mx[:, o:o + TC]
        o += TC
        R.wgf = mx[:, o:o + 1024].rearrange("p (g i c) -> p g i c", g=2, i=2)
    R.atm = [sb(f"atm{i}", [128, 128], BF16) for i in range(2)]
    R.k2t = [sb(f"k2t{i}", [128, 128], BF16) for i in range(2)]
    R.Sf = [sb(f"Sf{i}", [128, 128], F32) for i in range(4)]
    R.Sb = [sb(f"Sb{i}", [128, 128], BF16) for i in range(4)]
    C.R = R


def emit_consts(P, C, dr):
    load_const_bf16(P, C, C.ones_b, dr["ones"], "ones_b")
    load_const_bf16(P, C, C.ident_b, dr["ident"], "ident_b")
    P.memset(C.eps_norm[:], NORM_EPS, writes=["eps_norm"])
    P.memset(C.one_c[:], 1.0, writes=["one_c"])


def build_rec_program(S, n_part, lbflag):
    nc = bass.Bass("TRN2", target_bir_lowering=False)
    di = lambda name, shape, dt=F32: nc.dram_tensor(name, shape, dt, kind="ExternalInput").ap()
    do = lambda name, shape, dt=F32: nc.dram_tensor(name, shape, dt, kind="ExternalOutput").ap()
    dr = {}
    dr["xT"] = di("xT", [D, S])
    dr["parts"] = [di(f"part{i}", [D, S]) for i in range(n_part)]
    dr["gnorm"] = di("gnorm", [128, KC])
    dr["w_in"] = di("w_in", [D, 6144])
    dr["w_out"] = di("w_out", [2048, D])
    dr["convw"] = di("convw", [128, 8, 4])
    dr["recvec"] = di("recvec", [128, 8, 4])
    dr["hglb"] = di("hglb", [128, 8, 2])
    dr["onorm"] = di("onorm", [128, 1])
    dr["wgate"] = di("wgate", [4, 128, 2, 2, 256])
    dr["cmask"] = di("cmask", [128, TC])
    dr["bmask"] = di("bmask", [128, 128])
    dr["ones"] = di("ones", [128, 128])
    dr["ident"] = di("ident", [128, 128])
    dr["lbflag"] = lbflag
    if n_part:
        dr["xout"] = do("xout", [D, S])
    dr["yp"] = do("yp", [D, S])
    dr["mixT"] = nc.dram_tensor("mixT", [2048, S], BF16, kind="Internal").ap()
    with ExitStack() as st:
        C = alloc_common(nc, st, S, "rec")
        C.eps_norm = st.enter_context(nc.sbuf_tensor("s_eps_norm", [128, 1], F32))
        C.one_c = st.enter_context(nc.sbuf_tensor("s_one_c", [128, 1], F32))
        alloc_rec(nc, st, C)
        P = Prog(nc)
        emit_consts(P, C, dr)
        emit_prologue(P, C, dr, n_part, bool(n_part))
        emit_rec_layer(P, C, dr)
        n = P.emit()
    return nc, n


def pvec(v):
    v = np.asarray(v, np.float32)
    return np.ascontiguousarray(v.reshape(-1, 128).T)


def rec_layout(inp, j, r):
    w_in = inp["rec_w_in"][j]
    w_out = inp["rec_w_out"][j]
    rg_heads = [4 * r + a for a in range(4)]
    hg_heads = [8 * r + e for e in range(8)]
    cols = []
    ia = ie = 0
    for gk in REC_GROUPS:
        if gk == "rg":
            hh = rg_heads[ia]; ia += 1
            cols.append(np.arange(256 * hh, 256 * hh + 256))
            cols.append(np.arange(2048 + 256 * hh, 2048 + 256 * hh + 256))
        else:
            e = hg_heads[ie]; ie += 1
            cols.append(np.arange(4096 + 128 * e, 4096 + 128 * e + 128))
            cols.append(np.arange(10240 + 128 * e, 10240 + 128 * e + 128))
            cols.append(np.arange(6144 + 128 * e, 6144 + 128 * e + 128))
            cols.append(np.arange(8192 + 128 * e, 8192 + 128 * e + 128))
    cols = np.concatenate(cols)
    rows = []
    for hh in rg_heads:
        rows.append(np.arange(256 * hh, 256 * hh + 256))
    for e in hg_heads:
        rows.append(np.arange(2048 + 128 * e, 2048 + 128 * e + 128))
    rows = np.concatenate(rows)
    ch = np.concatenate([np.arange(256 * hh, 256 * hh + 256) for hh in rg_heads])
    convw = np.ascontiguousarray(inp["rg_conv_w"][j][:, ch].T.reshape(8, 128, 4).transpose(1, 0, 2))
    vec = np.stack([inp["rg_conv_b"][j][ch], inp["rg_b_gate_a"][j][ch], inp["rg_b_gate_x"][j][ch],
                    inp["rg_lambda"][j][ch]], axis=-1)
    vec = np.ascontiguousarray(vec.reshape(8, 128, 4).transpose(1, 0, 2))
    hch = np.concatenate([np.arange(128 * e, 128 * e + 128) for e in hg_heads])
    lbr = np.stack([inp["hg_lb"][0][hch], inp["hg_lb"][j][hch]], axis=-1)
    lbr = np.ascontiguousarray(lbr.reshape(8, 128, 2).transpose(1, 0, 2))
    wg = np.stack([inp["rg_w_gate_a"][j][rg_heads], inp["rg_w_gate_x"][j][rg_heads]], axis=1)
    wg = wg.reshape(4, 2, 2, 128, 256).transpose(0, 3, 1, 2, 4)
    return dict(
        w_in=np.ascontiguousarray(w_in[:, cols]),
        w_out=np.ascontiguousarray(w_out[rows, :]),
        gnorm=pvec(inp["rec_norm"][j]),
        convw=convw.astype(np.float32), recvec=vec.astype(np.float32), hglb=lbr.astype(np.float32),
        onorm=np.ascontiguousarray(inp["hg_out_norm"][j].reshape(128, 1).astype(np.float32)),
        wgate=np.ascontiguousarray(wg.astype(np.float32)),
    )


def alloc_att(nc, st, C, mx=None):
    A = Ctx()
    S = C.S
    sb = lambda name, shape, dt: st.enter_context(nc.sbuf_tensor("s_" + name, shape, dt))
    if mx is None:
        A.KT = sb("KT", [128, 2, S], BF16)
        A.Vt = sb("Vt", [128, S // 128, 264], BF16)
    else:
        A.KT = mx[:, 0:S].bitcast(BF16).rearrange("p (a s) -> p a s", a=2)
        nv = (S // 128) * 132
        A.Vt = mx[:, S:S + nv].bitcast(BF16).rearrange("p (t c) -> p t c", c=264)
    A.cosT = sb("cosT", [128, S], F32)
    A.sinT = sb("sinT", [128, S], F32)
    A.pmat = sb("pmat", [128, 128], F32)
    A.ones_f = sb("ones_f", [128, 128], F32)
    A.qkg = sb("qkg", [128, 2], F32)
    A.subg = sb("subg", [128, 256], F32)
    A.lp = sb("lp", [128, 4], F32)
    A.pr = sb("pr", [128, 2], F32)
    A.ex = sb("ex", [128, 2], F32)
    A.neglam = sb("neglam", [128, 1], F32)
    A.tri2f = sb("tri2f", [128, 256], F32)
    A.tri2 = sb("tri2", [128, 256], BF16)
    A.pt = [sb(f"pt{i}", [128, 256], BF16) for i in range(2)]
    A.o = sb("o", [128, 256], F32)
    A.junk = sb("junk", [128, 256], F32)
    A.on = sb("on", [128, 256], BF16)
    A.sm = sb("sm", [128, 8], F32)
    A.eps_sub = sb("eps_sub", [128, 1], F32)
    C.A = A


def emit_att_layer(P, C, dr, lam_init, do_outproj=True):
    S, NT = C.S, C.NT
    T, B, ps, A = C.T, C.B, C.ps, C.A
    SCALE = 128 ** -0.5
    P.dma(A.cosT[:], dr["cosT"], slot="a0", reads=[], writes=["cosT"])
    P.dma(A.sinT[:], dr["sinT"], slot="a1", reads=[], writes=["sinT"])
    P.dma(A.pmat[:], dr["pmat"], slot="a2", reads=[], writes=["pmat"])
    P.dma(A.ones_f[:], dr["ones"], slot="a3", reads=[], writes=["ones_f"])
    P.dma(A.qkg[:], dr["qkg"], slot="a4", reads=[], writes=["qkg"])
    P.dma(A.subg[:], dr["subg"], slot="a5", reads=[], writes=["subg"])
    P.dma(A.lp[:], dr["lp"], slot="a6", reads=[], writes=["lp"])
    P.dma(A.tri2f[:], dr["tri2"], slot="a7", reads=[], writes=["tri2f"])
    P.cp(A.tri2[:], A.tri2f[:], reads=["tri2f"], writes=["tri2"], eng="dve")
    P.memset(A.eps_sub[:], SUBLN_EPS, writes=["eps_sub"], eng="dve")
    P.add("dve", lambda e: e.tensor_scalar_mul(A.subg[:], A.subg[:], 1.0 - lam_init), ["subg"], ["subg"])
    P.tt(A.pr[:, 0:1], A.lp[:, 0:1], A.lp[:, 1:2], ALU.mult, reads=["lp"], writes=["pr"])
    P.tt(A.pr[:, 1:2], A.lp[:, 2:3], A.lp[:, 3:4], ALU.mult, reads=["lp", "pr"], writes=["pr"])
    P.mm(ps[2][:, 0:2], A.ones_f[:], A.pr[:], start=True, stop=True, reads=["ones_f", "pr"], writes=[("ps", 2)])
    P.act(A.ex[:], ps[2][:, 0:2], AF.Exp, reads=[("ps", 2)], writes=["ex"])
    P.tt(A.neglam[:], A.ex[:, 1:2], A.ex[:, 0:1], ALU.subtract, reads=["ex"], writes=["neglam"])
    P.add("dve", lambda e: e.tensor_scalar_add(A.neglam[:], A.neglam[:], -lam_init), ["neglam"], ["neglam"])
    P.memset(A.Vt[:, :, 256:257], 1.0, writes=["Vt_ones"], eng="dve")

    ws = WStream(P, C)
    ws.load_group(dr["w_in"], 0, 4, 0)
    ngroups = 8
    bkc = [0]

    def nb():
        b = bkc[0] % 2
        bkc[0] += 1
        return b

    def norm_rope(bank, gcol, out_ap, out_keys, tc):
        cols = slice(tc * TC, (tc + 1) * TC)
        raw, rstd, t1, t2, sq = T[6], T[7], T[8], T[9], B[2]
        P.act(raw[:], ps[bank][:], AF.Copy, reads=[("ps", bank)], writes=[("T", 6)])
        P.act(sq[:], ps[bank][:], AF.Square, reads=[("ps", bank)], writes=[("B", 2)])
        P.mm(ps[2][:], C.ones_b[:], sq[:], start=True, stop=True, reads=["ones_b", ("B", 2)], writes=[("ps", 2)])
        rstd_act(P, rstd[:], ("T", 7), ps[2][:], [("ps", 2)], C.eps_norm[:], "eps_norm", 1.0 / 128)
        P.stt(raw[:], raw[:], A.qkg[:, gcol:gcol + 1], rstd[:], ALU.mult, ALU.mult, reads=[("T", 6), "qkg", ("T", 7)], writes=[("T", 6)])
        P.mm(ps[3][:], A.pmat[:], raw[:], start=True, stop=True, reads=["pmat", ("T", 6)], writes=[("ps", 3)])
        P.tt(t1[:], raw[:], A.cosT[:, cols], ALU.mult, reads=[("T", 6), "cosT"], writes=[("T", 8)])
        P.tt(t2[:], ps[3][:], A.sinT[:, cols], ALU.mult, reads=[("ps", 3), "sinT"], writes=[("T", 9)])
        P.tt(out_ap, t1[:], t2[:], ALU.add, reads=[("T", 8), ("T", 9)], writes=out_keys)

    for gi in range(ngroups):
        gslot = gi % 2
        a = gi // 2
        if gi + 1 < ngroups:
            ws.load_group(dr["w_in"], (gi + 1) * 512, 4, (gi + 1) % 2)
        if gi % 2 == 0:
            for tc in range(NT):
                cols = slice(tc * TC, (tc + 1) * TC)
                for i in range(2):
                    bank = nb()
                    inproj_fm(P, C, gslot, i, tc, bank)
                    norm_rope(bank, 1, A.KT[:, i, cols], [("KT", i, tc)], tc)
                for half in range(2):
                    bank = nb()
                    tile0 = tc * 4
                    v_tokmajor(P, C, gslot, 2 + half, tc, bank, 10 + half, A.Vt[:, tile0:tile0 + 4, half * 128:(half + 1) * 128],
                               [("Vt", tile0 + q_, half) for q_ in range(4)])
        else:
            for tc in range(NT):
                cols = slice(tc * TC, (tc + 1) * TC)
                Qt = [B[4], B[5]]
                sgt = [B[6], B[7]]
                for i in range(2):
                    bank = nb()
                    inproj_fm(P, C, gslot, i, tc, bank)
                    norm_rope(bank, 0, Qt[i][:], [("B", 4 + i)], tc)
                for i in range(2):
                    bank = nb()
                    inproj_fm(P, C, gslot, 2 + i, tc, bank)
                    silu2(P, C, sgt[i][:], ("B", 6 + i), bank)
                for qt in range(4):
                    j = tc * 4 + qt
                    qs = slice(qt * 128, (qt + 1) * 128)

                    def emit_qk(i):
                        sb_ = 4 if i % 2 == 0 else 7
                        ks = slice(i * 128, (i + 1) * 128)
                        for m in range(2):
                            P.mm(ps[sb_][:, m * 128:(m + 1) * 128], A.KT[:, m, ks], Qt[m][:, qs], start=True, stop=True,
                                 reads=[("KT", m, i // 4), ("B", 4 + m)], writes=[("ps", sb_)])
                        pt = A.pt[i % 2]
                        P.act(pt[:], ps[sb_][:, 0:256], AF.Exp, reads=[("ps", sb_)], writes=[("pt", i % 2)], scale=SCALE)
                        if i == j:
                            P.tt(pt[:], pt[:], A.tri2[:], ALU.mult, reads=[("pt", i % 2), "tri2"], writes=[("pt", i % 2)])

                    def emit_pv(i):
                        pt = A.pt[i % 2]
                        for m in range(2):
                            P.mm(ps[5 + m][:, 0:257], pt[:, m * 128:(m + 1) * 128], A.Vt[:, i, 0:257],
                                 start=(i == 0), stop=(i == j),
                                 reads=[("pt", i % 2), ("Vt", i, 0), ("Vt", i, 1), "Vt_ones"], writes=[("ps", 5 + m)])

                    emit_qk(0)
                    for i in range(j + 1):
                        if i + 1 <= j:
                            emit_qk(i + 1)
                        emit_pv(i)
                    sm = A.sm
                    P.recip(sm[:, 0:1], ps[5][:, 256:257], reads=[("ps", 5)], writes=["sm"])
                    P.recip(sm[:, 1:2], ps[6][:, 256:257], reads=[("ps", 6), "sm"], writes=["sm"])
                    P.tt(sm[:, 2:3], sm[:, 1:2], A.neglam[:], ALU.mult, reads=["sm", "neglam"], writes=["sm"])
                    P.add("dve", lambda e, sm=sm: e.tensor_scalar_mul(A.o[:], ps[5][:, 0:256], sm[:, 0:1]), [("ps", 5), "sm"], ["o"])
                    P.stt(A.o[:], ps[6][:, 0:256], sm[:, 2:3], A.o[:], ALU.mult, ALU.add, reads=[("ps", 6), "sm", "o"], writes=["o"])
                    P.add("act", lambda e, sm=sm: e.activation(A.junk[:], A.o[:], AF.Square, accum_out=sm[:, 3:4]), ["o"], ["junk", "sm3"])
                    rstd_act(P, sm[:, 5:6], "sm5", sm[:, 3:4], ["sm3"], A.eps_sub[:], "eps_sub", 1.0 / 256)
                    P.stt(A.on[:], A.o[:], sm[:, 5:6], A.subg[:], ALU.mult, ALU.mult, reads=["o", "sm5", "subg"], writes=["on"])
                    trv = ps[3][:].bitcast(BF16)
                    for m in range(2):
                        P.tr(trv[:, m * 128:(m + 1) * 128], A.on[:, m * 128:(m + 1) * 128], C.ident_b[:],
                             reads=["on", "ident_b"], writes=[("ps", 3)])
                    for m in range(2):
                        P.stt(B[8 + m][:, qs], trv[:, m * 128:(m + 1) * 128], 0.5, sgt[m][:, qs], ALU.mult, ALU.mult,
                              reads=[("ps", 3), ("B", 6 + m)], writes=[("B", 8 + m)])
                for m in range(2):
                    fc = 2 * a + m
                    P.dma(dr["mixT"][fc * 128:(fc + 1) * 128, cols], B[8 + m][:], slot=("mo", m),
                          reads=[("B", 8 + m)], writes=[("mixT", dr.get("mix_tag", 0), fc, tc)])
    if do_outproj:
        emit_outproj(P, C, ws, dr, 8)


def build_att_program(S, n_part, layer):
    nc = bass.Bass("TRN2", target_bir_lowering=False)
    di = lambda name, shape, dt=F32: nc.dram_tensor(name, shape, dt, kind="ExternalInput").ap()
    do = lambda name, shape, dt=F32: nc.dram_tensor(name, shape, dt, kind="ExternalOutput").ap()
    dr = {}
    dr["xT"] = di("xT", [D, S])
    dr["parts"] = [di(f"part{i}", [D, S]) for i in range(n_part)]
    dr["gnorm"] = di("gnorm", [128, KC])
    dr["w_in"] = di("w_in", [D, 4096])
    dr["w_out"] = di("w_out", [1024, D])
    dr["cosT"] = di("cosT", [128, S])
    dr["sinT"] = di("sinT", [128, S])
    dr["pmat"] = di("pmat", [128, 128])
    dr["qkg"] = di("qkg", [128, 2])
    dr["subg"] = di("subg", [128, 256])
    dr["lp"] = di("lp", [128, 4])
    dr["tri2"] = di("tri2", [128, 256])
    dr["ones"] = di("ones", [128, 128])
    dr["ident"] = di("ident", [128, 128])
    if n_part:
        dr["xout"] = do("xout", [D, S])
    dr["yp"] = do("yp", [D, S])
    dr["mixT"] = nc.dram_tensor("mixT", [1024, S], BF16, kind="Internal").ap()
    lam_init = 0.8 - 0.6 * math.exp(-0.3 * layer)
    with ExitStack() as st:
        C = alloc_common(nc, st, S, "att")
        C.eps_norm = st.enter_context(nc.sbuf_tensor("s_eps_norm", [128, 1], F32))
        C.one_c = st.enter_context(nc.sbuf_tensor("s_one_c", [128, 1], F32))
        alloc_att(nc, st, C)
        P = Prog(nc)
        emit_consts(P, C, dr)
        emit_prologue(P, C, dr, n_part, bool(n_part))
        emit_att_layer(P, C, dr, lam_init)
        n = P.emit()
    return nc, n


def att_layout(inp, j, r):
    w_in = inp["att_w_in"][j]
    w_out = inp["att_w_out"][j]
    heads = [4 * r + a for a in range(4)]
    cols = []
    for h in heads:
        cols.append(np.arange(2048 + 256 * h, 2048 + 256 * h + 256))
        cols.append(np.arange(4096 + 256 * h, 4096 + 256 * h + 256))
        cols.append(np.arange(256 * h, 256 * h + 256))
        cols.append(np.arange(6144 + 256 * h, 6144 + 256 * h + 256))
    cols = np.concatenate(cols)
    rows = np.concatenate([np.arange(256 * h, 256 * h + 256) for h in heads])
    qkg = np.stack([inp["att_q_norm"][j], inp["att_k_norm"][j]], axis=-1).astype(np.float32)
    subg = np.ascontiguousarray(np.broadcast_to(inp["att_sub_norm"][j][None, :], (128, 256)).astype(np.float32))
    lp = np.ascontiguousarray(inp["att_lambda"][j].T.astype(np.float32))
    return dict(
        w_in=np.ascontiguousarray(w_in[:, cols]),
        w_out=np.ascontiguousarray(w_out[rows, :]),
        gnorm=pvec(inp["att_norm"][j]),
        qkg=np.ascontiguousarray(qkg), subg=subg, lp=lp,
    )


def build_final_program(S):
    nc = bass.Bass("TRN2", target_bir_lowering=False)
    di = lambda name, shape, dt=F32: nc.dram_tensor(name, shape, dt, kind="ExternalInput").ap()
    x = di("xT", [1024, S])
    p0 = di("part0", [1024, S])
    p1 = di("part1", [1024, S])
    out = nc.dram_tensor("xout", [1024, S], F32, kind="ExternalOutput").ap()
    with ExitStack() as st:
        tl = [[st.enter_context(nc.sbuf_tensor(f"s_f{k}{i}", [128, S], F32)) for i in range(2)] for k in range(3)]
        P = Prog(nc)
        for c in range(8):
            sl = c % 2
            rows = slice(c * 128, (c + 1) * 128)
            xa, pa, pb = tl[0][sl], tl[1][sl], tl[2][sl]
            P.dma(xa[:], x[rows, :], slot=("fx", sl), reads=[], writes=[("fx", sl)])
            P.dma(pa[:], p0[rows, :], slot=("fa", sl), reads=[], writes=[("fa", sl)])
            P.dma(pb[:], p1[rows, :], slot=("fb", sl), reads=[], writes=[("fb", sl)])
            P.tt(xa[:], xa[:], pa[:], ALU.add, reads=[("fx", sl), ("fa", sl)], writes=[("fx", sl)])
            P.tt(xa[:], xa[:], pb[:], ALU.add, reads=[("fx", sl), ("fb", sl)], writes=[("fx", sl)])
            P.dma(out[rows, :], xa[:], slot=("fo", sl), reads=[("fx", sl)], writes=[("out", c)])
        P.emit()
    return nc


REC_KEYS = ("w_in", "w_out", "gnorm", "convw", "recvec", "hglb", "onorm", "wgate")
ATT_KEYS = ("w_in", "w_out", "gnorm", "qkg", "subg", "lp")
CONST_KEYS = ("ones", "ident", "cmask", "bmask", "cosT", "sinT", "pmat", "tri2")
ACTIVE_CORES = (0, 1, 2, 3)
NCORES = 4


def build_fused_program(S, nlayers=4, nseq=1):
    nc = bass.Bass("TRN2", target_bir_lowering=False)
    di = lambda name, shape, dt=F32: nc.dram_tensor(name, shape, dt, kind="ExternalInput").ap()
    x_ins = [di("xT" if q == 0 else f"xT{q}", [D, S]) for q in range(nseq)]
    outs = [nc.dram_tensor("outT" if q == 0 else f"outT{q}", [D, S], F32, kind="ExternalOutput").ap() for q in range(nseq)]
    yp = [nc.dram_tensor(f"yp{i}", [D, S], F32, kind="Internal").ap() for i in range(2)]
    xb = [nc.dram_tensor(f"xbuf{i}", [D, S], F32, kind="Internal").ap() for i in range(2)]
    mixTs = [nc.dram_tensor(f"mixT{i}", [2048, S], BF16, kind="Internal").ap() for i in range(2)]
    cshape = dict(ones=[128, 128], ident=[128, 128], cmask=[128, TC], bmask=[128, 128], cosT=[128, S], sinT=[128, S],
                  pmat=[128, 128], tri2=[128, 256])
    cd = {k: di(k, cshape[k]) for k in CONST_KEYS}
    rshape = dict(w_in=[D, 6144], w_out=[2048, D], gnorm=[128, KC], convw=[128, 8, 4], recvec=[128, 8, 4], hglb=[128, 8, 2],
                  onorm=[128, 1], wgate=[4, 128, 2, 2, 256])
    ashape = dict(w_in=[D, 4096], w_out=[1024, D], gnorm=[128, KC], qkg=[128, 2], subg=[128, 256], lp=[128, 4])
    drs = []
    for L in range(nlayers):
        sh = rshape if L % 2 == 0 else ashape
        halves = []
        for hf in range(2):
            dr = {k: di(f"{k}{L}h{hf}", sh[k]) for k in sh if not (k == "gnorm" and hf == 1)}
            if hf == 1:
                dr["gnorm"] = halves[0]["gnorm"]
            dr.update(cd)
            dr["mixT"] = mixTs[hf] if L % 2 == 0 else mixTs[hf][0:1024, :]
            dr["mix_tag"] = hf
            dr["yp"] = yp[hf]
            dr["yp_tag"] = hf
            dr["fc_off"] = 8 * hf if L % 2 == 1 else 0
            dr["lbflag"] = 0.0 if L == 0 else 1.0
            halves.append(dr)
        drs.append(halves)
    with ExitStack() as st:
        C = alloc_common(nc, st, S, "all")
        C.eps_norm = st.enter_context(nc.sbuf_tensor("s_eps_norm", [128, 1], F32))
        C.one_c = st.enter_context(nc.sbuf_tensor("s_one_c", [128, 1], F32))
        mx = st.enter_context(nc.sbuf_tensor("s_mx", [128, 5644], F32))
        alloc_rec(nc, st, C, mx)
        alloc_att(nc, st, C, mx)
        P = Prog(nc)
        emit_consts(P, C, cd)
        for q in range(nseq):
          x_in, out = x_ins[q], outs[q]
          for L in range(nlayers):
              x_cur = x_in if L == 0 else xb[(L - 1) % 2]
              x_next = out if L == nlayers - 1 else xb[L % 2]
              pr = dict(gnorm=drs[L][0]["gnorm"], xT=x_cur, parts=[])
              emit_prologue(P, C, pr, 0, False, stats_ready=(L > 0))
              P.barrier()
              for hf in range(2):
                  if L % 2 == 0:
                      emit_rec_layer(P, C, drs[L][hf], do_outproj=False)
                  else:
                      emit_att_layer(P, C, drs[L][hf], 0.8 - 0.6 * math.exp(-0.3 * L), do_outproj=False)
              xk = (lambda dc, tc: [("xout", dc, tc)])
              yk = (lambda dc, tc: [("yp", 0, dc, tc)])
              drs[L][0]["evac"] = dict(src=x_cur, src_keys=(xk if L > 0 else (lambda dc, tc: [])), dst=yp[0], dst_keys=yk)
              drs[L][1]["evac"] = dict(src=yp[0], src_keys=yk, dst=x_next, dst_keys=xk, stats=(L < nlayers - 1))
              if L % 2 == 0:
                  for hf in range(2):
                      emit_outproj(P, C, WStream(P, C), drs[L][hf], 16)
              else:
                  emit_outproj_pair(P, C, WStream(P, C), drs[L][0], drs[L][1],
                                    dict(src=x_cur, src_keys=xk, dst=x_next, dst_keys=xk, stats=(L < nlayers - 1)))
          P.barrier()
        n = P.emit()
    return nc, n


def fused_maps(inp, S=SEQ, nlayers=4, ncores=2, nseq=2):
    cs = host_consts(S)
    x = np.asarray(inp["x"], np.float32)
    wmap = {}
    for L in range(nlayers):
        j = L // 2
        for hf in range(2):
            lay = rec_layout(inp, j, hf) if L % 2 == 0 else att_layout(inp, j, hf)
            for k, v in lay.items():
                if k == "gnorm" and hf == 1:
                    continue
                wmap[f"{k}{L}h{hf}"] = v
    maps = []
    for c in range(ncores):
        m = {}
        for q in range(nseq):
            b = c * nseq + q
            m["xT" if q == 0 else f"xT{q}"] = np.ascontiguousarray(x[b][:S].T)
        m.update(wmap)
        for k in CONST_KEYS:
            m[k] = cs[k]
        maps.append(m)
    return maps


_PROGS = {}
NCORES = 2
NSEQ = 2


def kernel(**inp):
    inp = {k: np.asarray(v) for k, v in inp.items()}
    if "fused" not in _PROGS:
        _PROGS["fused"] = build_fused_program(SEQ, 4, NSEQ)[0]
    nc = _PROGS["fused"]
    maps = fused_maps(inp, SEQ, 4, NCORES, NSEQ)
    res = run_bass_kernel_spmd(nc, maps, core_ids=list(range(NCORES)))
    out = np.empty((BATCH, SEQ, D), np.float32)
    for c in range(NCORES):
        for q in range(NSEQ):
            out[c * NSEQ + q] = np.asarray(res.results[c]["outT" if q == 0 else f"outT{q}"]).T
    return out
```

```python
import math
from contextlib import ExitStack

import numpy as np
import concourse.bass as bass
import concourse.mybir as mybir
from concourse.bass_utils import run_bass_kernel_spmd

F32 = mybir.dt.float32
BF16 = mybir.dt.bfloat16
AF = mybir.ActivationFunctionType
ALU = mybir.AluOpType

D = 2048
KC = 16
SEQ = 2048
BATCH = 4
TC = 512
NORM_EPS = 1e-6
SUBLN_EPS = 1e-5
SAME_ENGINE_SYNC = True


def _fs(ap):
    try:
        return int(ap.free_size())
    except Exception:
        return 512


RESCHEDULE = True


class Prog:
    ENGS = ("pe", "act", "dve", "pool", "sp")

    def __init__(self, nc):
        self.nc = nc
        self.ops = []

    def add(self, eng, fn, reads=(), writes=(), dma=None, cost=0.5):
        self.ops.append(dict(kind="op", eng=eng, fn=fn, reads=tuple(reads), writes=tuple(writes), dma=dma, cost=cost))

    def barrier(self):
        self.ops.append(dict(kind="barrier"))

    def mm(self, out, lhsT, rhs, start, stop, reads, writes):
        self.add("pe", lambda e: e.matmul(out, lhsT, rhs, start=start, stop=stop), reads, writes,
                 cost=0.035 + _fs(rhs) / 2400.0)

    def tr(self, out, in_, ident, reads, writes):
        self.add("pe", lambda e: e.transpose(out, in_, ident), reads, writes, cost=0.1)

    def act(self, out, in_, func, reads, writes, bias=0.0, scale=1.0, eng="act"):
        self.add(eng, lambda e: e.activation(out, in_, func, bias=bias, scale=scale), reads, writes,
                 cost=0.2 + _fs(out) * 0.0008)
        self.ops[-1]["tset"] = "tanh" if func == AF.Tanh else ("ln" if func == AF.Ln else None)

    def ts(self, out, in0, s1, s2, op0, op1, reads, writes, eng="dve"):
        self.add(eng, lambda e: e.tensor_scalar(out, in0, s1, s2, op0, op1), reads, writes, cost=0.15 + _fs(out) * 0.001)

    def stt(self, out, in0, scalar, in1, op0, op1, reads, writes, eng="dve"):
        self.add(eng, lambda e: e.scalar_tensor_tensor(out, in0, scalar, in1, op0, op1), reads, writes,
                 cost=0.2 + _fs(out) * 0.0011)

    def tt(self, out, in0, in1, op, reads, writes, eng="dve"):
        c = 0.15 + _fs(out) * (0.001 if eng == "dve" else 0.0022)
        self.add(eng, lambda e: e.tensor_tensor(out, in0, in1, op), reads, writes, cost=c)

    def cp(self, out, in_, reads, writes, eng="dve"):
        c = 0.15 + _fs(out) * 0.001
        self.add(eng, lambda e: e.tensor_copy(out, in_), reads, writes, cost=c)

    def memset(self, ap, val, writes, eng="dve"):
        self.add(eng, lambda e: e.memset(ap, val), (), writes, cost=0.2)

    def recip(self, out, in_, reads, writes):
        self.add("dve", lambda e: e.reciprocal(out, in_), reads, writes, cost=0.15 + _fs(out) * 0.001)

    def scan(self, out, d0, d1, init, reads, writes):
        self.add("dve", lambda e: e.tensor_tensor_scan(out, d0, d1, init, ALU.mult, ALU.add), reads, writes,
                 cost=0.2 + _fs(out) * 0.002)

    def dma(self, out, in_, slot, reads, writes, eng="sp"):
        try:
            nbytes = int(out.nbytes())
        except Exception:
            nbytes = 1 << 18
        self.add(eng, lambda e: e.dma_start(out, in_), reads, writes, dma=slot, cost=3.0 + nbytes / 90e3)

    def emit(self):
        nc = self.nc
        real = []
        seg_of = []
        seg = 0
        last_w = {}
        readers = {}
        for o in self.ops:
            if o["kind"] == "barrier":
                seg += 1
                continue
            o["idx"] = len(real)
            real.append(o)
            o["seg"] = seg
            deps = set()
            for r in o["reads"]:
                if r in last_w:
                    deps.add(last_w[r])
            for w in o["writes"]:
                if w in last_w:
                    deps.add(last_w[w])
                deps.update(readers.get(w, ()))
            deps.discard(o["idx"])
            for w in o["writes"]:
                last_w[w] = o["idx"]
                readers[w] = []
            for r in o["reads"]:
                if r not in o["writes"]:
                    readers.setdefault(r, []).append(o["idx"])
            o["skey"] = ("dma", o["dma"]) if o["dma"] is not None else ("eng", o["eng"])
            o["deps"] = [d for d in deps if real[d]["seg"] == seg]
            o["signal"] = False
        nseg = seg + 1
        import heapq
        order = {e: [] for e in self.ENGS}
        seg_ops = [[] for _ in range(nseg)]
        for o in real:
            seg_ops[o["seg"]].append(o["idx"])
        fin = [0.0] * len(real)
        tnow = 0.0
        LAT = 0.3
        for sg in range(nseg):
            ids = seg_ops[sg]
            if not RESCHEDULE:
                for i in ids:
                    order[real[i]["eng"]].append(i)
                continue
            nd = {i: len(real[i]["deps"]) for i in ids}
            users = {i: [] for i in ids}
            for i in ids:
                for d in real[i]["deps"]:
                    users[d].append(i)
            ready = {e: [] for e in self.ENGS}
            rt = {}
            for i in ids:
                if nd[i] == 0:
                    rt[i] = tnow
                    heapq.heappush(ready[real[i]["eng"]], i)
            free = {e: tnow for e in self.ENGS}
            cur_set = [None]
            sp_list = [i for i in ids if real[i]["eng"] == "sp"]
            sp_pos = 0
            remaining = len(ids)
            while remaining:
                best = None
                for e in self.ENGS:
                    h = ready[e]
                    if not h:
                        continue
                    if e == "sp":
                        if sp_pos < len(sp_list) and nd[sp_list[sp_pos]] == 0:
                            i = sp_list[sp_pos]
                            st_ = max(free[e], rt[i])
                            cand = (st_, i, e)
                        else:
                            continue
                    else:
                        cands = heapq.nsmallest(12 if e == "act" else 6, h)
                        cand = None
                        for i in cands:
                            st_ = max(free[e], rt[i])
                            if e == "act":
                                ts_ = real[i].get("tset")
                                if ts_ is not None and cur_set[0] is not None and ts_ != cur_set[0]:
                                    st_ += 1.3
                            if cand is None or st_ < cand[0] - 1e-9:
                                cand = (st_, i, e)
                    if cand is not None and (best is None or cand[0] < best[0] - 1e-9 or
                                             (abs(cand[0] - best[0]) <= 1e-9 and cand[1] < best[1])):
                        best = cand
                if best is None:
                    raise RuntimeError("scheduler deadlock")
                st_, i, e = best
                if e == "sp":
                    sp_pos += 1
                    ready[e].remove(i)
                    heapq.heapify(ready[e])
                else:
                    ready[e].remove(i)
                    heapq.heapify(ready[e])
                o = real[i]
                if e == "act" and o.get("tset") is not None:
                    cur_set[0] = o["tset"]
                if o["dma"] is not None:
                    free[e] = st_ + 0.1
                    fin[i] = st_ + o["cost"]
                else:
                    free[e] = st_ + o["cost"]
                    fin[i] = free[e]
                order[e].append(i)
                remaining -= 1
                for u in users[i]:
                    nd[u] -= 1
                    rt[u] = max(rt.get(u, tnow), fin[i] + LAT)
                    if nd[u] == 0:
                        heapq.heappush(ready[real[u]["eng"]], u)
            tnow = max([tnow] + [fin[i] for i in ids]) + 1.0
        self.est_us = tnow
        pos = {}
        for e in self.ENGS:
            for p, i in enumerate(order[e]):
                pos[i] = p
        dma_seq = {}
        for i in order["sp"] + [i for e in self.ENGS if e != "sp" for i in order[e]]:
            pass
        cnt = {}
        for e in self.ENGS:
            for i in order[e]:
                o = real[i]
                if o["dma"] is not None:
                    cnt[o["skey"]] = cnt.get(o["skey"], 0) + 16
                    o["sigval"] = cnt[o["skey"]]
                    o["dpos"] = cnt[o["skey"]]
        last_in_seg = [dict() for _ in range(nseg)]
        for e in self.ENGS:
            for i in order[e]:
                o = real[i]
                last_in_seg[o["seg"]][o["skey"]] = i
        first_seen = set()
        for e in self.ENGS:
            for i in order[e]:
                o = real[i]
                if o["seg"] > 0 and (e, o["seg"]) not in first_seen:
                    first_seen.add((e, o["seg"]))
                    for sgp in range(o["seg"]):
                        o["deps"] = list(o["deps"]) + list(last_in_seg[sgp].values())
        for o in real:
            keep = {}
            for d in o["deps"]:
                od = real[d]
                if od["dma"] is None and od["eng"] == o["eng"]:
                    if o["eng"] == "pe" or not SAME_ENGINE_SYNC:
                        continue
                sk = od["skey"]
                key = od["dpos"] if od["dma"] is not None else pos[d]
                if sk not in keep or keep[sk][0] < key:
                    keep[sk] = (key, d)
            o["deps"] = [v[1] for v in keep.values()]
            for d in o["deps"]:
                real[d]["signal"] = True
        for e in self.ENGS:
            c = 0
            for i in order[e]:
                o = real[i]
                if o["dma"] is None and o["signal"]:
                    c += 1
                    o["sigval"] = c
        skeys = sorted({o["skey"] for o in real}, key=str)
        with ExitStack() as st:
            sems = {}
            for i, sk in enumerate(skeys):
                sems[sk] = st.enter_context(nc.semaphore(f"sem{i}"))
            dma_final = {sk: cnt[sk] for sk in skeys if sk[0] == "dma"}
            self.sem_names = {f"sem{i}": sk for i, sk in enumerate(skeys)}

            def run(ename, eng):
                waited = {}
                for i in order[ename]:
                    o = real[i]
                    for d in o["deps"]:
                        od = real[d]
                        sk, val = od["skey"], od["sigval"]
                        if waited.get(sk, 0) < val:
                            eng.wait_ge(sems[sk], val)
                            waited[sk] = val
                    ins = o["fn"](eng)
                    if o["dma"] is not None:
                        ins.then_inc(sems[o["skey"]], 16)
                    elif o["signal"]:
                        ins.then_inc(sems[o["skey"]], 1)
                if ename == "sp":
                    for sk, val in dma_final.items():
                        if waited.get(sk, 0) < val:
                            eng.wait_ge(sems[sk], val)

            with nc.Block() as block:
                @block.tensor
                def _(e):
                    run("pe", e)

                @block.scalar
                def _(e):
                    run("act", e)

                @block.vector
                def _(e):
                    run("dve", e)

                @block.gpsimd
                def _(e):
                    run("pool", e)

                @block.sync
                def _(e):
                    run("sp", e)
        return len(real)


def host_consts(S):
    c = {}
    c["ones"] = np.ones((128, 128), np.float32)
    c["ident"] = np.eye(128, dtype=np.float32)
    s = np.arange(128)[:, None]
    t = np.arange(128)[None, :]
    c["bmask"] = ((s // 64 == t // 64) & (s <= t)).astype(np.float32)
    tri = (s <= t).astype(np.float32)
    c["tri2"] = np.concatenate([tri, tri], axis=1)
    cm = np.ones((128, TC), np.float32)
    cm[:, ::64] = 0.0
    c["cmask"] = cm
    half = 16
    inv_freq = (500000.0 ** (-np.arange(0, 32, 2, dtype=np.float32) / 32)).astype(np.float32)
    pos = np.arange(S, dtype=np.float32)
    ang = pos[None, :] * inv_freq[:, None]
    cosT = np.ones((128, S), np.float32)
    sinT = np.zeros((128, S), np.float32)
    cosT[0:16] = np.cos(ang)
    cosT[16:32] = np.cos(ang)
    sinT[0:16] = np.sin(ang)
    sinT[16:32] = np.sin(ang)
    c["cosT"] = cosT
    c["sinT"] = sinT
    pm = np.zeros((128, 128), np.float32)
    for m in range(16):
        pm[m + 16, m] = -1.0
        pm[m, m + 16] = 1.0
    c["pmat"] = pm
    return c


class Ctx:
    pass


def alloc_common(nc, st, S, kind):
    C = Ctx()
    C.S = S
    C.NT = S // TC
    sb = lambda name, shape, dt: st.enter_context(nc.sbuf_tensor("s_" + name, shape, dt))
    C.hT = sb("hT", [128, KC, S], BF16)
    C.wf = [sb(f"wf{i}", [128, KC, 128], F32) for i in range(2)]
    C.wb = [sb(f"wb{i}", [128, KC, 512], BF16) for i in range(2)]
    C.T = [sb(f"T{i}", [128, TC], F32) for i in range(14)]
    C.B = [sb(f"B{i}", [128, TC], BF16) for i in range(12)]
    C.ones_b = sb("ones_b", [128, 128], BF16)
    C.ident_b = sb("ident_b", [128, 128], BF16)
    C.cst_f = sb("cst_f", [128, 128], F32)
    C.gnorm = sb("gnorm", [128, KC], F32)
    C.ps = [st.enter_context(nc.psum_tensor(f"ps{i}", [128, 512], F32)) for i in range(8)]
    return C


def load_const_bf16(P, C, dst, src_dram, name):
    P.dma(C.cst_f[:], src_dram, slot="cst", reads=[("dram", name)], writes=["cst_f"])
    P.cp(dst[:], C.cst_f[:], reads=["cst_f"], writes=[name], eng="dve")


def emit_prologue(P, C, dr, n_part, have_xout, final_only=False, stats_ready=False):
    S, NT = C.S, C.NT
    P.dma(C.gnorm[:], dr["gnorm"], slot="gn", reads=[], writes=["gnorm"])
    stat_banks = [C.ps[4 + i] for i in range(NT)]
    for kc in range(KC if not stats_ready else 0):
        for tc in range(NT):
            sl = (kc * NT + tc) % 4
            xa = C.T[sl]
            xk = ("T", sl)
            cols = slice(tc * TC, (tc + 1) * TC)
            rows = slice(kc * 128, (kc + 1) * 128)
            P.dma(xa[:], dr["xT"][rows, cols], slot=("xa", sl), reads=([("xout", kc, tc)] if dr.get("x_dep") else []), writes=[xk])
            for pi in range(n_part):
                pa = C.T[8 + sl]
                pk = ("T", 8 + sl)
                P.dma(pa[:], dr["parts"][pi][rows, cols], slot=("pa", sl), reads=[("yp", pi, kc, tc)], writes=[pk])
                P.tt(xa[:], xa[:], pa[:], ALU.add, reads=[xk, pk], writes=[xk])
            if have_xout:
                P.dma(dr["xout"][rows, cols], xa[:], slot=("xo", sl), reads=[xk], writes=[("xout", kc, tc), "xout_all"])
            if final_only:
                continue
            xs = C.B[sl]
            P.act(xs[:], xa[:], AF.Square, reads=[xk], writes=[("B", sl)])
            P.mm(stat_banks[tc][:], C.ones_b[:], xs[:], start=(kc == 0), stop=(kc == KC - 1),
                 reads=[("B", sl), "ones_b"], writes=[("ps", 4 + tc)])
    if final_only:
        return
    for tc in range(NT):
        r = C.T[4 + tc]
        rstd_act(P, r[:], ("T", 4 + tc), stat_banks[tc][:], [("ps", 4 + tc)], C.eps_norm[:], "eps_norm", 1.0 / D)
    src = dr["xout"] if have_xout else dr["xT"]
    for kc in range(KC):
        for tc in range(NT):
            sl = (kc * NT + tc) % 4
            xa = C.T[sl]
            xk = ("T", sl)
            cols = slice(tc * TC, (tc + 1) * TC)
            rows = slice(kc * 128, (kc + 1) * 128)
            rd = [("xout", kc, tc)] if (have_xout or stats_ready) else []
            P.dma(xa[:], src[rows, cols], slot=("xa", sl), reads=rd, writes=[xk])
            P.stt(C.hT[:, kc, cols], xa[:], C.gnorm[:, kc:kc + 1], C.T[4 + tc][:], ALU.mult, ALU.mult,
                  reads=[xk, "gnorm", ("T", 4 + tc)], writes=[("hT", kc, tc)])


class WStream:
    def __init__(self, P, C):
        self.P, self.C = P, C
        self.n = 0

    def load_group(self, w_dram, col0, nslab, gslot, kcn=KC, name="w", koff=0):
        P, C = self.P, self.C
        ncol = nslab * 128
        for q in range(kcn // 4):
            sl = self.n % 2
            self.n += 1
            src = w_dram[q * 512:(q + 1) * 512, col0:col0 + ncol].rearrange("(k p) c -> p k c", p=128)
            stage = C.wf[sl][:].rearrange("p k c -> p (k c)")[:, 0:4 * ncol].rearrange("p (k c) -> p k c", k=4)
            P.dma(stage, src, slot=("wf", sl), reads=[], writes=[("wf", sl)])
            P.act(C.wb[gslot][:, koff + 4 * q:koff + 4 * q + 4, 0:ncol], stage, AF.Copy,
                  reads=[("wf", sl)], writes=[("wb", gslot, koff // 4 + q)])


def inproj_fm(P, C, gslot, s, tc, bank):
    cols = slice(tc * TC, (tc + 1) * TC)
    for kc in range(KC):
        P.mm(C.ps[bank][:], C.wb[gslot][:, kc, s * 128:(s + 1) * 128], C.hT[:, kc, cols],
             start=(kc == 0), stop=(kc == KC - 1),
             reads=[("wb", gslot, kc // 4), ("hT", kc, tc)], writes=[("ps", bank)])


def inproj_tm(P, C, gslot, s0, ncol, tc, tt, out_ap, bank):
    t0 = tc * TC + tt * 128
    rd = [("wb", gslot, q) for q in range(4)]
    for kc in range(KC):
        P.mm(out_ap, C.hT[:, kc, t0:t0 + 128], C.wb[gslot][:, kc, s0 * 128:s0 * 128 + ncol],
             start=(kc == 0), stop=(kc == KC - 1),
             reads=rd + [("hT", kc, tc)], writes=[("ps", bank)])


def emit_outproj(P, C, ws, dr, nfc):
    S, NT = C.S, C.NT
    fo = dr.get("fc_off", 0)
    ws.load_group(dr["w_out"], 0, 4, 0, kcn=nfc)
    for tc in range(NT):
        cols = slice(tc * TC, (tc + 1) * TC)
        for fc in range(nfc):
            P.dma(C.hT[:, fo + fc, cols], dr["mixT"][fc * 128:(fc + 1) * 128, cols], slot=("mixld", fo, tc),
                  reads=[("mixT", dr.get("mix_tag", 0), fc, tc)], writes=[("hT", fo + fc, tc), ("mixall", fo, tc)],
                  eng="act")
    for g4 in range(4):
        g = g4 % 2
        if g4 + 1 < 4:
            ws.load_group(dr["w_out"], (g4 + 1) * 512, 4, (g4 + 1) % 2, kcn=nfc)
        for s_ in range(4):
            dc = g4 * 4 + s_
            for tc in range(NT):
                cols = slice(tc * TC, (tc + 1) * TC)
                bank = (dc * NT + tc) % 2
                for fc in range(nfc):
                    P.mm(C.ps[bank][:], C.wb[g][:, fc, s_ * 128:(s_ + 1) * 128], C.hT[:, fo + fc, cols],
                         start=(fc == 0), stop=(fc == nfc - 1),
                         reads=[("wb", g, fc // 4), ("mixall", fo, tc), ("hT", fo + fc, tc)], writes=[("ps", bank)])
                ysl = (dc * NT + tc) % 2
                yt = C.T[12 + ysl]
                rows = slice(dc * 128, (dc + 1) * 128)
                ev = dr.get("evac")
                if ev is None:
                    P.act(yt[:], C.ps[bank][:], AF.Copy, reads=[("ps", bank)], writes=[("T", 12 + ysl)])
                    P.dma(dr["yp"][rows, cols], yt[:], slot=("yst", ysl),
                          reads=[("T", 12 + ysl)], writes=[("yp", dr.get("yp_tag", 0), dc, tc)])
                else:
                    k4 = (dc * NT + tc) % 4
                    xin = C.T[8 + k4]
                    P.dma(xin[:], ev["src"][rows, cols], slot=("oin", k4), reads=ev["src_keys"](dc, tc), writes=[("T", 8 + k4)])
                    P.tt(yt[:], xin[:], C.ps[bank][:], ALU.add, reads=[("T", 8 + k4), ("ps", bank)], writes=[("T", 12 + ysl)])
                    P.dma(ev["dst"][rows, cols], yt[:], slot=("yst", ysl),
                          reads=[("T", 12 + ysl)], writes=ev["dst_keys"](dc, tc))
                    if ev.get("stats"):
                        sq = C.B[k4]
                        P.act(sq[:], yt[:], AF.Square, reads=[("T", 12 + ysl)], writes=[("B", k4)])
                        P.mm(C.ps[4 + tc][:], C.ones_b[:], sq[:], start=(dc == 0), stop=(dc == KC - 1),
                             reads=[("B", k4), "ones_b"], writes=[("ps", 4 + tc)])


def emit_outproj_pair(P, C, ws, dr0, dr1, ev):
    S, NT = C.S, C.NT
    halves = (dr0, dr1)
    for hf in range(2):
        ws.load_group(halves[hf]["w_out"], 0, 4, 0, kcn=8, koff=8 * hf)
    for tc in range(NT):
        cols = slice(tc * TC, (tc + 1) * TC)
        for hf in range(2):
            fo = 8 * hf
            for fc in range(8):
                P.dma(C.hT[:, fo + fc, cols], halves[hf]["mixT"][fc * 128:(fc + 1) * 128, cols], slot=("mixld", fo, tc),
                      reads=[("mixT", hf, fc, tc)], writes=[("hT", fo + fc, tc), ("mixall", fo, tc)], eng="act")
    for g4 in range(4):
        g = g4 % 2
        if g4 + 1 < 4:
            for hf in range(2):
                ws.load_group(halves[hf]["w_out"], (g4 + 1) * 512, 4, (g4 + 1) % 2, kcn=8, koff=8 * hf)
        for s_ in range(4):
            dc = g4 * 4 + s_
            rows = slice(dc * 128, (dc + 1) * 128)
            for tc in range(NT):
                cols = slice(tc * TC, (tc + 1) * TC)
                bank = (dc * NT + tc) % 2
                for fc in range(16):
                    P.mm(C.ps[bank][:], C.wb[g][:, fc, s_ * 128:(s_ + 1) * 128], C.hT[:, fc, cols],
                         start=(fc == 0), stop=(fc == 15),
                         reads=[("wb", g, fc // 4), ("mixall", 8 * (fc // 8), tc), ("hT", fc, tc)], writes=[("ps", bank)])
                ysl = (dc * NT + tc) % 2
                yt = C.T[12 + ysl]
                k4 = (dc * NT + tc) % 4
                xin = C.T[8 + k4]
                P.dma(xin[:], ev["src"][rows, cols], slot=("oin", k4), reads=ev["src_keys"](dc, tc), writes=[("T", 8 + k4)])
                P.tt(yt[:], xin[:], C.ps[bank][:], ALU.add, reads=[("T", 8 + k4), ("ps", bank)], writes=[("T", 12 + ysl)])
                P.dma(ev["dst"][rows, cols], yt[:], slot=("yst", ysl), reads=[("T", 12 + ysl)], writes=ev["dst_keys"](dc, tc))
                if ev.get("stats"):
                    sq = C.B[k4]
                    P.act(sq[:], yt[:], AF.Square, reads=[("T", 12 + ysl)], writes=[("B", k4)])
                    P.mm(C.ps[4 + tc][:], C.ones_b[:], sq[:], start=(dc == 0), stop=(dc == KC - 1),
                         reads=[("B", k4), "ones_b"], writes=[("ps", 4 + tc)])


REC_GROUPS = ["rg", "hg", "hg", "rg", "hg", "hg", "rg", "hg", "hg", "rg", "hg", "hg"]


def emit_rec_layer(P, C, dr, do_outproj=True):
    S, NT = C.S, C.NT
    nc = P.nc
    T, B, ps = C.T, C.B, C.ps
    R = C.R
    P.dma(R.convw[:], dr["convw"], slot="p0", reads=[], writes=["convw"])
    P.dma(R.vec[:], dr["recvec"], slot="p1", reads=[], writes=["recvec"])
    P.dma(R.lbraw[:], dr["hglb"], slot="p2", reads=[], writes=["lbraw"])
    P.dma(R.onorm[:], dr["onorm"], slot="p3", reads=[], writes=["onorm"])
    P.dma(R.cmask[:], dr["cmask"], slot="p4", reads=[], writes=["cmask"])
    P.dma(R.bmask[:], dr["bmask"], slot="p5", reads=[], writes=["bmask"])
    P.act(R.sp[:], R.vec[:, :, 3], AF.Exp, reads=["recvec"], writes=["sp"], scale=-1.0)
    P.act(R.sp[:], R.sp[:], AF.Ln, reads=["sp", "one_c"], writes=["sp"], bias=C.one_c[:], scale=1.0)
    P.add("dve", lambda e: e.tensor_scalar_mul(R.cneg[:], R.sp[:], -8.0), ["sp"], ["cneg"])
    P.add("dve", lambda e: e.tensor_scalar_mul(R.cneg2[:], R.sp[:], -16.0), ["sp"], ["cneg2"])
    P.tt(R.lb[:], R.lbraw[:, :, 1], R.lbraw[:, :, 0], ALU.subtract, reads=["lbraw"], writes=["lb"])
    P.act(R.lb[:], R.lb[:], AF.Sigmoid, reads=["lb"], writes=["lb"])
    P.add("dve", lambda e: e.tensor_scalar_mul(R.lb[:], R.lb[:], float(dr["lbflag"])), ["lb"], ["lb"])
    P.ts(R.oml[:], R.lb[:], -1.0, 1.0, ALU.mult, ALU.add, reads=["lb"], writes=["oml"])
    P.memset(R.zero_f[:], 0.0, writes=["zero_f"], eng="dve")
    P.memset(R.zero_b[:], 0.0, writes=["zero_b"], eng="dve")
    P.add("dve", lambda e: e.tensor_scalar_mul(R.vech[:], R.vec[:], 0.5), ["recvec"], ["vech"])
    P.add("dve", lambda e: e.tensor_scalar_mul(R.chalf[:], R.sp[:], -4.0), ["sp"], ["chalf"])
    P.add("dve", lambda e: e.tensor_scalar_mul(R.omh[:], R.oml[:], 0.5), ["oml"], ["omh"])
    P.tt(R.lbp[:], R.omh[:], R.lb[:], ALU.add, reads=["omh", "lb"], writes=["lbp"])

    ws = WStream(P, C)
    groups = REC_GROUPS
    objs = []
    ia = ie = 0
    for gi, gk in enumerate(groups):
        if gk == "rg":
            objs.append(RGGroup(P, C, dr, gi % 2, ia))
            ia += 1
        else:
            objs.append(HGGroup(P, C, dr, gi % 2, ie))
            ie += 1
    seq = [(gi, tc) for gi in range(len(groups)) for tc in range(NT)]
    ws.load_group(dr["w_in"], 0, 4, 0)
    objs[0].inproj(0, 0)
    for k, (gi, tc) in enumerate(seq):
        if tc == 0 and gi + 1 < len(groups):
            ws.load_group(dr["w_in"], (gi + 1) * 512, 4, (gi + 1) % 2)
        if k + 1 < len(seq):
            objs[seq[k + 1][0]].inproj(seq[k + 1][1], (k + 1) % 2)
        objs[gi].mixer(tc, k % 2)
    if do_outproj:
        emit_outproj(P, C, ws, dr, 16)


_BK = [0]
_SL = [0]


def rstd_act(P, out, out_key, in_, in_keys, eps_ap, eps_key, inv_n):
    P.act(out, in_, AF.Ln, reads=list(in_keys) + [eps_key], writes=[out_key], bias=eps_ap, scale=inv_n)
    P.act(out, out, AF.Exp, reads=[out_key], writes=[out_key], scale=-0.5)


def v_tokmajor(P, C, gslot, s, tc, bank, tmp_idx, out_ap, out_keys):
    inproj_fm(P, C, gslot, s, tc, bank)
    vT = C.B[tmp_idx]
    P.act(vT[:], C.ps[bank][:], AF.Copy, reads=[("ps", bank)], writes=[("B", tmp_idx)])
    pv = C.ps[bank][:].bitcast(BF16)
    for tt in range(4):
        P.tr(pv[:, tt * 128:(tt + 1) * 128], vT[:, tt * 128:(tt + 1) * 128], C.ident_b[:],
             reads=[("B", tmp_idx), "ident_b"], writes=[("ps", bank)])
    P.cp(out_ap, pv[:, 0:512].rearrange("p (a b) -> p a b", a=4), reads=[("ps", bank)], writes=out_keys, eng="dve")


def silu2(P, C, out_ap, out_key, bank):
    k = 4 + (_SL[0] % 2)
    _SL[0] += 1
    th = C.T[k]
    P.act(th[:], C.ps[bank][:], AF.Tanh, reads=[("ps", bank)], writes=[("T", k)], scale=0.5)
    P.stt(out_ap, th[:], 1.0, C.ps[bank][:], ALU.add, ALU.mult, reads=[("T", k), ("ps", bank)], writes=[out_key])


def next_bank():
    b = _BK[0] % 2
    _BK[0] += 1
    return b


class RGGroup:
    def __init__(self, P, C, dr, gslot, a):
        self.P, self.C, self.dr, self.gslot, self.a = P, C, dr, gslot, a

    def inproj(self, tc, par):
        P, C, dr, gslot, a = self.P, self.C, self.dr, self.gslot, self.a
        T, B, ps, R = C.T, C.B, C.ps, C.R
        if tc == 0:
            P.dma(R.wgf[:], dr["wgate"][a], slot="wg", reads=[], writes=["wgf"])
            P.act(R.wgb[:], R.wgf[:], AF.Copy, reads=["wgf"], writes=["wgb"])
        xr = R.xraw[par]
        for i in range(2):
            bank = next_bank()
            inproj_fm(P, C, gslot, i, tc, bank)
            if tc == 0:
                P.memset(xr[:, i, 0:3], 0.0, writes=[("xraw", par, i)], eng="dve")
            else:
                P.act(xr[:, i, 0:3], R.xraw[1 - par][:, i, TC:TC + 3], AF.Copy, reads=[("xraw", 1 - par, i)],
                      writes=[("xraw", par, i)])
            P.act(xr[:, i, 3:3 + TC], ps[bank][:], AF.Copy, reads=[("ps", bank)], writes=[("xraw", par, i)])
        for j in range(2):
            bank = next_bank()
            inproj_fm(P, C, gslot, 2 + j, tc, bank)
            silu2(P, C, B[par * 2 + j][:], ("B", par * 2 + j), bank)

    def mixer(self, tc, par):
        P, C, dr, gslot, a = self.P, self.C, self.dr, self.gslot, self.a
        T, B, ps, R = C.T, C.B, C.ps, C.R
        xr = R.xraw[par]
        for i in range(2):
            ch = 2 * a + i
            xc = T[6 + i]
            k = ("T", 6 + i)
            P.act(xc[:], xr[:, i, 3:3 + TC], AF.Identity, reads=[("xraw", par, i), "convw", "recvec"], writes=[k],
                  bias=R.vec[:, ch, 0:1], scale=R.convw[:, ch, 3:4])
            for tap in range(3):
                P.stt(xc[:], xr[:, i, tap:tap + TC], R.convw[:, ch, tap:tap + 1], xc[:], ALU.mult, ALU.add,
                      reads=[("xraw", par, i), "convw", k], writes=[k])
            P.act(B[6 + i][:], xc[:], AF.Copy, reads=[k], writes=[("B", 6 + i)])
        for j in range(2):
            ch = 2 * a + j
            for g in range(2):
                for i in range(2):
                    P.mm(ps[3 + g][:], R.wgb[:, g, i, j * 128:(j + 1) * 128], B[6 + i][:], start=(i == 0), stop=(i == 1),
                         reads=["wgb", ("B", 6 + i)], writes=[("ps", 3 + g)])
            r_, ig, aa, mm_, uu = T[8], T[9], T[10], T[11], T[12]
            P.act(r_[:], ps[3][:], AF.Tanh, reads=[("ps", 3), "vech"], writes=[("T", 8)], bias=R.vech[:, ch, 1:2], scale=0.5)
            P.act(ig[:], ps[4][:], AF.Tanh, reads=[("ps", 4), "vech"], writes=[("T", 9)], bias=R.vech[:, ch, 2:3], scale=0.5)
            P.act(aa[:], r_[:], AF.Exp, reads=[("T", 8), "chalf"], writes=[("T", 10)], scale=R.chalf[:, ch:ch + 1],
                  bias=R.chalf[:, ch:ch + 1])
            P.act(mm_[:], r_[:], AF.Exp, reads=[("T", 8), "cneg"], writes=[("T", 11)], scale=R.cneg[:, ch:ch + 1],
                  bias=R.cneg[:, ch:ch + 1])
            P.act(mm_[:], mm_[:], AF.Ln, reads=[("T", 11), "one_c"], writes=[("T", 11)], bias=C.one_c[:], scale=-1.0)
            P.act(mm_[:], mm_[:], AF.Exp, reads=[("T", 11)], writes=[("T", 11)], scale=0.5)
            P.stt(uu[:], ig[:], 1.0, T[6 + j][:], ALU.add, ALU.mult, reads=[("T", 9), ("T", 6 + j)], writes=[("T", 12)])
            P.stt(uu[:], uu[:], 0.5, mm_[:], ALU.mult, ALU.mult, reads=[("T", 12), ("T", 11)], writes=[("T", 12)])
            hcur = R.h[par][j]
            hk = ("h", par, j)
            if tc == 0:
                init = 0.0
                rd = []
            else:
                init = R.h[1 - par][j][:, TC - 1:TC]
                rd = [("h", 1 - par, j)]
            P.scan(hcur[:], aa[:], uu[:], init, reads=[("T", 10), ("T", 12)] + rd, writes=[hk])
            mo = B[8 + j]
            P.stt(mo[:], hcur[:], 0.5, B[par * 2 + j][:], ALU.mult, ALU.mult, reads=[hk, ("B", par * 2 + j)], writes=[("B", 8 + j)])
            fc = 2 * a + j
            P.dma(dr["mixT"][fc * 128:(fc + 1) * 128, tc * TC:(tc + 1) * TC], mo[:], slot=("mo", j),
                  reads=[("B", 8 + j)], writes=[("mixT", dr.get("mix_tag", 0), fc, tc)])


class HGGroup:
    def __init__(self, P, C, dr, gslot, e):
        self.P, self.C, self.dr, self.gslot, self.e = P, C, dr, gslot, e
        self.nS = 0

    def tiles(self, par):
        T, B = self.C.T, self.C.B
        qi, fi, si, vi = par, 2 + par, 2 * par, 2 * par + 1
        return (T[qi], ("T", qi)), (T[fi], ("T", fi)), (B[si], ("B", si)), (B[vi], ("B", vi))

    def inproj(self, tc, par):
        P, C, gslot = self.P, self.C, self.gslot
        ps = C.ps
        (qs, qk), (ff, fk), (sg, sk), (V, vk) = self.tiles(par)
        bank = next_bank()
        inproj_fm(P, C, gslot, 0, tc, bank)
        silu2(P, C, qs[:], qk, bank)
        bank = next_bank()
        inproj_fm(P, C, gslot, 1, tc, bank)
        silu2(P, C, sg[:], sk, bank)
        bank = next_bank()
        inproj_fm(P, C, gslot, 2, tc, bank)
        P.act(ff[:], ps[bank][:], AF.Tanh, reads=[("ps", bank)], writes=[fk], scale=0.5)
        bank = next_bank()
        v_tokmajor(P, C, gslot, 3, tc, bank, 10 + par, V[:].rearrange("p (a b) -> p a b", a=4), [vk])

    def mixer(self, tc, par):
        P, C, dr, e = self.P, self.C, self.dr, self.e
        T, B, ps, R = C.T, C.B, C.ps, C.R
        HG_SCALE = 128 ** -0.5
        (qs, qk), (ff, fk), (sg, sk), (V, vk) = self.tiles(par)
        lf, bb, eb, enb = T[8], T[9], T[10], T[11]
        P.ts(ff[:], ff[:], R.omh[:, e:e + 1], R.lbp[:, e:e + 1], ALU.mult, ALU.add, reads=[fk, "omh", "lbp"], writes=[fk])
        P.act(lf[:], ff[:], AF.Ln, reads=[fk], writes=[("T", 8)])
        P.scan(bb[:], R.cmask[:], lf[:], 0.0, reads=["cmask", ("T", 8)], writes=[("T", 9)])
        P.act(eb[:], bb[:], AF.Exp, reads=[("T", 9)], writes=[("T", 10)])
        P.act(enb[:], bb[:], AF.Exp, reads=[("T", 9)], writes=[("T", 11)], scale=-1.0)
        P.ts(ff[:], ff[:], -1.0, 1.0, ALU.mult, ALU.add, reads=[fk], writes=[fk])
        Qd, Kd, K2 = B[4], B[5], B[6]
        P.stt(Qd[:], qs[:], HG_SCALE * 0.5, eb[:], ALU.mult, ALU.mult, reads=[qk, ("T", 10)], writes=[("B", 4)])
        P.tt(Kd[:], ff[:], enb[:], ALU.mult, reads=[fk, ("T", 11)], writes=[("B", 5)])
        for c in range(8):
            cs = slice(c * 64, (c + 1) * 64)
            P.stt(K2[:, cs], ff[:, cs], eb[:, c * 64 + 63:c * 64 + 64], enb[:, cs], ALU.mult, ALU.mult,
                  reads=[fk, ("T", 10), ("T", 11)], writes=[("B", 6)])
        for tt in range(4):
            tsl = slice(tt * 128, (tt + 1) * 128)
            ap = tt % 2
            atb = 5 if ap == 0 else 3
            at_ps = ps[atb][:, 0:128]
            P.mm(at_ps, Kd[:, tsl], Qd[:, tsl], start=True, stop=True, reads=[("B", 5), ("B", 4)], writes=[("ps", atb)])
            atm = R.atm[ap]
            P.tt(atm[:], at_ps, R.bmask[:], ALU.mult, reads=[("ps", atb), "bmask"], writes=[("atm", ap)])
            tr_ps = ps[4][:].bitcast(BF16)[:, 0:128]
            P.tr(tr_ps, K2[:, tsl], C.ident_b[:], reads=[("B", 6), "ident_b"], writes=[("ps", 4)])
            k2t = R.k2t[ap]
            P.cp(k2t[:], tr_ps, reads=[("ps", 4)], writes=[("k2t", ap)], eng="dve")
            o_ps = ps[7][:, tsl]
            P.mm(o_ps, V[:, tsl], atm[:], start=True, stop=False, reads=[vk, ("atm", ap)], writes=[("ps", 7)])
            for h2 in range(2):
                nS = self.nS
                c = tt * 2 + h2
                if tc == 0 and c == 0:
                    sb_prev, sbk = R.zero_b, "zero_b"
                    s_prev, spk = R.zero_f, "zero_f"
                else:
                    sb_prev, sbk = R.Sb[(nS - 1) % 4], ("Sb", (nS - 1) % 4)
                    s_prev, spk = R.Sf[(nS - 1) % 4], ("Sf", (nS - 1) % 4)
                cg = slice(tt * 128 + h2 * 64, tt * 128 + (h2 + 1) * 64)
                P.mm(ps[7][:, cg], sb_prev[:], Qd[:, cg], start=False, stop=(h2 == 1),
                     reads=[sbk, ("B", 4)], writes=[("ps", 7)])
                ub = 6 if nS % 2 == 0 else 2
                u_ps = ps[ub][:, 0:128]
                P.mm(u_ps, k2t[h2 * 64:(h2 + 1) * 64, :], V[h2 * 64:(h2 + 1) * 64, tsl], start=True, stop=True,
                     reads=[("k2t", ap), vk], writes=[("ps", ub)])
                s_new = R.Sf[nS % 4]
                P.stt(s_new[:], s_prev[:], eb[:, c * 64 + 63:c * 64 + 64], u_ps, ALU.mult, ALU.add,
                      reads=[spk, ("T", 10), ("ps", ub)], writes=[("Sf", nS % 4)])
                P.cp(R.Sb[nS % 4][:], s_new[:], reads=[("Sf", nS % 4)], writes=[("Sb", nS % 4)], eng="dve")
                self.nS += 1
        osq = B[7]
        P.act(osq[:], ps[7][:], AF.Square, reads=[("ps", 7)], writes=[("B", 7)])
        P.mm(ps[4][:], C.ones_b[:], osq[:], start=True, stop=True, reads=["ones_b", ("B", 7)], writes=[("ps", 4)])
        rs = T[12]
        rstd_act(P, rs[:], ("T", 12), ps[4][:], [("ps", 4)], C.eps_norm[:], "eps_norm", 1.0 / 128)
        ot = T[13]
        P.stt(ot[:], ps[7][:], R.onorm[:, 0:1], rs[:], ALU.mult, ALU.mult, reads=[("ps", 7), "onorm", ("T", 12)], writes=[("T", 13)])
        mo = B[8 + (tc % 2)]
        P.stt(mo[:], ot[:], 0.5, sg[:], ALU.mult, ALU.mult, reads=[("T", 13), sk], writes=[("B", 8 + (tc % 2))])
        fc = 8 + e
        P.dma(dr["mixT"][fc * 128:(fc + 1) * 128, tc * TC:(tc + 1) * TC], mo[:], slot=("mo", tc % 2),
              reads=[("B", 8 + (tc % 2))], writes=[("mixT", dr.get("mix_tag", 0), fc, tc)])


def alloc_rec(nc, st, C, mx=None):
    R = Ctx()
    sb = lambda name, shape, dt: st.enter_context(nc.sbuf_tensor("s_" + name, shape, dt))
    R.convw = sb("convw", [128, 8, 4], F32)
    R.vec = sb("recvec", [128, 8, 4], F32)
    R.lbraw = sb("lbraw", [128, 8, 2], F32)
    R.onorm = sb("onorm", [128, 1], F32)
    if mx is None:
        R.cmask = sb("cmask", [128, TC], F32)
    R.bmask = sb("bmask", [128, 128], F32)
    R.sp = sb("sp", [128, 8], F32)
    R.cneg = sb("cneg", [128, 8], F32)
    R.cneg2 = sb("cneg2", [128, 8], F32)
    R.lb = sb("lb", [128, 8], F32)
    R.oml = sb("oml", [128, 8], F32)
    R.omh = sb("omh", [128, 8], F32)
    R.lbp = sb("lbp", [128, 8], F32)
    R.chalf = sb("chalf", [128, 8], F32)
    R.vech = sb("vech", [128, 8, 4], F32)
    R.zero_f = sb("zero_f", [128, 128], F32)
    R.zero_b = sb("zero_b", [128, 128], BF16)
    R.wgb = sb("wgb", [128, 2, 2, 256], BF16)
    if mx is None:
        R.wgf = sb("wgf", [128, 2, 2, 256], F32)
        R.xraw = [sb(f"xraw{i}", [128, 2, TC + 3], F32) for i in range(2)]
        R.h = [[sb(f"h{p}{j}", [128, TC], F32) for j in range(2)] for p in range(2)]
    else:
        XW = 2 * (TC + 3)
        R.xraw = [mx[:, i * XW:(i + 1) * XW].rearrange("p (a b) -> p a b", a=2) for i in range(2)]
        o = 2 * XW
        R.h = [[mx[:, o + (p * 2 + j) * TC: o + (p * 2 + j + 1) * TC] for j in range(2)] for p in range(2)]
        o += 4 * TC
        R.cmask = mx[:, o:o + TC]
        o += TC
        R.wgf = mx[:, o:o + 1024].rearrange("p (g i c) -> p g i c", g=2, i=2)
    R.atm = [sb(f"atm{i}", [128, 128], BF16) for i in range(2)]
    R.k2t = [sb(f"k2t{i}", [128, 128], BF16) for i in range(2)]
    R.Sf = [sb(f"Sf{i}", [128, 128], F32) for i in range(4)]
    R.Sb = [sb(f"Sb{i}", [128, 128], BF16) for i in range(4)]
    C.R = R


def emit_consts(P, C, dr):
    load_const_bf16(P, C, C.ones_b, dr["ones"], "ones_b")
    load_const_bf16(P, C, C.ident_b, dr["ident"], "ident_b")
    P.memset(C.eps_norm[:], NORM_EPS, writes=["eps_norm"])
    P.memset(C.one_c[:], 1.0, writes=["one_c"])


def build_rec_program(S, n_part, lbflag):
    nc = bass.Bass("TRN2", target_bir_lowering=False)
    di = lambda name, shape, dt=F32: nc.dram_tensor(name, shape, dt, kind="ExternalInput").ap()
    do = lambda name, shape, dt=F32: nc.dram_tensor(name, shape, dt, kind="ExternalOutput").ap()
    dr = {}
    dr["xT"] = di("xT", [D, S])
    dr["parts"] = [di(f"part{i}", [D, S]) for i in range(n_part)]
    dr["gnorm"] = di("gnorm", [128, KC])
    dr["w_in"] = di("w_in", [D, 6144])
    dr["w_out"] = di("w_out", [2048, D])
    dr["convw"] = di("convw", [128, 8, 4])
    dr["recvec"] = di("recvec", [128, 8, 4])
    dr["hglb"] = di("hglb", [128, 8, 2])
    dr["onorm"] = di("onorm", [128, 1])
    dr["wgate"] = di("wgate", [4, 128, 2, 2, 256])
    dr["cmask"] = di("cmask", [128, TC])
    dr["bmask"] = di("bmask", [128, 128])
    dr["ones"] = di("ones", [128, 128])
    dr["ident"] = di("ident", [128, 128])
    dr["lbflag"] = lbflag
    if n_part:
        dr["xout"] = do("xout", [D, S])
    dr["yp"] = do("yp", [D, S])
    dr["mixT"] = nc.dram_tensor("mixT", [2048, S], BF16, kind="Internal").ap()
    with ExitStack() as st:
        C = alloc_common(nc, st, S, "rec")
        C.eps_norm = st.enter_context(nc.sbuf_tensor("s_eps_norm", [128, 1], F32))
        C.one_c = st.enter_context(nc.sbuf_tensor("s_one_c", [128, 1], F32))
        alloc_rec(nc, st, C)
        P = Prog(nc)
        emit_consts(P, C, dr)
        emit_prologue(P, C, dr, n_part, bool(n_part))
        emit_rec_layer(P, C, dr)
        n = P.emit()
    return nc, n


def pvec(v):
    v = np.asarray(v, np.float32)
    return np.ascontiguousarray(v.reshape(-1, 128).T)


def rec_layout(inp, j, r):
    w_in = inp["rec_w_in"][j]
    w_out = inp["rec_w_out"][j]
    rg_heads = [4 * r + a for a in range(4)]
    hg_heads = [8 * r + e for e in range(8)]
    cols = []
    ia = ie = 0
    for gk in REC_GROUPS:
        if gk == "rg":
            hh = rg_heads[ia]; ia += 1
            cols.append(np.arange(256 * hh, 256 * hh + 256))
            cols.append(np.arange(2048 + 256 * hh, 2048 + 256 * hh + 256))
        else:
            e = hg_heads[ie]; ie += 1
            cols.append(np.arange(4096 + 128 * e, 4096 + 128 * e + 128))
            cols.append(np.arange(10240 + 128 * e, 10240 + 128 * e + 128))
            cols.append(np.arange(6144 + 128 * e, 6144 + 128 * e + 128))
            cols.append(np.arange(8192 + 128 * e, 8192 + 128 * e + 128))
    cols = np.concatenate(cols)
    rows = []
    for hh in rg_heads:
        rows.append(np.arange(256 * hh, 256 * hh + 256))
    for e in hg_heads:
        rows.append(np.arange(2048 + 128 * e, 2048 + 128 * e + 128))
    rows = np.concatenate(rows)
    ch = np.concatenate([np.arange(256 * hh, 256 * hh + 256) for hh in rg_heads])
    convw = np.ascontiguousarray(inp["rg_conv_w"][j][:, ch].T.reshape(8, 128, 4).transpose(1, 0, 2))
    vec = np.stack([inp["rg_conv_b"][j][ch], inp["rg_b_gate_a"][j][ch], inp["rg_b_gate_x"][j][ch],
                    inp["rg_lambda"][j][ch]], axis=-1)
    vec = np.ascontiguousarray(vec.reshape(8, 128, 4).transpose(1, 0, 2))
    hch = np.concatenate([np.arange(128 * e, 128 * e + 128) for e in hg_heads])
    lbr = np.stack([inp["hg_lb"][0][hch], inp["hg_lb"][j][hch]], axis=-1)
    lbr = np.ascontiguousarray(lbr.reshape(8, 128, 2).transpose(1, 0, 2))
    wg = np.stack([inp["rg_w_gate_a"][j][rg_heads], inp["rg_w_gate_x"][j][rg_heads]], axis=1)
    wg = wg.reshape(4, 2, 2, 128, 256).transpose(0, 3, 1, 2, 4)
    return dict(
        w_in=np.ascontiguousarray(w_in[:, cols]),
        w_out=np.ascontiguousarray(w_out[rows, :]),
        gnorm=pvec(inp["rec_norm"][j]),
        convw=convw.astype(np.float32), recvec=vec.astype(np.float32), hglb=lbr.astype(np.float32),
        onorm=np.ascontiguousarray(inp["hg_out_norm"][j].reshape(128, 1).astype(np.float32)),
        wgate=np.ascontiguousarray(wg.astype(np.float32)),
    )


def alloc_att(nc, st, C, mx=None):
    A = Ctx()
    S = C.S
    sb = lambda name, shape, dt: st.enter_context(nc.sbuf_tensor("s_" + name, shape, dt))
    if mx is None:
        A.KT = sb("KT", [128, 2, S], BF16)
        A.Vt = sb("Vt", [128, S // 128, 264], BF16)
    else:
        A.KT = mx[:, 0:S].bitcast(BF16).rearrange("p (a s) -> p a s", a=2)
        nv = (S // 128) * 132
        A.Vt = mx[:, S:S + nv].bitcast(BF16).rearrange("p (t c) -> p t c", c=264)
    A.cosT = sb("cosT", [128, S], F32)
    A.sinT = sb("sinT", [128, S], F32)
    A.pmat = sb("pmat", [128, 128], F32)
    A.ones_f = sb("ones_f", [128, 128], F32)
    A.qkg = sb("qkg", [128, 2], F32)
    A.subg = sb("subg", [128, 256], F32)
    A.lp = sb("lp", [128, 4], F32)
    A.pr = sb("pr", [128, 2], F32)
    A.ex = sb("ex", [128, 2], F32)
    A.neglam = sb("neglam", [128, 1], F32)
    A.tri2f = sb("tri2f", [128, 256], F32)
    A.tri2 = sb("tri2", [128, 256], BF16)
    A.pt = [sb(f"pt{i}", [128, 256], BF16) for i in range(2)]
    A.o = sb("o", [128, 256], F32)
    A.junk = sb("junk", [128, 256], F32)
    A.on = sb("on", [128, 256], BF16)
    A.sm = sb("sm", [128, 8], F32)
    A.eps_sub = sb("eps_sub", [128, 1], F32)
    C.A = A


def emit_att_layer(P, C, dr, lam_init, do_outproj=True):
    S, NT = C.S, C.NT
    T, B, ps, A = C.T, C.B, C.ps, C.A
    SCALE = 128 ** -0.5
    P.dma(A.cosT[:], dr["cosT"], slot="a0", reads=[], writes=["cosT"])
    P.dma(A.sinT[:], dr["sinT"], slot="a1", reads=[], writes=["sinT"])
    P.dma(A.pmat[:], dr["pmat"], slot="a2", reads=[], writes=["pmat"])
    P.dma(A.ones_f[:], dr["ones"], slot="a3", reads=[], writes=["ones_f"])
    P.dma(A.qkg[:], dr["qkg"], slot="a4", reads=[], writes=["qkg"])
    P.dma(A.subg[:], dr["subg"], slot="a5", reads=[], writes=["subg"])
    P.dma(A.lp[:], dr["lp"], slot="a6", reads=[], writes=["lp"])
    P.dma(A.tri2f[:], dr["tri2"], slot="a7", reads=[], writes=["tri2f"])
    P.cp(A.tri2[:], A.tri2f[:], reads=["tri2f"], writes=["tri2"], eng="dve")
    P.memset(A.eps_sub[:], SUBLN_EPS, writes=["eps_sub"], eng="dve")
    P.add("dve", lambda e: e.tensor_scalar_mul(A.subg[:], A.subg[:], 1.0 - lam_init), ["subg"], ["subg"])
    P.tt(A.pr[:, 0:1], A.lp[:, 0:1], A.lp[:, 1:2], ALU.mult, reads=["lp"], writes=["pr"])
    P.tt(A.pr[:, 1:2], A.lp[:, 2:3], A.lp[:, 3:4], ALU.mult, reads=["lp", "pr"], writes=["pr"])
    P.mm(ps[2][:, 0:2], A.ones_f[:], A.pr[:], start=True, stop=True, reads=["ones_f", "pr"], writes=[("ps", 2)])
    P.act(A.ex[:], ps[2][:, 0:2], AF.Exp, reads=[("ps", 2)], writes=["ex"])
    P.tt(A.neglam[:], A.ex[:, 1:2], A.ex[:, 0:1], ALU.subtract, reads=["ex"], writes=["neglam"])
    P.add("dve", lambda e: e.tensor_scalar_add(A.neglam[:], A.neglam[:], -lam_init), ["neglam"], ["neglam"])
    P.memset(A.Vt[:, :, 256:257], 1.0, writes=["Vt_ones"], eng="dve")

    ws = WStream(P, C)
    ws.load_group(dr["w_in"], 0, 4, 0)
    ngroups = 8
    bkc = [0]

    def nb():
        b = bkc[0] % 2
        bkc[0] += 1
        return b

    def norm_rope(bank, gcol, out_ap, out_keys, tc):
        cols = slice(tc * TC, (tc + 1) * TC)
        raw, rstd, t1, t2, sq = T[6], T[7], T[8], T[9], B[2]
        P.act(raw[:], ps[bank][:], AF.Copy, reads=[("ps", bank)], writes=[("T", 6)])
        P.act(sq[:], ps[bank][:], AF.Square, reads=[("ps", bank)], writes=[("B", 2)])
        P.mm(ps[2][:], C.ones_b[:], sq[:], start=True, stop=True, reads=["ones_b", ("B", 2)], writes=[("ps", 2)])
        rstd_act(P, rstd[:], ("T", 7), ps[2][:], [("ps", 2)], C.eps_norm[:], "eps_norm", 1.0 / 128)
        P.stt(raw[:], raw[:], A.qkg[:, gcol:gcol + 1], rstd[:], ALU.mult, ALU.mult, reads=[("T", 6), "qkg", ("T", 7)], writes=[("T", 6)])
        P.mm(ps[3][:], A.pmat[:], raw[:], start=True, stop=True, reads=["pmat", ("T", 6)], writes=[("ps", 3)])
        P.tt(t1[:], raw[:], A.cosT[:, cols], ALU.mult, reads=[("T", 6), "cosT"], writes=[("T", 8)])
        P.tt(t2[:], ps[3][:], A.sinT[:, cols], ALU.mult, reads=[("ps", 3), "sinT"], writes=[("T", 9)])
        P.tt(out_ap, t1[:], t2[:], ALU.add, reads=[("T", 8), ("T", 9)], writes=out_keys)

    for gi in range(ngroups):
        gslot = gi % 2
        a = gi // 2
        if gi + 1 < ngroups:
            ws.load_group(dr["w_in"], (gi + 1) * 512, 4, (gi + 1) % 2)
        if gi % 2 == 0:
            for tc in range(NT):
                cols = slice(tc * TC, (tc + 1) * TC)
                for i in range(2):
                    bank = nb()
                    inproj_fm(P, C, gslot, i, tc, bank)
                    norm_rope(bank, 1, A.KT[:, i, cols], [("KT", i, tc)], tc)
                for half in range(2):
                    bank = nb()
                    tile0 = tc * 4
                    v_tokmajor(P, C, gslot, 2 + half, tc, bank, 10 + half, A.Vt[:, tile0:tile0 + 4, half * 128:(half + 1) * 128],
                               [("Vt", tile0 + q_, half) for q_ in range(4)])
        else:
            for tc in range(NT):
                cols = slice(tc * TC, (tc + 1) * TC)
                Qt = [B[4], B[5]]
                sgt = [B[6], B[7]]
                for i in range(2):
                    bank = nb()
                    inproj_fm(P, C, gslot, i, tc, bank)
                    norm_rope(bank, 0, Qt[i][:], [("B", 4 + i)], tc)
                for i in range(2):
                    bank = nb()
                    inproj_fm(P, C, gslot, 2 + i, tc, bank)
                    silu2(P, C, sgt[i][:], ("B", 6 + i), bank)
                for qt in range(4):
                    j = tc * 4 + qt
                    qs = slice(qt * 128, (qt + 1) * 128)

                    def emit_qk(i):
                        sb_ = 4 if i % 2 == 0 else 7
                        ks = slice(i * 128, (i + 1) * 128)
                        for m in range(2):
                            P.mm(ps[sb_][:, m * 128:(m + 1) * 128], A.KT[:, m, ks], Qt[m][:, qs], start=True, stop=True,
                                 reads=[("KT", m, i // 4), ("B", 4 + m)], writes=[("ps", sb_)])
                        pt = A.pt[i % 2]
                        P.act(pt[:], ps[sb_][:, 0:256], AF.Exp, reads=[("ps", sb_)], writes=[("pt", i % 2)], scale=SCALE)
                        if i == j:
                            P.tt(pt[:], pt[:], A.tri2[:], ALU.mult, reads=[("pt", i % 2), "tri2"], writes=[("pt", i % 2)])

                    def emit_pv(i):
                        pt = A.pt[i % 2]
                        for m in range(2):
                            P.mm(ps[5 + m][:, 0:257], pt[:, m * 128:(m + 1) * 128], A.Vt[:, i, 0:257],
                                 start=(i == 0), stop=(i == j),
                                 reads=[("pt", i % 2), ("Vt", i, 0), ("Vt", i, 1), "Vt_ones"], writes=[("ps", 5 + m)])

                    emit_qk(0)
                    for i in range(j + 1):
                        if i + 1 <= j:
                            emit_qk(i + 1)
                        emit_pv(i)
                    sm = A.sm
                    P.recip(sm[:, 0:1], ps[5][:, 256:257], reads=[("ps", 5)], writes=["sm"])
                    P.recip(sm[:, 1:2], ps[6][:, 256:257], reads=[("ps", 6), "sm"], writes=["sm"])
                    P.tt(sm[:, 2:3], sm[:, 1:2], A.neglam[:], ALU.mult, reads=["sm", "neglam"], writes=["sm"])
                    P.add("dve", lambda e, sm=sm: e.tensor_scalar_mul(A.o[:], ps[5][:, 0:256], sm[:, 0:1]), [("ps", 5), "sm"], ["o"])
                    P.stt(A.o[:], ps[6][:, 0:256], sm[:, 2:3], A.o[:], ALU.mult, ALU.add, reads=[("ps", 6), "sm", "o"], writes=["o"])
                    P.add("act", lambda e, sm=sm: e.activation(A.junk[:], A.o[:], AF.Square, accum_out=sm[:, 3:4]), ["o"], ["junk", "sm3"])
                    rstd_act(P, sm[:, 5:6], "sm5", sm[:, 3:4], ["sm3"], A.eps_sub[:], "eps_sub", 1.0 / 256)
                    P.stt(A.on[:], A.o[:], sm[:, 5:6], A.subg[:], ALU.mult, ALU.mult, reads=["o", "sm5", "subg"], writes=["on"])
                    trv = ps[3][:].bitcast(BF16)
                    for m in range(2):
                        P.tr(trv[:, m * 128:(m + 1) * 128], A.on[:, m * 128:(m + 1) * 128], C.ident_b[:],
                             reads=["on", "ident_b"], writes=[("ps", 3)])
                    for m in range(2):
                        P.stt(B[8 + m][:, qs], trv[:, m * 128:(m + 1) * 128], 0.5, sgt[m][:, qs], ALU.mult, ALU.mult,
                              reads=[("ps", 3), ("B", 6 + m)], writes=[("B", 8 + m)])
                for m in range(2):
                    fc = 2 * a + m
                    P.dma(dr["mixT"][fc * 128:(fc + 1) * 128, cols], B[8 + m][:], slot=("mo", m),
                          reads=[("B", 8 + m)], writes=[("mixT", dr.get("mix_tag", 0), fc, tc)])
    if do_outproj:
        emit_outproj(P, C, ws, dr, 8)


def build_att_program(S, n_part, layer):
    nc = bass.Bass("TRN2", target_bir_lowering=False)
    di = lambda name, shape, dt=F32: nc.dram_tensor(name, shape, dt, kind="ExternalInput").ap()
    do = lambda name, shape, dt=F32: nc.dram_tensor(name, shape, dt, kind="ExternalOutput").ap()
    dr = {}
    dr["xT"] = di("xT", [D, S])
    dr["parts"] = [di(f"part{i}", [D, S]) for i in range(n_part)]
    dr["gnorm"] = di("gnorm", [128, KC])
    dr["w_in"] = di("w_in", [D, 4096])
    dr["w_out"] = di("w_out", [1024, D])
    dr["cosT"] = di("cosT", [128, S])
    dr["sinT"] = di("sinT", [128, S])
    dr["pmat"] = di("pmat", [128, 128])
    dr["qkg"] = di("qkg", [128, 2])
    dr["subg"] = di("subg", [128, 256])
    dr["lp"] = di("lp", [128, 4])
    dr["tri2"] = di("tri2", [128, 256])
    dr["ones"] = di("ones", [128, 128])
    dr["ident"] = di("ident", [128, 128])
    if n_part:
        dr["xout"] = do("xout", [D, S])
    dr["yp"] = do("yp", [D, S])
    dr["mixT"] = nc.dram_tensor("mixT", [1024, S], BF16, kind="Internal").ap()
    lam_init = 0.8 - 0.6 * math.exp(-0.3 * layer)
    with ExitStack() as st:
        C = alloc_common(nc, st, S, "att")
        C.eps_norm = st.enter_context(nc.sbuf_tensor("s_eps_norm", [128, 1], F32))
        C.one_c = st.enter_context(nc.sbuf_tensor("s_one_c", [128, 1], F32))
        alloc_att(nc, st, C)
        P = Prog(nc)
        emit_consts(P, C, dr)
        emit_prologue(P, C, dr, n_part, bool(n_part))
        emit_att_layer(P, C, dr, lam_init)
        n = P.emit()
    return nc, n


def att_layout(inp, j, r):
    w_in = inp["att_w_in"][j]
    w_out = inp["att_w_out"][j]
    heads = [4 * r + a for a in range(4)]
    cols = []
    for h in heads:
        cols.append(np.arange(2048 + 256 * h, 2048 + 256 * h + 256))
        cols.append(np.arange(4096 + 256 * h, 4096 + 256 * h + 256))
        cols.append(np.arange(256 * h, 256 * h + 256))
        cols.append(np.arange(6144 + 256 * h, 6144 + 256 * h + 256))
    cols = np.concatenate(cols)
    rows = np.concatenate([np.arange(256 * h, 256 * h + 256) for h in heads])
    qkg = np.stack([inp["att_q_norm"][j], inp["att_k_norm"][j]], axis=-1).astype(np.float32)
    subg = np.ascontiguousarray(np.broadcast_to(inp["att_sub_norm"][j][None, :], (128, 256)).astype(np.float32))
    lp = np.ascontiguousarray(inp["att_lambda"][j].T.astype(np.float32))
    return dict(
        w_in=np.ascontiguousarray(w_in[:, cols]),
        w_out=np.ascontiguousarray(w_out[rows, :]),
        gnorm=pvec(inp["att_norm"][j]),
        qkg=np.ascontiguousarray(qkg), subg=subg, lp=lp,
    )


def build_final_program(S):
    nc = bass.Bass("TRN2", target_bir_lowering=False)
    di = lambda name, shape, dt=F32: nc.dram_tensor(name, shape, dt, kind="ExternalInput").ap()
    x = di("xT", [1024, S])
    p0 = di("part0", [1024, S])
    p1 = di("part1", [1024, S])
    out = nc.dram_tensor("xout", [1024, S], F32, kind="ExternalOutput").ap()
    with ExitStack() as st:
        tl = [[st.enter_context(nc.sbuf_tensor(f"s_f{k}{i}", [128, S], F32)) for i in range(2)] for k in range(3)]
        P = Prog(nc)
        for c in range(8):
            sl = c % 2
            rows = slice(c * 128, (c + 1) * 128)
            xa, pa, pb = tl[0][sl], tl[1][sl], tl[2][sl]
            P.dma(xa[:], x[rows, :], slot=("fx", sl), reads=[], writes=[("fx", sl)])
            P.dma(pa[:], p0[rows, :], slot=("fa", sl), reads=[], writes=[("fa", sl)])
            P.dma(pb[:], p1[rows, :], slot=("fb", sl), reads=[], writes=[("fb", sl)])
            P.tt(xa[:], xa[:], pa[:], ALU.add, reads=[("fx", sl), ("fa", sl)], writes=[("fx", sl)])
            P.tt(xa[:], xa[:], pb[:], ALU.add, reads=[("fx", sl), ("fb", sl)], writes=[("fx", sl)])
            P.dma(out[rows, :], xa[:], slot=("fo", sl), reads=[("fx", sl)], writes=[("out", c)])
        P.emit()
    return nc


REC_KEYS = ("w_in", "w_out", "gnorm", "convw", "recvec", "hglb", "onorm", "wgate")
ATT_KEYS = ("w_in", "w_out", "gnorm", "qkg", "subg", "lp")
CONST_KEYS = ("ones", "ident", "cmask", "bmask", "cosT", "sinT", "pmat", "tri2")
ACTIVE_CORES = (0, 1, 2, 3)
NCORES = 4


def build_fused_program(S, nlayers=4, nseq=1):
    nc = bass.Bass("TRN2", target_bir_lowering=False)
    di = lambda name, shape, dt=F32: nc.dram_tensor(name, shape, dt, kind="ExternalInput").ap()
    x_ins = [di("xT" if q == 0 else f"xT{q}", [D, S]) for q in range(nseq)]
    outs = [nc.dram_tensor("outT" if q == 0 else f"outT{q}", [D, S], F32, kind="ExternalOutput").ap() for q in range(nseq)]
    yp = [nc.dram_tensor(f"yp{i}", [D, S], F32, kind="Internal").ap() for i in range(2)]
    xb = [nc.dram_tensor(f"xbuf{i}", [D, S], F32, kind="Internal").ap() for i in range(2)]
    mixTs = [nc.dram_tensor(f"mixT{i}", [2048, S], BF16, kind="Internal").ap() for i in range(2)]
    cshape = dict(ones=[128, 128], ident=[128, 128], cmask=[128, TC], bmask=[128, 128], cosT=[128, S], sinT=[128, S],
                  pmat=[128, 128], tri2=[128, 256])
    cd = {k: di(k, cshape[k]) for k in CONST_KEYS}
    rshape = dict(w_in=[D, 6144], w_out=[2048, D], gnorm=[128, KC], convw=[128, 8, 4], recvec=[128, 8, 4], hglb=[128, 8, 2],
                  onorm=[128, 1], wgate=[4, 128, 2, 2, 256])
    ashape = dict(w_in=[D, 4096], w_out=[1024, D], gnorm=[128, KC], qkg=[128, 2], subg=[128, 256], lp=[128, 4])
    drs = []
    for L in range(nlayers):
        sh = rshape if L % 2 == 0 else ashape
        halves = []
        for hf in range(2):
            dr = {k: di(f"{k}{L}h{hf}", sh[k]) for k in sh if not (k == "gnorm" and hf == 1)}
            if hf == 1:
                dr["gnorm"] = halves[0]["gnorm"]
            dr.update(cd)
            dr["mixT"] = mixTs[hf] if L % 2 == 0 else mixTs[hf][0:1024, :]
            dr["mix_tag"] = hf
            dr["yp"] = yp[hf]
            dr["yp_tag"] = hf
            dr["fc_off"] = 8 * hf if L % 2 == 1 else 0
            dr["lbflag"] = 0.0 if L == 0 else 1.0
            halves.append(dr)
        drs.append(halves)
    with ExitStack() as st:
        C = alloc_common(nc, st, S, "all")
        C.eps_norm = st.enter_context(nc.sbuf_tensor("s_eps_norm", [128, 1], F32))
        C.one_c = st.enter_context(nc.sbuf_tensor("s_one_c", [128, 1], F32))
        mx = st.enter_context(nc.sbuf_tensor("s_mx", [128, 5644], F32))
        alloc_rec(nc, st, C, mx)
        alloc_att(nc, st, C, mx)
        P = Prog(nc)
        emit_consts(P, C, cd)
        for q in range(nseq):
          x_in, out = x_ins[q], outs[q]
          for L in range(nlayers):
              x_cur = x_in if L == 0 else xb[(L - 1) % 2]
              x_next = out if L == nlayers - 1 else xb[L % 2]
              pr = dict(gnorm=drs[L][0]["gnorm"], xT=x_cur, parts=[])
              emit_prologue(P, C, pr, 0, False, stats_ready=(L > 0))
              P.barrier()
              for hf in range(2):
                  if L % 2 == 0:
                      emit_rec_layer(P, C, drs[L][hf], do_outproj=False)
                  else:
                      emit_att_layer(P, C, drs[L][hf], 0.8 - 0.6 * math.exp(-0.3 * L), do_outproj=False)
              xk = (lambda dc, tc: [("xout", dc, tc)])
              yk = (lambda dc, tc: [("yp", 0, dc, tc)])
              drs[L][0]["evac"] = dict(src=x_cur, src_keys=(xk if L > 0 else (lambda dc, tc: [])), dst=yp[0], dst_keys=yk)
              drs[L][1]["evac"] = dict(src=yp[0], src_keys=yk, dst=x_next, dst_keys=xk, stats=(L < nlayers - 1))
              if L % 2 == 0:
                  for hf in range(2):
                      emit_outproj(P, C, WStream(P, C), drs[L][hf], 16)
              else:
                  emit_outproj_pair(P, C, WStream(P, C), drs[L][0], drs[L][1],
                                    dict(src=x_cur, src_keys=xk, dst=x_next, dst_keys=xk, stats=(L < nlayers - 1)))
          P.barrier()
        n = P.emit()
    return nc, n


def fused_maps(inp, S=SEQ, nlayers=4, ncores=2, nseq=2):
    cs = host_consts(S)
    x = np.asarray(inp["x"], np.float32)
    wmap = {}
    for L in range(nlayers):
        j = L // 2
        for hf in range(2):
            lay = rec_layout(inp, j, hf) if L % 2 == 0 else att_layout(inp, j, hf)
            for k, v in lay.items():
                if k == "gnorm" and hf == 1:
                    continue
                wmap[f"{k}{L}h{hf}"] = v
    maps = []
    for c in range(ncores):
        m = {}
        for q in range(nseq):
            b = c * nseq + q
            m["xT" if q == 0 else f"xT{q}"] = np.ascontiguousarray(x[b][:S].T)
        m.update(wmap)
        for k in CONST_KEYS:
            m[k] = cs[k]
        maps.append(m)
    return maps


_PROGS = {}
NCORES = 2
NSEQ = 2


def kernel(**inp):
    inp = {k: np.asarray(v) for k, v in inp.items()}
    if "fused" not in _PROGS:
        _PROGS["fused"] = build_fused_program(SEQ, 4, NSEQ)[0]
    nc = _PROGS["fused"]
    maps = fused_maps(inp, SEQ, 4, NCORES, NSEQ)
    res = run_bass_kernel_spmd(nc, maps, core_ids=list(range(NCORES)))
    out = np.empty((BATCH, SEQ, D), np.float32)
    for c in range(NCORES):
        for q in range(NSEQ):
            out[c * NSEQ + q] = np.asarray(res.results[c]["outT" if q == 0 else f"outT{q}"]).T
    return out
```
